# Optimizing a Trainium2 kernel written in Bass

```python
import jax, jax.numpy as jnp
from jax import lax
import numpy as np

D_MODEL = 1024
BATCH = 2
SEQ = 8192
DEPTH = 2
DEC_BATCH = 128
DEC_SEQ = 8
PAST_LEN = 2048
PAGE_SIZE = 128

HG_HEADS = 4
HG_KEY = 128
HG_VAL = 128
HG_QK_W = HG_HEADS * HG_KEY
HG_V_W = HG_HEADS * HG_VAL
HG_CHUNK = 64
FOX_HEADS = 8
FOX_HD = 64
FOX_W = FOX_HEADS * FOX_HD
Q_BLOCK = 128
AB_SPLIT_SIZES = (HG_QK_W, HG_QK_W, HG_V_W, HG_V_W, FOX_W, FOX_W, FOX_W, FOX_HEADS)
AB_IN = sum(AB_SPLIT_SIZES)
AB_OUT = HG_V_W + FOX_W
C_GROUPS = 8
C_GROUP_DIM = D_MODEL // C_GROUPS
C_HALF = C_GROUPS * C_GROUP_DIM
C_CHUNK = 128
FF_HIDDEN = 2816
N_AB = (DEPTH + 1) // 2
N_C = DEPTH // 2
ALPHA = (2.0 * DEPTH) ** 0.25
BETA = (8.0 * DEPTH) ** -0.25
NORM_EPS = 1e-5

kernel_name = 'hgrn2_fox_chunkmlp_macaron_deepnorm_step'


def layer_norm(x, g, b):
    xf = x.astype(jnp.float32)
    mu = jnp.mean(xf, axis=-1, keepdims=True)
    var = jnp.mean(jnp.square(xf - mu), axis=-1, keepdims=True)
    return ((xf - mu) * lax.rsqrt(var + NORM_EPS) * g.astype(jnp.float32) + b.astype(jnp.float32)).astype(x.dtype)


def rms_norm(x, g):
    xf = x.astype(jnp.float32)
    return xf * lax.rsqrt(jnp.mean(jnp.square(xf), axis=-1, keepdims=True) + NORM_EPS) * g.astype(jnp.float32)


def swiglu(x, w_gate, w_up, w_down):
    return (jax.nn.silu(x @ w_gate) * (x @ w_up)) @ w_down


def split_cols(h, sizes):
    idx = np.cumsum(sizes)[:-1].tolist()
    return jnp.split(h, idx, axis=-1)


def hgrn2_chunked(q, k, v, log_f, s0):
    B, T, H, K = q.shape
    L = min(HG_CHUNK, T)
    pad = (-T) % L
    padw = ((0, 0), (0, pad), (0, 0), (0, 0))
    q, k, v, log_f = [jnp.pad(a, padw) for a in (q, k, v, log_f)]
    n = (T + pad) // L

    def to_chunks(a):
        return a.reshape(B, n, L, H, a.shape[-1]).transpose(1, 0, 3, 2, 4)

    causal = jnp.tril(jnp.ones((L, L), bool))[:, :, None]

    def step(s, inp):
        qb, kb, vb, gb = inp
        b = jnp.cumsum(gb, axis=2)
        diff = b[:, :, :, None, :] - b[:, :, None, :, :]
        decay = jnp.exp(jnp.where(causal, diff, -jnp.inf))
        attn = jnp.einsum('bhtk,bhsk,bhtsk->bhts', qb, kb, decay)
        o = jnp.einsum('bhts,bhsv->bhtv', attn, vb) + jnp.einsum('bhtk,bhkv->bhtv', qb * jnp.exp(b), s)
        b_last = b[:, :, -1, :]
        s_new = jnp.exp(b_last)[..., None] * s + jnp.einsum('bhsk,bhsv->bhkv', kb * jnp.exp(b_last[:, :, None, :] - b), vb)
        return s_new, o

    s_T, oc = lax.scan(step, s0, tuple(to_chunks(a) for a in (q, k, v, log_f)))
    o = oc.transpose(1, 0, 3, 2, 4).reshape(B, n * L, H, v.shape[-1])[:, :T]
    return o, s_T


def fox_block(qb, fqb, pb, k, v, fk, k_pos):
    s = jnp.einsum('bqhd,bkhd->bhqk', qb, k).astype(jnp.float32) * (FOX_HD ** -0.5)
    s = s + jnp.transpose(fqb, (0, 2, 1))[:, :, :, None] - jnp.transpose(fk, (0, 2, 1))[:, :, None, :]
    s = jnp.where(pb[:, None] >= k_pos[None, :], s, -jnp.inf)
    p = jax.nn.softmax(s, axis=-1)
    return jnp.einsum('bhqk,bkhd->bqhd', p.astype(v.dtype), v)


def fox_attention(q, k, v, fq, fk, q_pos, k_pos):
    B, Tq, H, D = q.shape
    blk = min(Q_BLOCK, Tq)
    pad = (-Tq) % blk
    nb = (Tq + pad) // blk
    q = jnp.pad(q, ((0, 0), (0, pad), (0, 0), (0, 0)))
    fq = jnp.pad(fq, ((0, 0), (0, pad), (0, 0)))
    q_pos = jnp.pad(q_pos, (0, pad), mode='edge')
    qb = q.reshape(B, nb, blk, H, D).transpose(1, 0, 2, 3, 4)
    fqb = fq.reshape(B, nb, blk, H).transpose(1, 0, 2, 3)
    pb = q_pos.reshape(nb, blk)
    o = lax.map(lambda a: fox_block(a[0], a[1], a[2], k, v, fk, k_pos), (qb, fqb, pb))
    return o.transpose(1, 0, 2, 3, 4).reshape(B, nb * blk, H, D)[:, :Tq]


def parallel_ab_mixer(x, s0, past, lb, w_in, norm_g, f_bias, w_out):
    B, T, _ = x.shape
    hq, hf, hi, hgate, fq, fk, fv, ff = split_cols(x @ w_in, AB_SPLIT_SIZES)
    q_h = jax.nn.silu(hq.astype(jnp.float32)).reshape(B, T, HG_HEADS, HG_KEY)
    lb_h = lb.reshape(HG_HEADS, HG_KEY)
    f_gate = lb_h + (1.0 - lb_h) * jax.nn.sigmoid(hf.astype(jnp.float32).reshape(B, T, HG_HEADS, HG_KEY))
    i_h = hi.astype(jnp.float32).reshape(B, T, HG_HEADS, HG_VAL)
    o_h, s_T = hgrn2_chunked(q_h, 1.0 - f_gate, i_h, jnp.log(f_gate), s0.astype(jnp.float32))
    o_h = rms_norm(o_h, norm_g) * jax.nn.silu(hgate.astype(jnp.float32).reshape(B, T, HG_HEADS, HG_VAL))
    o_h = o_h.reshape(B, T, HG_V_W).astype(x.dtype)
    q_f = fq.reshape(B, T, FOX_HEADS, FOX_HD)
    k_f = fk.reshape(B, T, FOX_HEADS, FOX_HD)
    v_f = fv.reshape(B, T, FOX_HEADS, FOX_HD)
    lf = jax.nn.log_sigmoid(ff.astype(jnp.float32) + f_bias.astype(jnp.float32))
    if past is None:
        P = 0
        k_all, v_all, lf_all = k_f, v_f, lf
    else:
        P = past[0].shape[1]
        k_all = jnp.concatenate([past[0].astype(k_f.dtype), k_f], axis=1)
        v_all = jnp.concatenate([past[1].astype(v_f.dtype), v_f], axis=1)
        lf_all = jnp.concatenate([past[2].astype(jnp.float32), lf], axis=1)
    F = jnp.cumsum(lf_all, axis=1)
    q_pos = P + jnp.arange(T)
    k_pos = jnp.arange(P + T)
    o_f = fox_attention(q_f, k_all, v_all, F[:, P:], F, q_pos, k_pos).reshape(B, T, FOX_W).astype(x.dtype)
    out = jnp.concatenate([o_h, o_f], axis=-1) @ w_out
    return out, s_T, k_f, v_f, lf


def chunk_gating_mixer(x, w_in, ln_g, ln_b, w_s, b_s, w_out):
    B, T, _ = x.shape
    u, v = jnp.split(jax.nn.gelu(x @ w_in), 2, axis=-1)
    v = layer_norm(v, ln_g, ln_b)
    pad = (-T) % C_CHUNK
    n = (T + pad) // C_CHUNK
    vc = jnp.pad(v, ((0, 0), (0, pad), (0, 0))).reshape(B, n, C_CHUNK, C_GROUPS, C_GROUP_DIM)
    w = jnp.where(jnp.tril(jnp.ones((C_CHUNK, C_CHUNK), bool)), w_s, 0)
    mixed = jnp.einsum('gts,bnsgc->bntgc', w, vc) + jnp.transpose(b_s)[:, :, None]
    mixed = mixed.reshape(B, n * C_CHUNK, C_HALF)[:, :T]
    return (u * mixed) @ w_out, v


def run_trunk(x, hg_s0, fox_past, prm):
    B = x.shape[0]
    hg_new, k_new, v_new, lf_new, cv_new = [], [], [], [], []
    lb_all = jnp.cumsum(jax.nn.softmax(prm['hg_lb_logits'].astype(jnp.float32), axis=0), axis=0)
    for l in range(DEPTH):
        j = l // 2
        x = layer_norm(ALPHA * x + 0.5 * swiglu(x, prm['ffn_w_gate'][l, 0], prm['ffn_w_up'][l, 0], prm['ffn_w_down'][l, 0]), prm['ln_g'][l, 0], prm['ln_b'][l, 0])
        if l % 2 == 0:
            s0 = jnp.zeros((B, HG_HEADS, HG_KEY, HG_VAL), jnp.float32) if hg_s0 is None else hg_s0[j]
            past = None if fox_past is None else (fox_past[0][j], fox_past[1][j], fox_past[2][j])
            mix, s_T, k_f, v_f, lf = parallel_ab_mixer(x, s0, past, lb_all[l], prm['ab_w_in'][j], prm['hg_norm_g'][j], prm['fox_f_bias'][j], prm['ab_w_out'][j])
            hg_new.append(s_T)
            k_new.append(k_f)
            v_new.append(v_f)
            lf_new.append(lf)
        else:
            mix, v_rows = chunk_gating_mixer(x, prm['c_w_in'][j], prm['c_ln_g'][j], prm['c_ln_b'][j], prm['c_w_s'][j], prm['c_b_s'][j], prm['c_w_out'][j])
            cv_new.append(v_rows)
        x = layer_norm(ALPHA * x + mix, prm['ln_g'][l, 1], prm['ln_b'][l, 1])
        x = layer_norm(ALPHA * x + 0.5 * swiglu(x, prm['ffn_w_gate'][l, 1], prm['ffn_w_up'][l, 1], prm['ffn_w_down'][l, 1]), prm['ln_g'][l, 2], prm['ln_b'][l, 2])
    return x, hg_new, k_new, v_new, lf_new, cv_new


def setup_inputs(seed: int = 0) -> dict:
    key = jax.random.key(seed)
    keys = jax.random.split(key, 24)

    def nrm(i, shape, scale):
        return jax.random.normal(keys[i], shape, jnp.float32) * scale

    n_pages = PAST_LEN // PAGE_SIZE
    n_used = DEC_BATCH * n_pages
    n_phys = (5 * n_used) // 4
    page_table = jax.random.permutation(keys[6], n_phys)[:n_used].reshape(DEC_BATCH, n_pages).astype(jnp.int32)
    return {
        'x_prompt': nrm(0, (BATCH, SEQ, D_MODEL), 1.0),
        'x_sample': nrm(1, (DEC_BATCH, DEC_SEQ, D_MODEL), 1.0),
        'cache_fox_k': nrm(2, (N_AB, n_phys, PAGE_SIZE, FOX_HEADS, FOX_HD), 1.0),
        'cache_fox_v': nrm(3, (N_AB, n_phys, PAGE_SIZE, FOX_HEADS, FOX_HD), 1.0),
        'cache_fox_logf': jax.nn.log_sigmoid(nrm(4, (N_AB, n_phys, PAGE_SIZE, FOX_HEADS), 1.0) + 2.0),
        'state_hg': nrm(5, (N_AB, DEC_BATCH, HG_HEADS, HG_KEY, HG_VAL), 0.5),
        'page_table': page_table,
        'ln_g': 1.0 + nrm(7, (DEPTH, 3, D_MODEL), 0.02),
        'ln_b': nrm(8, (DEPTH, 3, D_MODEL), 0.02),
        'ffn_w_gate': nrm(9, (DEPTH, 2, D_MODEL, FF_HIDDEN), D_MODEL ** -0.5),
        'ffn_w_up': nrm(10, (DEPTH, 2, D_MODEL, FF_HIDDEN), D_MODEL ** -0.5),
        'ffn_w_down': nrm(11, (DEPTH, 2, FF_HIDDEN, D_MODEL), FF_HIDDEN ** -0.5 * BETA),
        'ab_w_in': nrm(12, (N_AB, D_MODEL, AB_IN), D_MODEL ** -0.5),
        'hg_lb_logits': nrm(13, (DEPTH + 1, HG_QK_W), 0.5),
        'hg_norm_g': 1.0 + nrm(14, (N_AB, HG_VAL), 0.02),
        'fox_f_bias': 2.0 + nrm(15, (N_AB, FOX_HEADS), 0.5),
        'ab_w_out': nrm(16, (N_AB, AB_OUT, D_MODEL), AB_OUT ** -0.5 * BETA),
        'c_w_in': nrm(17, (N_C, D_MODEL, 2 * C_HALF), D_MODEL ** -0.5),
        'c_ln_g': 1.0 + nrm(18, (N_C, C_HALF), 0.02),
        'c_ln_b': nrm(19, (N_C, C_HALF), 0.02),
        'c_w_s': nrm(20, (N_C, C_GROUPS, C_CHUNK, C_CHUNK), C_CHUNK ** -0.5),
        'c_b_s': 1.0 + nrm(21, (N_C, C_GROUPS, C_CHUNK), 0.1),
        'c_w_out': nrm(22, (N_C, C_HALF, D_MODEL), C_HALF ** -0.5 * BETA),
    }


def reference(x_prompt, x_sample, cache_fox_k, cache_fox_v, cache_fox_logf, state_hg, page_table,
              ln_g, ln_b, ffn_w_gate, ffn_w_up, ffn_w_down, ab_w_in, hg_lb_logits, hg_norm_g, fox_f_bias,
              ab_w_out, c_w_in, c_ln_g, c_ln_b, c_w_s, c_b_s, c_w_out):
    prm = dict(ln_g=ln_g, ln_b=ln_b, ffn_w_gate=ffn_w_gate, ffn_w_up=ffn_w_up, ffn_w_down=ffn_w_down,
               ab_w_in=ab_w_in, hg_lb_logits=hg_lb_logits, hg_norm_g=hg_norm_g, fox_f_bias=fox_f_bias,
               ab_w_out=ab_w_out, c_w_in=c_w_in, c_ln_g=c_ln_g, c_ln_b=c_ln_b, c_w_s=c_w_s, c_b_s=c_b_s,
               c_w_out=c_w_out)
    y_prompt, hg_p, k_p, v_p, lf_p, _ = run_trunk(x_prompt, None, None, prm)
    dec_b, n_pages = page_table.shape
    past_len = n_pages * PAGE_SIZE
    past_k = cache_fox_k[:, page_table].reshape(N_AB, dec_b, past_len, FOX_HEADS, FOX_HD)
    past_v = cache_fox_v[:, page_table].reshape(N_AB, dec_b, past_len, FOX_HEADS, FOX_HD)
    past_lf = cache_fox_logf[:, page_table].reshape(N_AB, dec_b, past_len, FOX_HEADS)
    y_sample, hg_s, k_s, v_s, lf_s, cv_s = run_trunk(x_sample, state_hg, (past_k, past_v, past_lf), prm)
    return (y_prompt, y_sample,
            jnp.stack(k_p), jnp.stack(v_p), jnp.stack(lf_p), jnp.stack(hg_p).astype(x_prompt.dtype),
            jnp.stack(k_s), jnp.stack(v_s), jnp.stack(lf_s), jnp.stack(hg_s).astype(state_hg.dtype),
            jnp.stack(cv_s))
```

```python
import os
import numpy as np
import concourse.bass as bass
import concourse.mybir as mybir
from concourse.bass_utils import run_bass_kernel_spmd

F32 = mybir.dt.float32
BF16 = mybir.dt.bfloat16
I32 = mybir.dt.int32
AF = mybir.ActivationFunctionType
ALU = mybir.AluOpType

D = 1024
FF = 2816
NJ = 22
NPT = 64
NT = 65
NTOK = NT * 128
ALPHA = 4.0 ** 0.25
EPS = 1e-5
N_PHYS = 2560
ENGS = ("tensor", "vector", "scalar", "gpsimd", "sync")
N_DMA_SEMS = 16
SEM_EPOCH = 16000

C_ID, C_TRI, C_SEL127, C_TRI2, C_BLK2, C_TRI16, C_BLK16, C_SELEND2, C_SELEND16, C_R, C_SUP, C_ONES, C_NEG = range(13)
NCST = 13


def make_consts():
    c = np.zeros((128, NCST, 128), np.float32)
    s = np.arange(128)[:, None]
    t = np.arange(128)[None, :]
    c[:, C_ID] = (s == t)
    c[:, C_TRI] = (s <= t)
    c[:, C_SEL127] = (s == 127) * np.ones((1, 128))
    c[:, C_TRI2] = (s // 64 == t // 64) & (s <= t)
    c[:, C_BLK2] = (s // 64 == t // 64)
    c[:, C_TRI16] = (s // 8 == t // 8) & (s <= t)
    c[:, C_BLK16] = (s // 8 == t // 8)
    c[:, C_SELEND2][:, 0] = (np.arange(128) == 63)
    c[:, C_SELEND2][:, 1] = (np.arange(128) == 127)
    for q in range(16):
        c[q * 8 + 7, C_SELEND16, q] = 1.0
    for i in range(8):
        c[i, C_R, i::8] = 1.0
    c[:, C_SUP] = (s > t)
    c[:, C_ONES] = 1.0
    c[:, C_NEG] = -1.0
    return c


class Sched:
    def __init__(self, nc):
        self.nc = nc
        self.q = {e: [] for e in ENGS}
        self.cnt = {e: 0 for e in ENGS}
        self.epoch = {e: 0 for e in ENGS}
        self.sem = {e: nc.alloc_semaphore(f"s_{e}_0") for e in ENGS}
        self.dsem = {e: [nc.alloc_semaphore(f"d_{e}_{i}") for i in range(N_DMA_SEMS)]
                     for e in ("sync", "scalar", "gpsimd")}
        self.dcnt = {e: [0] * N_DMA_SEMS for e in ("sync", "scalar", "gpsimd")}
        self.dnext = {e: 0 for e in ("sync", "scalar", "gpsimd")}
        self.known = {e: {} for e in ENGS}
        self.lastw = {}
        self.reads = {}
        self.semobj = {}
        for e in ENGS:
            self.semobj[("e", e, 0)] = self.sem[e]
        for e in self.dsem:
            for i, s in enumerate(self.dsem[e]):
                self.semobj[("d", e, i)] = s
        self.out_deps = []
        self.alias = {}

    def _need(self, eng, deps):
        need = {}
        for (k, v) in deps:
            if self.known[eng].get(k, 0) >= v:
                continue
            if need.get(k, 0) < v:
                need[k] = v
        for k, v in need.items():
            self.known[eng][k] = v
        return list(need.items())

    def _deps(self, reads, writes):
        reads = [self.alias.get(r, r) for r in reads]
        writes = [self.alias.get(w, w) for w in writes]
        deps = []
        for r in reads:
            if r in self.lastw:
                deps.append(self.lastw[r])
        for w in writes:
            if w in self.lastw:
                deps.append(self.lastw[w])
            deps.extend(self.reads.get(w, ()))
        return deps

    def _commit(self, tok, reads, writes):
        reads = [self.alias.get(r, r) for r in reads]
        writes = [self.alias.get(w, w) for w in writes]
        for r in reads:
            self.reads.setdefault(r, []).append(tok)
        for w in writes:
            self.lastw[w] = tok
            self.reads[w] = []

    def op(self, eng, fn, reads=(), writes=()):
        pr = [r for r in reads if r.startswith("ps")]
        if pr:
            reads = [r for r in reads if not r.startswith("ps")]
            writes = list(writes) + pr
        deps = self._deps(reads, writes)
        waits = self._need(eng, deps)
        if self.cnt[eng] >= SEM_EPOCH:
            self.epoch[eng] += 1
            self.cnt[eng] = 0
            self.sem[eng] = self.nc.alloc_semaphore(f"s_{eng}_{self.epoch[eng]}")
            self.semobj[("e", eng, self.epoch[eng])] = self.sem[eng]
        self.cnt[eng] += 1
        key = ("e", eng, self.epoch[eng])
        tok = (key, self.cnt[eng])
        if eng == "tensor":
            self.known[eng][key] = self.cnt[eng]
        self.q[eng].append((waits, fn, (self.sem[eng], 1)))
        self._commit(tok, reads, writes)
        return tok

    def dma(self, eng, fn, reads=(), writes=(), is_output=False):
        deps = self._deps(reads, writes)
        i = self.dnext[eng]
        self.dnext[eng] = (i + 1) % N_DMA_SEMS
        key = ("d", eng, i)
        if self.dcnt[eng][i] > 0:
            deps.append((key, self.dcnt[eng][i]))
        waits = self._need(eng, deps)
        self.dcnt[eng][i] += 16
        tok = (key, self.dcnt[eng][i])
        self.q[eng].append((waits, fn, (self.dsem[eng][i], 16)))
        self._commit(tok, reads, writes)
        if is_output:
            self.out_deps.append(tok)
        return tok

    def emit(self):
        nc = self.nc
        fin = list(self.out_deps)
        for r, t in self.lastw.items():
            fin.append(t)
        waits = self._need("sync", fin)
        self.q["sync"].append((waits, None, None))
        with nc.Block() as block:
            def run(engname):
                def body(eng):
                    for waits, fn, inc in self.q[engname]:
                        for k, v in waits:
                            eng.wait_ge(self.semobj[k], v)
                        if fn is not None:
                            ins = fn(eng)
                            ins.then_inc(inc[0], inc[1])
                return body
            block.tensor(run("tensor"))
            block.vector(run("vector"))
            block.scalar(run("scalar"))
            block.gpsimd(run("gpsimd"))
            block.sync(run("sync"))


def build_program(NPT=NPT, N_PHYS=N_PHYS, NS=4):
    NT = NPT + NS
    NTOK = NT * 128
    SMP = NPT
    nc = bass.Bass("TRN2", target_bir_lowering=False)
    S = Sched(nc)

    def din(name, shape, dt=F32):
        return nc.dram_tensor(name, list(shape), dt, kind="ExternalInput").ap()

    def dout(name, shape, dt=F32):
        return nc.dram_tensor(name, list(shape), dt, kind="ExternalOutput").ap()

    def dscr(name, shape, dt):
        return nc.dram_tensor(name, list(shape), dt).ap()

    def sb(name, shape, dt=F32):
        return nc.alloc_sbuf_tensor(name, list(shape), dt)

    xin = din("xin", [NTOK, D])
    ck = din("cache_k", [N_PHYS * 128, 512])
    cv_ = din("cache_v", [N_PHYS * 128, 512])
    clf = din("cache_lf", [N_PHYS * 128, 8])
    st_hg = din("state_hg", [NS * 16 * 4 * 128, 128])
    ptab = din("ptab", [1, NS * 256], I32)
    ln_g = din("ln_g", [6, D])
    ln_b = din("ln_b", [6, D])
    wgate = din("w_gate", [4, D, FF])
    wup = din("w_up", [4, D, FF])
    wdown = din("w_down", [4, FF, D])
    w_in = din("ab_w_in", [D, 3592])
    lb_log = din("lb_logits", [1, 3 * 512])
    hg_ng = din("hg_norm_g", [1, 128])
    fbias = din("fox_f_bias", [1, 8])
    w_out = din("ab_w_out", [D, D])
    c_w_in = din("c_w_in", [D, 2048])
    c_ln_g = din("c_ln_g", [1, D])
    c_ln_b = din("c_ln_b", [1, D])
    c_w_s = din("c_w_s", [8 * 128, 128])
    c_b_s = din("c_b_s", [8, 128])
    c_w_out = din("c_w_out", [D, D])
    cst_d = din("cst", [128, NCST * 128])

    y_o = dout("y", [NTOK, D])
    k_o = dout("k_new", [NTOK, 512])
    v_o = dout("v_new", [NTOK, 512])
    lf_o = dout("lf_new", [NTOK, 8])
    hgp_o = dout("hg_p", [4 * 128, 128])
    hgs_o = dout("hg_s", [NS * 16 * 4 * 128, 128])
    cv_o = dout("cv_s", [NS * 128, D])

    wg_scr = dscr("wg_scr", [4 * NJ * 128, 1024], BF16)
    wu_scr = dscr("wu_scr", [4 * NJ * 128, 1024], BF16)
    wd_scr = dscr("wd_scr", [4 * 4 * 128, NJ * 256], BF16)
    win_scr = dscr("win_scr", [128, 8 * 3592], BF16)
    wout_scr = dscr("wout_scr", [128, 8 * D], BF16)
    cwin_scr = dscr("cwin_scr", [128, 8 * 2048], BF16)
    cwout_scr = dscr("cwout_scr", [128, 8 * D], BF16)
    x1_scr = dscr("x1_scr", [NTOK, D], F32)
    oh_scr = dscr("oh_scr", [NTOK, 512], BF16)
    of_scr = dscr("of_scr", [NTOK, 512], BF16)
    qT_scr = dscr("qT_scr", [NT * 128, 512], BF16)
    kT_scr = dscr("kT_scr", [NT * 128, 512], BF16)
    vp_scr = dscr("vp_scr", [NT * 128, 8 * 65], BF16)

    cst = sb("cstt", [128, NCST, 128])
    cstb = sb("cstb", [128, NCST, 128], BF16)
    identb = cstb[:, C_ID, :]
    ps = [nc.alloc_psum_tensor(f"ps{i}", [128, 512], F32) for i in range(8)]

    def psb(i):
        return ps[i][:].bitcast(BF16)

    xg = sb("xg", [128, 4, D])
    xT = sb("xT", [128, 8, 512], BF16)
    hT = sb("hT", [128, NJ, 512], BF16)
    wgs = [sb(f"wgs{i}", [128, 8, 128], BF16) for i in range(2)]
    wus = [sb(f"wus{i}", [128, 8, 128], BF16) for i in range(2)]
    wds = sb("wds", [128, NJ, 256], BF16)
    wps = [sb(f"wps{i}", [128, 8, 512], BF16) for i in range(2)]
    gb = [sb(f"gb{i}", [128, D]) for i in range(2)]
    zt = sb("zt", [128, D])
    tf = [sb(f"tf{i}", [128, D]) for i in range(4)]
    tb = [sb(f"tb{i}", [128, D], BF16) for i in range(4)]
    S.alias.update({"qh": "tf0", "kk": "tf0", "gl": "tf1", "gs": "tf1", "bcs": "tf2", "ebl": "tf2", "fk_f": "tf3", "fv_f": "tf3",
                    "vh": "tb0", "qtl": "tb0", "ktl": "tb1", "khat": "tb1", "ohb": "tb2", "fq_b": "tb2", "fk_b": "tb3",
                    "uu": "tf0", "vv": "tf1", "vvb": "tb0", "umx": "tb1", "ocat": "tb2", "tmpA": "tf2"})
    xb16 = sb("xb16", [128, D], BF16)
    sg_t = sb("sg_t", [128, 512], BF16)
    stats = sb("stats", [128, 2, 6])
    mv = sb("mv", [128, 2])
    rstd = sb("rstd", [128, 1])

    S.dma("sync", lambda e: e.dma_start(out=cst[:].rearrange("p a b -> p (a b)"), in_=cst_d[:, :]), writes=["cst"])
    S.op("vector", lambda e: e.tensor_copy(out=cstb[:], in_=cst[:]), reads=["cst"], writes=["cstb"])

    def V(fn, reads=(), writes=()):
        return S.op("vector", fn, reads, writes)

    def A(fn, reads=(), writes=()):
        return S.op("scalar", fn, reads, writes)

    def PE(fn, reads=(), writes=()):
        return S.op("tensor", fn, reads, writes)

    def bcast_load(dst, src_row, res):
        S.dma("sync", lambda e: e.dma_start(out=dst, in_=src_row.partition_broadcast(128)), writes=[res])

    def transpose_to(dstT, dst_res, src_bf, src_res, nk, col0, psi=7):
        pb = psb(psi)
        for k in range(nk):
            PE(lambda e, k=k: e.transpose(pb[:, k * 128:(k + 1) * 128], src_bf[:, k * 128:(k + 1) * 128], identb),
               reads=[src_res, "cstb"], writes=[f"ps{psi}"])
        V(lambda e: e.tensor_copy(out=dstT[:, 0:nk, col0:col0 + 128],
                                  in_=pb[:, 0:nk * 128].rearrange("p (k t) -> p k t", k=nk)),
          reads=[f"ps{psi}"], writes=[dst_res])

    def layer_norm(src, src_res, dst, dst_res, gi, g_ap=None, b_ap=None):
        ga = ln_g[gi:gi + 1, :] if g_ap is None else g_ap
        ba = ln_b[gi:gi + 1, :] if b_ap is None else b_ap
        bcast_load(gb[0][:], ga, "gb0")
        bcast_load(gb[1][:], ba, "gb1")
        for h in range(2):
            V(lambda e, h=h: e.bn_stats(out=stats[:, h, :], in_=src[:, h * 512:(h + 1) * 512]),
              reads=[src_res], writes=["stats"] if h == 0 else ["stats"])
        V(lambda e: e.bn_aggr(out=mv[:], in_=stats[:].rearrange("p a b -> p (a b)")), reads=["stats"], writes=["mv"])
        V(lambda e: e.tensor_scalar(out=rstd[:], in0=mv[:, 1:2], scalar1=EPS, scalar2=None, op0=ALU.add), reads=["mv"], writes=["rstd"])
        A(lambda e: e.activation(out=rstd[:], in_=rstd[:], func=AF.Sqrt), reads=["rstd"], writes=["rstd"])
        V(lambda e: e.reciprocal(out=rstd[:], in_=rstd[:]), reads=["rstd"], writes=["rstd"])
        V(lambda e: e.tensor_scalar(out=dst, in0=src, scalar1=mv[:, 0:1], scalar2=rstd[:, 0:1], op0=ALU.subtract, op1=ALU.mult),
          reads=[src_res, "mv", "rstd"], writes=[dst_res])
        V(lambda e: e.tensor_tensor(out=dst, in0=dst, in1=gb[0][:], op=ALU.mult), reads=[dst_res, "gb0"], writes=[dst_res])
        V(lambda e: e.tensor_tensor(out=dst, in0=dst, in1=gb[1][:], op=ALU.add), reads=[dst_res, "gb1"], writes=[dst_res])

    def build_xT(nt):
        for ti in range(nt):
            A(lambda e, ti=ti: e.activation(out=xb16[:], in_=xg[:, ti, :], func=AF.Copy), reads=[f"xg{ti}"], writes=["xb16"])
            transpose_to(xT, "xT", xb16, "xb16", 8, ti * 128)

    def ffn(fi, nt):
        ntk = nt * 128
        build_xT(nt)
        for j in range(NJ):
            b = j % 2
            rj = (fi * NJ + j) * 128
            S.dma("sync", lambda e, rj=rj, b=b: e.dma_start(out=wgs[b][:].rearrange("p k m -> p (k m)"), in_=wg_scr[rj:rj + 128, :]),
                  reads=[f"S_wg{fi}"], writes=[f"wgs{b}"])
            S.dma("sync", lambda e, rj=rj, b=b: e.dma_start(out=wus[b][:].rearrange("p k m -> p (k m)"), in_=wu_scr[rj:rj + 128, :]),
                  reads=[f"S_wu{fi}"], writes=[f"wus{b}"])
            pg, pu = ps[2 * b], ps[2 * b + 1]
            for k in range(8):
                PE(lambda e, k=k, b=b, pg=pg: e.matmul(pg[:, 0:ntk], lhsT=wgs[b][:, k, :], rhs=xT[:, k, 0:ntk], start=(k == 0), stop=(k == 7)),
                   reads=[f"wgs{b}", "xT"], writes=[f"ps{2 * b}"])
            for k in range(8):
                PE(lambda e, k=k, b=b, pu=pu: e.matmul(pu[:, 0:ntk], lhsT=wus[b][:, k, :], rhs=xT[:, k, 0:ntk], start=(k == 0), stop=(k == 7)),
                   reads=[f"wus{b}", "xT"], writes=[f"ps{2 * b + 1}"])
            A(lambda e, pg=pg: e.activation(out=sg_t[:, 0:ntk], in_=pg[:, 0:ntk], func=AF.Silu), reads=[f"ps{2 * b}"], writes=["sg_t"])
            V(lambda e, j=j, pu=pu: e.tensor_tensor(out=hT[:, j, 0:ntk], in0=sg_t[:, 0:ntk], in1=pu[:, 0:ntk], op=ALU.mult),
              reads=["sg_t", f"ps{2 * b + 1}"], writes=["hT"])
        li = (fi // 2) * 3 + (0 if fi % 2 == 0 else 2)
        for c in range(4):
            cs = slice(c * 256, (c + 1) * 256)
            rq = (fi * 4 + c) * 128
            S.dma("sync", lambda e, rq=rq: e.dma_start(out=wds[:].rearrange("p j n -> p (j n)"), in_=wd_scr[rq:rq + 128, :]), reads=[f"S_wd{fi}"], writes=["wds"])
            for ti in range(nt):
                pd = ps[4 + (ti % 2)]
                for j in range(NJ):
                    PE(lambda e, j=j, ti=ti, pd=pd: e.matmul(pd[:, 0:256], lhsT=hT[:, j, ti * 128:(ti + 1) * 128], rhs=wds[:, j, :], start=(j == 0), stop=(j == NJ - 1)),
                       reads=["hT", "wds"], writes=[f"ps{4 + ti % 2}"])
                A(lambda e, pd=pd, cs=cs: e.activation(out=zt[:, cs], in_=pd[:, 0:256], func=AF.Copy, scale=0.5),
                  reads=[f"ps{4 + ti % 2}"], writes=["zt"])
                V(lambda e, ti=ti, cs=cs: e.scalar_tensor_tensor(out=xg[:, ti, cs], in0=xg[:, ti, cs], scalar=ALPHA,
                                                                in1=zt[:, cs], op0=ALU.mult, op1=ALU.add),
                  reads=["zt", f"xg{ti}"], writes=[f"xg{ti}"])
        for ti in range(nt):
            layer_norm(xg[:, ti, :], f"xg{ti}", xg[:, ti, :], f"xg{ti}", li)

    def proj_tokmajor(w_ap, c0, ncols, ti, psi, wb):
        w3, wres = WSCR[w_ap]
        S.dma("sync", lambda e: e.dma_start(out=wps[wb][:, :, 0:ncols], in_=w3[:, :, c0:c0 + ncols]), reads=[wres], writes=[f"wps{wb}"])
        for k in range(8):
            PE(lambda e, k=k: e.matmul(ps[psi][:, 0:ncols], lhsT=xT[:, k, ti * 128:(ti + 1) * 128], rhs=wps[wb][:, k, 0:ncols], start=(k == 0), stop=(k == 7)),
               reads=["xT", f"wps{wb}"], writes=[f"ps{psi}"])

    stg = [sb(f"stg{i}", [128, 8, 256], BF16) for i in range(2)]
    stg_i = [0]

    def conv_cols(src, ncols, dst3, res):
        for c0 in range(0, ncols, 256):
            w = min(256, ncols - c0)
            b = stg_i[0] % 2; stg_i[0] += 1
            S.dma("gpsimd", lambda e, b=b, c0=c0, w=w: e.dma_start(out=stg[b][:, :, 0:w], in_=src[:, c0:c0 + w].rearrange("(k p) n -> p k n", p=128)), writes=[f"stg{b}"])
            S.dma("gpsimd", lambda e, b=b, c0=c0, w=w: e.dma_start(out=dst3[:, :, c0:c0 + w], in_=stg[b][:, :, 0:w]), reads=[f"stg{b}"], writes=[res])

    def conv_gu(src, scr, fi, res):
        for jj in range(NJ // 2):
            b = stg_i[0] % 2; stg_i[0] += 1
            S.dma("gpsimd", lambda e, b=b, jj=jj: e.dma_start(out=stg[b][:], in_=src[fi, :, jj * 256:(jj + 1) * 256].rearrange("(k p) n -> p k n", p=128)), writes=[f"stg{b}"])
            for c in range(2):
                r0 = (fi * NJ + jj * 2 + c) * 128
                S.dma("gpsimd", lambda e, b=b, c=c, r0=r0: e.dma_start(out=scr[r0:r0 + 128, :].rearrange("p (k m) -> p k m", k=8), in_=stg[b][:, :, c * 128:(c + 1) * 128]),
                      reads=[f"stg{b}"], writes=[res])

    def conv_down(fi, res):
        for q in range(4):
            r0 = (fi * 4 + q) * 128
            for (j0, nj) in ((0, 8), (8, 8), (16, 6)):
                b = stg_i[0] % 2; stg_i[0] += 1
                S.dma("gpsimd", lambda e, b=b, q=q, j0=j0, nj=nj: e.dma_start(
                    out=stg[b][:, 0:nj, :], in_=wdown[fi, j0 * 128:(j0 + nj) * 128, q * 256:(q + 1) * 256].rearrange("(j p) n -> p j n", p=128)), writes=[f"stg{b}"])
                S.dma("gpsimd", lambda e, b=b, r0=r0, j0=j0, nj=nj: e.dma_start(
                    out=wd_scr[r0:r0 + 128, :].rearrange("p (j n) -> p j n", j=NJ)[:, j0:j0 + nj, :], in_=stg[b][:, 0:nj, :]), reads=[f"stg{b}"], writes=[res])

    def conv_ffn(fi):
        conv_gu(wgate, wg_scr, fi, f"S_wg{fi}")
        conv_gu(wup, wu_scr, fi, f"S_wu{fi}")
        conv_down(fi, f"S_wd{fi}")

    win3 = win_scr.rearrange("p (k n) -> p k n", k=8)
    wout3 = wout_scr.rearrange("p (k n) -> p k n", k=8)
    cwin3 = cwin_scr.rearrange("p (k n) -> p k n", k=8)
    cwout3 = cwout_scr.rearrange("p (k n) -> p k n", k=8)
    conv_ffn(0)
    conv_cols(w_in, 3592, win3, "S_win")
    conv_cols(w_out, D, wout3, "S_wout")
    conv_ffn(1)
    conv_ffn(2)
    conv_cols(c_w_in, 2048, cwin3, "S_cwin")
    conv_cols(c_w_out, D, cwout3, "S_cwout")
    conv_ffn(3)
    WSCR = {"w_in": (win3, "S_win"), "w_out": (wout3, "S_wout"), "c_w_in": (cwin3, "S_cwin"), "c_w_out": (cwout3, "S_cwout")}

    lbt = [tf[0][:, 0:512], tf[0][:, 512:1024], tf[1][:, 0:512]]
    LR = ["tf0", "tf1"]
    lb = sb("lb", [128, 512])
    oml = sb("oml", [128, 512])
    tmpA = tf[2][:, 0:512]
    ng_bc = sb("ng_bc", [128, 128])
    fb_bc = sb("fb_bc", [128, 8])
    for a in range(3):
        S.dma("sync", lambda e, a=a: e.dma_start(out=lbt[a], in_=lb_log[0:1, a * 512:(a + 1) * 512].partition_broadcast(128)), writes=LR)
    bcast_load(ng_bc[:], hg_ng[0:1, :], "ng_bc")
    bcast_load(fb_bc[:], fbias[0:1, :], "fb_bc")
    V(lambda e: e.tensor_tensor(out=tmpA, in0=lbt[0], in1=lbt[1], op=ALU.max), reads=LR, writes=["tmpA"])
    V(lambda e: e.tensor_tensor(out=tmpA, in0=tmpA, in1=lbt[2], op=ALU.max), reads=LR + ["tmpA"], writes=["tmpA"])
    for a in range(3):
        V(lambda e, a=a: e.tensor_tensor(out=lbt[a], in0=lbt[a], in1=tmpA, op=ALU.subtract), reads=LR + ["tmpA"], writes=LR)
        A(lambda e, a=a: e.activation(out=lbt[a], in_=lbt[a], func=AF.Exp), reads=LR, writes=LR)
    V(lambda e: e.tensor_tensor(out=tmpA, in0=lbt[0], in1=lbt[1], op=ALU.add), reads=LR, writes=["tmpA"])
    V(lambda e: e.tensor_tensor(out=tmpA, in0=tmpA, in1=lbt[2], op=ALU.add), reads=LR + ["tmpA"], writes=["tmpA"])
    V(lambda e: e.reciprocal(out=tmpA, in_=tmpA), reads=["tmpA"], writes=["tmpA"])
    V(lambda e: e.tensor_tensor(out=lb[:], in0=lbt[0], in1=tmpA, op=ALU.mult), reads=LR + ["tmpA"], writes=["lb"])
    V(lambda e: e.tensor_scalar(out=oml[:], in0=lb[:], scalar1=-1.0, scalar2=1.0, op0=ALU.mult, op1=ALU.add), reads=["lb"], writes=["oml"])

    Fk = sb("Fk", [128, NT, 8])
    Fend = sb("Fend", [128, NT, 8])
    Sst = [sb(f"Sst{h}", [128, 128]) for h in range(4)]
    Sb = [sb(f"Sb{h}", [128, 128], BF16) for h in range(4)]
    qh = tf[0][:, 0:512]
    kk = tf[0][:, 512:1024]
    gl = tf[1][:, 0:512]
    vh = tb[0][:, 0:512]
    gs = tf[1][:, 512:1024]
    bcs = tf[2][:, 0:512]
    ebl = tf[2][:, 512:1024]
    qtl = tb[0][:, 512:1024]
    ktl = tb[1][:, 0:512]
    khat = tb[1][:, 512:1024]
    qtT = sb("qtT", [128, 128], BF16)
    ktT = sb("ktT", [128, 128], BF16)
    qtT0 = sb("qtT0", [128, 128], BF16)
    qtT1 = sb("qtT1", [128, 128], BF16)
    attT = sb("attT", [128, 128], BF16)
    dcol = sb("dcol", [128, 16])
    ohb = tb[2][:, 0:512]
    ssq = sb("ssq", [128, 4])
    junk = sb("junk", [128, 128])
    fq_b = tb[2][:, 512:1024]
    fk_f = tf[3][:, 0:512]
    fk_b = tb[3][:, 0:512]
    fv_f = tf[3][:, 512:1024]
    vp = sb("vp", [128, 8, 65], BF16)
    lft = sb("lft", [128, 8])
    tq = sb("tq", [128, 4, 128], BF16)
    S0f = sb("S0f", [128, 16, 128])
    S0b = sb("S0b", [128, 16, 128], BF16)
    qz = sb("qz", [128, 16, 128], BF16)
    vbd = sb("vbd", [128, 16, 128], BF16)

    V(lambda e: e.memset(qtT0[:], 0.0), writes=["qtT0"])
    V(lambda e: e.memset(qtT1[:], 0.0), writes=["qtT1"])
    V(lambda e: e.memset(qz[:], 0.0), writes=["qz"])
    V(lambda e: e.memset(vp[:], 1.0), writes=["vp"])
    for h in range(4):
        V(lambda e, h=h: e.memset(Sst[h][:], 0.0), writes=[f"Sst{h}"])
        V(lambda e, h=h: e.memset(Sb[h][:], 0.0), writes=[f"Sb{h}"])

    KSTAGE = int(os.environ.get('KSTAGE', '9'))
    dcs = sb("dcs", [128, 64])

    def phaseA_tile(T, ti, sample):
        r0 = T * 128
        tri = C_TRI16 if sample else C_TRI2
        blk = C_BLK16 if sample else C_BLK2
        proj_tokmajor("w_in", 0, 512, ti, 0, 0)
        A(lambda e: e.activation(out=qh[:], in_=ps[0][:], func=AF.Silu), reads=["ps0"], writes=["qh"])
        proj_tokmajor("w_in", 512, 512, ti, 1, 1)
        A(lambda e: e.activation(out=kk[:], in_=ps[1][:], func=AF.Sigmoid), reads=["ps1"], writes=["kk"])
        V(lambda e: e.tensor_tensor(out=gl[:], in0=kk[:], in1=oml[:], op=ALU.mult), reads=["kk", "oml"], writes=["gl"])
        V(lambda e: e.tensor_tensor(out=gl[:], in0=gl[:], in1=lb[:], op=ALU.add), reads=["gl", "lb"], writes=["gl"])
        V(lambda e: e.tensor_scalar(out=kk[:], in0=gl[:], scalar1=-1.0, scalar2=1.0, op0=ALU.mult, op1=ALU.add), reads=["gl"], writes=["kk"])
        A(lambda e: e.activation(out=gl[:], in_=gl[:], func=AF.Ln), reads=["gl"], writes=["gl"])
        proj_tokmajor("w_in", 1024, 512, ti, 0, 0)
        A(lambda e: e.activation(out=vh[:], in_=ps[0][:], func=AF.Copy), reads=["ps0"], writes=["vh"])
        proj_tokmajor("w_in", 1536, 512, ti, 1, 1)
        A(lambda e: e.activation(out=gs[:], in_=ps[1][:], func=AF.Silu), reads=["ps1"], writes=["gs"])
        for h in range(4):
            V(lambda e, h=h: e.tensor_tensor(out=gs[:, h * 128:(h + 1) * 128], in0=gs[:, h * 128:(h + 1) * 128], in1=ng_bc[:], op=ALU.mult),
              reads=["gs", "ng_bc"], writes=["gs"])
        if KSTAGE < 2:
            return
        PE(lambda e: e.matmul(ps[2][:], lhsT=cst[:, tri, :], rhs=gl[:], start=True, stop=True), reads=["cst", "gl"], writes=["ps2"])
        PE(lambda e: e.matmul(ps[3][:], lhsT=cst[:, blk, :], rhs=gl[:], start=True, stop=True), reads=["cst", "gl"], writes=["ps3"])
        A(lambda e: e.activation(out=bcs[:], in_=ps[2][:], func=AF.Exp), reads=["ps2"], writes=["bcs"])
        V(lambda e: e.tensor_tensor(out=qtl[:], in0=qh[:], in1=bcs[:], op=ALU.mult), reads=["qh", "bcs"], writes=["qtl"])
        A(lambda e: e.activation(out=bcs[:], in_=ps[2][:], func=AF.Exp, scale=-1.0), reads=["ps2"], writes=["bcs"])
        V(lambda e: e.tensor_tensor(out=ktl[:], in0=kk[:], in1=bcs[:], op=ALU.mult), reads=["kk", "bcs"], writes=["ktl"])
        A(lambda e: e.activation(out=ebl[:], in_=ps[3][:], func=AF.Exp), reads=["ps3"], writes=["ebl"])
        V(lambda e: e.tensor_tensor(out=bcs[:], in0=bcs[:], in1=ebl[:], op=ALU.mult), reads=["bcs", "ebl"], writes=["bcs"])
        V(lambda e: e.tensor_tensor(out=khat[:], in0=kk[:], in1=bcs[:], op=ALU.mult), reads=["kk", "bcs"], writes=["khat"])
        nb = 16 if sample else 2
        selc = C_SELEND16 if sample else C_SELEND2
        for h in range(4):
            PE(lambda e, h=h: e.matmul(ps[3][:, 256 + h * 16:256 + h * 16 + nb], lhsT=ebl[:, h * 128:(h + 1) * 128], rhs=cst[:, selc, 0:nb], start=True, stop=True),
               reads=["ebl", "cst"], writes=["ps3"])
        if sample:
            sq0 = (T - SMP) * 16
            V(lambda e: e.tensor_copy(out=dcs[:], in_=ps[3][:, 256:320]), reads=["ps3"], writes=["dcs"])
        else:
            V(lambda e: e.tensor_copy(out=dcol[:].rearrange("p (h c) -> p h c", h=4)[:, :, 0:2],
                                      in_=ps[3][:, 256:320].rearrange("p (h c) -> p h c", h=4)[:, :, 0:2]), reads=["ps3"], writes=["dcol"])
        if KSTAGE < 3:
            return
        for h in range(4):
            hs = slice(h * 128, (h + 1) * 128)
            pb = psb(7)
            PE(lambda e, hs=hs: e.transpose(pb[:, 0:128], qtl[:, hs], identb), reads=["qtl", "cstb"], writes=["ps7"])
            PE(lambda e, hs=hs: e.transpose(pb[:, 128:256], ktl[:, hs], identb), reads=["ktl", "cstb"], writes=["ps7"])
            V(lambda e: e.tensor_copy(out=qtT[:], in_=pb[:, 0:128]), reads=["ps7"], writes=["qtT"])
            A(lambda e: e.activation(out=ktT[:], in_=pb[:, 128:256], func=AF.Copy), reads=["ps7"], writes=["ktT"])
            PE(lambda e: e.matmul(ps[6][:, 0:128], lhsT=ktT[:], rhs=qtT[:], start=True, stop=True), reads=["ktT", "qtT"], writes=["ps6"])
            V(lambda e: e.tensor_tensor(out=attT[:], in0=ps[6][:, 0:128], in1=cst[:, tri, :], op=ALU.mult), reads=["ps6", "cst"], writes=["attT"])
            if not sample:
                V(lambda e: e.tensor_copy(out=qtT0[:, 0:64], in_=qtT[:, 0:64]), reads=["qtT"], writes=["qtT0"])
                V(lambda e: e.tensor_copy(out=qtT1[:, 64:128], in_=qtT[:, 64:128]), reads=["qtT"], writes=["qtT1"])
                PE(lambda e, hs=hs: e.matmul(ps[6][:, 128:256], lhsT=khat[0:64, hs], rhs=vh[0:64, hs], start=True, stop=True),
                   reads=["khat", "vh"], writes=["ps6"])
                PE(lambda e, hs=hs: e.matmul(ps[5][:, 0:128], lhsT=attT[:], rhs=vh[:, hs], start=True, stop=False), reads=["attT", "vh"], writes=["ps5"])
                PE(lambda e, h=h: e.matmul(ps[5][:, 0:128], lhsT=qtT0[:], rhs=Sb[h][:], start=False, stop=False), reads=["qtT0", f"Sb{h}"], writes=["ps5"])
                V(lambda e, h=h: e.scalar_tensor_tensor(out=Sst[h][:], in0=Sst[h][:], scalar=dcol[:, h * 4:h * 4 + 1], in1=ps[6][:, 128:256], op0=ALU.mult, op1=ALU.add),
                  reads=[f"Sst{h}", "dcol", "ps6"], writes=[f"Sst{h}"])
                A(lambda e, h=h: e.activation(out=Sb[h][:], in_=Sst[h][:], func=AF.Copy), reads=[f"Sst{h}"], writes=[f"Sb{h}"])
                PE(lambda e, h=h: e.matmul(ps[5][:, 0:128], lhsT=qtT1[:], rhs=Sb[h][:], start=False, stop=True), reads=["qtT1", f"Sb{h}"], writes=["ps5"])
                PE(lambda e, hs=hs: e.matmul(ps[6][:, 256:384], lhsT=khat[64:128, hs], rhs=vh[64:128, hs], start=True, stop=True),
                   reads=["khat", "vh"], writes=["ps6"])
                V(lambda e, h=h: e.scalar_tensor_tensor(out=Sst[h][:], in0=Sst[h][:], scalar=dcol[:, h * 4 + 1:h * 4 + 2], in1=ps[6][:, 256:384], op0=ALU.mult, op1=ALU.add),
                  reads=[f"Sst{h}", "dcol", "ps6"], writes=[f"Sst{h}"])
                A(lambda e, h=h: e.activation(out=Sb[h][:], in_=Sst[h][:], func=AF.Copy), reads=[f"Sst{h}"], writes=[f"Sb{h}"])
            else:
                S.dma("sync", lambda e, h=h: e.dma_start(out=S0f[:], in_=st_hg.rearrange("(q h p) v -> h p q v", h=4, p=128)[h][:, sq0:sq0 + 16, :]), writes=["S0f"])
                A(lambda e: e.activation(out=S0b[:], in_=S0f[:], func=AF.Copy), reads=["S0f"], writes=["S0b"])
                for q in range(16):
                    V(lambda e, q=q: e.tensor_copy(out=qz[:, q, q * 8:(q + 1) * 8], in_=qtT[:, q * 8:(q + 1) * 8]), reads=["qtT"], writes=["qz"])
                PE(lambda e, hs=hs: e.matmul(ps[5][:, 0:128], lhsT=attT[:], rhs=vh[:, hs], start=True, stop=False), reads=["attT", "vh"], writes=["ps5"])
                for q in range(16):
                    PE(lambda e, q=q, h=h: e.matmul(ps[5][:, 0:128], lhsT=qz[:, q, :], rhs=S0b[:, q, :], start=False, stop=(q == 15)),
                       reads=["qz", "S0b"], writes=["ps5"])
                for q in range(16):
                    V(lambda e, q=q, hs=hs: e.tensor_scalar(out=vbd[:, q, :], in0=vh[:, hs], scalar1=cst[:, C_BLK16, q * 8:q * 8 + 1], scalar2=None, op0=ALU.mult),
                      reads=["vh", "cst"], writes=["vbd"])
                for c4 in range(4):
                    PE(lambda e, c4=c4, hs=hs: e.matmul(ps[c4][:, :], lhsT=khat[:, hs], rhs=vbd[:, c4 * 4:(c4 + 1) * 4, :].rearrange("p a b -> p (a b)"), start=True, stop=True),
                       reads=["khat", "vbd"], writes=[f"ps{c4}"])
                for q in range(16):
                    V(lambda e, q=q, h=h: e.scalar_tensor_tensor(out=S0f[:, q, :], in0=S0f[:, q, :], scalar=dcs[:, h * 16 + q:h * 16 + q + 1],
                                                                 in1=ps[q // 4][:, (q % 4) * 128:(q % 4 + 1) * 128], op0=ALU.mult, op1=ALU.add),
                      reads=["S0f", "dcs", f"ps{q // 4}"], writes=["S0f"])
                S.dma("sync", lambda e, h=h: e.dma_start(out=hgs_o.rearrange("(q h p) v -> h p q v", h=4, p=128)[h][:, sq0:sq0 + 16, :], in_=S0f[:]), reads=["S0f"], is_output=True)
            A(lambda e, h=h: e.activation(out=junk[:], in_=ps[5][:, 0:128], func=AF.Square, accum_out=ssq[:, h:h + 1]), reads=["ps5"], writes=["junk", "ssq"])
            V(lambda e, h=h: e.tensor_scalar(out=ssq[:, h:h + 1], in0=ssq[:, h:h + 1], scalar1=1.0 / 128, scalar2=EPS, op0=ALU.mult, op1=ALU.add), reads=["ssq"], writes=["ssq"])
            A(lambda e, h=h: e.activation(out=ssq[:, h:h + 1], in_=ssq[:, h:h + 1], func=AF.Sqrt), reads=["ssq"], writes=["ssq"])
            V(lambda e, h=h: e.reciprocal(out=ssq[:, h:h + 1], in_=ssq[:, h:h + 1]), reads=["ssq"], writes=["ssq"])
            V(lambda e, h=h, hs=hs: e.scalar_tensor_tensor(out=ohb[:, hs], in0=ps[5][:, 0:128], scalar=ssq[:, h:h + 1], in1=gs[:, hs], op0=ALU.mult, op1=ALU.mult),
              reads=["ps5", "ssq", "gs"], writes=["ohb"])
        if KSTAGE < 4:
            return
        S.dma("sync", lambda e: e.dma_start(out=oh_scr[r0:r0 + 128, :], in_=ohb[:]), reads=["ohb"], writes=["oh_scr"])
        proj_tokmajor("w_in", 2048, 512, ti, 0, 0)
        A(lambda e: e.activation(out=fq_b[:], in_=ps[0][:], func=AF.Copy, scale=0.125), reads=["ps0"], writes=["fq_b"])
        proj_tokmajor("w_in", 2560, 512, ti, 1, 1)
        A(lambda e: e.activation(out=fk_f[:], in_=ps[1][:], func=AF.Copy), reads=["ps1"], writes=["fk_f"])
        V(lambda e: e.tensor_copy(out=fk_b[:], in_=ps[1][:]), reads=["ps1"], writes=["fk_b"])
        S.dma("sync", lambda e: e.dma_start(out=k_o[r0:r0 + 128, :], in_=fk_f[:]), reads=["fk_f"], is_output=True)
        proj_tokmajor("w_in", 3072, 512, ti, 0, 0)
        A(lambda e: e.activation(out=fv_f[:], in_=ps[0][:], func=AF.Copy), reads=["ps0"], writes=["fv_f"])
        V(lambda e: e.tensor_copy(out=vp[:, :, 0:64], in_=ps[0][:].rearrange("p (h d) -> p h d", h=8)), reads=["ps0"], writes=["vp"])
        S.dma("sync", lambda e: e.dma_start(out=v_o[r0:r0 + 128, :], in_=fv_f[:]), reads=["fv_f"], is_output=True)
        S.dma("sync", lambda e: e.dma_start(out=vp_scr[r0:r0 + 128, :], in_=vp[:].rearrange("p h d -> p (h d)")), reads=["vp"], writes=["vp_scr"])
        if KSTAGE < 5:
            return
        proj_tokmajor("w_in", 3584, 8, ti, 1, 1)
        V(lambda e: e.tensor_tensor(out=lft[:], in0=ps[1][:, 0:8], in1=fb_bc[:], op=ALU.add), reads=["ps1", "fb_bc"], writes=["lft"])
        A(lambda e: e.activation(out=lft[:], in_=lft[:], func=AF.Exp, scale=-1.0), reads=["lft"], writes=["lft"])
        A(lambda e: e.activation(out=lft[:], in_=lft[:], func=AF.Ln, bias=1.0), reads=["lft"], writes=["lft"])
        V(lambda e: e.tensor_scalar(out=lft[:], in0=lft[:], scalar1=-1.0, scalar2=None, op0=ALU.mult), reads=["lft"], writes=["lft"])
        S.dma("sync", lambda e: e.dma_start(out=lf_o[r0:r0 + 128, :], in_=lft[:]), reads=["lft"], is_output=True)
        if not sample:
            PE(lambda e: e.matmul(ps[2][:, 0:8], lhsT=cst[:, C_TRI, :], rhs=lft[:], start=True, stop=(T == 0)), reads=["cst", "lft"], writes=["ps2"])
            if T > 0:
                PE(lambda e: e.matmul(ps[2][:, 0:8], lhsT=cst[:, C_SEL127, :], rhs=Fk[:, T - 1, :], start=False, stop=True), reads=["cst", "Fk"], writes=["ps2"])
            V(lambda e: e.tensor_copy(out=Fk[:, T, :], in_=ps[2][:, 0:8]), reads=["ps2"], writes=["Fk"])
            PE(lambda e: e.matmul(ps[2][:, 8:16], lhsT=cst[:, C_SEL127, :], rhs=Fk[:, T, :], start=True, stop=True), reads=["cst", "Fk"], writes=["ps2"])
            V(lambda e: e.tensor_copy(out=Fend[:, T, :], in_=ps[2][:, 8:16]), reads=["ps2"], writes=["Fend"])
        else:
            PE(lambda e: e.matmul(ps[2][:, 0:8], lhsT=cst[:, C_TRI16, :], rhs=lft[:], start=True, stop=True), reads=["cst", "lft"], writes=["ps2"])
            V(lambda e: e.tensor_copy(out=Fk[:, T, :], in_=ps[2][:, 0:8]), reads=["ps2"], writes=["Fk"])
        for (src, res, scr) in ((fq_b, "fq_b", qT_scr), (fk_b, "fk_b", kT_scr)):
            transpose_to(tq, "tq", src, res, 4, 0, psi=7)
            S.dma("sync", lambda e, scr=scr: e.dma_start(out=scr[r0:r0 + 128, :], in_=tq[:].rearrange("p a b -> p (a b)")), reads=["tq"], writes=[scr.tensor.name])

    PH0 = os.environ.get('KPH', 'ABSC')
    for G in range(NPT // 4):
        for ti in range(4):
            T = G * 4 + ti
            S.dma("sync", lambda e, T=T, ti=ti: e.dma_start(out=xg[:, ti, :], in_=xin[T * 128:(T + 1) * 128, :]), writes=[f"xg{ti}"])
        if 'f' not in PH0:
            ffn(0, 4)
        build_xT(4)
        for ti in range(4):
            T = G * 4 + ti
            S.dma("sync", lambda e, T=T, ti=ti: e.dma_start(out=x1_scr[T * 128:(T + 1) * 128, :], in_=xg[:, ti, :]), reads=[f"xg{ti}"], writes=["x1_scr"])
            if 'F' in PH0:
                S.dma("sync", lambda e, T=T, ti=ti: e.dma_start(out=y_o[T * 128:(T + 1) * 128, :], in_=xg[:, ti, :]), reads=[f"xg{ti}"], is_output=True)
            else:
                phaseA_tile(T, ti, False)
    for h in range(4):
        S.dma("sync", lambda e, h=h: e.dma_start(out=hgp_o[h * 128:(h + 1) * 128, :], in_=Sst[h][:]), reads=[f"Sst{h}"], is_output=True)
    for st in range(NS):
        S.dma("sync", lambda e, st=st: e.dma_start(out=xg[:, st, :], in_=xin[(SMP + st) * 128:(SMP + st + 1) * 128, :]), writes=[f"xg{st}"])
    if 'F' in PH0:
        S.emit()
        return nc
    ffn(0, NS)
    build_xT(NS)
    for st in range(NS):
        S.dma("sync", lambda e, st=st: e.dma_start(out=x1_scr[(SMP + st) * 128:(SMP + st + 1) * 128, :], in_=xg[:, st, :]), reads=[f"xg{st}"], writes=["x1_scr"])
        phaseA_tile(SMP + st, st, True)

    PH = os.environ.get('KPH', 'ABSC')
    kTt = [sb(f"kTt{i}", [128, 4, 128], BF16) for i in range(2)]
    vpt = [sb(f"vpt{i}", [128, 8, 65], BF16) for i in range(2)]
    qTt = sb("qTt", [128, 4, 128], BF16)
    wq = [sb(f"wq{i}", [128, 8]) for i in range(2)]
    Vs = [sb(f"Vs{i}", [128, 8, 65], BF16) for i in range(2)]
    pT = [sb(f"pT{i}", [128, 2, 4, 128], BF16) for i in range(2)]
    ofb = sb("ofb", [128, 512], BF16)
    rs = sb("rs", [128, 8])
    G_ = lambda fn, reads=(), writes=(): S.op(os.environ.get("KGENG", "vector"), fn, reads, writes)

    def attn_block(b, kq, k_tile, k_res, q_cols, v_src, v_res, w_ap, w_res, first, last, mask_c, pt_out):
        for h in range(8):
            G_(lambda e, h=h: e.tensor_scalar(out=Vs[b][:, h, :], in0=v_src[:, h, :], scalar1=w_ap[:, h:h + 1], scalar2=None, op0=ALU.mult),
               reads=[v_res, w_res], writes=[f"Vs{b}"])
        nq = q_cols.stop - q_cols.start
        for h in range(8):
            pr, po = h // 2, (h % 2) * 64
            bank = 2 * kq + h % 2
            PE(lambda e, h=h, pr=pr, po=po, bank=bank: e.matmul(ps[bank][:, (h // 2) * nq:(h // 2 + 1) * nq], lhsT=k_tile[po:po + 64, pr, :], rhs=qTt[po:po + 64, pr, q_cols],
                                                              start=True, stop=True, skip_group_check=True),
               reads=[k_res, "qTt"], writes=[f"ps{bank}"])
        for half in range(2):
            bank = 2 * kq + half
            A(lambda e, half=half, bank=bank: e.activation(out=pt_out(half), in_=ps[bank][:, 0:4 * nq].rearrange("p (h q) -> p h q", h=4), func=AF.Exp),
              reads=[f"ps{bank}"], writes=[f"pT{b}"])
        if mask_c is not None:
            for h in range(8):
                V(lambda e, h=h: e.tensor_tensor(out=pT[b][:, h % 2, h // 2, :], in0=pT[b][:, h % 2, h // 2, :], in1=cstb[:, mask_c, :], op=ALU.mult), reads=[f"pT{b}", "cstb"], writes=[f"pT{b}"])
        for h in range(8):
            bank, c0 = 4 + h // 4, (h % 4) * 65
            PE(lambda e, h=h, bank=bank, c0=c0: e.matmul(ps[bank][:, c0:c0 + 65], lhsT=pT[b][:, h % 2, h // 2, :], rhs=Vs[b][:, h, :],
                                                       start=(first and h % 4 == 0), stop=last, skip_group_check=True),
               reads=[f"pT{b}", f"Vs{b}"], writes=[f"ps{bank}"])

    def attn_finish(row0):
        for h in range(8):
            bank, c0 = 4 + h // 4, (h % 4) * 65
            V(lambda e, h=h, bank=bank, c0=c0: e.reciprocal(out=rs[:, h:h + 1], in_=ps[bank][:, c0 + 64:c0 + 65]), reads=[f"ps{bank}"], writes=["rs"])
            V(lambda e, h=h, bank=bank, c0=c0: e.tensor_scalar(out=ofb[:, h * 64:(h + 1) * 64], in0=ps[bank][:, c0:c0 + 64], scalar1=rs[:, h:h + 1], scalar2=None, op0=ALU.mult),
              reads=[f"ps{bank}", "rs"], writes=["ofb"])
        S.dma("sync", lambda e: e.dma_start(out=of_scr[row0:row0 + 128, :], in_=ofb[:]), reads=["ofb"], writes=["of_scr"])

    cnt = [0]
    for qt in range(NPT if 'B' in PH else 0):
        S.dma("sync", lambda e, qt=qt: e.dma_start(out=qTt[:].rearrange("p a b -> p (a b)"), in_=qT_scr[qt * 128:(qt + 1) * 128, :]), reads=["qT_scr"], writes=["qTt"])
        for kt in range(qt + 1):
            b = cnt[0] % 2; cnt[0] += 1
            S.dma("sync", lambda e, kt=kt, b=b: e.dma_start(out=kTt[b][:].rearrange("p a b -> p (a b)"), in_=kT_scr[kt * 128:(kt + 1) * 128, :]), reads=["kT_scr"], writes=[f"kTt{b}"])
            S.dma("sync", lambda e, kt=kt, b=b: e.dma_start(out=vpt[b][:].rearrange("p a b -> p (a b)"), in_=vp_scr[kt * 128:(kt + 1) * 128, :]), reads=["vp_scr"], writes=[f"vpt{b}"])
            V(lambda e, qt=qt, kt=kt, b=b: e.tensor_tensor(out=wq[b][:], in0=Fend[:, qt, :], in1=Fk[:, kt, :], op=ALU.subtract), reads=["Fend", "Fk"], writes=[f"wq{b}"])
            V(lambda e, b=b: e.tensor_scalar(out=wq[b][:], in0=wq[b][:], scalar1=0.0, scalar2=None, op0=ALU.min), reads=[f"wq{b}"], writes=[f"wq{b}"])
            A(lambda e, b=b: e.activation(out=wq[b][:], in_=wq[b][:], func=AF.Exp), reads=[f"wq{b}"], writes=[f"wq{b}"])
            attn_block(b, b, kTt[b], f"kTt{b}", slice(0, 128), vpt[b], f"vpt{b}", wq[b], f"wq{b}", kt == 0, kt == qt,
                       C_TRI if kt == qt else None, lambda half, b=b: pT[b][:, half, :, :])
        attn_finish(qt * 128)

    def sample_attn():
        pts = sb("pts", [128, NS * 256], I32)
        pidx = pts
        iot = sb("iot", [128, 1], I32)
        kpg = [sb(f"kpg{i}", [128, 512], BF16) for i in range(2)]
        vpg = [sb(f"vpg{i}", [128, 8, 65], BF16) for i in range(2)]
        vgt = [sb(f"vgt{i}", [128, 512], BF16) for i in range(2)]
        lfp = sb("lfp", [128, 16, 8])
        lat = sb("lat", [128, 16, 8])
        Rb = sb("Rb", [128, 16, 8])
        kTp = [sb(f"kTp{i}", [128, 4, 128], BF16) for i in range(2)]
        S.dma("sync", lambda e: e.dma_start(out=pts[:], in_=ptab[0:1, :].partition_broadcast(128)), writes=["pts"])
        S.op("gpsimd", lambda e: e.iota(iot[:], pattern=[[0, 1]], base=0, channel_multiplier=1), writes=["iot"])
        S.op("gpsimd", lambda e: e.tensor_scalar(out=pidx[:], in0=pts[:], scalar1=128, scalar2=iot[:, 0:1], op0=ALU.mult, op1=ALU.add), reads=["pts", "iot"], writes=["pts"])
        for i in range(2):
            V(lambda e, i=i: e.memset(pT[i][:], 0.0), writes=[f"pT{i}"])
            V(lambda e, i=i: e.memset(vpg[i][:], 1.0), writes=[f"vpg{i}"])
        pcnt = [0]
        for st in range(NS):
            TS = SMP + st
            S.dma("sync", lambda e, TS=TS: e.dma_start(out=qTt[:].rearrange("p a b -> p (a b)"), in_=qT_scr[TS * 128:(TS + 1) * 128, :]), reads=["qT_scr"], writes=["qTt"])
            S.dma("sync", lambda e, TS=TS: e.dma_start(out=kTt[0][:].rearrange("p a b -> p (a b)"), in_=kT_scr[TS * 128:(TS + 1) * 128, :]), reads=["kT_scr"], writes=["kTt0"])
            S.dma("sync", lambda e, TS=TS: e.dma_start(out=vpt[0][:].rearrange("p a b -> p (a b)"), in_=vp_scr[TS * 128:(TS + 1) * 128, :]), reads=["vp_scr"], writes=["vpt0"])
            A(lambda e, TS=TS: e.activation(out=wq[0][:], in_=Fk[:, TS, :], func=AF.Exp, scale=-1.0), reads=["Fk"], writes=["wq0"])
            attn_block(0, 0, kTt[0], "kTt0", slice(0, 128), vpt[0], "vpt0", wq[0], "wq0", True, False, C_TRI16,
                       lambda half: pT[0][:, half, :, :])
            V(lambda e: e.memset(pT[0][:], 0.0), reads=[], writes=["pT0"])
            V(lambda e: e.memset(pT[1][:], 0.0), reads=[], writes=["pT1"])
            for q in range(16):
                for pg in range(16):
                    S.dma("gpsimd", lambda e, q=q, pg=pg, st=st: e.indirect_dma_start(out=lfp[:, pg, :], out_offset=None, in_=clf,
                                                                                    in_offset=bass.IndirectOffsetOnAxis(ap=pidx[:, st * 256 + q * 16 + pg:st * 256 + q * 16 + pg + 1], axis=0)),
                          reads=["pts"], writes=["lfp"])
                V(lambda e: e.memset(lat[:, 15, :], 0.0), writes=["lat"])
                for pg in range(14, -1, -1):
                    V(lambda e, pg=pg: e.tensor_tensor(out=lat[:, pg, :], in0=lat[:, pg + 1, :], in1=lfp[:, pg + 1, :], op=ALU.add), reads=["lat", "lfp"], writes=["lat"])
                PE(lambda e: e.matmul(ps[6][:, 0:128], lhsT=cst[:, C_SUP, :], rhs=lfp[:].rearrange("p a b -> p (a b)"), start=True, stop=False), reads=["cst", "lfp"], writes=["ps6"])
                PE(lambda e: e.matmul(ps[6][:, 0:128], lhsT=cst[:, C_ONES, :], rhs=lat[:].rearrange("p a b -> p (a b)"), start=False, stop=True), reads=["cst", "lat"], writes=["ps6"])
                A(lambda e: e.activation(out=Rb[:].rearrange("p a b -> p (a b)"), in_=ps[6][:, 0:128], func=AF.Exp), reads=["ps6"], writes=["Rb"])
                for pg in range(16):
                    b = pcnt[0] % 2; pcnt[0] += 1
                    col = st * 256 + q * 16 + pg
                    S.dma("gpsimd", lambda e, b=b, col=col: e.indirect_dma_start(out=kpg[b][:], out_offset=None, in_=ck,
                                                                               in_offset=bass.IndirectOffsetOnAxis(ap=pidx[:, col:col + 1], axis=0)),
                          reads=["pts"], writes=[f"kpg{b}"])
                    S.dma("gpsimd", lambda e, b=b, col=col: e.indirect_dma_start(out=vgt[b][:], out_offset=None, in_=cv_,
                                                                               in_offset=bass.IndirectOffsetOnAxis(ap=pidx[:, col:col + 1], axis=0)),
                          reads=["pts"], writes=[f"vgt{b}"])
                    V(lambda e, b=b: e.tensor_copy(out=vpg[b][:, :, 0:64], in_=vgt[b][:].rearrange("p (h d) -> p h d", h=8)), reads=[f"vgt{b}"], writes=[f"vpg{b}"])
                    transpose_to(kTp[b], f"kTp{b}", kpg[b], f"kpg{b}", 4, 0, psi=7)
                    last = (q == 15 and pg == 15)
                    attn_block(b, b, kTp[b], f"kTp{b}", slice(q * 8, (q + 1) * 8), vpg[b], f"vpg{b}", Rb[:, pg, :], "Rb", False, last, None,
                               lambda half, b=b, q=q: pT[b][:, half, :, q * 8:(q + 1) * 8])
                for i in range(2):
                    V(lambda e, i=i, q=q: e.memset(pT[i][:].rearrange("p a b q -> p (a b) q")[:, :, q * 8:(q + 1) * 8], 0.0), writes=[f"pT{i}"])
            attn_finish(TS * 128)

    if 'S' in PH:
        sample_attn()

    ocat = tb[2][:, :]
    uu = tf[0][:, :]
    vv = tf[1][:, :]
    vvb = tb[0][:, :]
    wsT = sb("wsT", [128, 8, 128], BF16)
    wsS = sb("wsS", [128, 8, 128], BF16)
    bsT = sb("bsT", [128, 8])
    bsS = sb("bsS", [128, 8])
    wsl = sb("wsl", [128, 128])
    bsl = sb("bsl", [8, 128])
    w8 = sb("w8", [8, 8])
    b8 = sb("b8", [8, 128])
    umx = tb[1][:, :]
    gq = tf[2][:, 0:512]
    S.alias["gq"] = "tf2"

    S.dma("sync", lambda e: e.dma_start(out=bsl[:], in_=c_b_s[:, :]), writes=["bsl"])
    PE(lambda e: e.transpose(ps[0][:, 0:8], bsl[:], cst[0:8, C_ID, 0:8]), reads=["bsl", "cst"], writes=["ps0"])
    V(lambda e: e.tensor_copy(out=bsT[:], in_=ps[0][:, 0:8]), reads=["ps0"], writes=["bsT"])
    PE(lambda e: e.matmul(ps[0][:, 8:16], lhsT=cst[0:8, C_R, :], rhs=bsT[0:8, :], start=True, stop=True), reads=["cst", "bsT"], writes=["ps0"])
    V(lambda e: e.tensor_copy(out=bsS[:], in_=ps[0][:, 8:16]), reads=["ps0"], writes=["bsS"])
    for g in range(8):
        S.dma("sync", lambda e, g=g: e.dma_start(out=wsl[:], in_=c_w_s[g * 128:(g + 1) * 128, :]), writes=["wsl"])
        PE(lambda e: e.transpose(ps[1][:, 0:128], wsl[:], cst[:, C_ID, :]), reads=["wsl", "cst"], writes=["ps1"])
        V(lambda e, g=g: e.tensor_tensor(out=wsT[:, g, :], in0=ps[1][:, 0:128], in1=cst[:, C_TRI, :], op=ALU.mult), reads=["ps1", "cst"], writes=["wsT"])
        PE(lambda e: e.matmul(ps[1][0:8, 128:256], lhsT=wsl[0:8, 0:8], rhs=cst[0:8, C_R, :], start=True, stop=True), reads=["wsl", "cst"], writes=["ps1"])
        V(lambda e: e.tensor_copy(out=b8[:], in_=ps[1][0:8, 128:256]), reads=["ps1"], writes=["b8"])
        PE(lambda e: e.matmul(ps[1][:, 256:384], lhsT=cst[0:8, C_R, :], rhs=b8[:], start=True, stop=True), reads=["cst", "b8"], writes=["ps1"])
        V(lambda e, g=g: e.tensor_tensor(out=wsS[:, g, :], in0=ps[1][:, 256:384], in1=cst[:, C_TRI16, :], op=ALU.mult), reads=["ps1", "cst"], writes=["wsS"])

    def phaseC_group(T0, nt, sample):
        for ti in range(nt):
            r0 = (T0 + ti) * 128
            S.dma("sync", lambda e, r0=r0, ti=ti: e.dma_start(out=xg[:, ti, :], in_=x1_scr[r0:r0 + 128, :]), reads=["x1_scr"], writes=[f"xg{ti}"])
            S.dma("sync", lambda e, r0=r0: e.dma_start(out=ocat[:, 0:512], in_=oh_scr[r0:r0 + 128, :]), reads=["oh_scr"], writes=["ocat"])
            S.dma("sync", lambda e, r0=r0: e.dma_start(out=ocat[:, 512:1024], in_=of_scr[r0:r0 + 128, :]), reads=["of_scr"], writes=["ocat"])
            transpose_to(xT, "xT", ocat, "ocat", 8, ti * 128)
        for ti in range(nt):
            for c in range(2):
                proj_tokmajor("w_out", c * 512, 512, ti, 4 + c, c)
                V(lambda e, ti=ti, c=c: e.scalar_tensor_tensor(out=xg[:, ti, c * 512:(c + 1) * 512], in0=xg[:, ti, c * 512:(c + 1) * 512], scalar=ALPHA,
                                                               in1=ps[4 + c][:, :], op0=ALU.mult, op1=ALU.add), reads=[f"ps{4 + c}", f"xg{ti}"], writes=[f"xg{ti}"])
            layer_norm(xg[:, ti, :], f"xg{ti}", xg[:, ti, :], f"xg{ti}", 1)
        ffn(1, nt)
        ffn(2, nt)
        build_xT(nt)
        for ti in range(nt):
            for c in range(4):
                proj_tokmajor("c_w_in", c * 512, 512, ti, c % 2, c % 2)
                dstt = uu if c < 2 else vv
                dres = "uu" if c < 2 else "vv"
                dsl = dstt[:, (c % 2) * 512:(c % 2 + 1) * 512]
                A(lambda e, c=c: e.activation(out=gq[:], in_=ps[c % 2][:, :], func=AF.Square), reads=[f"ps{c % 2}"], writes=["gq"])
                V(lambda e: e.tensor_scalar(out=gq[:], in0=gq[:], scalar1=0.044715, scalar2=1.0, op0=ALU.mult, op1=ALU.add), reads=["gq"], writes=["gq"])
                V(lambda e, c=c: e.tensor_tensor(out=gq[:], in0=gq[:], in1=ps[c % 2][:, :], op=ALU.mult), reads=["gq", f"ps{c % 2}"], writes=["gq"])
                A(lambda e: e.activation(out=gq[:], in_=gq[:], func=AF.Sigmoid, scale=1.5957691216057308), reads=["gq"], writes=["gq"])
                V(lambda e, c=c, dsl=dsl: e.tensor_tensor(out=dsl, in0=gq[:], in1=ps[c % 2][:, :], op=ALU.mult), reads=["gq", f"ps{c % 2}"], writes=[dres])
            layer_norm(vv[:], "vv", vv[:], "vv", 0, g_ap=c_ln_g[0:1, :], b_ap=c_ln_b[0:1, :])
            if sample:
                S.dma("sync", lambda e, ti=ti: e.dma_start(out=cv_o[ti * 128:(ti + 1) * 128, :], in_=vv[:]), reads=["vv"], is_output=True)
            A(lambda e: e.activation(out=vvb[:], in_=vv[:], func=AF.Copy), reads=["vv"], writes=["vvb"])
            wsx = wsS if sample else wsT
            bsx = bsS if sample else bsT
            for g in range(8):
                bank = 4 + g // 4
                cs = slice((g % 4) * 128, (g % 4 + 1) * 128)
                PE(lambda e, g=g, bank=bank, cs=cs, wsx=wsx: e.matmul(ps[bank][:, cs], lhsT=wsx[:, g, :], rhs=vvb[:, g * 128:(g + 1) * 128], start=True, stop=True),
                   reads=["wsT", "wsS", "vvb"], writes=[f"ps{bank}"])
                V(lambda e, g=g, bank=bank, cs=cs, bsx=bsx: e.scalar_tensor_tensor(out=umx[:, g * 128:(g + 1) * 128], in0=ps[bank][:, cs], scalar=bsx[:, g:g + 1],
                                                                                   in1=uu[:, g * 128:(g + 1) * 128], op0=ALU.add, op1=ALU.mult),
                  reads=[f"ps{bank}", "bsT", "bsS", "uu"], writes=["umx"])
            transpose_to(xT, "xT", umx, "umx", 8, ti * 128)
        for ti in range(nt):
            for c in range(2):
                proj_tokmajor("c_w_out", c * 512, 512, ti, 4 + c, c)
                V(lambda e, ti=ti, c=c: e.scalar_tensor_tensor(out=xg[:, ti, c * 512:(c + 1) * 512], in0=xg[:, ti, c * 512:(c + 1) * 512], scalar=ALPHA,
                                                               in1=ps[4 + c][:, :], op0=ALU.mult, op1=ALU.add), reads=[f"ps{4 + c}", f"xg{ti}"], writes=[f"xg{ti}"])
            layer_norm(xg[:, ti, :], f"xg{ti}", xg[:, ti, :], f"xg{ti}", 4)
        ffn(3, nt)
        for ti in range(nt):
            r0 = (T0 + ti) * 128
            S.dma("sync", lambda e, r0=r0, ti=ti: e.dma_start(out=y_o[r0:r0 + 128, :], in_=xg[:, ti, :]), reads=[f"xg{ti}"], is_output=True)

    if 'C' in PH:
        for G in range(NPT // 4):
            phaseC_group(G * 4, 4, False)
        phaseC_group(SMP, NS, True)

    print('sbuf bytes remaining', nc.sbuf_bytes_remaining)
    S.emit()
    print('instr counts', {e: len(S.q[e]) for e in ENGS})
    return nc


_NC_CACHE = {}
NCORES = 2
NS_ = 4


def kernel(x_prompt, x_sample, cache_fox_k, cache_fox_v, cache_fox_logf, state_hg, page_table,
           ln_g, ln_b, ffn_w_gate, ffn_w_up, ffn_w_down, ab_w_in, hg_lb_logits, hg_norm_g, fox_f_bias,
           ab_w_out, c_w_in, c_ln_g, c_ln_b, c_w_s, c_b_s, c_w_out):
    f = lambda a: np.ascontiguousarray(np.asarray(a, dtype=np.float32))
    x_prompt = f(x_prompt); x_sample = f(x_sample)
    P = x_prompt.shape[1]
    nphys = int(os.environ.get('KNPHYS', str(N_PHYS)))
    if nphys != N_PHYS:
        cache_fox_k = np.asarray(cache_fox_k)[:, :nphys]; cache_fox_v = np.asarray(cache_fox_v)[:, :nphys]
        cache_fox_logf = np.asarray(cache_fox_logf)[:, :nphys]; page_table = np.asarray(page_table) % nphys
    if P not in _NC_CACHE:
        _NC_CACHE[P] = build_program(P // 128, nphys, NS_)
    nc = _NC_CACHE[P]
    shared = {
        "cache_k": f(cache_fox_k).reshape(nphys * 128, 512),
        "cache_v": f(cache_fox_v).reshape(nphys * 128, 512),
        "cache_lf": f(cache_fox_logf).reshape(nphys * 128, 8),
        "ln_g": f(ln_g).reshape(6, D), "ln_b": f(ln_b).reshape(6, D),
        "w_gate": f(ffn_w_gate).reshape(4, D, FF), "w_up": f(ffn_w_up).reshape(4, D, FF),
        "w_down": f(ffn_w_down).reshape(4, FF, D),
        "ab_w_in": f(ab_w_in).reshape(D, 3592), "lb_logits": f(hg_lb_logits).reshape(1, 1536),
        "hg_norm_g": f(hg_norm_g).reshape(1, 128), "fox_f_bias": f(fox_f_bias).reshape(1, 8),
        "ab_w_out": f(ab_w_out).reshape(D, D), "c_w_in": f(c_w_in).reshape(D, 2048),
        "c_ln_g": f(c_ln_g).reshape(1, D), "c_ln_b": f(c_ln_b).reshape(1, D),
        "c_w_s": f(c_w_s).reshape(1024, 128), "c_b_s": f(c_b_s).reshape(8, 128),
        "c_w_out": f(c_w_out).reshape(D, D), "cst": make_consts().reshape(128, NCST * 128),
    }
    pt = np.asarray(page_table, dtype=np.int32)
    SQ = 16 * NS_
    in_maps = []
    for c in range(NCORES):
        m = dict(shared)
        m["xin"] = np.concatenate([x_prompt[c], x_sample[SQ * c:SQ * (c + 1)].reshape(SQ * 8, D)], axis=0)
        m["state_hg"] = f(state_hg)[0, SQ * c:SQ * (c + 1)].reshape(SQ * 4 * 128, 128)
        m["ptab"] = np.ascontiguousarray(pt[SQ * c:SQ * (c + 1)].reshape(1, SQ * 16))
        in_maps.append(m)
    ncores = int(os.environ.get('KCORES', str(NCORES)))
    res = run_bass_kernel_spmd(nc, in_maps[:ncores], core_ids=list(range(ncores))).results
    res = list(res) + [res[0]] * (NCORES - ncores)
    R = range(NCORES)
    y_p = np.stack([res[b]["y"][:P] for b in R])
    y_s = np.concatenate([res[c]["y"][P:].reshape(SQ, 8, D) for c in R])
    k_p = np.stack([res[b]["k_new"][:P].reshape(P, 8, 64) for b in R])[None]
    v_p = np.stack([res[b]["v_new"][:P].reshape(P, 8, 64) for b in R])[None]
    lf_p = np.stack([res[b]["lf_new"][:P] for b in R])[None]
    hg_p = np.stack([res[b]["hg_p"].reshape(4, 128, 128) for b in R])[None]
    k_s = np.concatenate([res[c]["k_new"][P:].reshape(SQ, 8, 8, 64) for c in R])[None]
    v_s = np.concatenate([res[c]["v_new"][P:].reshape(SQ, 8, 8, 64) for c in R])[None]
    lf_s = np.concatenate([res[c]["lf_new"][P:].reshape(SQ, 8, 8) for c in R])[None]
    hg_s = np.concatenate([res[c]["hg_s"].reshape(SQ, 4, 128, 128) for c in R])[None]
    cv_s = np.concatenate([res[c]["cv_s"].reshape(SQ, 8, D) for c in R])[None]
    return (y_p, y_s, k_p, v_p, lf_p, hg_p, k_s, v_s, lf_s, hg_s, cv_s)
```

```python
import os
import numpy as np
import concourse.bass as bass
import concourse.mybir as mybir
from concourse.bass_utils import run_bass_kernel_spmd

F32 = mybir.dt.float32
BF16 = mybir.dt.bfloat16
I32 = mybir.dt.int32
AF = mybir.ActivationFunctionType
ALU = mybir.AluOpType

D = 1024
FF = 2816
NJ = 22
NPT = 64
NT = 65
NTOK = NT * 128
ALPHA = 4.0 ** 0.25
EPS = 1e-5
N_PHYS = 2560
ENGS = ("tensor", "vector", "scalar", "gpsimd", "sync")
N_DMA_SEMS = 16
SEM_EPOCH = 16000

C_ID, C_TRI, C_SEL127, C_TRI2, C_BLK2, C_TRI16, C_BLK16, C_SELEND2, C_SELEND16, C_R, C_SUP, C_ONES, C_NEG = range(13)
NCST = 13


def make_consts():
    c = np.zeros((128, NCST, 128), np.float32)
    s = np.arange(128)[:, None]
    t = np.arange(128)[None, :]
    c[:, C_ID] = (s == t)
    c[:, C_TRI] = (s <= t)
    c[:, C_SEL127] = (s == 127) * np.ones((1, 128))
    c[:, C_TRI2] = (s // 64 == t // 64) & (s <= t)
    c[:, C_BLK2] = (s // 64 == t // 64)
    c[:, C_TRI16] = (s // 8 == t // 8) & (s <= t)
    c[:, C_BLK16] = (s // 8 == t // 8)
    c[:, C_SELEND2][:, 0] = (np.arange(128) == 63)
    c[:, C_SELEND2][:, 1] = (np.arange(128) == 127)
    for q in range(16):
        c[q * 8 + 7, C_SELEND16, q] = 1.0
    for i in range(8):
        c[i, C_R, i::8] = 1.0
    c[:, C_SUP] = (s > t)
    c[:, C_ONES] = 1.0
    c[:, C_NEG] = -1.0
    return c


class Sched:
    def __init__(self, nc):
        self.nc = nc
        self.q = {e: [] for e in ENGS}
        self.cnt = {e: 0 for e in ENGS}
        self.epoch = {e: 0 for e in ENGS}
        self.sem = {e: nc.alloc_semaphore(f"s_{e}_0") for e in ENGS}
        self.dsem = {e: [nc.alloc_semaphore(f"d_{e}_{i}") for i in range(N_DMA_SEMS)]
                     for e in ("sync", "scalar", "gpsimd")}
        self.dcnt = {e: [0] * N_DMA_SEMS for e in ("sync", "scalar", "gpsimd")}
        self.dnext = {e: 0 for e in ("sync", "scalar", "gpsimd")}
        self.known = {e: {} for e in ENGS}
        self.lastw = {}
        self.reads = {}
        self.semobj = {}
        for e in ENGS:
            self.semobj[("e", e, 0)] = self.sem[e]
        for e in self.dsem:
            for i, s in enumerate(self.dsem[e]):
                self.semobj[("d", e, i)] = s
        self.out_deps = []
        self.alias = {}

    def _need(self, eng, deps):
        need = {}
        for (k, v) in deps:
            if self.known[eng].get(k, 0) >= v:
                continue
            if need.get(k, 0) < v:
                need[k] = v
        for k, v in need.items():
            self.known[eng][k] = v
        return list(need.items())

    def _deps(self, reads, writes):
        reads = [self.alias.get(r, r) for r in reads]
        writes = [self.alias.get(w, w) for w in writes]
        deps = []
        for r in reads:
            if r in self.lastw:
                deps.append(self.lastw[r])
        for w in writes:
            if w in self.lastw:
                deps.append(self.lastw[w])
            deps.extend(self.reads.get(w, ()))
        return deps

    def _commit(self, tok, reads, writes):
        reads = [self.alias.get(r, r) for r in reads]
        writes = [self.alias.get(w, w) for w in writes]
        for r in reads:
            self.reads.setdefault(r, []).append(tok)
        for w in writes:
            self.lastw[w] = tok
            self.reads[w] = []

    def op(self, eng, fn, reads=(), writes=()):
        pr = [r for r in reads if r.startswith("ps")]
        if pr:
            reads = [r for r in reads if not r.startswith("ps")]
            writes = list(writes) + pr
        deps = self._deps(reads, writes)
        waits = self._need(eng, deps)
        if self.cnt[eng] >= SEM_EPOCH:
            self.epoch[eng] += 1
            self.cnt[eng] = 0
            self.sem[eng] = self.nc.alloc_semaphore(f"s_{eng}_{self.epoch[eng]}")
            self.semobj[("e", eng, self.epoch[eng])] = self.sem[eng]
        self.cnt[eng] += 1
        key = ("e", eng, self.epoch[eng])
        tok = (key, self.cnt[eng])
        if eng == "tensor":
            self.known[eng][key] = self.cnt[eng]
        self.q[eng].append((waits, fn, (self.sem[eng], 1)))
        self._commit(tok, reads, writes)
        return tok

    def dma(self, eng, fn, reads=(), writes=(), is_output=False):
        deps = self._deps(reads, writes)
        i = self.dnext[eng]
        self.dnext[eng] = (i + 1) % N_DMA_SEMS
        key = ("d", eng, i)
        if self.dcnt[eng][i] > 0:
            deps.append((key, self.dcnt[eng][i]))
        waits = self._need(eng, deps)
        self.dcnt[eng][i] += 16
        tok = (key, self.dcnt[eng][i])
        self.q[eng].append((waits, fn, (self.dsem[eng][i], 16)))
        self._commit(tok, reads, writes)
        if is_output:
            self.out_deps.append(tok)
        return tok

    def emit(self):
        nc = self.nc
        fin = list(self.out_deps)
        for r, t in self.lastw.items():
            fin.append(t)
        waits = self._need("sync", fin)
        self.q["sync"].append((waits, None, None))
        with nc.Block() as block:
            def run(engname):
                def body(eng):
                    for waits, fn, inc in self.q[engname]:
                        for k, v in waits:
                            eng.wait_ge(self.semobj[k], v)
                        if fn is not None:
                            ins = fn(eng)
                            ins.then_inc(inc[0], inc[1])
                return body
            block.tensor(run("tensor"))
            block.vector(run("vector"))
            block.scalar(run("scalar"))
            block.gpsimd(run("gpsimd"))
            block.sync(run("sync"))


def build_program(NPT=NPT, N_PHYS=N_PHYS, NS=4):
    NT = NPT + NS
    NTOK = NT * 128
    SMP = NPT
    nc = bass.Bass("TRN2", target_bir_lowering=False)
    S = Sched(nc)

    def din(name, shape, dt=F32):
        return nc.dram_tensor(name, list(shape), dt, kind="ExternalInput").ap()

    def dout(name, shape, dt=F32):
        return nc.dram_tensor(name, list(shape), dt, kind="ExternalOutput").ap()

    def dscr(name, shape, dt):
        return nc.dram_tensor(name, list(shape), dt).ap()

    def sb(name, shape, dt=F32):
        return nc.alloc_sbuf_tensor(name, list(shape), dt)

    xin = din("xin", [NTOK, D])
    ck = din("cache_k", [N_PHYS * 128, 512])
    cv_ = din("cache_v", [N_PHYS * 128, 512])
    clf = din("cache_lf", [N_PHYS * 128, 8])
    st_hg = din("state_hg", [NS * 16 * 4 * 128, 128])
    ptab = din("ptab", [1, NS * 256], I32)
    ln_g = din("ln_g", [6, D])
    ln_b = din("ln_b", [6, D])
    wgate = din("w_gate", [4, D, FF])
    wup = din("w_up", [4, D, FF])
    wdown = din("w_down", [4, FF, D])
    w_in = din("ab_w_in", [D, 3592])
    lb_log = din("lb_logits", [1, 3 * 512])
    hg_ng = din("hg_norm_g", [1, 128])
    fbias = din("fox_f_bias", [1, 8])
    w_out = din("ab_w_out", [D, D])
    c_w_in = din("c_w_in", [D, 2048])
    c_ln_g = din("c_ln_g", [1, D])
    c_ln_b = din("c_ln_b", [1, D])
    c_w_s = din("c_w_s", [8 * 128, 128])
    c_b_s = din("c_b_s", [8, 128])
    c_w_out = din("c_w_out", [D, D])
    cst_d = din("cst", [128, NCST * 128])

    y_o = dout("y", [NTOK, D])
    k_o = dout("k_new", [NTOK, 512])
    v_o = dout("v_new", [NTOK, 512])
    lf_o = dout("lf_new", [NTOK, 8])
    hgp_o = dout("hg_p", [4 * 128, 128])
    hgs_o = dout("hg_s", [NS * 16 * 4 * 128, 128])
    cv_o = dout("cv_s", [NS * 128, D])

    wg_scr = dscr("wg_scr", [4 * NJ * 128, 1024], BF16)
    wu_scr = dscr("wu_scr", [4 * NJ * 128, 1024], BF16)
    wd_scr = dscr("wd_scr", [4 * 4 * 128, NJ * 256], BF16)
    win_scr = dscr("win_scr", [128, 8 * 3592], BF16)
    wout_scr = dscr("wout_scr", [128, 8 * D], BF16)
    cwin_scr = dscr("cwin_scr", [128, 8 * 2048], BF16)
    cwout_scr = dscr("cwout_scr", [128, 8 * D], BF16)
    x1_scr = dscr("x1_scr", [NTOK, D], F32)
    oh_scr = dscr("oh_scr", [NTOK, 512], BF16)
    of_scr = dscr("of_scr", [NTOK, 512], BF16)
    qT_scr = dscr("qT_scr", [NT * 128, 512], BF16)
    kT_scr = dscr("kT_scr", [NT * 128, 512], BF16)
    vp_scr = dscr("vp_scr", [NT * 128, 8 * 65], BF16)

    cst = sb("cstt", [128, NCST, 128])
    cstb = sb("cstb", [128, NCST, 128], BF16)
    identb = cstb[:, C_ID, :]
    ps = [nc.alloc_psum_tensor(f"ps{i}", [128, 512], F32) for i in range(8)]

    def psb(i):
        return ps[i][:].bitcast(BF16)

    xg = sb("xg", [128, 4, D])
    xT = sb("xT", [128, 8, 512], BF16)
    hT = sb("hT", [128, NJ, 512], BF16)
    wgs = [sb(f"wgs{i}", [128, 8, 128], BF16) for i in range(2)]
    wus = [sb(f"wus{i}", [128, 8, 128], BF16) for i in range(2)]
    wds = sb("wds", [128, NJ, 256], BF16)
    wps = [sb(f"wps{i}", [128, 8, 512], BF16) for i in range(2)]
    gb = [sb(f"gb{i}", [128, D]) for i in range(2)]
    zt = sb("zt", [128, D])
    tf = [sb(f"tf{i}", [128, D]) for i in range(4)]
    tb = [sb(f"tb{i}", [128, D], BF16) for i in range(4)]
    S.alias.update({"qh": "tf0", "kk": "tf0", "gl": "tf1", "gs": "tf1", "bcs": "tf2", "ebl": "tf2", "fk_f": "tf3", "fv_f": "tf3",
                    "vh": "tb0", "qtl": "tb0", "ktl": "tb1", "khat": "tb1", "ohb": "tb2", "fq_b": "tb2", "fk_b": "tb3",
                    "uu": "tf0", "vv": "tf1", "vvb": "tb0", "umx": "tb1", "ocat": "tb2", "tmpA": "tf2"})
    xb16 = sb("xb16", [128, D], BF16)
    sg_t = sb("sg_t", [128, 512], BF16)
    stats = sb("stats", [128, 2, 6])
    mv = sb("mv", [128, 2])
    rstd = sb("rstd", [128, 1])

    S.dma("sync", lambda e: e.dma_start(out=cst[:].rearrange("p a b -> p (a b)"), in_=cst_d[:, :]), writes=["cst"])
    S.op("vector", lambda e: e.tensor_copy(out=cstb[:], in_=cst[:]), reads=["cst"], writes=["cstb"])

    def V(fn, reads=(), writes=()):
        return S.op("vector", fn, reads, writes)

    def A(fn, reads=(), writes=()):
        return S.op("scalar", fn, reads, writes)

    def PE(fn, reads=(), writes=()):
        return S.op("tensor", fn, reads, writes)

    def bcast_load(dst, src_row, res):
        S.dma("sync", lambda e: e.dma_start(out=dst, in_=src_row.partition_broadcast(128)), writes=[res])

    def transpose_to(dstT, dst_res, src_bf, src_res, nk, col0, psi=7):
        pb = psb(psi)
        for k in range(nk):
            PE(lambda e, k=k: e.transpose(pb[:, k * 128:(k + 1) * 128], src_bf[:, k * 128:(k + 1) * 128], identb),
               reads=[src_res, "cstb"], writes=[f"ps{psi}"])
        V(lambda e: e.tensor_copy(out=dstT[:, 0:nk, col0:col0 + 128],
                                  in_=pb[:, 0:nk * 128].rearrange("p (k t) -> p k t", k=nk)),
          reads=[f"ps{psi}"], writes=[dst_res])

    def layer_norm(src, src_res, dst, dst_res, gi, g_ap=None, b_ap=None):
        ga = ln_g[gi:gi + 1, :] if g_ap is None else g_ap
        ba = ln_b[gi:gi + 1, :] if b_ap is None else b_ap
        bcast_load(gb[0][:], ga, "gb0")
        bcast_load(gb[1][:], ba, "gb1")
        for h in range(2):
            V(lambda e, h=h: e.bn_stats(out=stats[:, h, :], in_=src[:, h * 512:(h + 1) * 512]),
              reads=[src_res], writes=["stats"] if h == 0 else ["stats"])
        V(lambda e: e.bn_aggr(out=mv[:], in_=stats[:].rearrange("p a b -> p (a b)")), reads=["stats"], writes=["mv"])
        V(lambda e: e.tensor_scalar(out=rstd[:], in0=mv[:, 1:2], scalar1=EPS, scalar2=None, op0=ALU.add), reads=["mv"], writes=["rstd"])
        A(lambda e: e.activation(out=rstd[:], in_=rstd[:], func=AF.Sqrt), reads=["rstd"], writes=["rstd"])
        V(lambda e: e.reciprocal(out=rstd[:], in_=rstd[:]), reads=["rstd"], writes=["rstd"])
        V(lambda e: e.tensor_scalar(out=dst, in0=src, scalar1=mv[:, 0:1], scalar2=rstd[:, 0:1], op0=ALU.subtract, op1=ALU.mult),
          reads=[src_res, "mv", "rstd"], writes=[dst_res])
        V(lambda e: e.tensor_tensor(out=dst, in0=dst, in1=gb[0][:], op=ALU.mult), reads=[dst_res, "gb0"], writes=[dst_res])
        V(lambda e: e.tensor_tensor(out=dst, in0=dst, in1=gb[1][:], op=ALU.add), reads=[dst_res, "gb1"], writes=[dst_res])

    def build_xT(nt):
        for ti in range(nt):
            A(lambda e, ti=ti: e.activation(out=xb16[:], in_=xg[:, ti, :], func=AF.Copy), reads=[f"xg{ti}"], writes=["xb16"])
            transpose_to(xT, "xT", xb16, "xb16", 8, ti * 128)

    def ffn(fi, nt):
        ntk = nt * 128
        build_xT(nt)
        for j in range(NJ):
            b = j % 2
            rj = (fi * NJ + j) * 128
            S.dma("sync", lambda e, rj=rj, b=b: e.dma_start(out=wgs[b][:].rearrange("p k m -> p (k m)"), in_=wg_scr[rj:rj + 128, :]),
                  reads=[f"S_wg{fi}"], writes=[f"wgs{b}"])
            S.dma("sync", lambda e, rj=rj, b=b: e.dma_start(out=wus[b][:].rearrange("p k m -> p (k m)"), in_=wu_scr[rj:rj + 128, :]),
                  reads=[f"S_wu{fi}"], writes=[f"wus{b}"])
            pg, pu = ps[2 * b], ps[2 * b + 1]
            for k in range(8):
                PE(lambda e, k=k, b=b, pg=pg: e.matmul(pg[:, 0:ntk], lhsT=wgs[b][:, k, :], rhs=xT[:, k, 0:ntk], start=(k == 0), stop=(k == 7)),
                   reads=[f"wgs{b}", "xT"], writes=[f"ps{2 * b}"])
            for k in range(8):
                PE(lambda e, k=k, b=b, pu=pu: e.matmul(pu[:, 0:ntk], lhsT=wus[b][:, k, :], rhs=xT[:, k, 0:ntk], start=(k == 0), stop=(k == 7)),
                   reads=[f"wus{b}", "xT"], writes=[f"ps{2 * b + 1}"])
            A(lambda e, pg=pg: e.activation(out=sg_t[:, 0:ntk], in_=pg[:, 0:ntk], func=AF.Silu), reads=[f"ps{2 * b}"], writes=["sg_t"])
            V(lambda e, j=j, pu=pu: e.tensor_tensor(out=hT[:, j, 0:ntk], in0=sg_t[:, 0:ntk], in1=pu[:, 0:ntk], op=ALU.mult),
              reads=["sg_t", f"ps{2 * b + 1}"], writes=["hT"])
        li = (fi // 2) * 3 + (0 if fi % 2 == 0 else 2)
        for c in range(4):
            cs = slice(c * 256, (c + 1) * 256)
            rq = (fi * 4 + c) * 128
            S.dma("sync", lambda e, rq=rq: e.dma_start(out=wds[:].rearrange("p j n -> p (j n)"), in_=wd_scr[rq:rq + 128, :]), reads=[f"S_wd{fi}"], writes=["wds"])
            for ti in range(nt):
                pd = ps[4 + (ti % 2)]
                for j in range(NJ):
                    PE(lambda e, j=j, ti=ti, pd=pd: e.matmul(pd[:, 0:256], lhsT=hT[:, j, ti * 128:(ti + 1) * 128], rhs=wds[:, j, :], start=(j == 0), stop=(j == NJ - 1)),
                       reads=["hT", "wds"], writes=[f"ps{4 + ti % 2}"])
                A(lambda e, pd=pd, cs=cs: e.activation(out=zt[:, cs], in_=pd[:, 0:256], func=AF.Copy, scale=0.5),
                  reads=[f"ps{4 + ti % 2}"], writes=["zt"])
                V(lambda e, ti=ti, cs=cs: e.scalar_tensor_tensor(out=xg[:, ti, cs], in0=xg[:, ti, cs], scalar=ALPHA,
                                                                in1=zt[:, cs], op0=ALU.mult, op1=ALU.add),
                  reads=["zt", f"xg{ti}"], writes=[f"xg{ti}"])
        for ti in range(nt):
            layer_norm(xg[:, ti, :], f"xg{ti}", xg[:, ti, :], f"xg{ti}", li)

    def proj_tokmajor(w_ap, c0, ncols, ti, psi, wb):
        w3, wres = WSCR[w_ap]
        S.dma("sync", lambda e: e.dma_start(out=wps[wb][:, :, 0:ncols], in_=w3[:, :, c0:c0 + ncols]), reads=[wres], writes=[f"wps{wb}"])
        for k in range(8):
            PE(lambda e, k=k: e.matmul(ps[psi][:, 0:ncols], lhsT=xT[:, k, ti * 128:(ti + 1) * 128], rhs=wps[wb][:, k, 0:ncols], start=(k == 0), stop=(k == 7)),
               reads=["xT", f"wps{wb}"], writes=[f"ps{psi}"])

    stg = [sb(f"stg{i}", [128, 8, 256], BF16) for i in range(2)]
    stg_i = [0]

    def conv_cols(src, ncols, dst3, res):
        for c0 in range(0, ncols, 256):
            w = min(256, ncols - c0)
            b = stg_i[0] % 2; stg_i[0] += 1
            S.dma("gpsimd", lambda e, b=b, c0=c0, w=w: e.dma_start(out=stg[b][:, :, 0:w], in_=src[:, c0:c0 + w].rearrange("(k p) n -> p k n", p=128)), writes=[f"stg{b}"])
            S.dma("gpsimd", lambda e, b=b, c0=c0, w=w: e.dma_start(out=dst3[:, :, c0:c0 + w], in_=stg[b][:, :, 0:w]), reads=[f"stg{b}"], writes=[res])

    def conv_gu(src, scr, fi, res):
        for jj in range(NJ // 2):
            b = stg_i[0] % 2; stg_i[0] += 1
            S.dma("gpsimd", lambda e, b=b, jj=jj: e.dma_start(out=stg[b][:], in_=src[fi, :, jj * 256:(jj + 1) * 256].rearrange("(k p) n -> p k n", p=128)), writes=[f"stg{b}"])
            for c in range(2):
                r0 = (fi * NJ + jj * 2 + c) * 128
                S.dma("gpsimd", lambda e, b=b, c=c, r0=r0: e.dma_start(out=scr[r0:r0 + 128, :].rearrange("p (k m) -> p k m", k=8), in_=stg[b][:, :, c * 128:(c + 1) * 128]),
                      reads=[f"stg{b}"], writes=[res])

    def conv_down(fi, res):
        for q in range(4):
            r0 = (fi * 4 + q) * 128
            for (j0, nj) in ((0, 8), (8, 8), (16, 6)):
                b = stg_i[0] % 2; stg_i[0] += 1
                S.dma("gpsimd", lambda e, b=b, q=q, j0=j0, nj=nj: e.dma_start(
                    out=stg[b][:, 0:nj, :], in_=wdown[fi, j0 * 128:(j0 + nj) * 128, q * 256:(q + 1) * 256].rearrange("(j p) n -> p j n", p=128)), writes=[f"stg{b}"])
                S.dma("gpsimd", lambda e, b=b, r0=r0, j0=j0, nj=nj: e.dma_start(
                    out=wd_scr[r0:r0 + 128, :].rearrange("p (j n) -> p j n", j=NJ)[:, j0:j0 + nj, :], in_=stg[b][:, 0:nj, :]), reads=[f"stg{b}"], writes=[res])

    def conv_ffn(fi):
        conv_gu(wgate, wg_scr, fi, f"S_wg{fi}")
        conv_gu(wup, wu_scr, fi, f"S_wu{fi}")
        conv_down(fi, f"S_wd{fi}")

    win3 = win_scr.rearrange("p (k n) -> p k n", k=8)
    wout3 = wout_scr.rearrange("p (k n) -> p k n", k=8)
    cwin3 = cwin_scr.rearrange("p (k n) -> p k n", k=8)
    cwout3 = cwout_scr.rearrange("p (k n) -> p k n", k=8)
    conv_ffn(0)
    conv_cols(w_in, 3592, win3, "S_win")
    conv_cols(w_out, D, wout3, "S_wout")
    conv_ffn(1)
    conv_ffn(2)
    conv_cols(c_w_in, 2048, cwin3, "S_cwin")
    conv_cols(c_w_out, D, cwout3, "S_cwout")
    conv_ffn(3)
    WSCR = {"w_in": (win3, "S_win"), "w_out": (wout3, "S_wout"), "c_w_in": (cwin3, "S_cwin"), "c_w_out": (cwout3, "S_cwout")}

    lbt = [tf[0][:, 0:512], tf[0][:, 512:1024], tf[1][:, 0:512]]
    LR = ["tf0", "tf1"]
    lb = sb("lb", [128, 512])
    oml = sb("oml", [128, 512])
    tmpA = tf[2][:, 0:512]
    ng_bc = sb("ng_bc", [128, 128])
    fb_bc = sb("fb_bc", [128, 8])
    for a in range(3):
        S.dma("sync", lambda e, a=a: e.dma_start(out=lbt[a], in_=lb_log[0:1, a * 512:(a + 1) * 512].partition_broadcast(128)), writes=LR)
    bcast_load(ng_bc[:], hg_ng[0:1, :], "ng_bc")
    bcast_load(fb_bc[:], fbias[0:1, :], "fb_bc")
    V(lambda e: e.tensor_tensor(out=tmpA, in0=lbt[0], in1=lbt[1], op=ALU.max), reads=LR, writes=["tmpA"])
    V(lambda e: e.tensor_tensor(out=tmpA, in0=tmpA, in1=lbt[2], op=ALU.max), reads=LR + ["tmpA"], writes=["tmpA"])
    for a in range(3):
        V(lambda e, a=a: e.tensor_tensor(out=lbt[a], in0=lbt[a], in1=tmpA, op=ALU.subtract), reads=LR + ["tmpA"], writes=LR)
        A(lambda e, a=a: e.activation(out=lbt[a], in_=lbt[a], func=AF.Exp), reads=LR, writes=LR)
    V(lambda e: e.tensor_tensor(out=tmpA, in0=lbt[0], in1=lbt[1], op=ALU.add), reads=LR, writes=["tmpA"])
    V(lambda e: e.tensor_tensor(out=tmpA, in0=tmpA, in1=lbt[2], op=ALU.add), reads=LR + ["tmpA"], writes=["tmpA"])
    V(lambda e: e.reciprocal(out=tmpA, in_=tmpA), reads=["tmpA"], writes=["tmpA"])
    V(lambda e: e.tensor_tensor(out=lb[:], in0=lbt[0], in1=tmpA, op=ALU.mult), reads=LR + ["tmpA"], writes=["lb"])
    V(lambda e: e.tensor_scalar(out=oml[:], in0=lb[:], scalar1=-1.0, scalar2=1.0, op0=ALU.mult, op1=ALU.add), reads=["lb"], writes=["oml"])

    Fk = sb("Fk", [128, NT, 8])
    Fend = sb("Fend", [128, NT, 8])
    Sst = [sb(f"Sst{h}", [128, 128]) for h in range(4)]
    Sb = [sb(f"Sb{h}", [128, 128], BF16) for h in range(4)]
    qh = tf[0][:, 0:512]
    kk = tf[0][:, 512:1024]
    gl = tf[1][:, 0:512]
    vh = tb[0][:, 0:512]
    gs = tf[1][:, 512:1024]
    bcs = tf[2][:, 0:512]
    ebl = tf[2][:, 512:1024]
    qtl = tb[0][:, 512:1024]
    ktl = tb[1][:, 0:512]
    khat = tb[1][:, 512:1024]
    qtT = sb("qtT", [128, 128], BF16)
    ktT = sb("ktT", [128, 128], BF16)
    qtT0 = sb("qtT0", [128, 128], BF16)
    qtT1 = sb("qtT1", [128, 128], BF16)
    attT = sb("attT", [128, 128], BF16)
    dcol = sb("dcol", [128, 16])
    ohb = tb[2][:, 0:512]
    ssq = sb("ssq", [128, 4])
    junk = sb("junk", [128, 128])
    fq_b = tb[2][:, 512:1024]
    fk_f = tf[3][:, 0:512]
    fk_b = tb[3][:, 0:512]
    fv_f = tf[3][:, 512:1024]
    vp = sb("vp", [128, 8, 65], BF16)
    lft = sb("lft", [128, 8])
    tq = sb("tq", [128, 4, 128], BF16)
    S0f = sb("S0f", [128, 16, 128])
    S0b = sb("S0b", [128, 16, 128], BF16)
    qz = sb("qz", [128, 16, 128], BF16)
    vbd = sb("vbd", [128, 16, 128], BF16)

    V(lambda e: e.memset(qtT0[:], 0.0), writes=["qtT0"])
    V(lambda e: e.memset(qtT1[:], 0.0), writes=["qtT1"])
    V(lambda e: e.memset(qz[:], 0.0), writes=["qz"])
    V(lambda e: e.memset(vp[:], 1.0), writes=["vp"])
    for h in range(4):
        V(lambda e, h=h: e.memset(Sst[h][:], 0.0), writes=[f"Sst{h}"])
        V(lambda e, h=h: e.memset(Sb[h][:], 0.0), writes=[f"Sb{h}"])

    KSTAGE = int(os.environ.get('KSTAGE', '9'))
    dcs = sb("dcs", [128, 64])

    def phaseA_tile(T, ti, sample):
        r0 = T * 128
        tri = C_TRI16 if sample else C_TRI2
        blk = C_BLK16 if sample else C_BLK2
        proj_tokmajor("w_in", 0, 512, ti, 0, 0)
        A(lambda e: e.activation(out=qh[:], in_=ps[0][:], func=AF.Silu), reads=["ps0"], writes=["qh"])
        proj_tokmajor("w_in", 512, 512, ti, 1, 1)
        A(lambda e: e.activation(out=kk[:], in_=ps[1][:], func=AF.Sigmoid), reads=["ps1"], writes=["kk"])
        V(lambda e: e.tensor_tensor(out=gl[:], in0=kk[:], in1=oml[:], op=ALU.mult), reads=["kk", "oml"], writes=["gl"])
        V(lambda e: e.tensor_tensor(out=gl[:], in0=gl[:], in1=lb[:], op=ALU.add), reads=["gl", "lb"], writes=["gl"])
        V(lambda e: e.tensor_scalar(out=kk[:], in0=gl[:], scalar1=-1.0, scalar2=1.0, op0=ALU.mult, op1=ALU.add), reads=["gl"], writes=["kk"])
        A(lambda e: e.activation(out=gl[:], in_=gl[:], func=AF.Ln), reads=["gl"], writes=["gl"])
        proj_tokmajor("w_in", 1024, 512, ti, 0, 0)
        A(lambda e: e.activation(out=vh[:], in_=ps[0][:], func=AF.Copy), reads=["ps0"], writes=["vh"])
        proj_tokmajor("w_in", 1536, 512, ti, 1, 1)
        A(lambda e: e.activation(out=gs[:], in_=ps[1][:], func=AF.Silu), reads=["ps1"], writes=["gs"])
        for h in range(4):
            V(lambda e, h=h: e.tensor_tensor(out=gs[:, h * 128:(h + 1) * 128], in0=gs[:, h * 128:(h + 1) * 128], in1=ng_bc[:], op=ALU.mult),
              reads=["gs", "ng_bc"], writes=["gs"])
        if KSTAGE < 2:
            return
        PE(lambda e: e.matmul(ps[2][:], lhsT=cst[:, tri, :], rhs=gl[:], start=True, stop=True), reads=["cst", "gl"], writes=["ps2"])
        PE(lambda e: e.matmul(ps[3][:], lhsT=cst[:, blk, :], rhs=gl[:], start=True, stop=True), reads=["cst", "gl"], writes=["ps3"])
        A(lambda e: e.activation(out=bcs[:], in_=ps[2][:], func=AF.Exp), reads=["ps2"], writes=["bcs"])
        V(lambda e: e.tensor_tensor(out=qtl[:], in0=qh[:], in1=bcs[:], op=ALU.mult), reads=["qh", "bcs"], writes=["qtl"])
        A(lambda e: e.activation(out=bcs[:], in_=ps[2][:], func=AF.Exp, scale=-1.0), reads=["ps2"], writes=["bcs"])
        V(lambda e: e.tensor_tensor(out=ktl[:], in0=kk[:], in1=bcs[:], op=ALU.mult), reads=["kk", "bcs"], writes=["ktl"])
        A(lambda e: e.activation(out=ebl[:], in_=ps[3][:], func=AF.Exp), reads=["ps3"], writes=["ebl"])
        V(lambda e: e.tensor_tensor(out=bcs[:], in0=bcs[:], in1=ebl[:], op=ALU.mult), reads=["bcs", "ebl"], writes=["bcs"])
        V(lambda e: e.tensor_tensor(out=khat[:], in0=kk[:], in1=bcs[:], op=ALU.mult), reads=["kk", "bcs"], writes=["khat"])
        nb = 16 if sample else 2
        selc = C_SELEND16 if sample else C_SELEND2
        for h in range(4):
            PE(lambda e, h=h: e.matmul(ps[3][:, 256 + h * 16:256 + h * 16 + nb], lhsT=ebl[:, h * 128:(h + 1) * 128], rhs=cst[:, selc, 0:nb], start=True, stop=True),
               reads=["ebl", "cst"], writes=["ps3"])
        if sample:
            sq0 = (T - SMP) * 16
            V(lambda e: e.tensor_copy(out=dcs[:], in_=ps[3][:, 256:320]), reads=["ps3"], writes=["dcs"])
        else:
            V(lambda e: e.tensor_copy(out=dcol[:].rearrange("p (h c) -> p h c", h=4)[:, :, 0:2],
                                      in_=ps[3][:, 256:320].rearrange("p (h c) -> p h c", h=4)[:, :, 0:2]), reads=["ps3"], writes=["dcol"])
        if KSTAGE < 3:
            return
        for h in range(4):
            hs = slice(h * 128, (h + 1) * 128)
            pb = psb(7)
            PE(lambda e, hs=hs: e.transpose(pb[:, 0:128], qtl[:, hs], identb), reads=["qtl", "cstb"], writes=["ps7"])
            PE(lambda e, hs=hs: e.transpose(pb[:, 128:256], ktl[:, hs], identb), reads=["ktl", "cstb"], writes=["ps7"])
            V(lambda e: e.tensor_copy(out=qtT[:], in_=pb[:, 0:128]), reads=["ps7"], writes=["qtT"])
            A(lambda e: e.activation(out=ktT[:], in_=pb[:, 128:256], func=AF.Copy), reads=["ps7"], writes=["ktT"])
            PE(lambda e: e.matmul(ps[6][:, 0:128], lhsT=ktT[:], rhs=qtT[:], start=True, stop=True), reads=["ktT", "qtT"], writes=["ps6"])
            V(lambda e: e.tensor_tensor(out=attT[:], in0=ps[6][:, 0:128], in1=cst[:, tri, :], op=ALU.mult), reads=["ps6", "cst"], writes=["attT"])
            if not sample:
                V(lambda e: e.tensor_copy(out=qtT0[:, 0:64], in_=qtT[:, 0:64]), reads=["qtT"], writes=["qtT0"])
                V(lambda e: e.tensor_copy(out=qtT1[:, 64:128], in_=qtT[:, 64:128]), reads=["qtT"], writes=["qtT1"])
                PE(lambda e, hs=hs: e.matmul(ps[6][:, 128:256], lhsT=khat[0:64, hs], rhs=vh[0:64, hs], start=True, stop=True),
                   reads=["khat", "vh"], writes=["ps6"])
                PE(lambda e, hs=hs: e.matmul(ps[5][:, 0:128], lhsT=attT[:], rhs=vh[:, hs], start=True, stop=False), reads=["attT", "vh"], writes=["ps5"])
                PE(lambda e, h=h: e.matmul(ps[5][:, 0:128], lhsT=qtT0[:], rhs=Sb[h][:], start=False, stop=False), reads=["qtT0", f"Sb{h}"], writes=["ps5"])
                V(lambda e, h=h: e.scalar_tensor_tensor(out=Sst[h][:], in0=Sst[h][:], scalar=dcol[:, h * 4:h * 4 + 1], in1=ps[6][:, 128:256], op0=ALU.mult, op1=ALU.add),
                  reads=[f"Sst{h}", "dcol", "ps6"], writes=[f"Sst{h}"])
                A(lambda e, h=h: e.activation(out=Sb[h][:], in_=Sst[h][:], func=AF.Copy), reads=[f"Sst{h}"], writes=[f"Sb{h}"])
                PE(lambda e, h=h: e.matmul(ps[5][:, 0:128], lhsT=qtT1[:], rhs=Sb[h][:], start=False, stop=True), reads=["qtT1", f"Sb{h}"], writes=["ps5"])
                PE(lambda e, hs=hs: e.matmul(ps[6][:, 256:384], lhsT=khat[64:128, hs], rhs=vh[64:128, hs], start=True, stop=True),
                   reads=["khat", "vh"], writes=["ps6"])
                V(lambda e, h=h: e.scalar_tensor_tensor(out=Sst[h][:], in0=Sst[h][:], scalar=dcol[:, h * 4 + 1:h * 4 + 2], in1=ps[6][:, 256:384], op0=ALU.mult, op1=ALU.add),
                  reads=[f"Sst{h}", "dcol", "ps6"], writes=[f"Sst{h}"])
                A(lambda e, h=h: e.activation(out=Sb[h][:], in_=Sst[h][:], func=AF.Copy), reads=[f"Sst{h}"], writes=[f"Sb{h}"])
            else:
                S.dma("sync", lambda e, h=h: e.dma_start(out=S0f[:], in_=st_hg.rearrange("(q h p) v -> h p q v", h=4, p=128)[h][:, sq0:sq0 + 16, :]), writes=["S0f"])
                A(lambda e: e.activation(out=S0b[:], in_=S0f[:], func=AF.Copy), reads=["S0f"], writes=["S0b"])
                for q in range(16):
                    V(lambda e, q=q: e.tensor_copy(out=qz[:, q, q * 8:(q + 1) * 8], in_=qtT[:, q * 8:(q + 1) * 8]), reads=["qtT"], writes=["qz"])
                PE(lambda e, hs=hs: e.matmul(ps[5][:, 0:128], lhsT=attT[:], rhs=vh[:, hs], start=True, stop=False), reads=["attT", "vh"], writes=["ps5"])
                for q in range(16):
                    PE(lambda e, q=q, h=h: e.matmul(ps[5][:, 0:128], lhsT=qz[:, q, :], rhs=S0b[:, q, :], start=False, stop=(q == 15)),
                       reads=["qz", "S0b"], writes=["ps5"])
                for q in range(16):
                    V(lambda e, q=q, hs=hs: e.tensor_scalar(out=vbd[:, q, :], in0=vh[:, hs], scalar1=cst[:, C_BLK16, q * 8:q * 8 + 1], scalar2=None, op0=ALU.mult),
                      reads=["vh", "cst"], writes=["vbd"])
                for c4 in range(4):
                    PE(lambda e, c4=c4, hs=hs: e.matmul(ps[c4][:, :], lhsT=khat[:, hs], rhs=vbd[:, c4 * 4:(c4 + 1) * 4, :].rearrange("p a b -> p (a b)"), start=True, stop=True),
                       reads=["khat", "vbd"], writes=[f"ps{c4}"])
                for q in range(16):
                    V(lambda e, q=q, h=h: e.scalar_tensor_tensor(out=S0f[:, q, :], in0=S0f[:, q, :], scalar=dcs[:, h * 16 + q:h * 16 + q + 1],
                                                                 in1=ps[q // 4][:, (q % 4) * 128:(q % 4 + 1) * 128], op0=ALU.mult, op1=ALU.add),
                      reads=["S0f", "dcs", f"ps{q // 4}"], writes=["S0f"])
                S.dma("sync", lambda e, h=h: e.dma_start(out=hgs_o.rearrange("(q h p) v -> h p q v", h=4, p=128)[h][:, sq0:sq0 + 16, :], in_=S0f[:]), reads=["S0f"], is_output=True)
            A(lambda e, h=h: e.activation(out=junk[:], in_=ps[5][:, 0:128], func=AF.Square, accum_out=ssq[:, h:h + 1]), reads=["ps5"], writes=["junk", "ssq"])
            V(lambda e, h=h: e.tensor_scalar(out=ssq[:, h:h + 1], in0=ssq[:, h:h + 1], scalar1=1.0 / 128, scalar2=EPS, op0=ALU.mult, op1=ALU.add), reads=["ssq"], writes=["ssq"])
            A(lambda e, h=h: e.activation(out=ssq[:, h:h + 1], in_=ssq[:, h:h + 1], func=AF.Sqrt), reads=["ssq"], writes=["ssq"])
            V(lambda e, h=h: e.reciprocal(out=ssq[:, h:h + 1], in_=ssq[:, h:h + 1]), reads=["ssq"], writes=["ssq"])
            V(lambda e, h=h, hs=hs: e.scalar_tensor_tensor(out=ohb[:, hs], in0=ps[5][:, 0:128], scalar=ssq[:, h:h + 1], in1=gs[:, hs], op0=ALU.mult, op1=ALU.mult),
              reads=["ps5", "ssq", "gs"], writes=["ohb"])
        if KSTAGE < 4:
            return
        S.dma("sync", lambda e: e.dma_start(out=oh_scr[r0:r0 + 128, :], in_=ohb[:]), reads=["ohb"], writes=["oh_scr"])
        proj_tokmajor("w_in", 2048, 512, ti, 0, 0)
        A(lambda e: e.activation(out=fq_b[:], in_=ps[0][:], func=AF.Copy, scale=0.125), reads=["ps0"], writes=["fq_b"])
        proj_tokmajor("w_in", 2560, 512, ti, 1, 1)
        A(lambda e: e.activation(out=fk_f[:], in_=ps[1][:], func=AF.Copy), reads=["ps1"], writes=["fk_f"])
        V(lambda e: e.tensor_copy(out=fk_b[:], in_=ps[1][:]), reads=["ps1"], writes=["fk_b"])
        S.dma("sync", lambda e: e.dma_start(out=k_o[r0:r0 + 128, :], in_=fk_f[:]), reads=["fk_f"], is_output=True)
        proj_tokmajor("w_in", 3072, 512, ti, 0, 0)
        A(lambda e: e.activation(out=fv_f[:], in_=ps[0][:], func=AF.Copy), reads=["ps0"], writes=["fv_f"])
        V(lambda e: e.tensor_copy(out=vp[:, :, 0:64], in_=ps[0][:].rearrange("p (h d) -> p h d", h=8)), reads=["ps0"], writes=["vp"])
        S.dma("sync", lambda e: e.dma_start(out=v_o[r0:r0 + 128, :], in_=fv_f[:]), reads=["fv_f"], is_output=True)
        S.dma("sync", lambda e: e.dma_start(out=vp_scr[r0:r0 + 128, :], in_=vp[:].rearrange("p h d -> p (h d)")), reads=["vp"], writes=["vp_scr"])
        if KSTAGE < 5:
            return
        proj_tokmajor("w_in", 3584, 8, ti, 1, 1)
        V(lambda e: e.tensor_tensor(out=lft[:], in0=ps[1][:, 0:8], in1=fb_bc[:], op=ALU.add), reads=["ps1", "fb_bc"], writes=["lft"])
        A(lambda e: e.activation(out=lft[:], in_=lft[:], func=AF.Exp, scale=-1.0), reads=["lft"], writes=["lft"])
        A(lambda e: e.activation(out=lft[:], in_=lft[:], func=AF.Ln, bias=1.0), reads=["lft"], writes=["lft"])
        V(lambda e: e.tensor_scalar(out=lft[:], in0=lft[:], scalar1=-1.0, scalar2=None, op0=ALU.mult), reads=["lft"], writes=["lft"])
        S.dma("sync", lambda e: e.dma_start(out=lf_o[r0:r0 + 128, :], in_=lft[:]), reads=["lft"], is_output=True)
        if not sample:
            PE(lambda e: e.matmul(ps[2][:, 0:8], lhsT=cst[:, C_TRI, :], rhs=lft[:], start=True, stop=(T == 0)), reads=["cst", "lft"], writes=["ps2"])
            if T > 0:
                PE(lambda e: e.matmul(ps[2][:, 0:8], lhsT=cst[:, C_SEL127, :], rhs=Fk[:, T - 1, :], start=False, stop=True), reads=["cst", "Fk"], writes=["ps2"])
            V(lambda e: e.tensor_copy(out=Fk[:, T, :], in_=ps[2][:, 0:8]), reads=["ps2"], writes=["Fk"])
            PE(lambda e: e.matmul(ps[2][:, 8:16], lhsT=cst[:, C_SEL127, :], rhs=Fk[:, T, :], start=True, stop=True), reads=["cst", "Fk"], writes=["ps2"])
            V(lambda e: e.tensor_copy(out=Fend[:, T, :], in_=ps[2][:, 8:16]), reads=["ps2"], writes=["Fend"])
        else:
            PE(lambda e: e.matmul(ps[2][:, 0:8], lhsT=cst[:, C_TRI16, :], rhs=lft[:], start=True, stop=True), reads=["cst", "lft"], writes=["ps2"])
            V(lambda e: e.tensor_copy(out=Fk[:, T, :], in_=ps[2][:, 0:8]), reads=["ps2"], writes=["Fk"])
        for (src, res, scr) in ((fq_b, "fq_b", qT_scr), (fk_b, "fk_b", kT_scr)):
            transpose_to(tq, "tq", src, res, 4, 0, psi=7)
            S.dma("sync", lambda e, scr=scr: e.dma_start(out=scr[r0:r0 + 128, :], in_=tq[:].rearrange("p a b -> p (a b)")), reads=["tq"], writes=[scr.tensor.name])

    PH0 = os.environ.get('KPH', 'ABSC')
    for G in range(NPT // 4):
        for ti in range(4):
            T = G * 4 + ti
            S.dma("sync", lambda e, T=T, ti=ti: e.dma_start(out=xg[:, ti, :], in_=xin[T * 128:(T + 1) * 128, :]), writes=[f"xg{ti}"])
        if 'f' not in PH0:
            ffn(0, 4)
        build_xT(4)
        for ti in range(4):
            T = G * 4 + ti
            S.dma("sync", lambda e, T=T, ti=ti: e.dma_start(out=x1_scr[T * 128:(T + 1) * 128, :], in_=xg[:, ti, :]), reads=[f"xg{ti}"], writes=["x1_scr"])
            if 'F' in PH0:
                S.dma("sync", lambda e, T=T, ti=ti: e.dma_start(out=y_o[T * 128:(T + 1) * 128, :], in_=xg[:, ti, :]), reads=[f"xg{ti}"], is_output=True)
            else:
                phaseA_tile(T, ti, False)
    for h in range(4):
        S.dma("sync", lambda e, h=h: e.dma_start(out=hgp_o[h * 128:(h + 1) * 128, :], in_=Sst[h][:]), reads=[f"Sst{h}"], is_output=True)
    for st in range(NS):
        S.dma("sync", lambda e, st=st: e.dma_start(out=xg[:, st, :], in_=xin[(SMP + st) * 128:(SMP + st + 1) * 128, :]), writes=[f"xg{st}"])
    if 'F' in PH0:
        S.emit()
        return nc
    ffn(0, NS)
    build_xT(NS)
    for st in range(NS):
        S.dma("sync", lambda e, st=st: e.dma_start(out=x1_scr[(SMP + st) * 128:(SMP + st + 1) * 128, :], in_=xg[:, st, :]), reads=[f"xg{st}"], writes=["x1_scr"])
        phaseA_tile(SMP + st, st, True)

    PH = os.environ.get('KPH', 'ABSC')
    NKB = 3
    kTt = [sb(f"kTt{i}", [128, 4, 128], BF16) for i in range(NKB)]
    vpt = [sb(f"vpt{i}", [128, 8, 65], BF16) for i in range(NKB)]
    qTt = sb("qTt", [128, 4, 128], BF16)
    wq = [sb(f"wq{i}", [128, 8]) for i in range(2)]
    Vs = [sb(f"Vs{i}", [128, 8, 65], BF16) for i in range(2)]
    pT = [sb(f"pT{i}", [128, 2, 4, 128], BF16) for i in range(2)]
    ofb = sb("ofb", [128, 512], BF16)
    rs = sb("rs", [128, 8])
    G_ = lambda fn, reads=(), writes=(): S.op(os.environ.get("KGENG", "vector"), fn, reads, writes)

    def attn_block(b, kq, k_tile, k_res, q_cols, v_src, v_res, w_ap, w_res, first, last, mask_c, pt_out):
        for h in range(8):
            G_(lambda e, h=h: e.tensor_scalar(out=Vs[b][:, h, :], in0=v_src[:, h, :], scalar1=w_ap[:, h:h + 1], scalar2=None, op0=ALU.mult),
               reads=[v_res, w_res], writes=[f"Vs{b}"])
        nq = q_cols.stop - q_cols.start
        for h in range(8):
            pr, po = h // 2, (h % 2) * 64
            bank = 2 * kq + h % 2
            PE(lambda e, h=h, pr=pr, po=po, bank=bank: e.matmul(ps[bank][:, (h // 2) * nq:(h // 2 + 1) * nq], lhsT=k_tile[po:po + 64, pr, :], rhs=qTt[po:po + 64, pr, q_cols],
                                                              start=True, stop=True, skip_group_check=True),
               reads=[k_res, "qTt"], writes=[f"ps{bank}"])
        for half in range(2):
            bank = 2 * kq + half
            A(lambda e, half=half, bank=bank: e.activation(out=pt_out(half), in_=ps[bank][:, 0:4 * nq].rearrange("p (h q) -> p h q", h=4), func=AF.Exp),
              reads=[f"ps{bank}"], writes=[f"pT{b}"])
        if mask_c is not None:
            for h in range(8):
                V(lambda e, h=h: e.tensor_tensor(out=pT[b][:, h % 2, h // 2, :], in0=pT[b][:, h % 2, h // 2, :], in1=cstb[:, mask_c, :], op=ALU.mult), reads=[f"pT{b}", "cstb"], writes=[f"pT{b}"])
        for h in range(8):
            bank, c0 = 4 + h // 4, (h % 4) * 65
            PE(lambda e, h=h, bank=bank, c0=c0: e.matmul(ps[bank][:, c0:c0 + 65], lhsT=pT[b][:, h % 2, h // 2, :], rhs=Vs[b][:, h, :],
                                                       start=(first and h % 4 == 0), stop=last, skip_group_check=True),
               reads=[f"pT{b}", f"Vs{b}"], writes=[f"ps{bank}"])

    def attn_finish(row0):
        for h in range(8):
            bank, c0 = 4 + h // 4, (h % 4) * 65
            V(lambda e, h=h, bank=bank, c0=c0: e.reciprocal(out=rs[:, h:h + 1], in_=ps[bank][:, c0 + 64:c0 + 65]), reads=[f"ps{bank}"], writes=["rs"])
            V(lambda e, h=h, bank=bank, c0=c0: e.tensor_scalar(out=ofb[:, h * 64:(h + 1) * 64], in0=ps[bank][:, c0:c0 + 64], scalar1=rs[:, h:h + 1], scalar2=None, op0=ALU.mult),
              reads=[f"ps{bank}", "rs"], writes=["ofb"])
        S.dma("sync", lambda e: e.dma_start(out=of_scr[row0:row0 + 128, :], in_=ofb[:]), reads=["ofb"], writes=["of_scr"])

    cnt = [0]
    for qt in range(NPT if 'B' in PH else 0):
        S.dma("sync", lambda e, qt=qt: e.dma_start(out=qTt[:].rearrange("p a b -> p (a b)"), in_=qT_scr[qt * 128:(qt + 1) * 128, :]), reads=["qT_scr"], writes=["qTt"])
        for kt in range(qt + 1):
            b = cnt[0] % 2; g = cnt[0] % NKB; cnt[0] += 1
            S.dma("sync", lambda e, kt=kt, g=g: e.dma_start(out=kTt[g][:].rearrange("p a b -> p (a b)"), in_=kT_scr[kt * 128:(kt + 1) * 128, :]), reads=["kT_scr"], writes=[f"kTt{g}"])
            S.dma("sync", lambda e, kt=kt, g=g: e.dma_start(out=vpt[g][:].rearrange("p a b -> p (a b)"), in_=vp_scr[kt * 128:(kt + 1) * 128, :]), reads=["vp_scr"], writes=[f"vpt{g}"])
            V(lambda e, qt=qt, kt=kt, b=b: e.tensor_tensor(out=wq[b][:], in0=Fend[:, qt, :], in1=Fk[:, kt, :], op=ALU.subtract), reads=["Fend", "Fk"], writes=[f"wq{b}"])
            V(lambda e, b=b: e.tensor_scalar(out=wq[b][:], in0=wq[b][:], scalar1=0.0, scalar2=None, op0=ALU.min), reads=[f"wq{b}"], writes=[f"wq{b}"])
            A(lambda e, b=b: e.activation(out=wq[b][:], in_=wq[b][:], func=AF.Exp), reads=[f"wq{b}"], writes=[f"wq{b}"])
            attn_block(b, b, kTt[g], f"kTt{g}", slice(0, 128), vpt[g], f"vpt{g}", wq[b], f"wq{b}", kt == 0, kt == qt,
                       C_TRI if kt == qt else None, lambda half, b=b: pT[b][:, half, :, :])
        attn_finish(qt * 128)

    def sample_attn():
        pts = sb("pts", [128, NS * 256], I32)
        pidx = pts
        iot = sb("iot", [128, 1], I32)
        NKS = 2
        kpg = [sb(f"kpg{i}", [128, 512], BF16) for i in range(NKS)]
        vpg = [sb(f"vpg{i}", [128, 8, 65], BF16) for i in range(2)]
        vgt = [sb(f"vgt{i}", [128, 512], BF16) for i in range(NKS)]
        lfp = sb("lfp", [128, 16, 8])
        lat = sb("lat", [128, 16, 8])
        Rb = sb("Rb", [128, 16, 8])
        kTp = [sb(f"kTp{i}", [128, 4, 128], BF16) for i in range(2)]
        S.dma("sync", lambda e: e.dma_start(out=pts[:], in_=ptab[0:1, :].partition_broadcast(128)), writes=["pts"])
        S.op("gpsimd", lambda e: e.iota(iot[:], pattern=[[0, 1]], base=0, channel_multiplier=1), writes=["iot"])
        S.op("gpsimd", lambda e: e.tensor_scalar(out=pidx[:], in0=pts[:], scalar1=128, scalar2=iot[:, 0:1], op0=ALU.mult, op1=ALU.add), reads=["pts", "iot"], writes=["pts"])
        for i in range(2):
            V(lambda e, i=i: e.memset(pT[i][:], 0.0), writes=[f"pT{i}"])
            V(lambda e, i=i: e.memset(vpg[i][:], 1.0), writes=[f"vpg{i}"])
        pcnt = [0]
        for st in range(NS):
            TS = SMP + st
            S.dma("sync", lambda e, TS=TS: e.dma_start(out=qTt[:].rearrange("p a b -> p (a b)"), in_=qT_scr[TS * 128:(TS + 1) * 128, :]), reads=["qT_scr"], writes=["qTt"])
            S.dma("sync", lambda e, TS=TS: e.dma_start(out=kTt[0][:].rearrange("p a b -> p (a b)"), in_=kT_scr[TS * 128:(TS + 1) * 128, :]), reads=["kT_scr"], writes=["kTt0"])
            S.dma("sync", lambda e, TS=TS: e.dma_start(out=vpt[0][:].rearrange("p a b -> p (a b)"), in_=vp_scr[TS * 128:(TS + 1) * 128, :]), reads=["vp_scr"], writes=["vpt0"])
            A(lambda e, TS=TS: e.activation(out=wq[0][:], in_=Fk[:, TS, :], func=AF.Exp, scale=-1.0), reads=["Fk"], writes=["wq0"])
            attn_block(0, 0, kTt[0], "kTt0", slice(0, 128), vpt[0], "vpt0", wq[0], "wq0", True, False, C_TRI16,
                       lambda half: pT[0][:, half, :, :])
            V(lambda e: e.memset(pT[0][:], 0.0), reads=[], writes=["pT0"])
            V(lambda e: e.memset(pT[1][:], 0.0), reads=[], writes=["pT1"])
            for q in range(16):
                for pg in range(16):
                    S.dma("gpsimd", lambda e, q=q, pg=pg, st=st: e.indirect_dma_start(out=lfp[:, pg, :], out_offset=None, in_=clf,
                                                                                    in_offset=bass.IndirectOffsetOnAxis(ap=pidx[:, st * 256 + q * 16 + pg:st * 256 + q * 16 + pg + 1], axis=0)),
                          reads=["pts"], writes=["lfp"])
                V(lambda e: e.memset(lat[:, 15, :], 0.0), writes=["lat"])
                for pg in range(14, -1, -1):
                    V(lambda e, pg=pg: e.tensor_tensor(out=lat[:, pg, :], in0=lat[:, pg + 1, :], in1=lfp[:, pg + 1, :], op=ALU.add), reads=["lat", "lfp"], writes=["lat"])
                PE(lambda e: e.matmul(ps[6][:, 0:128], lhsT=cst[:, C_SUP, :], rhs=lfp[:].rearrange("p a b -> p (a b)"), start=True, stop=False), reads=["cst", "lfp"], writes=["ps6"])
                PE(lambda e: e.matmul(ps[6][:, 0:128], lhsT=cst[:, C_ONES, :], rhs=lat[:].rearrange("p a b -> p (a b)"), start=False, stop=True), reads=["cst", "lat"], writes=["ps6"])
                A(lambda e: e.activation(out=Rb[:].rearrange("p a b -> p (a b)"), in_=ps[6][:, 0:128], func=AF.Exp), reads=["ps6"], writes=["Rb"])
                for pg in range(16):
                    b = pcnt[0] % 2; g = pcnt[0] % NKS; pcnt[0] += 1
                    col = st * 256 + q * 16 + pg
                    S.dma("gpsimd", lambda e, g=g, col=col: e.indirect_dma_start(out=kpg[g][:], out_offset=None, in_=ck,
                                                                               in_offset=bass.IndirectOffsetOnAxis(ap=pidx[:, col:col + 1], axis=0)),
                          reads=["pts"], writes=[f"kpg{g}"])
                    S.dma("gpsimd", lambda e, g=g, col=col: e.indirect_dma_start(out=vgt[g][:], out_offset=None, in_=cv_,
                                                                               in_offset=bass.IndirectOffsetOnAxis(ap=pidx[:, col:col + 1], axis=0)),
                          reads=["pts"], writes=[f"vgt{g}"])
                    V(lambda e, b=b, g=g: e.tensor_copy(out=vpg[b][:, :, 0:64], in_=vgt[g][:].rearrange("p (h d) -> p h d", h=8)), reads=[f"vgt{g}"], writes=[f"vpg{b}"])
                    transpose_to(kTp[b], f"kTp{b}", kpg[g], f"kpg{g}", 4, 0, psi=7)
                    last = (q == 15 and pg == 15)
                    attn_block(b, b, kTp[b], f"kTp{b}", slice(q * 8, (q + 1) * 8), vpg[b], f"vpg{b}", Rb[:, pg, :], "Rb", False, last, None,
                               lambda half, b=b, q=q: pT[b][:, half, :, q * 8:(q + 1) * 8])
                for i in range(2):
                    V(lambda e, i=i, q=q: e.memset(pT[i][:].rearrange("p a b q -> p (a b) q")[:, :, q * 8:(q + 1) * 8], 0.0), writes=[f"pT{i}"])
            attn_finish(TS * 128)

    if 'S' in PH:
        sample_attn()

    ocat = tb[2][:, :]
    uu = tf[0][:, :]
    vv = tf[1][:, :]
    vvb = tb[0][:, :]
    wsT = sb("wsT", [128, 8, 128], BF16)
    wsS = sb("wsS", [128, 8, 128], BF16)
    bsT = sb("bsT", [128, 8])
    bsS = sb("bsS", [128, 8])
    wsl = sb("wsl", [128, 128])
    bsl = sb("bsl", [8, 128])
    w8 = sb("w8", [8, 8])
    b8 = sb("b8", [8, 128])
    umx = tb[1][:, :]
    gq = tf[2][:, 0:512]
    S.alias["gq"] = "tf2"

    S.dma("sync", lambda e: e.dma_start(out=bsl[:], in_=c_b_s[:, :]), writes=["bsl"])
    PE(lambda e: e.transpose(ps[0][:, 0:8], bsl[:], cst[0:8, C_ID, 0:8]), reads=["bsl", "cst"], writes=["ps0"])
    V(lambda e: e.tensor_copy(out=bsT[:], in_=ps[0][:, 0:8]), reads=["ps0"], writes=["bsT"])
    PE(lambda e: e.matmul(ps[0][:, 8:16], lhsT=cst[0:8, C_R, :], rhs=bsT[0:8, :], start=True, stop=True), reads=["cst", "bsT"], writes=["ps0"])
    V(lambda e: e.tensor_copy(out=bsS[:], in_=ps[0][:, 8:16]), reads=["ps0"], writes=["bsS"])
    for g in range(8):
        S.dma("sync", lambda e, g=g: e.dma_start(out=wsl[:], in_=c_w_s[g * 128:(g + 1) * 128, :]), writes=["wsl"])
        PE(lambda e: e.transpose(ps[1][:, 0:128], wsl[:], cst[:, C_ID, :]), reads=["wsl", "cst"], writes=["ps1"])
        V(lambda e, g=g: e.tensor_tensor(out=wsT[:, g, :], in0=ps[1][:, 0:128], in1=cst[:, C_TRI, :], op=ALU.mult), reads=["ps1", "cst"], writes=["wsT"])
        PE(lambda e: e.matmul(ps[1][0:8, 128:256], lhsT=wsl[0:8, 0:8], rhs=cst[0:8, C_R, :], start=True, stop=True), reads=["wsl", "cst"], writes=["ps1"])
        V(lambda e: e.tensor_copy(out=b8[:], in_=ps[1][0:8, 128:256]), reads=["ps1"], writes=["b8"])
        PE(lambda e: e.matmul(ps[1][:, 256:384], lhsT=cst[0:8, C_R, :], rhs=b8[:], start=True, stop=True), reads=["cst", "b8"], writes=["ps1"])
        V(lambda e, g=g: e.tensor_tensor(out=wsS[:, g, :], in0=ps[1][:, 256:384], in1=cst[:, C_TRI16, :], op=ALU.mult), reads=["ps1", "cst"], writes=["wsS"])

    def phaseC_group(T0, nt, sample):
        for ti in range(nt):
            r0 = (T0 + ti) * 128
            S.dma("sync", lambda e, r0=r0, ti=ti: e.dma_start(out=xg[:, ti, :], in_=x1_scr[r0:r0 + 128, :]), reads=["x1_scr"], writes=[f"xg{ti}"])
            S.dma("sync", lambda e, r0=r0: e.dma_start(out=ocat[:, 0:512], in_=oh_scr[r0:r0 + 128, :]), reads=["oh_scr"], writes=["ocat"])
            S.dma("sync", lambda e, r0=r0: e.dma_start(out=ocat[:, 512:1024], in_=of_scr[r0:r0 + 128, :]), reads=["of_scr"], writes=["ocat"])
            transpose_to(xT, "xT", ocat, "ocat", 8, ti * 128)
        for ti in range(nt):
            for c in range(2):
                proj_tokmajor("w_out", c * 512, 512, ti, 4 + c, c)
                V(lambda e, ti=ti, c=c: e.scalar_tensor_tensor(out=xg[:, ti, c * 512:(c + 1) * 512], in0=xg[:, ti, c * 512:(c + 1) * 512], scalar=ALPHA,
                                                               in1=ps[4 + c][:, :], op0=ALU.mult, op1=ALU.add), reads=[f"ps{4 + c}", f"xg{ti}"], writes=[f"xg{ti}"])
            layer_norm(xg[:, ti, :], f"xg{ti}", xg[:, ti, :], f"xg{ti}", 1)
        ffn(1, nt)
        ffn(2, nt)
        build_xT(nt)
        for ti in range(nt):
            for c in range(4):
                proj_tokmajor("c_w_in", c * 512, 512, ti, c % 2, c % 2)
                dstt = uu if c < 2 else vv
                dres = "uu" if c < 2 else "vv"
                dsl = dstt[:, (c % 2) * 512:(c % 2 + 1) * 512]
                A(lambda e, c=c: e.activation(out=gq[:], in_=ps[c % 2][:, :], func=AF.Square), reads=[f"ps{c % 2}"], writes=["gq"])
                V(lambda e: e.tensor_scalar(out=gq[:], in0=gq[:], scalar1=0.044715, scalar2=1.0, op0=ALU.mult, op1=ALU.add), reads=["gq"], writes=["gq"])
                V(lambda e, c=c: e.tensor_tensor(out=gq[:], in0=gq[:], in1=ps[c % 2][:, :], op=ALU.mult), reads=["gq", f"ps{c % 2}"], writes=["gq"])
                A(lambda e: e.activation(out=gq[:], in_=gq[:], func=AF.Sigmoid, scale=1.5957691216057308), reads=["gq"], writes=["gq"])
                V(lambda e, c=c, dsl=dsl: e.tensor_tensor(out=dsl, in0=gq[:], in1=ps[c % 2][:, :], op=ALU.mult), reads=["gq", f"ps{c % 2}"], writes=[dres])
            layer_norm(vv[:], "vv", vv[:], "vv", 0, g_ap=c_ln_g[0:1, :], b_ap=c_ln_b[0:1, :])
            if sample:
                S.dma("sync", lambda e, ti=ti: e.dma_start(out=cv_o[ti * 128:(ti + 1) * 128, :], in_=vv[:]), reads=["vv"], is_output=True)
            A(lambda e: e.activation(out=vvb[:], in_=vv[:], func=AF.Copy), reads=["vv"], writes=["vvb"])
            wsx = wsS if sample else wsT
            bsx = bsS if sample else bsT
            for g in range(8):
                bank = 4 + g // 4
                cs = slice((g % 4) * 128, (g % 4 + 1) * 128)
                PE(lambda e, g=g, bank=bank, cs=cs, wsx=wsx: e.matmul(ps[bank][:, cs], lhsT=wsx[:, g, :], rhs=vvb[:, g * 128:(g + 1) * 128], start=True, stop=True),
                   reads=["wsT", "wsS", "vvb"], writes=[f"ps{bank}"])
                V(lambda e, g=g, bank=bank, cs=cs, bsx=bsx: e.scalar_tensor_tensor(out=umx[:, g * 128:(g + 1) * 128], in0=ps[bank][:, cs], scalar=bsx[:, g:g + 1],
                                                                                   in1=uu[:, g * 128:(g + 1) * 128], op0=ALU.add, op1=ALU.mult),
                  reads=[f"ps{bank}", "bsT", "bsS", "uu"], writes=["umx"])
            transpose_to(xT, "xT", umx, "umx", 8, ti * 128)
        for ti in range(nt):
            for c in range(2):
                proj_tokmajor("c_w_out", c * 512, 512, ti, 4 + c, c)
                V(lambda e, ti=ti, c=c: e.scalar_tensor_tensor(out=xg[:, ti, c * 512:(c + 1) * 512], in0=xg[:, ti, c * 512:(c + 1) * 512], scalar=ALPHA,
                                                               in1=ps[4 + c][:, :], op0=ALU.mult, op1=ALU.add), reads=[f"ps{4 + c}", f"xg{ti}"], writes=[f"xg{ti}"])
            layer_norm(xg[:, ti, :], f"xg{ti}", xg[:, ti, :], f"xg{ti}", 4)
        ffn(3, nt)
        for ti in range(nt):
            r0 = (T0 + ti) * 128
            S.dma("sync", lambda e, r0=r0, ti=ti: e.dma_start(out=y_o[r0:r0 + 128, :], in_=xg[:, ti, :]), reads=[f"xg{ti}"], is_output=True)

    if 'C' in PH:
        for G in range(NPT // 4):
            phaseC_group(G * 4, 4, False)
        phaseC_group(SMP, NS, True)

    print('sbuf bytes remaining', nc.sbuf_bytes_remaining)
    S.emit()
    print('instr counts', {e: len(S.q[e]) for e in ENGS})
    return nc


_NC_CACHE = {}
NCORES = 2
NS_ = 4


def kernel(x_prompt, x_sample, cache_fox_k, cache_fox_v, cache_fox_logf, state_hg, page_table,
           ln_g, ln_b, ffn_w_gate, ffn_w_up, ffn_w_down, ab_w_in, hg_lb_logits, hg_norm_g, fox_f_bias,
           ab_w_out, c_w_in, c_ln_g, c_ln_b, c_w_s, c_b_s, c_w_out):
    f = lambda a: np.ascontiguousarray(np.asarray(a, dtype=np.float32))
    x_prompt = f(x_prompt); x_sample = f(x_sample)
    P = x_prompt.shape[1]
    nphys = int(os.environ.get('KNPHYS', str(N_PHYS)))
    if nphys != N_PHYS:
        cache_fox_k = np.asarray(cache_fox_k)[:, :nphys]; cache_fox_v = np.asarray(cache_fox_v)[:, :nphys]
        cache_fox_logf = np.asarray(cache_fox_logf)[:, :nphys]; page_table = np.asarray(page_table) % nphys
    if P not in _NC_CACHE:
        _NC_CACHE[P] = build_program(P // 128, nphys, NS_)
    nc = _NC_CACHE[P]
    shared = {
        "cache_k": f(cache_fox_k).reshape(nphys * 128, 512),
        "cache_v": f(cache_fox_v).reshape(nphys * 128, 512),
        "cache_lf": f(cache_fox_logf).reshape(nphys * 128, 8),
        "ln_g": f(ln_g).reshape(6, D), "ln_b": f(ln_b).reshape(6, D),
        "w_gate": f(ffn_w_gate).reshape(4, D, FF), "w_up": f(ffn_w_up).reshape(4, D, FF),
        "w_down": f(ffn_w_down).reshape(4, FF, D),
        "ab_w_in": f(ab_w_in).reshape(D, 3592), "lb_logits": f(hg_lb_logits).reshape(1, 1536),
        "hg_norm_g": f(hg_norm_g).reshape(1, 128), "fox_f_bias": f(fox_f_bias).reshape(1, 8),
        "ab_w_out": f(ab_w_out).reshape(D, D), "c_w_in": f(c_w_in).reshape(D, 2048),
        "c_ln_g": f(c_ln_g).reshape(1, D), "c_ln_b": f(c_ln_b).reshape(1, D),
        "c_w_s": f(c_w_s).reshape(1024, 128), "c_b_s": f(c_b_s).reshape(8, 128),
        "c_w_out": f(c_w_out).reshape(D, D), "cst": make_consts().reshape(128, NCST * 128),
    }
    pt = np.asarray(page_table, dtype=np.int32)
    SQ = 16 * NS_
    in_maps = []
    for c in range(NCORES):
        m = dict(shared)
        m["xin"] = np.concatenate([x_prompt[c], x_sample[SQ * c:SQ * (c + 1)].reshape(SQ * 8, D)], axis=0)
        m["state_hg"] = f(state_hg)[0, SQ * c:SQ * (c + 1)].reshape(SQ * 4 * 128, 128)
        m["ptab"] = np.ascontiguousarray(pt[SQ * c:SQ * (c + 1)].reshape(1, SQ * 16))
        in_maps.append(m)
    ncores = int(os.environ.get('KCORES', str(NCORES)))
    res = run_bass_kernel_spmd(nc, in_maps[:ncores], core_ids=list(range(ncores))).results
    res = list(res) + [res[0]] * (NCORES - ncores)
    R = range(NCORES)
    y_p = np.stack([res[b]["y"][:P] for b in R])
    y_s = np.concatenate([res[c]["y"][P:].reshape(SQ, 8, D) for c in R])
    k_p = np.stack([res[b]["k_new"][:P].reshape(P, 8, 64) for b in R])[None]
    v_p = np.stack([res[b]["v_new"][:P].reshape(P, 8, 64) for b in R])[None]
    lf_p = np.stack([res[b]["lf_new"][:P] for b in R])[None]
    hg_p = np.stack([res[b]["hg_p"].reshape(4, 128, 128) for b in R])[None]
    k_s = np.concatenate([res[c]["k_new"][P:].reshape(SQ, 8, 8, 64) for c in R])[None]
    v_s = np.concatenate([res[c]["v_new"][P:].reshape(SQ, 8, 8, 64) for c in R])[None]
    lf_s = np.concatenate([res[c]["lf_new"][P:].reshape(SQ, 8, 8) for c in R])[None]
    hg_s = np.concatenate([res[c]["hg_s"].reshape(SQ, 4, 128, 128) for c in R])[None]
    cv_s = np.concatenate([res[c]["cv_s"].reshape(SQ, 8, D) for c in R])[None]
    return (y_p, y_s, k_p, v_p, lf_p, hg_p, k_s, v_s, lf_s, hg_s, cv_s)
```

```python
import os
import numpy as np
import concourse.bass as bass
import concourse.mybir as mybir
from concourse.bass_utils import run_bass_kernel_spmd

F32 = mybir.dt.float32
BF16 = mybir.dt.bfloat16
I32 = mybir.dt.int32
AF = mybir.ActivationFunctionType
ALU = mybir.AluOpType

D = 1024
FF = 2816
NJ = 22
NPT = 64
NT = 65
NTOK = NT * 128
ALPHA = 4.0 ** 0.25
EPS = 1e-5
N_PHYS = 2560
ENGS = ("tensor", "vector", "scalar", "gpsimd", "sync")
N_DMA_SEMS = 16
SEM_EPOCH = 16000

C_ID, C_TRI, C_SEL127, C_TRI2, C_BLK2, C_TRI16, C_BLK16, C_SELEND2, C_SELEND16, C_R, C_SUP, C_ONES, C_NEG = range(13)
NCST = 13


def make_consts():
    c = np.zeros((128, NCST, 128), np.float32)
    s = np.arange(128)[:, None]
    t = np.arange(128)[None, :]
    c[:, C_ID] = (s == t)
    c[:, C_TRI] = (s <= t)
    c[:, C_SEL127] = (s == 127) * np.ones((1, 128))
    c[:, C_TRI2] = (s // 64 == t // 64) & (s <= t)
    c[:, C_BLK2] = (s // 64 == t // 64)
    c[:, C_TRI16] = (s // 8 == t // 8) & (s <= t)
    c[:, C_BLK16] = (s // 8 == t // 8)
    c[:, C_SELEND2][:, 0] = (np.arange(128) == 63)
    c[:, C_SELEND2][:, 1] = (np.arange(128) == 127)
    for q in range(16):
        c[q * 8 + 7, C_SELEND16, q] = 1.0
    for i in range(8):
        c[i, C_R, i::8] = 1.0
    c[:, C_SUP] = (s > t)
    c[:, C_ONES] = 1.0
    c[:, C_NEG] = -1.0
    return c


class Sched:
    def __init__(self, nc):
        self.nc = nc
        self.q = {e: [] for e in ENGS}
        self.cnt = {e: 0 for e in ENGS}
        self.epoch = {e: 0 for e in ENGS}
        self.sem = {e: nc.alloc_semaphore(f"s_{e}_0") for e in ENGS}
        self.dsem = {e: [nc.alloc_semaphore(f"d_{e}_{i}") for i in range(N_DMA_SEMS)]
                     for e in ("sync", "scalar", "gpsimd")}
        self.dcnt = {e: [0] * N_DMA_SEMS for e in ("sync", "scalar", "gpsimd")}
        self.dnext = {e: 0 for e in ("sync", "scalar", "gpsimd")}
        self.known = {e: {} for e in ENGS}
        self.lastw = {}
        self.reads = {}
        self.semobj = {}
        for e in ENGS:
            self.semobj[("e", e, 0)] = self.sem[e]
        for e in self.dsem:
            for i, s in enumerate(self.dsem[e]):
                self.semobj[("d", e, i)] = s
        self.out_deps = []
        self.alias = {}

    def _need(self, eng, deps):
        need = {}
        for (k, v) in deps:
            if self.known[eng].get(k, 0) >= v:
                continue
            if need.get(k, 0) < v:
                need[k] = v
        for k, v in need.items():
            self.known[eng][k] = v
        return list(need.items())

    def _deps(self, reads, writes):
        reads = [self.alias.get(r, r) for r in reads]
        writes = [self.alias.get(w, w) for w in writes]
        deps = []
        for r in reads:
            if r in self.lastw:
                deps.append(self.lastw[r])
        for w in writes:
            if w in self.lastw:
                deps.append(self.lastw[w])
            deps.extend(self.reads.get(w, ()))
        return deps

    def _commit(self, tok, reads, writes):
        reads = [self.alias.get(r, r) for r in reads]
        writes = [self.alias.get(w, w) for w in writes]
        for r in reads:
            self.reads.setdefault(r, []).append(tok)
        for w in writes:
            self.lastw[w] = tok
            self.reads[w] = []

    def op(self, eng, fn, reads=(), writes=()):
        pr = [r for r in reads if r.startswith("ps")]
        if pr:
            reads = [r for r in reads if not r.startswith("ps")]
            writes = list(writes) + pr
        deps = self._deps(reads, writes)
        waits = self._need(eng, deps)
        if self.cnt[eng] >= SEM_EPOCH:
            self.epoch[eng] += 1
            self.cnt[eng] = 0
            self.sem[eng] = self.nc.alloc_semaphore(f"s_{eng}_{self.epoch[eng]}")
            self.semobj[("e", eng, self.epoch[eng])] = self.sem[eng]
        self.cnt[eng] += 1
        key = ("e", eng, self.epoch[eng])
        tok = (key, self.cnt[eng])
        if eng == "tensor":
            self.known[eng][key] = self.cnt[eng]
        self.q[eng].append((waits, fn, (self.sem[eng], 1)))
        self._commit(tok, reads, writes)
        return tok

    def dma(self, eng, fn, reads=(), writes=(), is_output=False):
        deps = self._deps(reads, writes)
        i = self.dnext[eng]
        self.dnext[eng] = (i + 1) % N_DMA_SEMS
        key = ("d", eng, i)
        if self.dcnt[eng][i] > 0:
            deps.append((key, self.dcnt[eng][i]))
        waits = self._need(eng, deps)
        self.dcnt[eng][i] += 16
        tok = (key, self.dcnt[eng][i])
        self.q[eng].append((waits, fn, (self.dsem[eng][i], 16)))
        self._commit(tok, reads, writes)
        if is_output:
            self.out_deps.append(tok)
        return tok

    def emit(self):
        nc = self.nc
        fin = list(self.out_deps)
        for r, t in self.lastw.items():
            fin.append(t)
        waits = self._need("sync", fin)
        self.q["sync"].append((waits, None, None))
        with nc.Block() as block:
            def run(engname):
                def body(eng):
                    for waits, fn, inc in self.q[engname]:
                        for k, v in waits:
                            eng.wait_ge(self.semobj[k], v)
                        if fn is not None:
                            ins = fn(eng)
                            ins.then_inc(inc[0], inc[1])
                return body
            block.tensor(run("tensor"))
            block.vector(run("vector"))
            block.scalar(run("scalar"))
            block.gpsimd(run("gpsimd"))
            block.sync(run("sync"))


def build_program(NPT=NPT, N_PHYS=N_PHYS, NS=4):
    NT = NPT + NS
    NTOK = NT * 128
    SMP = NPT
    nc = bass.Bass("TRN2", target_bir_lowering=False)
    S = Sched(nc)

    def din(name, shape, dt=F32):
        return nc.dram_tensor(name, list(shape), dt, kind="ExternalInput").ap()

    def dout(name, shape, dt=F32):
        return nc.dram_tensor(name, list(shape), dt, kind="ExternalOutput").ap()

    def dscr(name, shape, dt):
        return nc.dram_tensor(name, list(shape), dt).ap()

    def sb(name, shape, dt=F32):
        return nc.alloc_sbuf_tensor(name, list(shape), dt)

    xin = din("xin", [NTOK, D])
    ck = din("cache_k", [N_PHYS * 128, 512])
    cv_ = din("cache_v", [N_PHYS * 128, 512])
    clf = din("cache_lf", [N_PHYS * 128, 8])
    st_hg = din("state_hg", [NS * 16 * 4 * 128, 128])
    ptab = din("ptab", [1, NS * 256], I32)
    ln_g = din("ln_g", [6, D])
    ln_b = din("ln_b", [6, D])
    wgate = din("w_gate", [4, D, FF])
    wup = din("w_up", [4, D, FF])
    wdown = din("w_down", [4, FF, D])
    w_in = din("ab_w_in", [D, 3592])
    lb_log = din("lb_logits", [1, 3 * 512])
    hg_ng = din("hg_norm_g", [1, 128])
    fbias = din("fox_f_bias", [1, 8])
    w_out = din("ab_w_out", [D, D])
    c_w_in = din("c_w_in", [D, 2048])
    c_ln_g = din("c_ln_g", [1, D])
    c_ln_b = din("c_ln_b", [1, D])
    c_w_s = din("c_w_s", [8 * 128, 128])
    c_b_s = din("c_b_s", [8, 128])
    c_w_out = din("c_w_out", [D, D])
    cst_d = din("cst", [128, NCST * 128])

    y_o = dout("y", [NTOK, D])
    k_o = dout("k_new", [NTOK, 512])
    v_o = dout("v_new", [NTOK, 512])
    lf_o = dout("lf_new", [NTOK, 8])
    hgp_o = dout("hg_p", [4 * 128, 128])
    hgs_o = dout("hg_s", [NS * 16 * 4 * 128, 128])
    cv_o = dout("cv_s", [NS * 128, D])

    wg_scr = dscr("wg_scr", [4 * NJ * 128, 1024], BF16)
    wu_scr = dscr("wu_scr", [4 * NJ * 128, 1024], BF16)
    wd_scr = dscr("wd_scr", [4 * 4 * 128, NJ * 256], BF16)
    win_scr = dscr("win_scr", [128, 8 * 3592], BF16)
    wout_scr = dscr("wout_scr", [128, 8 * D], BF16)
    cwin_scr = dscr("cwin_scr", [128, 8 * 2048], BF16)
    cwout_scr = dscr("cwout_scr", [128, 8 * D], BF16)
    x1_scr = dscr("x1_scr", [NTOK, D], F32)
    oh_scr = dscr("oh_scr", [NTOK, 512], BF16)
    of_scr = dscr("of_scr", [NTOK, 512], BF16)
    qT_scr = dscr("qT_scr", [NT * 128, 512], BF16)
    kT_scr = dscr("kT_scr", [NT * 128, 512], BF16)
    vp_scr = dscr("vp_scr", [NT * 128, 8 * 65], BF16)

    cst = sb("cstt", [128, NCST, 128])
    cstb = sb("cstb", [128, NCST, 128], BF16)
    identb = cstb[:, C_ID, :]
    ps = [nc.alloc_psum_tensor(f"ps{i}", [128, 512], F32) for i in range(8)]

    def psb(i):
        return ps[i][:].bitcast(BF16)

    xg = sb("xg", [128, 4, D])
    xT = sb("xT", [128, 8, 512], BF16)
    hT = sb("hT", [128, NJ, 512], BF16)
    wgs = [sb(f"wgs{i}", [128, 8, 128], BF16) for i in range(2)]
    wus = [sb(f"wus{i}", [128, 8, 128], BF16) for i in range(2)]
    wds = sb("wds", [128, NJ, 256], BF16)
    wps = [sb(f"wps{i}", [128, 8, 512], BF16) for i in range(2)]
    gb = [sb(f"gb{i}", [128, D]) for i in range(2)]
    zt = sb("zt", [128, D])
    tf = [sb(f"tf{i}", [128, D]) for i in range(4)]
    tb = [sb(f"tb{i}", [128, D], BF16) for i in range(4)]
    S.alias.update({"qh": "tf0", "kk": "tf0", "gl": "tf1", "gs": "tf1", "bcs": "tf2", "ebl": "tf2", "fk_f": "tf3", "fv_f": "tf3",
                    "vh": "tb0", "qtl": "tb0", "ktl": "tb1", "khat": "tb1", "ohb": "tb2", "fq_b": "tb2", "fk_b": "tb3",
                    "uu": "tf0", "vv": "tf1", "vvb": "tb0", "umx": "tb1", "ocat": "tb2", "tmpA": "tf2"})
    xb16 = sb("xb16", [128, D], BF16)
    sg_t = sb("sg_t", [128, 512], BF16)
    stats = sb("stats", [128, 2, 6])
    mv = sb("mv", [128, 2])
    rstd = sb("rstd", [128, 1])

    S.dma("sync", lambda e: e.dma_start(out=cst[:].rearrange("p a b -> p (a b)"), in_=cst_d[:, :]), writes=["cst"])
    S.op("vector", lambda e: e.tensor_copy(out=cstb[:], in_=cst[:]), reads=["cst"], writes=["cstb"])

    def V(fn, reads=(), writes=()):
        return S.op("vector", fn, reads, writes)

    def A(fn, reads=(), writes=()):
        return S.op("scalar", fn, reads, writes)

    def PE(fn, reads=(), writes=()):
        return S.op("tensor", fn, reads, writes)

    def bcast_load(dst, src_row, res):
        S.dma("sync", lambda e: e.dma_start(out=dst, in_=src_row.partition_broadcast(128)), writes=[res])

    def transpose_to(dstT, dst_res, src_bf, src_res, nk, col0, psi=7):
        pb = psb(psi)
        for k in range(nk):
            PE(lambda e, k=k: e.transpose(pb[:, k * 128:(k + 1) * 128], src_bf[:, k * 128:(k + 1) * 128], identb),
               reads=[src_res, "cstb"], writes=[f"ps{psi}"])
        V(lambda e: e.tensor_copy(out=dstT[:, 0:nk, col0:col0 + 128],
                                  in_=pb[:, 0:nk * 128].rearrange("p (k t) -> p k t", k=nk)),
          reads=[f"ps{psi}"], writes=[dst_res])

    def layer_norm(src, src_res, dst, dst_res, gi, g_ap=None, b_ap=None):
        ga = ln_g[gi:gi + 1, :] if g_ap is None else g_ap
        ba = ln_b[gi:gi + 1, :] if b_ap is None else b_ap
        bcast_load(gb[0][:], ga, "gb0")
        bcast_load(gb[1][:], ba, "gb1")
        for h in range(2):
            V(lambda e, h=h: e.bn_stats(out=stats[:, h, :], in_=src[:, h * 512:(h + 1) * 512]),
              reads=[src_res], writes=["stats"] if h == 0 else ["stats"])
        V(lambda e: e.bn_aggr(out=mv[:], in_=stats[:].rearrange("p a b -> p (a b)")), reads=["stats"], writes=["mv"])
        V(lambda e: e.tensor_scalar(out=rstd[:], in0=mv[:, 1:2], scalar1=EPS, scalar2=None, op0=ALU.add), reads=["mv"], writes=["rstd"])
        A(lambda e: e.activation(out=rstd[:], in_=rstd[:], func=AF.Sqrt), reads=["rstd"], writes=["rstd"])
        V(lambda e: e.reciprocal(out=rstd[:], in_=rstd[:]), reads=["rstd"], writes=["rstd"])
        V(lambda e: e.tensor_scalar(out=dst, in0=src, scalar1=mv[:, 0:1], scalar2=rstd[:, 0:1], op0=ALU.subtract, op1=ALU.mult),
          reads=[src_res, "mv", "rstd"], writes=[dst_res])
        V(lambda e: e.tensor_tensor(out=dst, in0=dst, in1=gb[0][:], op=ALU.mult), reads=[dst_res, "gb0"], writes=[dst_res])
        V(lambda e: e.tensor_tensor(out=dst, in0=dst, in1=gb[1][:], op=ALU.add), reads=[dst_res, "gb1"], writes=[dst_res])

    def build_xT(nt):
        for ti in range(nt):
            A(lambda e, ti=ti: e.activation(out=xb16[:], in_=xg[:, ti, :], func=AF.Copy), reads=[f"xg{ti}"], writes=["xb16"])
            transpose_to(xT, "xT", xb16, "xb16", 8, ti * 128)

    def ffn(fi, nt):
        ntk = nt * 128
        build_xT(nt)
        for j in range(NJ):
            b = j % 2
            rj = (fi * NJ + j) * 128
            S.dma("sync", lambda e, rj=rj, b=b: e.dma_start(out=wgs[b][:].rearrange("p k m -> p (k m)"), in_=wg_scr[rj:rj + 128, :]),
                  reads=[f"S_wg{fi}"], writes=[f"wgs{b}"])
            S.dma("sync", lambda e, rj=rj, b=b: e.dma_start(out=wus[b][:].rearrange("p k m -> p (k m)"), in_=wu_scr[rj:rj + 128, :]),
                  reads=[f"S_wu{fi}"], writes=[f"wus{b}"])
            pg, pu = ps[2 * b], ps[2 * b + 1]
            for k in range(8):
                PE(lambda e, k=k, b=b, pg=pg: e.matmul(pg[:, 0:ntk], lhsT=wgs[b][:, k, :], rhs=xT[:, k, 0:ntk], start=(k == 0), stop=(k == 7)),
                   reads=[f"wgs{b}", "xT"], writes=[f"ps{2 * b}"])
            for k in range(8):
                PE(lambda e, k=k, b=b, pu=pu: e.matmul(pu[:, 0:ntk], lhsT=wus[b][:, k, :], rhs=xT[:, k, 0:ntk], start=(k == 0), stop=(k == 7)),
                   reads=[f"wus{b}", "xT"], writes=[f"ps{2 * b + 1}"])
            A(lambda e, pg=pg: e.activation(out=sg_t[:, 0:ntk], in_=pg[:, 0:ntk], func=AF.Silu), reads=[f"ps{2 * b}"], writes=["sg_t"])
            V(lambda e, j=j, pu=pu: e.tensor_tensor(out=hT[:, j, 0:ntk], in0=sg_t[:, 0:ntk], in1=pu[:, 0:ntk], op=ALU.mult),
              reads=["sg_t", f"ps{2 * b + 1}"], writes=["hT"])
        li = (fi // 2) * 3 + (0 if fi % 2 == 0 else 2)
        for c in range(4):
            cs = slice(c * 256, (c + 1) * 256)
            rq = (fi * 4 + c) * 128
            S.dma("sync", lambda e, rq=rq: e.dma_start(out=wds[:].rearrange("p j n -> p (j n)"), in_=wd_scr[rq:rq + 128, :]), reads=[f"S_wd{fi}"], writes=["wds"])
            for ti in range(nt):
                pd = ps[4 + (ti % 2)]
                for j in range(NJ):
                    PE(lambda e, j=j, ti=ti, pd=pd: e.matmul(pd[:, 0:256], lhsT=hT[:, j, ti * 128:(ti + 1) * 128], rhs=wds[:, j, :], start=(j == 0), stop=(j == NJ - 1)),
                       reads=["hT", "wds"], writes=[f"ps{4 + ti % 2}"])
                A(lambda e, pd=pd, cs=cs: e.activation(out=zt[:, cs], in_=pd[:, 0:256], func=AF.Copy, scale=0.5),
                  reads=[f"ps{4 + ti % 2}"], writes=["zt"])
                V(lambda e, ti=ti, cs=cs: e.scalar_tensor_tensor(out=xg[:, ti, cs], in0=xg[:, ti, cs], scalar=ALPHA,
                                                                in1=zt[:, cs], op0=ALU.mult, op1=ALU.add),
                  reads=["zt", f"xg{ti}"], writes=[f"xg{ti}"])
        for ti in range(nt):
            layer_norm(xg[:, ti, :], f"xg{ti}", xg[:, ti, :], f"xg{ti}", li)

    def proj_tokmajor(w_ap, c0, ncols, ti, psi, wb):
        w3, wres = WSCR[w_ap]
        S.dma("sync", lambda e: e.dma_start(out=wps[wb][:, :, 0:ncols], in_=w3[:, :, c0:c0 + ncols]), reads=[wres], writes=[f"wps{wb}"])
        for k in range(8):
            PE(lambda e, k=k: e.matmul(ps[psi][:, 0:ncols], lhsT=xT[:, k, ti * 128:(ti + 1) * 128], rhs=wps[wb][:, k, 0:ncols], start=(k == 0), stop=(k == 7)),
               reads=["xT", f"wps{wb}"], writes=[f"ps{psi}"])

    stg = [sb(f"stg{i}", [128, 8, 256], BF16) for i in range(2)]
    stg_i = [0]

    def conv_cols(src, ncols, dst3, res):
        for c0 in range(0, ncols, 256):
            w = min(256, ncols - c0)
            b = stg_i[0] % 2; stg_i[0] += 1
            S.dma("gpsimd", lambda e, b=b, c0=c0, w=w: e.dma_start(out=stg[b][:, :, 0:w], in_=src[:, c0:c0 + w].rearrange("(k p) n -> p k n", p=128)), writes=[f"stg{b}"])
            S.dma("gpsimd", lambda e, b=b, c0=c0, w=w: e.dma_start(out=dst3[:, :, c0:c0 + w], in_=stg[b][:, :, 0:w]), reads=[f"stg{b}"], writes=[res])

    def conv_gu(src, scr, fi, res):
        for jj in range(NJ // 2):
            b = stg_i[0] % 2; stg_i[0] += 1
            S.dma("gpsimd", lambda e, b=b, jj=jj: e.dma_start(out=stg[b][:], in_=src[fi, :, jj * 256:(jj + 1) * 256].rearrange("(k p) n -> p k n", p=128)), writes=[f"stg{b}"])
            for c in range(2):
                r0 = (fi * NJ + jj * 2 + c) * 128
                S.dma("gpsimd", lambda e, b=b, c=c, r0=r0: e.dma_start(out=scr[r0:r0 + 128, :].rearrange("p (k m) -> p k m", k=8), in_=stg[b][:, :, c * 128:(c + 1) * 128]),
                      reads=[f"stg{b}"], writes=[res])

    def conv_down(fi, res):
        for q in range(4):
            r0 = (fi * 4 + q) * 128
            for (j0, nj) in ((0, 8), (8, 8), (16, 6)):
                b = stg_i[0] % 2; stg_i[0] += 1
                S.dma("gpsimd", lambda e, b=b, q=q, j0=j0, nj=nj: e.dma_start(
                    out=stg[b][:, 0:nj, :], in_=wdown[fi, j0 * 128:(j0 + nj) * 128, q * 256:(q + 1) * 256].rearrange("(j p) n -> p j n", p=128)), writes=[f"stg{b}"])
                S.dma("gpsimd", lambda e, b=b, r0=r0, j0=j0, nj=nj: e.dma_start(
                    out=wd_scr[r0:r0 + 128, :].rearrange("p (j n) -> p j n", j=NJ)[:, j0:j0 + nj, :], in_=stg[b][:, 0:nj, :]), reads=[f"stg{b}"], writes=[res])

    def conv_ffn(fi):
        conv_gu(wgate, wg_scr, fi, f"S_wg{fi}")
        conv_gu(wup, wu_scr, fi, f"S_wu{fi}")
        conv_down(fi, f"S_wd{fi}")

    win3 = win_scr.rearrange("p (k n) -> p k n", k=8)
    wout3 = wout_scr.rearrange("p (k n) -> p k n", k=8)
    cwin3 = cwin_scr.rearrange("p (k n) -> p k n", k=8)
    cwout3 = cwout_scr.rearrange("p (k n) -> p k n", k=8)
    conv_ffn(0)
    conv_cols(w_in, 3592, win3, "S_win")
    conv_cols(w_out, D, wout3, "S_wout")
    conv_ffn(1)
    conv_ffn(2)
    conv_cols(c_w_in, 2048, cwin3, "S_cwin")
    conv_cols(c_w_out, D, cwout3, "S_cwout")
    conv_ffn(3)
    WSCR = {"w_in": (win3, "S_win"), "w_out": (wout3, "S_wout"), "c_w_in": (cwin3, "S_cwin"), "c_w_out": (cwout3, "S_cwout")}

    lbt = [tf[0][:, 0:512], tf[0][:, 512:1024], tf[1][:, 0:512]]
    LR = ["tf0", "tf1"]
    lb = sb("lb", [128, 512])
    oml = sb("oml", [128, 512])
    tmpA = tf[2][:, 0:512]
    ng_bc = sb("ng_bc", [128, 128])
    fb_bc = sb("fb_bc", [128, 8])
    for a in range(3):
        S.dma("sync", lambda e, a=a: e.dma_start(out=lbt[a], in_=lb_log[0:1, a * 512:(a + 1) * 512].partition_broadcast(128)), writes=LR)
    bcast_load(ng_bc[:], hg_ng[0:1, :], "ng_bc")
    bcast_load(fb_bc[:], fbias[0:1, :], "fb_bc")
    V(lambda e: e.tensor_tensor(out=tmpA, in0=lbt[0], in1=lbt[1], op=ALU.max), reads=LR, writes=["tmpA"])
    V(lambda e: e.tensor_tensor(out=tmpA, in0=tmpA, in1=lbt[2], op=ALU.max), reads=LR + ["tmpA"], writes=["tmpA"])
    for a in range(3):
        V(lambda e, a=a: e.tensor_tensor(out=lbt[a], in0=lbt[a], in1=tmpA, op=ALU.subtract), reads=LR + ["tmpA"], writes=LR)
        A(lambda e, a=a: e.activation(out=lbt[a], in_=lbt[a], func=AF.Exp), reads=LR, writes=LR)
    V(lambda e: e.tensor_tensor(out=tmpA, in0=lbt[0], in1=lbt[1], op=ALU.add), reads=LR, writes=["tmpA"])
    V(lambda e: e.tensor_tensor(out=tmpA, in0=tmpA, in1=lbt[2], op=ALU.add), reads=LR + ["tmpA"], writes=["tmpA"])
    V(lambda e: e.reciprocal(out=tmpA, in_=tmpA), reads=["tmpA"], writes=["tmpA"])
    V(lambda e: e.tensor_tensor(out=lb[:], in0=lbt[0], in1=tmpA, op=ALU.mult), reads=LR + ["tmpA"], writes=["lb"])
    V(lambda e: e.tensor_scalar(out=oml[:], in0=lb[:], scalar1=-1.0, scalar2=1.0, op0=ALU.mult, op1=ALU.add), reads=["lb"], writes=["oml"])

    Fk = sb("Fk", [128, NT, 8])
    Fend = sb("Fend", [128, NT, 8])
    Sst = [sb(f"Sst{h}", [128, 128]) for h in range(4)]
    Sb = [sb(f"Sb{h}", [128, 128], BF16) for h in range(4)]
    qh = tf[0][:, 0:512]
    kk = tf[0][:, 512:1024]
    gl = tf[1][:, 0:512]
    vh = tb[0][:, 0:512]
    gs = tf[1][:, 512:1024]
    bcs = tf[2][:, 0:512]
    ebl = tf[2][:, 512:1024]
    qtl = tb[0][:, 512:1024]
    ktl = tb[1][:, 0:512]
    khat = tb[1][:, 512:1024]
    qtT = sb("qtT", [128, 128], BF16)
    ktT = sb("ktT", [128, 128], BF16)
    qtT0 = sb("qtT0", [128, 128], BF16)
    qtT1 = sb("qtT1", [128, 128], BF16)
    attT = sb("attT", [128, 128], BF16)
    dcol = sb("dcol", [128, 16])
    ohb = tb[2][:, 0:512]
    ssq = sb("ssq", [128, 4])
    junk = sb("junk", [128, 128])
    fq_b = tb[2][:, 512:1024]
    fk_f = tf[3][:, 0:512]
    fk_b = tb[3][:, 0:512]
    fv_f = tf[3][:, 512:1024]
    vp = sb("vp", [128, 8, 65], BF16)
    lft = sb("lft", [128, 8])
    tq = sb("tq", [128, 4, 128], BF16)
    S0f = sb("S0f", [128, 16, 128])
    S0b = sb("S0b", [128, 16, 128], BF16)
    qz = sb("qz", [128, 16, 128], BF16)
    vbd = sb("vbd", [128, 16, 128], BF16)

    V(lambda e: e.memset(qtT0[:], 0.0), writes=["qtT0"])
    V(lambda e: e.memset(qtT1[:], 0.0), writes=["qtT1"])
    V(lambda e: e.memset(qz[:], 0.0), writes=["qz"])
    V(lambda e: e.memset(vp[:], 1.0), writes=["vp"])
    for h in range(4):
        V(lambda e, h=h: e.memset(Sst[h][:], 0.0), writes=[f"Sst{h}"])
        V(lambda e, h=h: e.memset(Sb[h][:], 0.0), writes=[f"Sb{h}"])

    KSTAGE = int(os.environ.get('KSTAGE', '9'))
    dcs = sb("dcs", [128, 64])

    def phaseA_tile(T, ti, sample):
        r0 = T * 128
        tri = C_TRI16 if sample else C_TRI2
        blk = C_BLK16 if sample else C_BLK2
        proj_tokmajor("w_in", 0, 512, ti, 0, 0)
        A(lambda e: e.activation(out=qh[:], in_=ps[0][:], func=AF.Silu), reads=["ps0"], writes=["qh"])
        proj_tokmajor("w_in", 512, 512, ti, 1, 1)
        A(lambda e: e.activation(out=kk[:], in_=ps[1][:], func=AF.Sigmoid), reads=["ps1"], writes=["kk"])
        V(lambda e: e.tensor_tensor(out=gl[:], in0=kk[:], in1=oml[:], op=ALU.mult), reads=["kk", "oml"], writes=["gl"])
        V(lambda e: e.tensor_tensor(out=gl[:], in0=gl[:], in1=lb[:], op=ALU.add), reads=["gl", "lb"], writes=["gl"])
        V(lambda e: e.tensor_scalar(out=kk[:], in0=gl[:], scalar1=-1.0, scalar2=1.0, op0=ALU.mult, op1=ALU.add), reads=["gl"], writes=["kk"])
        A(lambda e: e.activation(out=gl[:], in_=gl[:], func=AF.Ln), reads=["gl"], writes=["gl"])
        proj_tokmajor("w_in", 1024, 512, ti, 0, 0)
        A(lambda e: e.activation(out=vh[:], in_=ps[0][:], func=AF.Copy), reads=["ps0"], writes=["vh"])
        proj_tokmajor("w_in", 1536, 512, ti, 1, 1)
        A(lambda e: e.activation(out=gs[:], in_=ps[1][:], func=AF.Silu), reads=["ps1"], writes=["gs"])
        gs3 = gs.rearrange("p (h v) -> p h v", h=4)
        V(lambda e: e.tensor_tensor(out=gs3, in0=gs3, in1=ng_bc[:].unsqueeze(1).broadcast_to([128, 4, 128]), op=ALU.mult), reads=["gs", "ng_bc"], writes=["gs"])
        if KSTAGE < 2:
            return
        PE(lambda e: e.matmul(ps[2][:], lhsT=cst[:, tri, :], rhs=gl[:], start=True, stop=True), reads=["cst", "gl"], writes=["ps2"])
        PE(lambda e: e.matmul(ps[3][:], lhsT=cst[:, blk, :], rhs=gl[:], start=True, stop=True), reads=["cst", "gl"], writes=["ps3"])
        A(lambda e: e.activation(out=bcs[:], in_=ps[2][:], func=AF.Exp), reads=["ps2"], writes=["bcs"])
        V(lambda e: e.tensor_tensor(out=qtl[:], in0=qh[:], in1=bcs[:], op=ALU.mult), reads=["qh", "bcs"], writes=["qtl"])
        A(lambda e: e.activation(out=bcs[:], in_=ps[2][:], func=AF.Exp, scale=-1.0), reads=["ps2"], writes=["bcs"])
        V(lambda e: e.tensor_tensor(out=ktl[:], in0=kk[:], in1=bcs[:], op=ALU.mult), reads=["kk", "bcs"], writes=["ktl"])
        A(lambda e: e.activation(out=ebl[:], in_=ps[3][:], func=AF.Exp), reads=["ps3"], writes=["ebl"])
        V(lambda e: e.tensor_tensor(out=bcs[:], in0=bcs[:], in1=ebl[:], op=ALU.mult), reads=["bcs", "ebl"], writes=["bcs"])
        V(lambda e: e.tensor_tensor(out=khat[:], in0=kk[:], in1=bcs[:], op=ALU.mult), reads=["kk", "bcs"], writes=["khat"])
        nb = 16 if sample else 2
        selc = C_SELEND16 if sample else C_SELEND2
        for h in range(4):
            PE(lambda e, h=h: e.matmul(ps[3][:, 256 + h * 16:256 + h * 16 + nb], lhsT=ebl[:, h * 128:(h + 1) * 128], rhs=cst[:, selc, 0:nb], start=True, stop=True),
               reads=["ebl", "cst"], writes=["ps3"])
        if sample:
            sq0 = (T - SMP) * 16
            V(lambda e: e.tensor_copy(out=dcs[:], in_=ps[3][:, 256:320]), reads=["ps3"], writes=["dcs"])
        else:
            V(lambda e: e.tensor_copy(out=dcol[:].rearrange("p (h c) -> p h c", h=4)[:, :, 0:2],
                                      in_=ps[3][:, 256:320].rearrange("p (h c) -> p h c", h=4)[:, :, 0:2]), reads=["ps3"], writes=["dcol"])
        if KSTAGE < 3:
            return
        for h in range(4):
            hs = slice(h * 128, (h + 1) * 128)
            pb = psb(7)
            PE(lambda e, hs=hs: e.transpose(pb[:, 0:128], qtl[:, hs], identb), reads=["qtl", "cstb"], writes=["ps7"])
            PE(lambda e, hs=hs: e.transpose(pb[:, 128:256], ktl[:, hs], identb), reads=["ktl", "cstb"], writes=["ps7"])
            V(lambda e: e.tensor_copy(out=qtT[:], in_=pb[:, 0:128]), reads=["ps7"], writes=["qtT"])
            A(lambda e: e.activation(out=ktT[:], in_=pb[:, 128:256], func=AF.Copy), reads=["ps7"], writes=["ktT"])
            PE(lambda e: e.matmul(ps[6][:, 0:128], lhsT=ktT[:], rhs=qtT[:], start=True, stop=True), reads=["ktT", "qtT"], writes=["ps6"])
            V(lambda e: e.tensor_tensor(out=attT[:], in0=ps[6][:, 0:128], in1=cst[:, tri, :], op=ALU.mult), reads=["ps6", "cst"], writes=["attT"])
            if not sample:
                V(lambda e: e.tensor_copy(out=qtT0[:, 0:64], in_=qtT[:, 0:64]), reads=["qtT"], writes=["qtT0"])
                V(lambda e: e.tensor_copy(out=qtT1[:, 64:128], in_=qtT[:, 64:128]), reads=["qtT"], writes=["qtT1"])
                PE(lambda e, hs=hs: e.matmul(ps[6][:, 128:256], lhsT=khat[0:64, hs], rhs=vh[0:64, hs], start=True, stop=True),
                   reads=["khat", "vh"], writes=["ps6"])
                PE(lambda e, hs=hs: e.matmul(ps[5][:, 0:128], lhsT=attT[:], rhs=vh[:, hs], start=True, stop=False), reads=["attT", "vh"], writes=["ps5"])
                PE(lambda e, h=h: e.matmul(ps[5][:, 0:128], lhsT=qtT0[:], rhs=Sb[h][:], start=False, stop=False), reads=["qtT0", f"Sb{h}"], writes=["ps5"])
                V(lambda e, h=h: e.scalar_tensor_tensor(out=Sst[h][:], in0=Sst[h][:], scalar=dcol[:, h * 4:h * 4 + 1], in1=ps[6][:, 128:256], op0=ALU.mult, op1=ALU.add),
                  reads=[f"Sst{h}", "dcol", "ps6"], writes=[f"Sst{h}"])
                A(lambda e, h=h: e.activation(out=Sb[h][:], in_=Sst[h][:], func=AF.Copy), reads=[f"Sst{h}"], writes=[f"Sb{h}"])
                PE(lambda e, h=h: e.matmul(ps[5][:, 0:128], lhsT=qtT1[:], rhs=Sb[h][:], start=False, stop=True), reads=["qtT1", f"Sb{h}"], writes=["ps5"])
                PE(lambda e, hs=hs: e.matmul(ps[6][:, 256:384], lhsT=khat[64:128, hs], rhs=vh[64:128, hs], start=True, stop=True),
                   reads=["khat", "vh"], writes=["ps6"])
                V(lambda e, h=h: e.scalar_tensor_tensor(out=Sst[h][:], in0=Sst[h][:], scalar=dcol[:, h * 4 + 1:h * 4 + 2], in1=ps[6][:, 256:384], op0=ALU.mult, op1=ALU.add),
                  reads=[f"Sst{h}", "dcol", "ps6"], writes=[f"Sst{h}"])
                A(lambda e, h=h: e.activation(out=Sb[h][:], in_=Sst[h][:], func=AF.Copy), reads=[f"Sst{h}"], writes=[f"Sb{h}"])
            else:
                S.dma("sync", lambda e, h=h: e.dma_start(out=S0f[:], in_=st_hg.rearrange("(q h p) v -> h p q v", h=4, p=128)[h][:, sq0:sq0 + 16, :]), writes=["S0f"])
                A(lambda e: e.activation(out=S0b[:], in_=S0f[:], func=AF.Copy), reads=["S0f"], writes=["S0b"])
                for q in range(16):
                    V(lambda e, q=q: e.tensor_copy(out=qz[:, q, q * 8:(q + 1) * 8], in_=qtT[:, q * 8:(q + 1) * 8]), reads=["qtT"], writes=["qz"])
                PE(lambda e, hs=hs: e.matmul(ps[5][:, 0:128], lhsT=attT[:], rhs=vh[:, hs], start=True, stop=False), reads=["attT", "vh"], writes=["ps5"])
                for q in range(16):
                    PE(lambda e, q=q, h=h: e.matmul(ps[5][:, 0:128], lhsT=qz[:, q, :], rhs=S0b[:, q, :], start=False, stop=(q == 15)),
                       reads=["qz", "S0b"], writes=["ps5"])
                for q in range(16):
                    V(lambda e, q=q, hs=hs: e.tensor_scalar(out=vbd[:, q, :], in0=vh[:, hs], scalar1=cst[:, C_BLK16, q * 8:q * 8 + 1], scalar2=None, op0=ALU.mult),
                      reads=["vh", "cst"], writes=["vbd"])
                for c4 in range(4):
                    PE(lambda e, c4=c4, hs=hs: e.matmul(ps[c4][:, :], lhsT=khat[:, hs], rhs=vbd[:, c4 * 4:(c4 + 1) * 4, :].rearrange("p a b -> p (a b)"), start=True, stop=True),
                       reads=["khat", "vbd"], writes=[f"ps{c4}"])
                for q in range(16):
                    V(lambda e, q=q, h=h: e.scalar_tensor_tensor(out=S0f[:, q, :], in0=S0f[:, q, :], scalar=dcs[:, h * 16 + q:h * 16 + q + 1],
                                                                 in1=ps[q // 4][:, (q % 4) * 128:(q % 4 + 1) * 128], op0=ALU.mult, op1=ALU.add),
                      reads=["S0f", "dcs", f"ps{q // 4}"], writes=["S0f"])
                S.dma("sync", lambda e, h=h: e.dma_start(out=hgs_o.rearrange("(q h p) v -> h p q v", h=4, p=128)[h][:, sq0:sq0 + 16, :], in_=S0f[:]), reads=["S0f"], is_output=True)
            A(lambda e, h=h: e.activation(out=junk[:], in_=ps[5][:, 0:128], func=AF.Square, accum_out=ssq[:, h:h + 1]), reads=["ps5"], writes=["junk", "ssq"])
            V(lambda e, h=h: e.tensor_scalar(out=ssq[:, h:h + 1], in0=ssq[:, h:h + 1], scalar1=1.0 / 128, scalar2=EPS, op0=ALU.mult, op1=ALU.add), reads=["ssq"], writes=["ssq"])
            A(lambda e, h=h: e.activation(out=ssq[:, h:h + 1], in_=ssq[:, h:h + 1], func=AF.Sqrt), reads=["ssq"], writes=["ssq"])
            V(lambda e, h=h: e.reciprocal(out=ssq[:, h:h + 1], in_=ssq[:, h:h + 1]), reads=["ssq"], writes=["ssq"])
            V(lambda e, h=h, hs=hs: e.scalar_tensor_tensor(out=ohb[:, hs], in0=ps[5][:, 0:128], scalar=ssq[:, h:h + 1], in1=gs[:, hs], op0=ALU.mult, op1=ALU.mult),
              reads=["ps5", "ssq", "gs"], writes=["ohb"])
        if KSTAGE < 4:
            return
        S.dma("sync", lambda e: e.dma_start(out=oh_scr[r0:r0 + 128, :], in_=ohb[:]), reads=["ohb"], writes=["oh_scr"])
        proj_tokmajor("w_in", 2048, 512, ti, 0, 0)
        A(lambda e: e.activation(out=fq_b[:], in_=ps[0][:], func=AF.Copy, scale=0.125), reads=["ps0"], writes=["fq_b"])
        proj_tokmajor("w_in", 2560, 512, ti, 1, 1)
        A(lambda e: e.activation(out=fk_f[:], in_=ps[1][:], func=AF.Copy), reads=["ps1"], writes=["fk_f"])
        V(lambda e: e.tensor_copy(out=fk_b[:], in_=ps[1][:]), reads=["ps1"], writes=["fk_b"])
        S.dma("sync", lambda e: e.dma_start(out=k_o[r0:r0 + 128, :], in_=fk_f[:]), reads=["fk_f"], is_output=True)
        proj_tokmajor("w_in", 3072, 512, ti, 0, 0)
        A(lambda e: e.activation(out=fv_f[:], in_=ps[0][:], func=AF.Copy), reads=["ps0"], writes=["fv_f"])
        V(lambda e: e.tensor_copy(out=vp[:, :, 0:64], in_=ps[0][:].rearrange("p (h d) -> p h d", h=8)), reads=["ps0"], writes=["vp"])
        S.dma("sync", lambda e: e.dma_start(out=v_o[r0:r0 + 128, :], in_=fv_f[:]), reads=["fv_f"], is_output=True)
        S.dma("sync", lambda e: e.dma_start(out=vp_scr[r0:r0 + 128, :], in_=vp[:].rearrange("p h d -> p (h d)")), reads=["vp"], writes=["vp_scr"])
        if KSTAGE < 5:
            return
        proj_tokmajor("w_in", 3584, 8, ti, 1, 1)
        V(lambda e: e.tensor_tensor(out=lft[:], in0=ps[1][:, 0:8], in1=fb_bc[:], op=ALU.add), reads=["ps1", "fb_bc"], writes=["lft"])
        A(lambda e: e.activation(out=lft[:], in_=lft[:], func=AF.Exp, scale=-1.0), reads=["lft"], writes=["lft"])
        A(lambda e: e.activation(out=lft[:], in_=lft[:], func=AF.Ln, bias=1.0), reads=["lft"], writes=["lft"])
        V(lambda e: e.tensor_scalar(out=lft[:], in0=lft[:], scalar1=-1.0, scalar2=None, op0=ALU.mult), reads=["lft"], writes=["lft"])
        S.dma("sync", lambda e: e.dma_start(out=lf_o[r0:r0 + 128, :], in_=lft[:]), reads=["lft"], is_output=True)
        if not sample:
            PE(lambda e: e.matmul(ps[2][:, 0:8], lhsT=cst[:, C_TRI, :], rhs=lft[:], start=True, stop=(T == 0)), reads=["cst", "lft"], writes=["ps2"])
            if T > 0:
                PE(lambda e: e.matmul(ps[2][:, 0:8], lhsT=cst[:, C_SEL127, :], rhs=Fk[:, T - 1, :], start=False, stop=True), reads=["cst", "Fk"], writes=["ps2"])
            V(lambda e: e.tensor_copy(out=Fk[:, T, :], in_=ps[2][:, 0:8]), reads=["ps2"], writes=["Fk"])
            PE(lambda e: e.matmul(ps[2][:, 8:16], lhsT=cst[:, C_SEL127, :], rhs=Fk[:, T, :], start=True, stop=True), reads=["cst", "Fk"], writes=["ps2"])
            V(lambda e: e.tensor_copy(out=Fend[:, T, :], in_=ps[2][:, 8:16]), reads=["ps2"], writes=["Fend"])
        else:
            PE(lambda e: e.matmul(ps[2][:, 0:8], lhsT=cst[:, C_TRI16, :], rhs=lft[:], start=True, stop=True), reads=["cst", "lft"], writes=["ps2"])
            V(lambda e: e.tensor_copy(out=Fk[:, T, :], in_=ps[2][:, 0:8]), reads=["ps2"], writes=["Fk"])
        for (src, res, scr) in ((fq_b, "fq_b", qT_scr), (fk_b, "fk_b", kT_scr)):
            transpose_to(tq, "tq", src, res, 4, 0, psi=7)
            S.dma("sync", lambda e, scr=scr: e.dma_start(out=scr[r0:r0 + 128, :], in_=tq[:].rearrange("p a b -> p (a b)")), reads=["tq"], writes=[scr.tensor.name])

    PH0 = os.environ.get('KPH', 'ABSC')
    for G in range(NPT // 4):
        for ti in range(4):
            T = G * 4 + ti
            S.dma("sync", lambda e, T=T, ti=ti: e.dma_start(out=xg[:, ti, :], in_=xin[T * 128:(T + 1) * 128, :]), writes=[f"xg{ti}"])
        if 'f' not in PH0:
            ffn(0, 4)
        build_xT(4)
        for ti in range(4):
            T = G * 4 + ti
            S.dma("sync", lambda e, T=T, ti=ti: e.dma_start(out=x1_scr[T * 128:(T + 1) * 128, :], in_=xg[:, ti, :]), reads=[f"xg{ti}"], writes=["x1_scr"])
            if 'F' in PH0:
                S.dma("sync", lambda e, T=T, ti=ti: e.dma_start(out=y_o[T * 128:(T + 1) * 128, :], in_=xg[:, ti, :]), reads=[f"xg{ti}"], is_output=True)
            else:
                phaseA_tile(T, ti, False)
    for h in range(4):
        S.dma("sync", lambda e, h=h: e.dma_start(out=hgp_o[h * 128:(h + 1) * 128, :], in_=Sst[h][:]), reads=[f"Sst{h}"], is_output=True)
    for st in range(NS):
        S.dma("sync", lambda e, st=st: e.dma_start(out=xg[:, st, :], in_=xin[(SMP + st) * 128:(SMP + st + 1) * 128, :]), writes=[f"xg{st}"])
    if 'F' in PH0:
        S.emit()
        return nc
    ffn(0, NS)
    build_xT(NS)
    for st in range(NS):
        S.dma("sync", lambda e, st=st: e.dma_start(out=x1_scr[(SMP + st) * 128:(SMP + st + 1) * 128, :], in_=xg[:, st, :]), reads=[f"xg{st}"], writes=["x1_scr"])
        phaseA_tile(SMP + st, st, True)

    PH = os.environ.get('KPH', 'ABSC')
    NKB = 3
    kTt = [sb(f"kTt{i}", [128, 4, 128], BF16) for i in range(NKB)]
    vpt = [sb(f"vpt{i}", [128, 8, 65], BF16) for i in range(NKB)]
    qTt = sb("qTt", [128, 4, 128], BF16)
    wq = [sb(f"wq{i}", [128, 8]) for i in range(2)]
    Vs = [sb(f"Vs{i}", [128, 8, 65], BF16) for i in range(2)]
    pT = [sb(f"pT{i}", [128, 2, 4, 128], BF16) for i in range(2)]
    ofb = sb("ofb", [128, 512], BF16)
    rs = sb("rs", [128, 8])
    G_ = lambda fn, reads=(), writes=(): S.op(os.environ.get("KGENG", "vector"), fn, reads, writes)

    def attn_block(b, kq, k_tile, k_res, q_cols, v_src, v_res, w_ap, w_res, first, last, mask_c, pt_out):
        V(lambda e: e.tensor_tensor(out=Vs[b][:], in0=v_src[:], in1=w_ap[:, :].unsqueeze(2).broadcast_to([128, 8, 65]), op=ALU.mult),
          reads=[v_res, w_res], writes=[f"Vs{b}"])
        nq = q_cols.stop - q_cols.start
        for h in range(8):
            pr, po = h // 2, (h % 2) * 64
            bank = 2 * kq + h % 2
            PE(lambda e, h=h, pr=pr, po=po, bank=bank: e.matmul(ps[bank][:, (h // 2) * nq:(h // 2 + 1) * nq], lhsT=k_tile[po:po + 64, pr, :], rhs=qTt[po:po + 64, pr, q_cols],
                                                              start=True, stop=True, skip_group_check=True),
               reads=[k_res, "qTt"], writes=[f"ps{bank}"])
        for half in range(2):
            bank = 2 * kq + half
            A(lambda e, half=half, bank=bank: e.activation(out=pt_out(half), in_=ps[bank][:, 0:4 * nq].rearrange("p (h q) -> p h q", h=4), func=AF.Exp),
              reads=[f"ps{bank}"], writes=[f"pT{b}"])
        if mask_c is not None:
            pv = pT[b][:].rearrange("p a b q -> p (a b) q")
            V(lambda e: e.tensor_tensor(out=pv, in0=pv, in1=cstb[:, mask_c, :].unsqueeze(1).broadcast_to([128, 8, 128]), op=ALU.mult), reads=[f"pT{b}", "cstb"], writes=[f"pT{b}"])
        for h in range(8):
            bank, c0 = 4 + h // 4, (h % 4) * 65
            PE(lambda e, h=h, bank=bank, c0=c0: e.matmul(ps[bank][:, c0:c0 + 65], lhsT=pT[b][:, h % 2, h // 2, :], rhs=Vs[b][:, h, :],
                                                       start=(first and h % 4 == 0), stop=last, skip_group_check=True),
               reads=[f"pT{b}", f"Vs{b}"], writes=[f"ps{bank}"])

    def attn_finish(row0):
        for half in range(2):
            bank = 4 + half
            pv3 = ps[bank][:, 0:260].rearrange("p (h c) -> p h c", c=65)
            V(lambda e, half=half, pv3=pv3: e.reciprocal(out=rs[:, half * 4:(half + 1) * 4].unsqueeze(2), in_=pv3[:, :, 64:65]), reads=[f"ps{bank}"], writes=["rs"])
            V(lambda e, half=half, pv3=pv3: e.tensor_tensor(out=ofb[:, half * 256:(half + 1) * 256].rearrange("p (h d) -> p h d", d=64), in0=pv3[:, :, 0:64],
                                                          in1=rs[:, half * 4:(half + 1) * 4].unsqueeze(2).broadcast_to([128, 4, 64]), op=ALU.mult),
              reads=[f"ps{bank}", "rs"], writes=["ofb"])
        S.dma("sync", lambda e: e.dma_start(out=of_scr[row0:row0 + 128, :], in_=ofb[:]), reads=["ofb"], writes=["of_scr"])

    cnt = [0]
    for qt in range(NPT if 'B' in PH else 0):
        S.dma("sync", lambda e, qt=qt: e.dma_start(out=qTt[:].rearrange("p a b -> p (a b)"), in_=qT_scr[qt * 128:(qt + 1) * 128, :]), reads=["qT_scr"], writes=["qTt"])
        for kt in range(qt + 1):
            b = cnt[0] % 2; g = cnt[0] % NKB; cnt[0] += 1
            S.dma("sync", lambda e, kt=kt, g=g: e.dma_start(out=kTt[g][:].rearrange("p a b -> p (a b)"), in_=kT_scr[kt * 128:(kt + 1) * 128, :]), reads=["kT_scr"], writes=[f"kTt{g}"])
            S.dma("sync", lambda e, kt=kt, g=g: e.dma_start(out=vpt[g][:].rearrange("p a b -> p (a b)"), in_=vp_scr[kt * 128:(kt + 1) * 128, :]), reads=["vp_scr"], writes=[f"vpt{g}"])
            V(lambda e, qt=qt, kt=kt, b=b: e.tensor_tensor(out=wq[b][:], in0=Fend[:, qt, :], in1=Fk[:, kt, :], op=ALU.subtract), reads=["Fend", "Fk"], writes=[f"wq{b}"])
            V(lambda e, b=b: e.tensor_scalar(out=wq[b][:], in0=wq[b][:], scalar1=0.0, scalar2=None, op0=ALU.min), reads=[f"wq{b}"], writes=[f"wq{b}"])
            A(lambda e, b=b: e.activation(out=wq[b][:], in_=wq[b][:], func=AF.Exp), reads=[f"wq{b}"], writes=[f"wq{b}"])
            attn_block(b, b, kTt[g], f"kTt{g}", slice(0, 128), vpt[g], f"vpt{g}", wq[b], f"wq{b}", kt == 0, kt == qt,
                       C_TRI if kt == qt else None, lambda half, b=b: pT[b][:, half, :, :])
        attn_finish(qt * 128)

    def sample_attn():
        pts = sb("pts", [128, NS * 256], I32)
        pidx = pts
        iot = sb("iot", [128, 1], I32)
        NKS = 2
        kpg = [sb(f"kpg{i}", [128, 512], BF16) for i in range(NKS)]
        vpg = [sb(f"vpg{i}", [128, 8, 65], BF16) for i in range(2)]
        vgt = [sb(f"vgt{i}", [128, 512], BF16) for i in range(NKS)]
        lfp = sb("lfp", [128, 16, 8])
        lat = sb("lat", [128, 16, 8])
        Rb = sb("Rb", [128, 16, 8])
        kTp = [sb(f"kTp{i}", [128, 4, 128], BF16) for i in range(2)]
        S.dma("sync", lambda e: e.dma_start(out=pts[:], in_=ptab[0:1, :].partition_broadcast(128)), writes=["pts"])
        S.op("gpsimd", lambda e: e.iota(iot[:], pattern=[[0, 1]], base=0, channel_multiplier=1), writes=["iot"])
        S.op("gpsimd", lambda e: e.tensor_scalar(out=pidx[:], in0=pts[:], scalar1=128, scalar2=iot[:, 0:1], op0=ALU.mult, op1=ALU.add), reads=["pts", "iot"], writes=["pts"])
        for i in range(2):
            V(lambda e, i=i: e.memset(pT[i][:], 0.0), writes=[f"pT{i}"])
            V(lambda e, i=i: e.memset(vpg[i][:], 1.0), writes=[f"vpg{i}"])
        pcnt = [0]
        for st in range(NS):
            TS = SMP + st
            S.dma("sync", lambda e, TS=TS: e.dma_start(out=qTt[:].rearrange("p a b -> p (a b)"), in_=qT_scr[TS * 128:(TS + 1) * 128, :]), reads=["qT_scr"], writes=["qTt"])
            S.dma("sync", lambda e, TS=TS: e.dma_start(out=kTt[0][:].rearrange("p a b -> p (a b)"), in_=kT_scr[TS * 128:(TS + 1) * 128, :]), reads=["kT_scr"], writes=["kTt0"])
            S.dma("sync", lambda e, TS=TS: e.dma_start(out=vpt[0][:].rearrange("p a b -> p (a b)"), in_=vp_scr[TS * 128:(TS + 1) * 128, :]), reads=["vp_scr"], writes=["vpt0"])
            A(lambda e, TS=TS: e.activation(out=wq[0][:], in_=Fk[:, TS, :], func=AF.Exp, scale=-1.0), reads=["Fk"], writes=["wq0"])
            attn_block(0, 0, kTt[0], "kTt0", slice(0, 128), vpt[0], "vpt0", wq[0], "wq0", True, False, C_TRI16,
                       lambda half: pT[0][:, half, :, :])
            V(lambda e: e.memset(pT[0][:], 0.0), reads=[], writes=["pT0"])
            V(lambda e: e.memset(pT[1][:], 0.0), reads=[], writes=["pT1"])
            for q in range(16):
                for pg in range(16):
                    S.dma("gpsimd", lambda e, q=q, pg=pg, st=st: e.indirect_dma_start(out=lfp[:, pg, :], out_offset=None, in_=clf,
                                                                                    in_offset=bass.IndirectOffsetOnAxis(ap=pidx[:, st * 256 + q * 16 + pg:st * 256 + q * 16 + pg + 1], axis=0)),
                          reads=["pts"], writes=["lfp"])
                V(lambda e: e.memset(lat[:, 15, :], 0.0), writes=["lat"])
                for pg in range(14, -1, -1):
                    V(lambda e, pg=pg: e.tensor_tensor(out=lat[:, pg, :], in0=lat[:, pg + 1, :], in1=lfp[:, pg + 1, :], op=ALU.add), reads=["lat", "lfp"], writes=["lat"])
                PE(lambda e: e.matmul(ps[6][:, 0:128], lhsT=cst[:, C_SUP, :], rhs=lfp[:].rearrange("p a b -> p (a b)"), start=True, stop=False), reads=["cst", "lfp"], writes=["ps6"])
                PE(lambda e: e.matmul(ps[6][:, 0:128], lhsT=cst[:, C_ONES, :], rhs=lat[:].rearrange("p a b -> p (a b)"), start=False, stop=True), reads=["cst", "lat"], writes=["ps6"])
                A(lambda e: e.activation(out=Rb[:].rearrange("p a b -> p (a b)"), in_=ps[6][:, 0:128], func=AF.Exp), reads=["ps6"], writes=["Rb"])
                for pg in range(16):
                    b = pcnt[0] % 2; g = pcnt[0] % NKS; pcnt[0] += 1
                    col = st * 256 + q * 16 + pg
                    S.dma("gpsimd", lambda e, g=g, col=col: e.indirect_dma_start(out=kpg[g][:], out_offset=None, in_=ck,
                                                                               in_offset=bass.IndirectOffsetOnAxis(ap=pidx[:, col:col + 1], axis=0)),
                          reads=["pts"], writes=[f"kpg{g}"])
                    S.dma("gpsimd", lambda e, g=g, col=col: e.indirect_dma_start(out=vgt[g][:], out_offset=None, in_=cv_,
                                                                               in_offset=bass.IndirectOffsetOnAxis(ap=pidx[:, col:col + 1], axis=0)),
                          reads=["pts"], writes=[f"vgt{g}"])
                    V(lambda e, b=b, g=g: e.tensor_copy(out=vpg[b][:, :, 0:64], in_=vgt[g][:].rearrange("p (h d) -> p h d", h=8)), reads=[f"vgt{g}"], writes=[f"vpg{b}"])
                    transpose_to(kTp[b], f"kTp{b}", kpg[g], f"kpg{g}", 4, 0, psi=7)
                    last = (q == 15 and pg == 15)
                    attn_block(b, b, kTp[b], f"kTp{b}", slice(q * 8, (q + 1) * 8), vpg[b], f"vpg{b}", Rb[:, pg, :], "Rb", False, last, None,
                               lambda half, b=b, q=q: pT[b][:, half, :, q * 8:(q + 1) * 8])
                for i in range(2):
                    V(lambda e, i=i, q=q: e.memset(pT[i][:].rearrange("p a b q -> p (a b) q")[:, :, q * 8:(q + 1) * 8], 0.0), writes=[f"pT{i}"])
            attn_finish(TS * 128)

    if 'S' in PH:
        sample_attn()

    ocat = tb[2][:, :]
    uu = tf[0][:, :]
    vv = tf[1][:, :]
    vvb = tb[0][:, :]
    wsT = sb("wsT", [128, 8, 128], BF16)
    wsS = sb("wsS", [128, 8, 128], BF16)
    bsT = sb("bsT", [128, 8])
    bsS = sb("bsS", [128, 8])
    wsl = sb("wsl", [128, 128])
    bsl = sb("bsl", [8, 128])
    w8 = sb("w8", [8, 8])
    b8 = sb("b8", [8, 128])
    umx = tb[1][:, :]
    gq = tf[2][:, 0:512]
    S.alias["gq"] = "tf2"

    S.dma("sync", lambda e: e.dma_start(out=bsl[:], in_=c_b_s[:, :]), writes=["bsl"])
    PE(lambda e: e.transpose(ps[0][:, 0:8], bsl[:], cst[0:8, C_ID, 0:8]), reads=["bsl", "cst"], writes=["ps0"])
    V(lambda e: e.tensor_copy(out=bsT[:], in_=ps[0][:, 0:8]), reads=["ps0"], writes=["bsT"])
    PE(lambda e: e.matmul(ps[0][:, 8:16], lhsT=cst[0:8, C_R, :], rhs=bsT[0:8, :], start=True, stop=True), reads=["cst", "bsT"], writes=["ps0"])
    V(lambda e: e.tensor_copy(out=bsS[:], in_=ps[0][:, 8:16]), reads=["ps0"], writes=["bsS"])
    for g in range(8):
        S.dma("sync", lambda e, g=g: e.dma_start(out=wsl[:], in_=c_w_s[g * 128:(g + 1) * 128, :]), writes=["wsl"])
        PE(lambda e: e.transpose(ps[1][:, 0:128], wsl[:], cst[:, C_ID, :]), reads=["wsl", "cst"], writes=["ps1"])
        V(lambda e, g=g: e.tensor_tensor(out=wsT[:, g, :], in0=ps[1][:, 0:128], in1=cst[:, C_TRI, :], op=ALU.mult), reads=["ps1", "cst"], writes=["wsT"])
        PE(lambda e: e.matmul(ps[1][0:8, 128:256], lhsT=wsl[0:8, 0:8], rhs=cst[0:8, C_R, :], start=True, stop=True), reads=["wsl", "cst"], writes=["ps1"])
        V(lambda e: e.tensor_copy(out=b8[:], in_=ps[1][0:8, 128:256]), reads=["ps1"], writes=["b8"])
        PE(lambda e: e.matmul(ps[1][:, 256:384], lhsT=cst[0:8, C_R, :], rhs=b8[:], start=True, stop=True), reads=["cst", "b8"], writes=["ps1"])
        V(lambda e, g=g: e.tensor_tensor(out=wsS[:, g, :], in0=ps[1][:, 256:384], in1=cst[:, C_TRI16, :], op=ALU.mult), reads=["ps1", "cst"], writes=["wsS"])

    def phaseC_group(T0, nt, sample):
        for ti in range(nt):
            r0 = (T0 + ti) * 128
            S.dma("sync", lambda e, r0=r0, ti=ti: e.dma_start(out=xg[:, ti, :], in_=x1_scr[r0:r0 + 128, :]), reads=["x1_scr"], writes=[f"xg{ti}"])
            S.dma("sync", lambda e, r0=r0: e.dma_start(out=ocat[:, 0:512], in_=oh_scr[r0:r0 + 128, :]), reads=["oh_scr"], writes=["ocat"])
            S.dma("sync", lambda e, r0=r0: e.dma_start(out=ocat[:, 512:1024], in_=of_scr[r0:r0 + 128, :]), reads=["of_scr"], writes=["ocat"])
            transpose_to(xT, "xT", ocat, "ocat", 8, ti * 128)
        for ti in range(nt):
            for c in range(2):
                proj_tokmajor("w_out", c * 512, 512, ti, 4 + c, c)
                V(lambda e, ti=ti, c=c: e.scalar_tensor_tensor(out=xg[:, ti, c * 512:(c + 1) * 512], in0=xg[:, ti, c * 512:(c + 1) * 512], scalar=ALPHA,
                                                               in1=ps[4 + c][:, :], op0=ALU.mult, op1=ALU.add), reads=[f"ps{4 + c}", f"xg{ti}"], writes=[f"xg{ti}"])
            layer_norm(xg[:, ti, :], f"xg{ti}", xg[:, ti, :], f"xg{ti}", 1)
        ffn(1, nt)
        ffn(2, nt)
        build_xT(nt)
        for ti in range(nt):
            for c in range(4):
                proj_tokmajor("c_w_in", c * 512, 512, ti, c % 2, c % 2)
                dstt = uu if c < 2 else vv
                dres = "uu" if c < 2 else "vv"
                dsl = dstt[:, (c % 2) * 512:(c % 2 + 1) * 512]
                A(lambda e, c=c: e.activation(out=gq[:], in_=ps[c % 2][:, :], func=AF.Square), reads=[f"ps{c % 2}"], writes=["gq"])
                V(lambda e: e.tensor_scalar(out=gq[:], in0=gq[:], scalar1=0.044715, scalar2=1.0, op0=ALU.mult, op1=ALU.add), reads=["gq"], writes=["gq"])
                V(lambda e, c=c: e.tensor_tensor(out=gq[:], in0=gq[:], in1=ps[c % 2][:, :], op=ALU.mult), reads=["gq", f"ps{c % 2}"], writes=["gq"])
                A(lambda e: e.activation(out=gq[:], in_=gq[:], func=AF.Sigmoid, scale=1.5957691216057308), reads=["gq"], writes=["gq"])
                V(lambda e, c=c, dsl=dsl: e.tensor_tensor(out=dsl, in0=gq[:], in1=ps[c % 2][:, :], op=ALU.mult), reads=["gq", f"ps{c % 2}"], writes=[dres])
            layer_norm(vv[:], "vv", vv[:], "vv", 0, g_ap=c_ln_g[0:1, :], b_ap=c_ln_b[0:1, :])
            if sample:
                S.dma("sync", lambda e, ti=ti: e.dma_start(out=cv_o[ti * 128:(ti + 1) * 128, :], in_=vv[:]), reads=["vv"], is_output=True)
            A(lambda e: e.activation(out=vvb[:], in_=vv[:], func=AF.Copy), reads=["vv"], writes=["vvb"])
            wsx = wsS if sample else wsT
            bsx = bsS if sample else bsT
            for g in range(8):
                bank = 4 + g // 4
                cs = slice((g % 4) * 128, (g % 4 + 1) * 128)
                PE(lambda e, g=g, bank=bank, cs=cs, wsx=wsx: e.matmul(ps[bank][:, cs], lhsT=wsx[:, g, :], rhs=vvb[:, g * 128:(g + 1) * 128], start=True, stop=True),
                   reads=["wsT", "wsS", "vvb"], writes=[f"ps{bank}"])
                V(lambda e, g=g, bank=bank, cs=cs, bsx=bsx: e.scalar_tensor_tensor(out=umx[:, g * 128:(g + 1) * 128], in0=ps[bank][:, cs], scalar=bsx[:, g:g + 1],
                                                                                   in1=uu[:, g * 128:(g + 1) * 128], op0=ALU.add, op1=ALU.mult),
                  reads=[f"ps{bank}", "bsT", "bsS", "uu"], writes=["umx"])
            transpose_to(xT, "xT", umx, "umx", 8, ti * 128)
        for ti in range(nt):
            for c in range(2):
                proj_tokmajor("c_w_out", c * 512, 512, ti, 4 + c, c)
                V(lambda e, ti=ti, c=c: e.scalar_tensor_tensor(out=xg[:, ti, c * 512:(c + 1) * 512], in0=xg[:, ti, c * 512:(c + 1) * 512], scalar=ALPHA,
                                                               in1=ps[4 + c][:, :], op0=ALU.mult, op1=ALU.add), reads=[f"ps{4 + c}", f"xg{ti}"], writes=[f"xg{ti}"])
            layer_norm(xg[:, ti, :], f"xg{ti}", xg[:, ti, :], f"xg{ti}", 4)
        ffn(3, nt)
        for ti in range(nt):
            r0 = (T0 + ti) * 128
            S.dma("sync", lambda e, r0=r0, ti=ti: e.dma_start(out=y_o[r0:r0 + 128, :], in_=xg[:, ti, :]), reads=[f"xg{ti}"], is_output=True)

    if 'C' in PH:
        for G in range(NPT // 4):
            phaseC_group(G * 4, 4, False)
        phaseC_group(SMP, NS, True)

    print('sbuf bytes remaining', nc.sbuf_bytes_remaining)
    S.emit()
    print('instr counts', {e: len(S.q[e]) for e in ENGS})
    return nc


_NC_CACHE = {}
NCORES = 2
NS_ = 4


def kernel(x_prompt, x_sample, cache_fox_k, cache_fox_v, cache_fox_logf, state_hg, page_table,
           ln_g, ln_b, ffn_w_gate, ffn_w_up, ffn_w_down, ab_w_in, hg_lb_logits, hg_norm_g, fox_f_bias,
           ab_w_out, c_w_in, c_ln_g, c_ln_b, c_w_s, c_b_s, c_w_out):
    f = lambda a: np.ascontiguousarray(np.asarray(a, dtype=np.float32))
    x_prompt = f(x_prompt); x_sample = f(x_sample)
    P = x_prompt.shape[1]
    nphys = int(os.environ.get('KNPHYS', str(N_PHYS)))
    if nphys != N_PHYS:
        cache_fox_k = np.asarray(cache_fox_k)[:, :nphys]; cache_fox_v = np.asarray(cache_fox_v)[:, :nphys]
        cache_fox_logf = np.asarray(cache_fox_logf)[:, :nphys]; page_table = np.asarray(page_table) % nphys
    if P not in _NC_CACHE:
        _NC_CACHE[P] = build_program(P // 128, nphys, NS_)
    nc = _NC_CACHE[P]
    shared = {
        "cache_k": f(cache_fox_k).reshape(nphys * 128, 512),
        "cache_v": f(cache_fox_v).reshape(nphys * 128, 512),
        "cache_lf": f(cache_fox_logf).reshape(nphys * 128, 8),
        "ln_g": f(ln_g).reshape(6, D), "ln_b": f(ln_b).reshape(6, D),
        "w_gate": f(ffn_w_gate).reshape(4, D, FF), "w_up": f(ffn_w_up).reshape(4, D, FF),
        "w_down": f(ffn_w_down).reshape(4, FF, D),
        "ab_w_in": f(ab_w_in).reshape(D, 3592), "lb_logits": f(hg_lb_logits).reshape(1, 1536),
        "hg_norm_g": f(hg_norm_g).reshape(1, 128), "fox_f_bias": f(fox_f_bias).reshape(1, 8),
        "ab_w_out": f(ab_w_out).reshape(D, D), "c_w_in": f(c_w_in).reshape(D, 2048),
        "c_ln_g": f(c_ln_g).reshape(1, D), "c_ln_b": f(c_ln_b).reshape(1, D),
        "c_w_s": f(c_w_s).reshape(1024, 128), "c_b_s": f(c_b_s).reshape(8, 128),
        "c_w_out": f(c_w_out).reshape(D, D), "cst": make_consts().reshape(128, NCST * 128),
    }
    pt = np.asarray(page_table, dtype=np.int32)
    SQ = 16 * NS_
    in_maps = []
    for c in range(NCORES):
        m = dict(shared)
        m["xin"] = np.concatenate([x_prompt[c], x_sample[SQ * c:SQ * (c + 1)].reshape(SQ * 8, D)], axis=0)
        m["state_hg"] = f(state_hg)[0, SQ * c:SQ * (c + 1)].reshape(SQ * 4 * 128, 128)
        m["ptab"] = np.ascontiguousarray(pt[SQ * c:SQ * (c + 1)].reshape(1, SQ * 16))
        in_maps.append(m)
    ncores = int(os.environ.get('KCORES', str(NCORES)))
    res = run_bass_kernel_spmd(nc, in_maps[:ncores], core_ids=list(range(ncores))).results
    res = list(res) + [res[0]] * (NCORES - ncores)
    R = range(NCORES)
    y_p = np.stack([res[b]["y"][:P] for b in R])
    y_s = np.concatenate([res[c]["y"][P:].reshape(SQ, 8, D) for c in R])
    k_p = np.stack([res[b]["k_new"][:P].reshape(P, 8, 64) for b in R])[None]
    v_p = np.stack([res[b]["v_new"][:P].reshape(P, 8, 64) for b in R])[None]
    lf_p = np.stack([res[b]["lf_new"][:P] for b in R])[None]
    hg_p = np.stack([res[b]["hg_p"].reshape(4, 128, 128) for b in R])[None]
    k_s = np.concatenate([res[c]["k_new"][P:].reshape(SQ, 8, 8, 64) for c in R])[None]
    v_s = np.concatenate([res[c]["v_new"][P:].reshape(SQ, 8, 8, 64) for c in R])[None]
    lf_s = np.concatenate([res[c]["lf_new"][P:].reshape(SQ, 8, 8) for c in R])[None]
    hg_s = np.concatenate([res[c]["hg_s"].reshape(SQ, 4, 128, 128) for c in R])[None]
    cv_s = np.concatenate([res[c]["cv_s"].reshape(SQ, 8, D) for c in R])[None]
    return (y_p, y_s, k_p, v_p, lf_p, hg_p, k_s, v_s, lf_s, hg_s, cv_s)
```

```python
import os
import numpy as np
import concourse.bass as bass
import concourse.mybir as mybir
from concourse.bass_utils import run_bass_kernel_spmd

F32 = mybir.dt.float32
BF16 = mybir.dt.bfloat16
I32 = mybir.dt.int32
AF = mybir.ActivationFunctionType
ALU = mybir.AluOpType

D = 1024
FF = 2816
NJ = 22
NPT = 64
NT = 65
NTOK = NT * 128
ALPHA = 4.0 ** 0.25
EPS = 1e-5
N_PHYS = 2560
ENGS = ("tensor", "vector", "scalar", "gpsimd", "sync")
N_DMA_SEMS = 16
SEM_EPOCH = 16000

C_ID, C_TRI, C_SEL127, C_TRI2, C_BLK2, C_TRI16, C_BLK16, C_SELEND2, C_SELEND16, C_R, C_SUP, C_ONES, C_NEG = range(13)
NCST = 13


def make_consts():
    c = np.zeros((128, NCST, 128), np.float32)
    s = np.arange(128)[:, None]
    t = np.arange(128)[None, :]
    c[:, C_ID] = (s == t)
    c[:, C_TRI] = (s <= t)
    c[:, C_SEL127] = (s == 127) * np.ones((1, 128))
    c[:, C_TRI2] = (s // 64 == t // 64) & (s <= t)
    c[:, C_BLK2] = (s // 64 == t // 64)
    c[:, C_TRI16] = (s // 8 == t // 8) & (s <= t)
    c[:, C_BLK16] = (s // 8 == t // 8)
    c[:, C_SELEND2][:, 0] = (np.arange(128) == 63)
    c[:, C_SELEND2][:, 1] = (np.arange(128) == 127)
    for q in range(16):
        c[q * 8 + 7, C_SELEND16, q] = 1.0
    for i in range(8):
        c[i, C_R, i::8] = 1.0
    c[:, C_SUP] = (s > t)
    c[:, C_ONES] = 1.0
    c[:, C_NEG] = -1.0
    return c


class Sched:
    def __init__(self, nc):
        self.nc = nc
        self.q = {e: [] for e in ENGS}
        self.cnt = {e: 0 for e in ENGS}
        self.epoch = {e: 0 for e in ENGS}
        self.sem = {e: nc.alloc_semaphore(f"s_{e}_0") for e in ENGS}
        self.dsem = {e: [nc.alloc_semaphore(f"d_{e}_{i}") for i in range(N_DMA_SEMS)]
                     for e in ("sync", "scalar", "gpsimd")}
        self.dcnt = {e: [0] * N_DMA_SEMS for e in ("sync", "scalar", "gpsimd")}
        self.dnext = {e: 0 for e in ("sync", "scalar", "gpsimd")}
        self.known = {e: {} for e in ENGS}
        self.lastw = {}
        self.reads = {}
        self.semobj = {}
        for e in ENGS:
            self.semobj[("e", e, 0)] = self.sem[e]
        for e in self.dsem:
            for i, s in enumerate(self.dsem[e]):
                self.semobj[("d", e, i)] = s
        self.out_deps = []
        self.alias = {}

    def _need(self, eng, deps):
        need = {}
        for (k, v) in deps:
            if self.known[eng].get(k, 0) >= v:
                continue
            if need.get(k, 0) < v:
                need[k] = v
        for k, v in need.items():
            self.known[eng][k] = v
        return list(need.items())

    def _deps(self, reads, writes):
        reads = [self.alias.get(r, r) for r in reads]
        writes = [self.alias.get(w, w) for w in writes]
        deps = []
        for r in reads:
            if r in self.lastw:
                deps.append(self.lastw[r])
        for w in writes:
            if w in self.lastw:
                deps.append(self.lastw[w])
            deps.extend(self.reads.get(w, ()))
        return deps

    def _commit(self, tok, reads, writes):
        reads = [self.alias.get(r, r) for r in reads]
        writes = [self.alias.get(w, w) for w in writes]
        for r in reads:
            self.reads.setdefault(r, []).append(tok)
        for w in writes:
            self.lastw[w] = tok
            self.reads[w] = []

    def op(self, eng, fn, reads=(), writes=()):
        pr = [r for r in reads if r.startswith("ps")]
        if pr:
            reads = [r for r in reads if not r.startswith("ps")]
            writes = list(writes) + pr
        deps = self._deps(reads, writes)
        waits = self._need(eng, deps)
        if self.cnt[eng] >= SEM_EPOCH:
            self.epoch[eng] += 1
            self.cnt[eng] = 0
            self.sem[eng] = self.nc.alloc_semaphore(f"s_{eng}_{self.epoch[eng]}")
            self.semobj[("e", eng, self.epoch[eng])] = self.sem[eng]
        self.cnt[eng] += 1
        key = ("e", eng, self.epoch[eng])
        tok = (key, self.cnt[eng])
        if eng == "tensor":
            self.known[eng][key] = self.cnt[eng]
        self.q[eng].append((waits, fn, (self.sem[eng], 1)))
        self._commit(tok, reads, writes)
        return tok

    def dma(self, eng, fn, reads=(), writes=(), is_output=False):
        deps = self._deps(reads, writes)
        i = self.dnext[eng]
        self.dnext[eng] = (i + 1) % N_DMA_SEMS
        key = ("d", eng, i)
        if self.dcnt[eng][i] > 0:
            deps.append((key, self.dcnt[eng][i]))
        waits = self._need(eng, deps)
        self.dcnt[eng][i] += 16
        tok = (key, self.dcnt[eng][i])
        self.q[eng].append((waits, fn, (self.dsem[eng][i], 16)))
        self._commit(tok, reads, writes)
        if is_output:
            self.out_deps.append(tok)
        return tok

    def emit(self):
        nc = self.nc
        fin = list(self.out_deps)
        for r, t in self.lastw.items():
            fin.append(t)
        waits = self._need("sync", fin)
        self.q["sync"].append((waits, None, None))
        with nc.Block() as block:
            def run(engname):
                def body(eng):
                    for waits, fn, inc in self.q[engname]:
                        for k, v in waits:
                            eng.wait_ge(self.semobj[k], v)
                        if fn is not None:
                            ins = fn(eng)
                            ins.then_inc(inc[0], inc[1])
                return body
            block.tensor(run("tensor"))
            block.vector(run("vector"))
            block.scalar(run("scalar"))
            block.gpsimd(run("gpsimd"))
            block.sync(run("sync"))


def build_program(NPT=NPT, N_PHYS=N_PHYS, NS=4):
    NT = NPT + NS
    NTOK = NT * 128
    SMP = NPT
    nc = bass.Bass("TRN2", target_bir_lowering=False)
    S = Sched(nc)

    def din(name, shape, dt=F32):
        return nc.dram_tensor(name, list(shape), dt, kind="ExternalInput").ap()

    def dout(name, shape, dt=F32):
        return nc.dram_tensor(name, list(shape), dt, kind="ExternalOutput").ap()

    def dscr(name, shape, dt):
        return nc.dram_tensor(name, list(shape), dt).ap()

    def sb(name, shape, dt=F32):
        return nc.alloc_sbuf_tensor(name, list(shape), dt)

    xin = din("xin", [NTOK, D])
    ck = din("cache_k", [N_PHYS * 128, 512])
    cv_ = din("cache_v", [N_PHYS * 128, 512])
    clf = din("cache_lf", [N_PHYS * 128, 8])
    st_hg = din("state_hg", [NS * 16 * 4 * 128, 128])
    ptab = din("ptab", [1, NS * 256], I32)
    ln_g = din("ln_g", [6, D])
    ln_b = din("ln_b", [6, D])
    wgate = din("w_gate", [4, D, FF])
    wup = din("w_up", [4, D, FF])
    wdown = din("w_down", [4, FF, D])
    w_in = din("ab_w_in", [D, 3592])
    lb_log = din("lb_logits", [1, 3 * 512])
    hg_ng = din("hg_norm_g", [1, 128])
    fbias = din("fox_f_bias", [1, 8])
    w_out = din("ab_w_out", [D, D])
    c_w_in = din("c_w_in", [D, 2048])
    c_ln_g = din("c_ln_g", [1, D])
    c_ln_b = din("c_ln_b", [1, D])
    c_w_s = din("c_w_s", [8 * 128, 128])
    c_b_s = din("c_b_s", [8, 128])
    c_w_out = din("c_w_out", [D, D])
    cst_d = din("cst", [128, NCST * 128])

    y_o = dout("y", [NTOK, D])
    k_o = dout("k_new", [NTOK, 512])
    v_o = dout("v_new", [NTOK, 512])
    lf_o = dout("lf_new", [NTOK, 8])
    hgp_o = dout("hg_p", [4 * 128, 128])
    hgs_o = dout("hg_s", [NS * 16 * 4 * 128, 128])
    cv_o = dout("cv_s", [NS * 128, D])

    wg_scr = dscr("wg_scr", [4 * NJ * 128, 1024], BF16)
    wu_scr = dscr("wu_scr", [4 * NJ * 128, 1024], BF16)
    wd_scr = dscr("wd_scr", [4 * 4 * 128, NJ * 256], BF16)
    win_scr = dscr("win_scr", [128, 8 * 3592], BF16)
    wout_scr = dscr("wout_scr", [128, 8 * D], BF16)
    cwin_scr = dscr("cwin_scr", [128, 8 * 2048], BF16)
    cwout_scr = dscr("cwout_scr", [128, 8 * D], BF16)
    x1_scr = dscr("x1_scr", [NTOK, D], F32)
    oh_scr = dscr("oh_scr", [NTOK, 512], BF16)
    of_scr = dscr("of_scr", [NTOK, 512], BF16)
    qT_scr = dscr("qT_scr", [NT * 128, 512], BF16)
    kT_scr = dscr("kT_scr", [NT * 128, 512], BF16)
    vp_scr = dscr("vp_scr", [NT * 128, 8 * 65], BF16)

    cst = sb("cstt", [128, NCST, 128])
    cstb = sb("cstb", [128, NCST, 128], BF16)
    identb = cstb[:, C_ID, :]
    ps = [nc.alloc_psum_tensor(f"ps{i}", [128, 512], F32) for i in range(8)]

    def psb(i):
        return ps[i][:].bitcast(BF16)

    xg = sb("xg", [128, 4, D])
    xT = sb("xT", [128, 8, 512], BF16)
    hT = sb("hT", [128, NJ, 512], BF16)
    wgs = [sb(f"wgs{i}", [128, 8, 128], BF16) for i in range(2)]
    wus = [sb(f"wus{i}", [128, 8, 128], BF16) for i in range(2)]
    wds = sb("wds", [128, NJ, 256], BF16)
    wps = [sb(f"wps{i}", [128, 8, 512], BF16) for i in range(2)]
    gb = [sb(f"gb{i}", [128, D]) for i in range(2)]
    zt = sb("zt", [128, D])
    tf = [sb(f"tf{i}", [128, D]) for i in range(4)]
    tb = [sb(f"tb{i}", [128, D], BF16) for i in range(4)]
    S.alias.update({"qh": "tf0", "kk": "tf0", "gl": "tf1", "gs": "tf1", "bcs": "tf2", "ebl": "tf2", "fk_f": "tf3", "fv_f": "tf3",
                    "vh": "tb0", "qtl": "tb0", "ktl": "tb1", "khat": "tb1", "ohb": "tb2", "fq_b": "tb2", "fk_b": "tb3",
                    "uu": "tf0", "vv": "tf1", "vvb": "tb0", "umx": "tb1", "ocat": "tb2", "tmpA": "tf2"})
    xb16 = sb("xb16", [128, D], BF16)
    sg_t = sb("sg_t", [128, 512], BF16)
    stats = sb("stats", [128, 2, 6])
    mv = sb("mv", [128, 2])
    rstd = sb("rstd", [128, 1])

    S.dma("sync", lambda e: e.dma_start(out=cst[:].rearrange("p a b -> p (a b)"), in_=cst_d[:, :]), writes=["cst"])
    S.op("vector", lambda e: e.tensor_copy(out=cstb[:], in_=cst[:]), reads=["cst"], writes=["cstb"])

    def V(fn, reads=(), writes=()):
        return S.op("vector", fn, reads, writes)

    def A(fn, reads=(), writes=()):
        return S.op("scalar", fn, reads, writes)

    def PE(fn, reads=(), writes=()):
        return S.op("tensor", fn, reads, writes)

    def bcast_load(dst, src_row, res):
        S.dma("sync", lambda e: e.dma_start(out=dst, in_=src_row.partition_broadcast(128)), writes=[res])

    def transpose_to(dstT, dst_res, src_bf, src_res, nk, col0, psi=7):
        pb = psb(psi)
        for k in range(nk):
            PE(lambda e, k=k: e.transpose(pb[:, k * 128:(k + 1) * 128], src_bf[:, k * 128:(k + 1) * 128], identb),
               reads=[src_res, "cstb"], writes=[f"ps{psi}"])
        V(lambda e: e.tensor_copy(out=dstT[:, 0:nk, col0:col0 + 128],
                                  in_=pb[:, 0:nk * 128].rearrange("p (k t) -> p k t", k=nk)),
          reads=[f"ps{psi}"], writes=[dst_res])

    def layer_norm(src, src_res, dst, dst_res, gi, g_ap=None, b_ap=None):
        ga = ln_g[gi:gi + 1, :] if g_ap is None else g_ap
        ba = ln_b[gi:gi + 1, :] if b_ap is None else b_ap
        bcast_load(gb[0][:], ga, "gb0")
        bcast_load(gb[1][:], ba, "gb1")
        for h in range(2):
            V(lambda e, h=h: e.bn_stats(out=stats[:, h, :], in_=src[:, h * 512:(h + 1) * 512]),
              reads=[src_res], writes=["stats"] if h == 0 else ["stats"])
        V(lambda e: e.bn_aggr(out=mv[:], in_=stats[:].rearrange("p a b -> p (a b)")), reads=["stats"], writes=["mv"])
        V(lambda e: e.tensor_scalar(out=rstd[:], in0=mv[:, 1:2], scalar1=EPS, scalar2=None, op0=ALU.add), reads=["mv"], writes=["rstd"])
        A(lambda e: e.activation(out=rstd[:], in_=rstd[:], func=AF.Sqrt), reads=["rstd"], writes=["rstd"])
        V(lambda e: e.reciprocal(out=rstd[:], in_=rstd[:]), reads=["rstd"], writes=["rstd"])
        V(lambda e: e.tensor_scalar(out=dst, in0=src, scalar1=mv[:, 0:1], scalar2=rstd[:, 0:1], op0=ALU.subtract, op1=ALU.mult),
          reads=[src_res, "mv", "rstd"], writes=[dst_res])
        V(lambda e: e.tensor_tensor(out=dst, in0=dst, in1=gb[0][:], op=ALU.mult), reads=[dst_res, "gb0"], writes=[dst_res])
        V(lambda e: e.tensor_tensor(out=dst, in0=dst, in1=gb[1][:], op=ALU.add), reads=[dst_res, "gb1"], writes=[dst_res])

    def build_xT(nt):
        for ti in range(nt):
            A(lambda e, ti=ti: e.activation(out=xb16[:], in_=xg[:, ti, :], func=AF.Copy), reads=[f"xg{ti}"], writes=["xb16"])
            transpose_to(xT, "xT", xb16, "xb16", 8, ti * 128)

    def ffn(fi, nt):
        ntk = nt * 128
        build_xT(nt)
        for j in range(NJ):
            b = j % 2
            rj = (fi * NJ + j) * 128
            S.dma("sync", lambda e, rj=rj, b=b: e.dma_start(out=wgs[b][:].rearrange("p k m -> p (k m)"), in_=wg_scr[rj:rj + 128, :]),
                  reads=[f"S_wg{fi}"], writes=[f"wgs{b}"])
            S.dma("sync", lambda e, rj=rj, b=b: e.dma_start(out=wus[b][:].rearrange("p k m -> p (k m)"), in_=wu_scr[rj:rj + 128, :]),
                  reads=[f"S_wu{fi}"], writes=[f"wus{b}"])
            pg, pu = ps[2 * b], ps[2 * b + 1]
            for k in range(8):
                PE(lambda e, k=k, b=b, pg=pg: e.matmul(pg[:, 0:ntk], lhsT=wgs[b][:, k, :], rhs=xT[:, k, 0:ntk], start=(k == 0), stop=(k == 7)),
                   reads=[f"wgs{b}", "xT"], writes=[f"ps{2 * b}"])
            for k in range(8):
                PE(lambda e, k=k, b=b, pu=pu: e.matmul(pu[:, 0:ntk], lhsT=wus[b][:, k, :], rhs=xT[:, k, 0:ntk], start=(k == 0), stop=(k == 7)),
                   reads=[f"wus{b}", "xT"], writes=[f"ps{2 * b + 1}"])
            A(lambda e, pg=pg: e.activation(out=sg_t[:, 0:ntk], in_=pg[:, 0:ntk], func=AF.Silu), reads=[f"ps{2 * b}"], writes=["sg_t"])
            V(lambda e, j=j, pu=pu: e.tensor_tensor(out=hT[:, j, 0:ntk], in0=sg_t[:, 0:ntk], in1=pu[:, 0:ntk], op=ALU.mult),
              reads=["sg_t", f"ps{2 * b + 1}"], writes=["hT"])
        li = (fi // 2) * 3 + (0 if fi % 2 == 0 else 2)
        for c in range(4):
            cs = slice(c * 256, (c + 1) * 256)
            rq = (fi * 4 + c) * 128
            S.dma("sync", lambda e, rq=rq: e.dma_start(out=wds[:].rearrange("p j n -> p (j n)"), in_=wd_scr[rq:rq + 128, :]), reads=[f"S_wd{fi}"], writes=["wds"])
            for ti in range(nt):
                pd = ps[4 + (ti % 2)]
                for j in range(NJ):
                    PE(lambda e, j=j, ti=ti, pd=pd: e.matmul(pd[:, 0:256], lhsT=hT[:, j, ti * 128:(ti + 1) * 128], rhs=wds[:, j, :], start=(j == 0), stop=(j == NJ - 1)),
                       reads=["hT", "wds"], writes=[f"ps{4 + ti % 2}"])
                A(lambda e, pd=pd, cs=cs: e.activation(out=zt[:, cs], in_=pd[:, 0:256], func=AF.Copy, scale=0.5),
                  reads=[f"ps{4 + ti % 2}"], writes=["zt"])
                V(lambda e, ti=ti, cs=cs: e.scalar_tensor_tensor(out=xg[:, ti, cs], in0=xg[:, ti, cs], scalar=ALPHA,
                                                                in1=zt[:, cs], op0=ALU.mult, op1=ALU.add),
                  reads=["zt", f"xg{ti}"], writes=[f"xg{ti}"])
        for ti in range(nt):
            layer_norm(xg[:, ti, :], f"xg{ti}", xg[:, ti, :], f"xg{ti}", li)

    def proj_tokmajor(w_ap, c0, ncols, ti, psi, wb):
        w3, wres = WSCR[w_ap]
        S.dma("sync", lambda e: e.dma_start(out=wps[wb][:, :, 0:ncols], in_=w3[:, :, c0:c0 + ncols]), reads=[wres], writes=[f"wps{wb}"])
        for k in range(8):
            PE(lambda e, k=k: e.matmul(ps[psi][:, 0:ncols], lhsT=xT[:, k, ti * 128:(ti + 1) * 128], rhs=wps[wb][:, k, 0:ncols], start=(k == 0), stop=(k == 7)),
               reads=["xT", f"wps{wb}"], writes=[f"ps{psi}"])

    stg = [sb(f"stg{i}", [128, 8, 256], BF16) for i in range(2)]
    stg_i = [0]

    def conv_cols(src, ncols, dst3, res):
        for c0 in range(0, ncols, 256):
            w = min(256, ncols - c0)
            b = stg_i[0] % 2; stg_i[0] += 1
            S.dma("gpsimd", lambda e, b=b, c0=c0, w=w: e.dma_start(out=stg[b][:, :, 0:w], in_=src[:, c0:c0 + w].rearrange("(k p) n -> p k n", p=128)), writes=[f"stg{b}"])
            S.dma("gpsimd", lambda e, b=b, c0=c0, w=w: e.dma_start(out=dst3[:, :, c0:c0 + w], in_=stg[b][:, :, 0:w]), reads=[f"stg{b}"], writes=[res])

    def conv_gu(src, scr, fi, res):
        for jj in range(NJ // 2):
            b = stg_i[0] % 2; stg_i[0] += 1
            S.dma("gpsimd", lambda e, b=b, jj=jj: e.dma_start(out=stg[b][:], in_=src[fi, :, jj * 256:(jj + 1) * 256].rearrange("(k p) n -> p k n", p=128)), writes=[f"stg{b}"])
            for c in range(2):
                r0 = (fi * NJ + jj * 2 + c) * 128
                S.dma("gpsimd", lambda e, b=b, c=c, r0=r0: e.dma_start(out=scr[r0:r0 + 128, :].rearrange("p (k m) -> p k m", k=8), in_=stg[b][:, :, c * 128:(c + 1) * 128]),
                      reads=[f"stg{b}"], writes=[res])

    def conv_down(fi, res):
        for q in range(4):
            r0 = (fi * 4 + q) * 128
            for (j0, nj) in ((0, 8), (8, 8), (16, 6)):
                b = stg_i[0] % 2; stg_i[0] += 1
                S.dma("gpsimd", lambda e, b=b, q=q, j0=j0, nj=nj: e.dma_start(
                    out=stg[b][:, 0:nj, :], in_=wdown[fi, j0 * 128:(j0 + nj) * 128, q * 256:(q + 1) * 256].rearrange("(j p) n -> p j n", p=128)), writes=[f"stg{b}"])
                S.dma("gpsimd", lambda e, b=b, r0=r0, j0=j0, nj=nj: e.dma_start(
                    out=wd_scr[r0:r0 + 128, :].rearrange("p (j n) -> p j n", j=NJ)[:, j0:j0 + nj, :], in_=stg[b][:, 0:nj, :]), reads=[f"stg{b}"], writes=[res])

    def conv_ffn(fi):
        conv_gu(wgate, wg_scr, fi, f"S_wg{fi}")
        conv_gu(wup, wu_scr, fi, f"S_wu{fi}")
        conv_down(fi, f"S_wd{fi}")

    win3 = win_scr.rearrange("p (k n) -> p k n", k=8)
    wout3 = wout_scr.rearrange("p (k n) -> p k n", k=8)
    cwin3 = cwin_scr.rearrange("p (k n) -> p k n", k=8)
    cwout3 = cwout_scr.rearrange("p (k n) -> p k n", k=8)
    conv_ffn(0)
    conv_cols(w_in, 3592, win3, "S_win")
    conv_cols(w_out, D, wout3, "S_wout")
    conv_ffn(1)
    conv_ffn(2)
    conv_cols(c_w_in, 2048, cwin3, "S_cwin")
    conv_cols(c_w_out, D, cwout3, "S_cwout")
    conv_ffn(3)
    WSCR = {"w_in": (win3, "S_win"), "w_out": (wout3, "S_wout"), "c_w_in": (cwin3, "S_cwin"), "c_w_out": (cwout3, "S_cwout")}

    lbt = [tf[0][:, 0:512], tf[0][:, 512:1024], tf[1][:, 0:512]]
    LR = ["tf0", "tf1"]
    lb = sb("lb", [128, 512])
    oml = sb("oml", [128, 512])
    tmpA = tf[2][:, 0:512]
    ng_bc = sb("ng_bc", [128, 128])
    fb_bc = sb("fb_bc", [128, 8])
    for a in range(3):
        S.dma("sync", lambda e, a=a: e.dma_start(out=lbt[a], in_=lb_log[0:1, a * 512:(a + 1) * 512].partition_broadcast(128)), writes=LR)
    bcast_load(ng_bc[:], hg_ng[0:1, :], "ng_bc")
    bcast_load(fb_bc[:], fbias[0:1, :], "fb_bc")
    V(lambda e: e.tensor_tensor(out=tmpA, in0=lbt[0], in1=lbt[1], op=ALU.max), reads=LR, writes=["tmpA"])
    V(lambda e: e.tensor_tensor(out=tmpA, in0=tmpA, in1=lbt[2], op=ALU.max), reads=LR + ["tmpA"], writes=["tmpA"])
    for a in range(3):
        V(lambda e, a=a: e.tensor_tensor(out=lbt[a], in0=lbt[a], in1=tmpA, op=ALU.subtract), reads=LR + ["tmpA"], writes=LR)
        A(lambda e, a=a: e.activation(out=lbt[a], in_=lbt[a], func=AF.Exp), reads=LR, writes=LR)
    V(lambda e: e.tensor_tensor(out=tmpA, in0=lbt[0], in1=lbt[1], op=ALU.add), reads=LR, writes=["tmpA"])
    V(lambda e: e.tensor_tensor(out=tmpA, in0=tmpA, in1=lbt[2], op=ALU.add), reads=LR + ["tmpA"], writes=["tmpA"])
    V(lambda e: e.reciprocal(out=tmpA, in_=tmpA), reads=["tmpA"], writes=["tmpA"])
    V(lambda e: e.tensor_tensor(out=lb[:], in0=lbt[0], in1=tmpA, op=ALU.mult), reads=LR + ["tmpA"], writes=["lb"])
    V(lambda e: e.tensor_scalar(out=oml[:], in0=lb[:], scalar1=-1.0, scalar2=1.0, op0=ALU.mult, op1=ALU.add), reads=["lb"], writes=["oml"])

    Fk = sb("Fk", [128, NT, 8])
    Fend = sb("Fend", [128, NT, 8])
    Sst = [sb(f"Sst{h}", [128, 128]) for h in range(4)]
    Sb = [sb(f"Sb{h}", [128, 128], BF16) for h in range(4)]
    qh = tf[0][:, 0:512]
    kk = tf[0][:, 512:1024]
    gl = tf[1][:, 0:512]
    vh = tb[0][:, 0:512]
    gs = tf[1][:, 512:1024]
    bcs = tf[2][:, 0:512]
    ebl = tf[2][:, 512:1024]
    qtl = tb[0][:, 512:1024]
    ktl = tb[1][:, 0:512]
    khat = tb[1][:, 512:1024]
    qtT = sb("qtT", [128, 128], BF16)
    ktT = sb("ktT", [128, 128], BF16)
    qtT0 = sb("qtT0", [128, 128], BF16)
    qtT1 = sb("qtT1", [128, 128], BF16)
    attT = sb("attT", [128, 128], BF16)
    dcol = sb("dcol", [128, 16])
    ohb = tb[2][:, 0:512]
    ssq = sb("ssq", [128, 4])
    junk = sb("junk", [128, 128])
    fq_b = tb[2][:, 512:1024]
    fk_f = tf[3][:, 0:512]
    fk_b = tb[3][:, 0:512]
    fv_f = tf[3][:, 512:1024]
    vp = sb("vp", [128, 8, 65], BF16)
    lft = sb("lft", [128, 8])
    tq = sb("tq", [128, 4, 128], BF16)
    S0f = sb("S0f", [128, 16, 128])
    S0b = sb("S0b", [128, 16, 128], BF16)
    qz = sb("qz", [128, 16, 128], BF16)
    vbd = sb("vbd", [128, 16, 128], BF16)

    V(lambda e: e.memset(qtT0[:], 0.0), writes=["qtT0"])
    V(lambda e: e.memset(qtT1[:], 0.0), writes=["qtT1"])
    V(lambda e: e.memset(qz[:], 0.0), writes=["qz"])
    V(lambda e: e.memset(vp[:], 1.0), writes=["vp"])
    for h in range(4):
        V(lambda e, h=h: e.memset(Sst[h][:], 0.0), writes=[f"Sst{h}"])
        V(lambda e, h=h: e.memset(Sb[h][:], 0.0), writes=[f"Sb{h}"])

    KSTAGE = int(os.environ.get('KSTAGE', '9'))
    dcs = sb("dcs", [128, 64])

    def phaseA_tile(T, ti, sample):
        r0 = T * 128
        tri = C_TRI16 if sample else C_TRI2
        blk = C_BLK16 if sample else C_BLK2
        proj_tokmajor("w_in", 0, 512, ti, 0, 0)
        A(lambda e: e.activation(out=qh[:], in_=ps[0][:], func=AF.Silu), reads=["ps0"], writes=["qh"])
        proj_tokmajor("w_in", 512, 512, ti, 1, 1)
        A(lambda e: e.activation(out=kk[:], in_=ps[1][:], func=AF.Sigmoid), reads=["ps1"], writes=["kk"])
        V(lambda e: e.tensor_tensor(out=gl[:], in0=kk[:], in1=oml[:], op=ALU.mult), reads=["kk", "oml"], writes=["gl"])
        V(lambda e: e.tensor_tensor(out=gl[:], in0=gl[:], in1=lb[:], op=ALU.add), reads=["gl", "lb"], writes=["gl"])
        V(lambda e: e.tensor_scalar(out=kk[:], in0=gl[:], scalar1=-1.0, scalar2=1.0, op0=ALU.mult, op1=ALU.add), reads=["gl"], writes=["kk"])
        A(lambda e: e.activation(out=gl[:], in_=gl[:], func=AF.Ln), reads=["gl"], writes=["gl"])
        proj_tokmajor("w_in", 1024, 512, ti, 0, 0)
        A(lambda e: e.activation(out=vh[:], in_=ps[0][:], func=AF.Copy), reads=["ps0"], writes=["vh"])
        proj_tokmajor("w_in", 1536, 512, ti, 1, 1)
        A(lambda e: e.activation(out=gs[:], in_=ps[1][:], func=AF.Silu), reads=["ps1"], writes=["gs"])
        gs3 = gs.rearrange("p (h v) -> p h v", h=4)
        V(lambda e: e.tensor_tensor(out=gs3, in0=gs3, in1=ng_bc[:].unsqueeze(1).broadcast_to([128, 4, 128]), op=ALU.mult), reads=["gs", "ng_bc"], writes=["gs"])
        if KSTAGE < 2:
            return
        PE(lambda e: e.matmul(ps[2][:], lhsT=cst[:, tri, :], rhs=gl[:], start=True, stop=True), reads=["cst", "gl"], writes=["ps2"])
        PE(lambda e: e.matmul(ps[3][:], lhsT=cst[:, blk, :], rhs=gl[:], start=True, stop=True), reads=["cst", "gl"], writes=["ps3"])
        A(lambda e: e.activation(out=bcs[:], in_=ps[2][:], func=AF.Exp), reads=["ps2"], writes=["bcs"])
        V(lambda e: e.tensor_tensor(out=qtl[:], in0=qh[:], in1=bcs[:], op=ALU.mult), reads=["qh", "bcs"], writes=["qtl"])
        A(lambda e: e.activation(out=bcs[:], in_=ps[2][:], func=AF.Exp, scale=-1.0), reads=["ps2"], writes=["bcs"])
        V(lambda e: e.tensor_tensor(out=ktl[:], in0=kk[:], in1=bcs[:], op=ALU.mult), reads=["kk", "bcs"], writes=["ktl"])
        A(lambda e: e.activation(out=ebl[:], in_=ps[3][:], func=AF.Exp), reads=["ps3"], writes=["ebl"])
        V(lambda e: e.tensor_tensor(out=bcs[:], in0=bcs[:], in1=ebl[:], op=ALU.mult), reads=["bcs", "ebl"], writes=["bcs"])
        V(lambda e: e.tensor_tensor(out=khat[:], in0=kk[:], in1=bcs[:], op=ALU.mult), reads=["kk", "bcs"], writes=["khat"])
        nb = 16 if sample else 2
        selc = C_SELEND16 if sample else C_SELEND2
        for h in range(4):
            PE(lambda e, h=h: e.matmul(ps[3][:, 256 + h * 16:256 + h * 16 + nb], lhsT=ebl[:, h * 128:(h + 1) * 128], rhs=cst[:, selc, 0:nb], start=True, stop=True),
               reads=["ebl", "cst"], writes=["ps3"])
        if sample:
            sq0 = (T - SMP) * 16
            V(lambda e: e.tensor_copy(out=dcs[:], in_=ps[3][:, 256:320]), reads=["ps3"], writes=["dcs"])
        else:
            V(lambda e: e.tensor_copy(out=dcol[:].rearrange("p (h c) -> p h c", h=4)[:, :, 0:2],
                                      in_=ps[3][:, 256:320].rearrange("p (h c) -> p h c", h=4)[:, :, 0:2]), reads=["ps3"], writes=["dcol"])
        if KSTAGE < 3:
            return
        for h in range(4):
            hs = slice(h * 128, (h + 1) * 128)
            pb = psb(7)
            PE(lambda e, hs=hs: e.transpose(pb[:, 0:128], qtl[:, hs], identb), reads=["qtl", "cstb"], writes=["ps7"])
            PE(lambda e, hs=hs: e.transpose(pb[:, 128:256], ktl[:, hs], identb), reads=["ktl", "cstb"], writes=["ps7"])
            V(lambda e: e.tensor_copy(out=qtT[:], in_=pb[:, 0:128]), reads=["ps7"], writes=["qtT"])
            A(lambda e: e.activation(out=ktT[:], in_=pb[:, 128:256], func=AF.Copy), reads=["ps7"], writes=["ktT"])
            PE(lambda e: e.matmul(ps[6][:, 0:128], lhsT=ktT[:], rhs=qtT[:], start=True, stop=True), reads=["ktT", "qtT"], writes=["ps6"])
            V(lambda e: e.tensor_tensor(out=attT[:], in0=ps[6][:, 0:128], in1=cst[:, tri, :], op=ALU.mult), reads=["ps6", "cst"], writes=["attT"])
            if not sample:
                V(lambda e: e.tensor_copy(out=qtT0[:, 0:64], in_=qtT[:, 0:64]), reads=["qtT"], writes=["qtT0"])
                V(lambda e: e.tensor_copy(out=qtT1[:, 64:128], in_=qtT[:, 64:128]), reads=["qtT"], writes=["qtT1"])
                PE(lambda e, hs=hs: e.matmul(ps[6][:, 128:256], lhsT=khat[0:64, hs], rhs=vh[0:64, hs], start=True, stop=True),
                   reads=["khat", "vh"], writes=["ps6"])
                PE(lambda e, hs=hs: e.matmul(ps[5][:, 0:128], lhsT=attT[:], rhs=vh[:, hs], start=True, stop=False), reads=["attT", "vh"], writes=["ps5"])
                PE(lambda e, h=h: e.matmul(ps[5][:, 0:128], lhsT=qtT0[:], rhs=Sb[h][:], start=False, stop=False), reads=["qtT0", f"Sb{h}"], writes=["ps5"])
                V(lambda e, h=h: e.scalar_tensor_tensor(out=Sst[h][:], in0=Sst[h][:], scalar=dcol[:, h * 4:h * 4 + 1], in1=ps[6][:, 128:256], op0=ALU.mult, op1=ALU.add),
                  reads=[f"Sst{h}", "dcol", "ps6"], writes=[f"Sst{h}"])
                A(lambda e, h=h: e.activation(out=Sb[h][:], in_=Sst[h][:], func=AF.Copy), reads=[f"Sst{h}"], writes=[f"Sb{h}"])
                PE(lambda e, h=h: e.matmul(ps[5][:, 0:128], lhsT=qtT1[:], rhs=Sb[h][:], start=False, stop=True), reads=["qtT1", f"Sb{h}"], writes=["ps5"])
                PE(lambda e, hs=hs: e.matmul(ps[6][:, 256:384], lhsT=khat[64:128, hs], rhs=vh[64:128, hs], start=True, stop=True),
                   reads=["khat", "vh"], writes=["ps6"])
                V(lambda e, h=h: e.scalar_tensor_tensor(out=Sst[h][:], in0=Sst[h][:], scalar=dcol[:, h * 4 + 1:h * 4 + 2], in1=ps[6][:, 256:384], op0=ALU.mult, op1=ALU.add),
                  reads=[f"Sst{h}", "dcol", "ps6"], writes=[f"Sst{h}"])
                A(lambda e, h=h: e.activation(out=Sb[h][:], in_=Sst[h][:], func=AF.Copy), reads=[f"Sst{h}"], writes=[f"Sb{h}"])
            else:
                S.dma("sync", lambda e, h=h: e.dma_start(out=S0f[:], in_=st_hg.rearrange("(q h p) v -> h p q v", h=4, p=128)[h][:, sq0:sq0 + 16, :]), writes=["S0f"])
                A(lambda e: e.activation(out=S0b[:], in_=S0f[:], func=AF.Copy), reads=["S0f"], writes=["S0b"])
                for q in range(16):
                    V(lambda e, q=q: e.tensor_copy(out=qz[:, q, q * 8:(q + 1) * 8], in_=qtT[:, q * 8:(q + 1) * 8]), reads=["qtT"], writes=["qz"])
                PE(lambda e, hs=hs: e.matmul(ps[5][:, 0:128], lhsT=attT[:], rhs=vh[:, hs], start=True, stop=False), reads=["attT", "vh"], writes=["ps5"])
                for q in range(16):
                    PE(lambda e, q=q, h=h: e.matmul(ps[5][:, 0:128], lhsT=qz[:, q, :], rhs=S0b[:, q, :], start=False, stop=(q == 15)),
                       reads=["qz", "S0b"], writes=["ps5"])
                for q in range(16):
                    V(lambda e, q=q, hs=hs: e.tensor_scalar(out=vbd[:, q, :], in0=vh[:, hs], scalar1=cst[:, C_BLK16, q * 8:q * 8 + 1], scalar2=None, op0=ALU.mult),
                      reads=["vh", "cst"], writes=["vbd"])
                for c4 in range(4):
                    PE(lambda e, c4=c4, hs=hs: e.matmul(ps[c4][:, :], lhsT=khat[:, hs], rhs=vbd[:, c4 * 4:(c4 + 1) * 4, :].rearrange("p a b -> p (a b)"), start=True, stop=True),
                       reads=["khat", "vbd"], writes=[f"ps{c4}"])
                for q in range(16):
                    V(lambda e, q=q, h=h: e.scalar_tensor_tensor(out=S0f[:, q, :], in0=S0f[:, q, :], scalar=dcs[:, h * 16 + q:h * 16 + q + 1],
                                                                 in1=ps[q // 4][:, (q % 4) * 128:(q % 4 + 1) * 128], op0=ALU.mult, op1=ALU.add),
                      reads=["S0f", "dcs", f"ps{q // 4}"], writes=["S0f"])
                S.dma("sync", lambda e, h=h: e.dma_start(out=hgs_o.rearrange("(q h p) v -> h p q v", h=4, p=128)[h][:, sq0:sq0 + 16, :], in_=S0f[:]), reads=["S0f"], is_output=True)
            A(lambda e, h=h: e.activation(out=junk[:], in_=ps[5][:, 0:128], func=AF.Square, accum_out=ssq[:, h:h + 1]), reads=["ps5"], writes=["junk", "ssq"])
            V(lambda e, h=h: e.tensor_scalar(out=ssq[:, h:h + 1], in0=ssq[:, h:h + 1], scalar1=1.0 / 128, scalar2=EPS, op0=ALU.mult, op1=ALU.add), reads=["ssq"], writes=["ssq"])
            A(lambda e, h=h: e.activation(out=ssq[:, h:h + 1], in_=ssq[:, h:h + 1], func=AF.Sqrt), reads=["ssq"], writes=["ssq"])
            V(lambda e, h=h: e.reciprocal(out=ssq[:, h:h + 1], in_=ssq[:, h:h + 1]), reads=["ssq"], writes=["ssq"])
            V(lambda e, h=h, hs=hs: e.scalar_tensor_tensor(out=ohb[:, hs], in0=ps[5][:, 0:128], scalar=ssq[:, h:h + 1], in1=gs[:, hs], op0=ALU.mult, op1=ALU.mult),
              reads=["ps5", "ssq", "gs"], writes=["ohb"])
        if KSTAGE < 4:
            return
        S.dma("sync", lambda e: e.dma_start(out=oh_scr[r0:r0 + 128, :], in_=ohb[:]), reads=["ohb"], writes=["oh_scr"])
        proj_tokmajor("w_in", 2048, 512, ti, 0, 0)
        A(lambda e: e.activation(out=fq_b[:], in_=ps[0][:], func=AF.Copy, scale=0.125), reads=["ps0"], writes=["fq_b"])
        proj_tokmajor("w_in", 2560, 512, ti, 1, 1)
        A(lambda e: e.activation(out=fk_f[:], in_=ps[1][:], func=AF.Copy), reads=["ps1"], writes=["fk_f"])
        V(lambda e: e.tensor_copy(out=fk_b[:], in_=ps[1][:]), reads=["ps1"], writes=["fk_b"])
        S.dma("sync", lambda e: e.dma_start(out=k_o[r0:r0 + 128, :], in_=fk_f[:]), reads=["fk_f"], is_output=True)
        proj_tokmajor("w_in", 3072, 512, ti, 0, 0)
        A(lambda e: e.activation(out=fv_f[:], in_=ps[0][:], func=AF.Copy), reads=["ps0"], writes=["fv_f"])
        V(lambda e: e.tensor_copy(out=vp[:, :, 0:64], in_=ps[0][:].rearrange("p (h d) -> p h d", h=8)), reads=["ps0"], writes=["vp"])
        S.dma("sync", lambda e: e.dma_start(out=v_o[r0:r0 + 128, :], in_=fv_f[:]), reads=["fv_f"], is_output=True)
        S.dma("sync", lambda e: e.dma_start(out=vp_scr[r0:r0 + 128, :], in_=vp[:].rearrange("p h d -> p (h d)")), reads=["vp"], writes=["vp_scr"])
        if KSTAGE < 5:
            return
        proj_tokmajor("w_in", 3584, 8, ti, 1, 1)
        V(lambda e: e.tensor_tensor(out=lft[:], in0=ps[1][:, 0:8], in1=fb_bc[:], op=ALU.add), reads=["ps1", "fb_bc"], writes=["lft"])
        A(lambda e: e.activation(out=lft[:], in_=lft[:], func=AF.Exp, scale=-1.0), reads=["lft"], writes=["lft"])
        A(lambda e: e.activation(out=lft[:], in_=lft[:], func=AF.Ln, bias=1.0), reads=["lft"], writes=["lft"])
        V(lambda e: e.tensor_scalar(out=lft[:], in0=lft[:], scalar1=-1.0, scalar2=None, op0=ALU.mult), reads=["lft"], writes=["lft"])
        S.dma("sync", lambda e: e.dma_start(out=lf_o[r0:r0 + 128, :], in_=lft[:]), reads=["lft"], is_output=True)
        if not sample:
            PE(lambda e: e.matmul(ps[2][:, 0:8], lhsT=cst[:, C_TRI, :], rhs=lft[:], start=True, stop=(T == 0)), reads=["cst", "lft"], writes=["ps2"])
            if T > 0:
                PE(lambda e: e.matmul(ps[2][:, 0:8], lhsT=cst[:, C_SEL127, :], rhs=Fk[:, T - 1, :], start=False, stop=True), reads=["cst", "Fk"], writes=["ps2"])
            V(lambda e: e.tensor_copy(out=Fk[:, T, :], in_=ps[2][:, 0:8]), reads=["ps2"], writes=["Fk"])
            PE(lambda e: e.matmul(ps[2][:, 8:16], lhsT=cst[:, C_SEL127, :], rhs=Fk[:, T, :], start=True, stop=True), reads=["cst", "Fk"], writes=["ps2"])
            V(lambda e: e.tensor_copy(out=Fend[:, T, :], in_=ps[2][:, 8:16]), reads=["ps2"], writes=["Fend"])
        else:
            PE(lambda e: e.matmul(ps[2][:, 0:8], lhsT=cst[:, C_TRI16, :], rhs=lft[:], start=True, stop=True), reads=["cst", "lft"], writes=["ps2"])
            V(lambda e: e.tensor_copy(out=Fk[:, T, :], in_=ps[2][:, 0:8]), reads=["ps2"], writes=["Fk"])
        for (src, res, scr) in ((fq_b, "fq_b", qT_scr), (fk_b, "fk_b", kT_scr)):
            transpose_to(tq, "tq", src, res, 4, 0, psi=7)
            S.dma("sync", lambda e, scr=scr: e.dma_start(out=scr[r0:r0 + 128, :], in_=tq[:].rearrange("p a b -> p (a b)")), reads=["tq"], writes=[scr.tensor.name])

    PH0 = os.environ.get('KPH', 'ABSC')
    for G in range(NPT // 4):
        for ti in range(4):
            T = G * 4 + ti
            S.dma("sync", lambda e, T=T, ti=ti: e.dma_start(out=xg[:, ti, :], in_=xin[T * 128:(T + 1) * 128, :]), writes=[f"xg{ti}"])
        if 'f' not in PH0:
            ffn(0, 4)
        build_xT(4)
        for ti in range(4):
            T = G * 4 + ti
            S.dma("sync", lambda e, T=T, ti=ti: e.dma_start(out=x1_scr[T * 128:(T + 1) * 128, :], in_=xg[:, ti, :]), reads=[f"xg{ti}"], writes=["x1_scr"])
            if 'F' in PH0:
                S.dma("sync", lambda e, T=T, ti=ti: e.dma_start(out=y_o[T * 128:(T + 1) * 128, :], in_=xg[:, ti, :]), reads=[f"xg{ti}"], is_output=True)
            else:
                phaseA_tile(T, ti, False)
    for h in range(4):
        S.dma("sync", lambda e, h=h: e.dma_start(out=hgp_o[h * 128:(h + 1) * 128, :], in_=Sst[h][:]), reads=[f"Sst{h}"], is_output=True)
    for st in range(NS):
        S.dma("sync", lambda e, st=st: e.dma_start(out=xg[:, st, :], in_=xin[(SMP + st) * 128:(SMP + st + 1) * 128, :]), writes=[f"xg{st}"])
    if 'F' in PH0:
        S.emit()
        return nc
    ffn(0, NS)
    build_xT(NS)
    for st in range(NS):
        S.dma("sync", lambda e, st=st: e.dma_start(out=x1_scr[(SMP + st) * 128:(SMP + st + 1) * 128, :], in_=xg[:, st, :]), reads=[f"xg{st}"], writes=["x1_scr"])
        phaseA_tile(SMP + st, st, True)

    PH = os.environ.get('KPH', 'ABSC')
    NKB = 2
    kTt = [sb(f"kTt{i}", [128, 4, 128], BF16) for i in range(NKB)]
    vpt = [sb(f"vpt{i}", [128, 8, 65], BF16) for i in range(NKB)]
    qTt = sb("qTt", [128, 4, 128], BF16)
    wq = [sb(f"wq{i}", [128, 8]) for i in range(2)]
    Vs = [sb(f"Vs{i}", [128, 8, 65], BF16) for i in range(2)]
    pT = [sb(f"pT{i}", [128, 2, 4, 128], BF16) for i in range(2)]
    ofb = sb("ofb", [128, 512], BF16)
    rs = sb("rs", [128, 8])
    G_ = lambda fn, reads=(), writes=(): S.op(os.environ.get("KGENG", "vector"), fn, reads, writes)

    qTt2 = [qTt, sb("qTt1", [128, 4, 128], BF16)]

    def attn_qk(b, kq, k_tile, k_res, qt_tile, qt_res, q_cols, v_src, v_res, w_ap, w_res, mask_c, pt_out):
        V(lambda e: e.tensor_tensor(out=Vs[b][:], in0=v_src[:], in1=w_ap[:, :].unsqueeze(2).broadcast_to([128, 8, 65]), op=ALU.mult),
          reads=[v_res, w_res], writes=[f"Vs{b}"])
        nq = q_cols.stop - q_cols.start
        for h in range(8):
            pr, po = h // 2, (h % 2) * 64
            bank = 2 * kq + h % 2
            PE(lambda e, h=h, pr=pr, po=po, bank=bank: e.matmul(ps[bank][:, (h // 2) * nq:(h // 2 + 1) * nq], lhsT=k_tile[po:po + 64, pr, :], rhs=qt_tile[po:po + 64, pr, q_cols],
                                                              start=True, stop=True, skip_group_check=True),
               reads=[k_res, qt_res], writes=[f"ps{bank}"])
        for half in range(2):
            bank = 2 * kq + half
            A(lambda e, half=half, bank=bank: e.activation(out=pt_out(half), in_=ps[bank][:, 0:4 * nq].rearrange("p (h q) -> p h q", h=4), func=AF.Exp),
              reads=[f"ps{bank}"], writes=[f"pT{b}"])
        if mask_c is not None:
            pv = pT[b][:].rearrange("p a b q -> p (a b) q")
            V(lambda e: e.tensor_tensor(out=pv, in0=pv, in1=cstb[:, mask_c, :].unsqueeze(1).broadcast_to([128, 8, 128]), op=ALU.mult), reads=[f"pT{b}", "cstb"], writes=[f"pT{b}"])

    def attn_pv(b, acc0, first, last):
        for h in range(8):
            bank, c0 = acc0 + h // 4, (h % 4) * 65
            PE(lambda e, h=h, bank=bank, c0=c0: e.matmul(ps[bank][:, c0:c0 + 65], lhsT=pT[b][:, h % 2, h // 2, :], rhs=Vs[b][:, h, :],
                                                       start=(first and h % 4 == 0), stop=last, skip_group_check=True),
               reads=[f"pT{b}", f"Vs{b}"], writes=[f"ps{bank}"])

    def attn_finish(row0, acc0):
        for half in range(2):
            bank = acc0 + half
            pv3 = ps[bank][:, 0:260].rearrange("p (h c) -> p h c", c=65)
            V(lambda e, half=half, pv3=pv3: e.reciprocal(out=rs[:, half * 4:(half + 1) * 4].unsqueeze(2), in_=pv3[:, :, 64:65]), reads=[f"ps{bank}"], writes=["rs"])
            V(lambda e, half=half, pv3=pv3: e.tensor_tensor(out=ofb[:, half * 256:(half + 1) * 256].rearrange("p (h d) -> p h d", d=64), in0=pv3[:, :, 0:64],
                                                          in1=rs[:, half * 4:(half + 1) * 4].unsqueeze(2).broadcast_to([128, 4, 64]), op=ALU.mult),
              reads=[f"ps{bank}", "rs"], writes=["ofb"])
        S.dma("sync", lambda e: e.dma_start(out=of_scr[row0:row0 + 128, :], in_=ofb[:]), reads=["ofb"], writes=["of_scr"])

    def run_pipelined(items):
        if not items:
            return
        items[0][0]()
        for i in range(len(items)):
            if i + 1 < len(items):
                items[i + 1][0]()
            items[i][1]()

    items = []
    pairs = [(qt, kt) for qt in range(NPT if 'B' in PH else 0) for kt in range(qt + 1)]
    for i, (qt, kt) in enumerate(pairs):
        b = i % 2; g = i % NKB; qb = qt % 2; acc0 = 4 if qt % 2 == 0 else 6

        def qk(qt=qt, kt=kt, b=b, g=g, qb=qb):
            if kt == 0:
                S.dma("sync", lambda e: e.dma_start(out=qTt2[qb][:].rearrange("p a b -> p (a b)"), in_=qT_scr[qt * 128:(qt + 1) * 128, :]), reads=["qT_scr"], writes=[f"qTt{qb}"])
            S.dma("sync", lambda e: e.dma_start(out=kTt[g][:].rearrange("p a b -> p (a b)"), in_=kT_scr[kt * 128:(kt + 1) * 128, :]), reads=["kT_scr"], writes=[f"kTt{g}"])
            S.dma("sync", lambda e: e.dma_start(out=vpt[g][:].rearrange("p a b -> p (a b)"), in_=vp_scr[kt * 128:(kt + 1) * 128, :]), reads=["vp_scr"], writes=[f"vpt{g}"])
            V(lambda e: e.tensor_tensor(out=wq[b][:], in0=Fend[:, qt, :], in1=Fk[:, kt, :], op=ALU.subtract), reads=["Fend", "Fk"], writes=[f"wq{b}"])
            V(lambda e: e.tensor_scalar(out=wq[b][:], in0=wq[b][:], scalar1=0.0, scalar2=None, op0=ALU.min), reads=[f"wq{b}"], writes=[f"wq{b}"])
            A(lambda e: e.activation(out=wq[b][:], in_=wq[b][:], func=AF.Exp), reads=[f"wq{b}"], writes=[f"wq{b}"])
            attn_qk(b, b, kTt[g], f"kTt{g}", qTt2[qb], f"qTt{qb}", slice(0, 128), vpt[g], f"vpt{g}", wq[b], f"wq{b}",
                    C_TRI if kt == qt else None, lambda half: pT[b][:, half, :, :])

        def pv(qt=qt, kt=kt, b=b, acc0=acc0):
            attn_pv(b, acc0, kt == 0, kt == qt)
            if kt == qt:
                attn_finish(qt * 128, acc0)
        items.append((qk, pv))
    run_pipelined(items)

    def sample_attn():
        pts = sb("pts", [128, NS * 256], I32)
        pidx = pts
        iot = sb("iot", [128, 1], I32)
        NKS = 2
        kpg = [sb(f"kpg{i}", [128, 512], BF16) for i in range(NKS)]
        vpg = [sb(f"vpg{i}", [128, 8, 65], BF16) for i in range(2)]
        vgt = [sb(f"vgt{i}", [128, 512], BF16) for i in range(NKS)]
        lfp = sb("lfp", [128, 16, 8])
        lat = sb("lat", [128, 16, 8])
        Rb = sb("Rb", [128, 16, 8])
        kTp = [sb(f"kTp{i}", [128, 4, 128], BF16) for i in range(2)]
        S.dma("sync", lambda e: e.dma_start(out=pts[:], in_=ptab[0:1, :].partition_broadcast(128)), writes=["pts"])
        S.op("gpsimd", lambda e: e.iota(iot[:], pattern=[[0, 1]], base=0, channel_multiplier=1), writes=["iot"])
        S.op("gpsimd", lambda e: e.tensor_scalar(out=pidx[:], in0=pts[:], scalar1=128, scalar2=iot[:, 0:1], op0=ALU.mult, op1=ALU.add), reads=["pts", "iot"], writes=["pts"])
        for i in range(2):
            V(lambda e, i=i: e.memset(vpg[i][:], 1.0), writes=[f"vpg{i}"])

        def seq_pre(st, q):
            for pg in range(16):
                S.dma("gpsimd", lambda e, pg=pg: e.indirect_dma_start(out=lfp[:, pg, :], out_offset=None, in_=clf,
                                                                     in_offset=bass.IndirectOffsetOnAxis(ap=pidx[:, st * 256 + q * 16 + pg:st * 256 + q * 16 + pg + 1], axis=0)),
                      reads=["pts"], writes=["lfp"])
            V(lambda e: e.memset(lat[:, 15, :], 0.0), writes=["lat"])
            for pg in range(14, -1, -1):
                V(lambda e, pg=pg: e.tensor_tensor(out=lat[:, pg, :], in0=lat[:, pg + 1, :], in1=lfp[:, pg + 1, :], op=ALU.add), reads=["lat", "lfp"], writes=["lat"])
            PE(lambda e: e.matmul(ps[6][:, 0:128], lhsT=cst[:, C_SUP, :], rhs=lfp[:].rearrange("p a b -> p (a b)"), start=True, stop=False), reads=["cst", "lfp"], writes=["ps6"])
            PE(lambda e: e.matmul(ps[6][:, 0:128], lhsT=cst[:, C_ONES, :], rhs=lat[:].rearrange("p a b -> p (a b)"), start=False, stop=True), reads=["cst", "lat"], writes=["ps6"])
            A(lambda e: e.activation(out=Rb[:].rearrange("p a b -> p (a b)"), in_=ps[6][:, 0:128], func=AF.Exp), reads=["ps6"], writes=["Rb"])

        for st in range(NS):
            TS = SMP + st
            items = []

            def qk0(TS=TS):
                S.dma("sync", lambda e: e.dma_start(out=qTt[:].rearrange("p a b -> p (a b)"), in_=qT_scr[TS * 128:(TS + 1) * 128, :]), reads=["qT_scr"], writes=["qTt0"])
                S.dma("sync", lambda e: e.dma_start(out=kTt[0][:].rearrange("p a b -> p (a b)"), in_=kT_scr[TS * 128:(TS + 1) * 128, :]), reads=["kT_scr"], writes=["kTt0"])
                S.dma("sync", lambda e: e.dma_start(out=vpt[0][:].rearrange("p a b -> p (a b)"), in_=vp_scr[TS * 128:(TS + 1) * 128, :]), reads=["vp_scr"], writes=["vpt0"])
                A(lambda e: e.activation(out=wq[0][:], in_=Fk[:, TS, :], func=AF.Exp, scale=-1.0), reads=["Fk"], writes=["wq0"])
                attn_qk(0, 0, kTt[0], "kTt0", qTt, "qTt0", slice(0, 128), vpt[0], "vpt0", wq[0], "wq0", C_TRI16, lambda half: pT[0][:, half, :, :])

            def pv0():
                attn_pv(0, 4, True, False)
                V(lambda e: e.memset(pT[0][:], 0.0), writes=["pT0"])
            items.append((qk0, pv0))
            n = 0
            for q in range(16):
                for pg in range(16):
                    n += 1
                    b = n % 2; g = n % NKS
                    col = st * 256 + q * 16 + pg

                    def qk(st=st, q=q, pg=pg, b=b, g=g, col=col, n=n):
                        if n == 1:
                            V(lambda e: e.memset(pT[1][:], 0.0), writes=["pT1"])
                        if pg == 0:
                            seq_pre(st, q)
                        if q > 0 and pg < 2:
                            V(lambda e: e.memset(pT[b][:].rearrange("p a b q -> p (a b) q")[:, :, (q - 1) * 8:q * 8], 0.0), writes=[f"pT{b}"])
                        S.dma("gpsimd", lambda e: e.indirect_dma_start(out=kpg[g][:], out_offset=None, in_=ck,
                                                                     in_offset=bass.IndirectOffsetOnAxis(ap=pidx[:, col:col + 1], axis=0)),
                              reads=["pts"], writes=[f"kpg{g}"])
                        S.dma("gpsimd", lambda e: e.indirect_dma_start(out=vgt[g][:], out_offset=None, in_=cv_,
                                                                     in_offset=bass.IndirectOffsetOnAxis(ap=pidx[:, col:col + 1], axis=0)),
                              reads=["pts"], writes=[f"vgt{g}"])
                        V(lambda e: e.tensor_copy(out=vpg[b][:, :, 0:64], in_=vgt[g][:].rearrange("p (h d) -> p h d", h=8)), reads=[f"vgt{g}"], writes=[f"vpg{b}"])
                        transpose_to(kTp[b], f"kTp{b}", kpg[g], f"kpg{g}", 4, 0, psi=7)
                        attn_qk(b, b, kTp[b], f"kTp{b}", qTt, "qTt0", slice(q * 8, (q + 1) * 8), vpg[b], f"vpg{b}", Rb[:, pg, :], "Rb", None,
                                lambda half: pT[b][:, half, :, q * 8:(q + 1) * 8])

                    def pv(b=b, q=q, pg=pg):
                        attn_pv(b, 4, False, q == 15 and pg == 15)
                    items.append((qk, pv))
            run_pipelined(items)
            attn_finish(TS * 128, 4)

    if 'S' in PH:
        sample_attn()

    ocat = tb[2][:, :]
    uu = tf[0][:, :]
    vv = tf[1][:, :]
    vvb = tb[0][:, :]
    wsT = sb("wsT", [128, 8, 128], BF16)
    wsS = sb("wsS", [128, 8, 128], BF16)
    bsT = sb("bsT", [128, 8])
    bsS = sb("bsS", [128, 8])
    wsl = sb("wsl", [128, 128])
    bsl = sb("bsl", [8, 128])
    w8 = sb("w8", [8, 8])
    b8 = sb("b8", [8, 128])
    umx = tb[1][:, :]
    gq = tf[2][:, 0:512]
    S.alias["gq"] = "tf2"

    S.dma("sync", lambda e: e.dma_start(out=bsl[:], in_=c_b_s[:, :]), writes=["bsl"])
    PE(lambda e: e.transpose(ps[0][:, 0:8], bsl[:], cst[0:8, C_ID, 0:8]), reads=["bsl", "cst"], writes=["ps0"])
    V(lambda e: e.tensor_copy(out=bsT[:], in_=ps[0][:, 0:8]), reads=["ps0"], writes=["bsT"])
    PE(lambda e: e.matmul(ps[0][:, 8:16], lhsT=cst[0:8, C_R, :], rhs=bsT[0:8, :], start=True, stop=True), reads=["cst", "bsT"], writes=["ps0"])
    V(lambda e: e.tensor_copy(out=bsS[:], in_=ps[0][:, 8:16]), reads=["ps0"], writes=["bsS"])
    for g in range(8):
        S.dma("sync", lambda e, g=g: e.dma_start(out=wsl[:], in_=c_w_s[g * 128:(g + 1) * 128, :]), writes=["wsl"])
        PE(lambda e: e.transpose(ps[1][:, 0:128], wsl[:], cst[:, C_ID, :]), reads=["wsl", "cst"], writes=["ps1"])
        V(lambda e, g=g: e.tensor_tensor(out=wsT[:, g, :], in0=ps[1][:, 0:128], in1=cst[:, C_TRI, :], op=ALU.mult), reads=["ps1", "cst"], writes=["wsT"])
        PE(lambda e: e.matmul(ps[1][0:8, 128:256], lhsT=wsl[0:8, 0:8], rhs=cst[0:8, C_R, :], start=True, stop=True), reads=["wsl", "cst"], writes=["ps1"])
        V(lambda e: e.tensor_copy(out=b8[:], in_=ps[1][0:8, 128:256]), reads=["ps1"], writes=["b8"])
        PE(lambda e: e.matmul(ps[1][:, 256:384], lhsT=cst[0:8, C_R, :], rhs=b8[:], start=True, stop=True), reads=["cst", "b8"], writes=["ps1"])
        V(lambda e, g=g: e.tensor_tensor(out=wsS[:, g, :], in0=ps[1][:, 256:384], in1=cst[:, C_TRI16, :], op=ALU.mult), reads=["ps1", "cst"], writes=["wsS"])

    def phaseC_group(T0, nt, sample):
        for ti in range(nt):
            r0 = (T0 + ti) * 128
            S.dma("sync", lambda e, r0=r0, ti=ti: e.dma_start(out=xg[:, ti, :], in_=x1_scr[r0:r0 + 128, :]), reads=["x1_scr"], writes=[f"xg{ti}"])
            S.dma("sync", lambda e, r0=r0: e.dma_start(out=ocat[:, 0:512], in_=oh_scr[r0:r0 + 128, :]), reads=["oh_scr"], writes=["ocat"])
            S.dma("sync", lambda e, r0=r0: e.dma_start(out=ocat[:, 512:1024], in_=of_scr[r0:r0 + 128, :]), reads=["of_scr"], writes=["ocat"])
            transpose_to(xT, "xT", ocat, "ocat", 8, ti * 128)
        for ti in range(nt):
            for c in range(2):
                proj_tokmajor("w_out", c * 512, 512, ti, 4 + c, c)
                V(lambda e, ti=ti, c=c: e.scalar_tensor_tensor(out=xg[:, ti, c * 512:(c + 1) * 512], in0=xg[:, ti, c * 512:(c + 1) * 512], scalar=ALPHA,
                                                               in1=ps[4 + c][:, :], op0=ALU.mult, op1=ALU.add), reads=[f"ps{4 + c}", f"xg{ti}"], writes=[f"xg{ti}"])
            layer_norm(xg[:, ti, :], f"xg{ti}", xg[:, ti, :], f"xg{ti}", 1)
        ffn(1, nt)
        ffn(2, nt)
        build_xT(nt)
        for ti in range(nt):
            for c in range(4):
                proj_tokmajor("c_w_in", c * 512, 512, ti, c % 2, c % 2)
                dstt = uu if c < 2 else vv
                dres = "uu" if c < 2 else "vv"
                dsl = dstt[:, (c % 2) * 512:(c % 2 + 1) * 512]
                A(lambda e, c=c: e.activation(out=gq[:], in_=ps[c % 2][:, :], func=AF.Square), reads=[f"ps{c % 2}"], writes=["gq"])
                V(lambda e: e.tensor_scalar(out=gq[:], in0=gq[:], scalar1=0.044715, scalar2=1.0, op0=ALU.mult, op1=ALU.add), reads=["gq"], writes=["gq"])
                V(lambda e, c=c: e.tensor_tensor(out=gq[:], in0=gq[:], in1=ps[c % 2][:, :], op=ALU.mult), reads=["gq", f"ps{c % 2}"], writes=["gq"])
                A(lambda e: e.activation(out=gq[:], in_=gq[:], func=AF.Sigmoid, scale=1.5957691216057308), reads=["gq"], writes=["gq"])
                V(lambda e, c=c, dsl=dsl: e.tensor_tensor(out=dsl, in0=gq[:], in1=ps[c % 2][:, :], op=ALU.mult), reads=["gq", f"ps{c % 2}"], writes=[dres])
            layer_norm(vv[:], "vv", vv[:], "vv", 0, g_ap=c_ln_g[0:1, :], b_ap=c_ln_b[0:1, :])
            if sample:
                S.dma("sync", lambda e, ti=ti: e.dma_start(out=cv_o[ti * 128:(ti + 1) * 128, :], in_=vv[:]), reads=["vv"], is_output=True)
            A(lambda e: e.activation(out=vvb[:], in_=vv[:], func=AF.Copy), reads=["vv"], writes=["vvb"])
            wsx = wsS if sample else wsT
            bsx = bsS if sample else bsT
            for g in range(8):
                bank = 4 + g // 4
                cs = slice((g % 4) * 128, (g % 4 + 1) * 128)
                PE(lambda e, g=g, bank=bank, cs=cs, wsx=wsx: e.matmul(ps[bank][:, cs], lhsT=wsx[:, g, :], rhs=vvb[:, g * 128:(g + 1) * 128], start=True, stop=True),
                   reads=["wsT", "wsS", "vvb"], writes=[f"ps{bank}"])
                V(lambda e, g=g, bank=bank, cs=cs, bsx=bsx: e.scalar_tensor_tensor(out=umx[:, g * 128:(g + 1) * 128], in0=ps[bank][:, cs], scalar=bsx[:, g:g + 1],
                                                                                   in1=uu[:, g * 128:(g + 1) * 128], op0=ALU.add, op1=ALU.mult),
                  reads=[f"ps{bank}", "bsT", "bsS", "uu"], writes=["umx"])
            transpose_to(xT, "xT", umx, "umx", 8, ti * 128)
        for ti in range(nt):
            for c in range(2):
                proj_tokmajor("c_w_out", c * 512, 512, ti, 4 + c, c)
                V(lambda e, ti=ti, c=c: e.scalar_tensor_tensor(out=xg[:, ti, c * 512:(c + 1) * 512], in0=xg[:, ti, c * 512:(c + 1) * 512], scalar=ALPHA,
                                                               in1=ps[4 + c][:, :], op0=ALU.mult, op1=ALU.add), reads=[f"ps{4 + c}", f"xg{ti}"], writes=[f"xg{ti}"])
            layer_norm(xg[:, ti, :], f"xg{ti}", xg[:, ti, :], f"xg{ti}", 4)
        ffn(3, nt)
        for ti in range(nt):
            r0 = (T0 + ti) * 128
            S.dma("sync", lambda e, r0=r0, ti=ti: e.dma_start(out=y_o[r0:r0 + 128, :], in_=xg[:, ti, :]), reads=[f"xg{ti}"], is_output=True)

    if 'C' in PH:
        for G in range(NPT // 4):
            phaseC_group(G * 4, 4, False)
        phaseC_group(SMP, NS, True)

    print('sbuf bytes remaining', nc.sbuf_bytes_remaining)
    S.emit()
    print('instr counts', {e: len(S.q[e]) for e in ENGS})
    return nc


_NC_CACHE = {}
NCORES = 2
NS_ = 4


def kernel(x_prompt, x_sample, cache_fox_k, cache_fox_v, cache_fox_logf, state_hg, page_table,
           ln_g, ln_b, ffn_w_gate, ffn_w_up, ffn_w_down, ab_w_in, hg_lb_logits, hg_norm_g, fox_f_bias,
           ab_w_out, c_w_in, c_ln_g, c_ln_b, c_w_s, c_b_s, c_w_out):
    f = lambda a: np.ascontiguousarray(np.asarray(a, dtype=np.float32))
    x_prompt = f(x_prompt); x_sample = f(x_sample)
    P = x_prompt.shape[1]
    nphys = int(os.environ.get('KNPHYS', str(N_PHYS)))
    if nphys != N_PHYS:
        cache_fox_k = np.asarray(cache_fox_k)[:, :nphys]; cache_fox_v = np.asarray(cache_fox_v)[:, :nphys]
        cache_fox_logf = np.asarray(cache_fox_logf)[:, :nphys]; page_table = np.asarray(page_table) % nphys
    if P not in _NC_CACHE:
        _NC_CACHE[P] = build_program(P // 128, nphys, NS_)
    nc = _NC_CACHE[P]
    shared = {
        "cache_k": f(cache_fox_k).reshape(nphys * 128, 512),
        "cache_v": f(cache_fox_v).reshape(nphys * 128, 512),
        "cache_lf": f(cache_fox_logf).reshape(nphys * 128, 8),
        "ln_g": f(ln_g).reshape(6, D), "ln_b": f(ln_b).reshape(6, D),
        "w_gate": f(ffn_w_gate).reshape(4, D, FF), "w_up": f(ffn_w_up).reshape(4, D, FF),
        "w_down": f(ffn_w_down).reshape(4, FF, D),
        "ab_w_in": f(ab_w_in).reshape(D, 3592), "lb_logits": f(hg_lb_logits).reshape(1, 1536),
        "hg_norm_g": f(hg_norm_g).reshape(1, 128), "fox_f_bias": f(fox_f_bias).reshape(1, 8),
        "ab_w_out": f(ab_w_out).reshape(D, D), "c_w_in": f(c_w_in).reshape(D, 2048),
        "c_ln_g": f(c_ln_g).reshape(1, D), "c_ln_b": f(c_ln_b).reshape(1, D),
        "c_w_s": f(c_w_s).reshape(1024, 128), "c_b_s": f(c_b_s).reshape(8, 128),
        "c_w_out": f(c_w_out).reshape(D, D), "cst": make_consts().reshape(128, NCST * 128),
    }
    pt = np.asarray(page_table, dtype=np.int32)
    SQ = 16 * NS_
    in_maps = []
    for c in range(NCORES):
        m = dict(shared)
        m["xin"] = np.concatenate([x_prompt[c], x_sample[SQ * c:SQ * (c + 1)].reshape(SQ * 8, D)], axis=0)
        m["state_hg"] = f(state_hg)[0, SQ * c:SQ * (c + 1)].reshape(SQ * 4 * 128, 128)
        m["ptab"] = np.ascontiguousarray(pt[SQ * c:SQ * (c + 1)].reshape(1, SQ * 16))
        in_maps.append(m)
    ncores = int(os.environ.get('KCORES', str(NCORES)))
    res = run_bass_kernel_spmd(nc, in_maps[:ncores], core_ids=list(range(ncores))).results
    res = list(res) + [res[0]] * (NCORES - ncores)
    R = range(NCORES)
    y_p = np.stack([res[b]["y"][:P] for b in R])
    y_s = np.concatenate([res[c]["y"][P:].reshape(SQ, 8, D) for c in R])
    k_p = np.stack([res[b]["k_new"][:P].reshape(P, 8, 64) for b in R])[None]
    v_p = np.stack([res[b]["v_new"][:P].reshape(P, 8, 64) for b in R])[None]
    lf_p = np.stack([res[b]["lf_new"][:P] for b in R])[None]
    hg_p = np.stack([res[b]["hg_p"].reshape(4, 128, 128) for b in R])[None]
    k_s = np.concatenate([res[c]["k_new"][P:].reshape(SQ, 8, 8, 64) for c in R])[None]
    v_s = np.concatenate([res[c]["v_new"][P:].reshape(SQ, 8, 8, 64) for c in R])[None]
    lf_s = np.concatenate([res[c]["lf_new"][P:].reshape(SQ, 8, 8) for c in R])[None]
    hg_s = np.concatenate([res[c]["hg_s"].reshape(SQ, 4, 128, 128) for c in R])[None]
    cv_s = np.concatenate([res[c]["cv_s"].reshape(SQ, 8, D) for c in R])[None]
    return (y_p, y_s, k_p, v_p, lf_p, hg_p, k_s, v_s, lf_s, hg_s, cv_s)
```

```python
import os
import numpy as np
import concourse.bass as bass
import concourse.mybir as mybir
from concourse.bass_utils import run_bass_kernel_spmd

F32 = mybir.dt.float32
BF16 = mybir.dt.bfloat16
I32 = mybir.dt.int32
AF = mybir.ActivationFunctionType
ALU = mybir.AluOpType

D = 1024
FF = 2816
NJ = 22
NPT = 64
NT = 65
NTOK = NT * 128
ALPHA = 4.0 ** 0.25
EPS = 1e-5
N_PHYS = 2560
ENGS = ("tensor", "vector", "scalar", "gpsimd", "sync")
N_DMA_SEMS = 16
SEM_EPOCH = 16000

C_ID, C_TRI, C_SEL127, C_TRI2, C_BLK2, C_TRI16, C_BLK16, C_SELEND2, C_SELEND16, C_R, C_SUP, C_ONES, C_NEG = range(13)
NCST = 13


def make_consts():
    c = np.zeros((128, NCST, 128), np.float32)
    s = np.arange(128)[:, None]
    t = np.arange(128)[None, :]
    c[:, C_ID] = (s == t)
    c[:, C_TRI] = (s <= t)
    c[:, C_SEL127] = (s == 127) * np.ones((1, 128))
    c[:, C_TRI2] = (s // 64 == t // 64) & (s <= t)
    c[:, C_BLK2] = (s // 64 == t // 64)
    c[:, C_TRI16] = (s // 8 == t // 8) & (s <= t)
    c[:, C_BLK16] = (s // 8 == t // 8)
    c[:, C_SELEND2][:, 0] = (np.arange(128) == 63)
    c[:, C_SELEND2][:, 1] = (np.arange(128) == 127)
    for q in range(16):
        c[q * 8 + 7, C_SELEND16, q] = 1.0
    for i in range(8):
        c[i, C_R, i::8] = 1.0
    c[:, C_SUP] = (s > t)
    c[:, C_ONES] = 1.0
    c[:, C_NEG] = -1.0
    return c


class Sched:
    def __init__(self, nc):
        self.nc = nc
        self.q = {e: [] for e in ENGS}
        self.cnt = {e: 0 for e in ENGS}
        self.epoch = {e: 0 for e in ENGS}
        self.sem = {e: nc.alloc_semaphore(f"s_{e}_0") for e in ENGS}
        self.dsem = {e: [nc.alloc_semaphore(f"d_{e}_{i}") for i in range(N_DMA_SEMS)]
                     for e in ("sync", "scalar", "gpsimd")}
        self.dcnt = {e: [0] * N_DMA_SEMS for e in ("sync", "scalar", "gpsimd")}
        self.dnext = {e: 0 for e in ("sync", "scalar", "gpsimd")}
        self.known = {e: {} for e in ENGS}
        self.lastw = {}
        self.reads = {}
        self.semobj = {}
        for e in ENGS:
            self.semobj[("e", e, 0)] = self.sem[e]
        for e in self.dsem:
            for i, s in enumerate(self.dsem[e]):
                self.semobj[("d", e, i)] = s
        self.out_deps = []
        self.alias = {}

    def _need(self, eng, deps):
        need = {}
        for (k, v) in deps:
            if self.known[eng].get(k, 0) >= v:
                continue
            if need.get(k, 0) < v:
                need[k] = v
        for k, v in need.items():
            self.known[eng][k] = v
        return list(need.items())

    def _deps(self, reads, writes):
        reads = [self.alias.get(r, r) for r in reads]
        writes = [self.alias.get(w, w) for w in writes]
        deps = []
        for r in reads:
            if r in self.lastw:
                deps.append(self.lastw[r])
        for w in writes:
            if w in self.lastw:
                deps.append(self.lastw[w])
            deps.extend(self.reads.get(w, ()))
        return deps

    def _commit(self, tok, reads, writes):
        reads = [self.alias.get(r, r) for r in reads]
        writes = [self.alias.get(w, w) for w in writes]
        for r in reads:
            self.reads.setdefault(r, []).append(tok)
        for w in writes:
            self.lastw[w] = tok
            self.reads[w] = []

    def op(self, eng, fn, reads=(), writes=()):
        pr = [r for r in reads if r.startswith("ps")]
        if pr:
            reads = [r for r in reads if not r.startswith("ps")]
            writes = list(writes) + pr
        deps = self._deps(reads, writes)
        waits = self._need(eng, deps)
        if self.cnt[eng] >= SEM_EPOCH:
            self.epoch[eng] += 1
            self.cnt[eng] = 0
            self.sem[eng] = self.nc.alloc_semaphore(f"s_{eng}_{self.epoch[eng]}")
            self.semobj[("e", eng, self.epoch[eng])] = self.sem[eng]
        self.cnt[eng] += 1
        key = ("e", eng, self.epoch[eng])
        tok = (key, self.cnt[eng])
        if eng == "tensor":
            self.known[eng][key] = self.cnt[eng]
        self.q[eng].append((waits, fn, (self.sem[eng], 1)))
        self._commit(tok, reads, writes)
        return tok

    def dma(self, eng, fn, reads=(), writes=(), is_output=False):
        deps = self._deps(reads, writes)
        i = self.dnext[eng]
        self.dnext[eng] = (i + 1) % N_DMA_SEMS
        key = ("d", eng, i)
        if self.dcnt[eng][i] > 0:
            deps.append((key, self.dcnt[eng][i]))
        waits = self._need(eng, deps)
        self.dcnt[eng][i] += 16
        tok = (key, self.dcnt[eng][i])
        self.q[eng].append((waits, fn, (self.dsem[eng][i], 16)))
        self._commit(tok, reads, writes)
        if is_output:
            self.out_deps.append(tok)
        return tok

    def emit(self):
        nc = self.nc
        fin = list(self.out_deps)
        for r, t in self.lastw.items():
            fin.append(t)
        waits = self._need("sync", fin)
        self.q["sync"].append((waits, None, None))
        with nc.Block() as block:
            def run(engname):
                def body(eng):
                    for waits, fn, inc in self.q[engname]:
                        for k, v in waits:
                            eng.wait_ge(self.semobj[k], v)
                        if fn is not None:
                            ins = fn(eng)
                            ins.then_inc(inc[0], inc[1])
                return body
            block.tensor(run("tensor"))
            block.vector(run("vector"))
            block.scalar(run("scalar"))
            block.gpsimd(run("gpsimd"))
            block.sync(run("sync"))


def build_program(NPT=NPT, N_PHYS=N_PHYS, NS=4):
    NT = NPT + NS
    NTOK = NT * 128
    SMP = NPT
    nc = bass.Bass("TRN2", target_bir_lowering=False)
    S = Sched(nc)

    def din(name, shape, dt=F32):
        return nc.dram_tensor(name, list(shape), dt, kind="ExternalInput").ap()

    def dout(name, shape, dt=F32):
        return nc.dram_tensor(name, list(shape), dt, kind="ExternalOutput").ap()

    def dscr(name, shape, dt):
        return nc.dram_tensor(name, list(shape), dt).ap()

    def sb(name, shape, dt=F32):
        return nc.alloc_sbuf_tensor(name, list(shape), dt)

    xin = din("xin", [NTOK, D])
    ck = din("cache_k", [N_PHYS * 128, 512])
    cv_ = din("cache_v", [N_PHYS * 128, 512])
    clf = din("cache_lf", [N_PHYS * 128, 8])
    st_hg = din("state_hg", [NS * 16 * 4 * 128, 128])
    ptab = din("ptab", [1, NS * 256], I32)
    ln_g = din("ln_g", [6, D])
    ln_b = din("ln_b", [6, D])
    wgate = din("w_gate", [4, D, FF])
    wup = din("w_up", [4, D, FF])
    wdown = din("w_down", [4, FF, D])
    w_in = din("ab_w_in", [D, 3592])
    lb_log = din("lb_logits", [1, 3 * 512])
    hg_ng = din("hg_norm_g", [1, 128])
    fbias = din("fox_f_bias", [1, 8])
    w_out = din("ab_w_out", [D, D])
    c_w_in = din("c_w_in", [D, 2048])
    c_ln_g = din("c_ln_g", [1, D])
    c_ln_b = din("c_ln_b", [1, D])
    c_w_s = din("c_w_s", [8 * 128, 128])
    c_b_s = din("c_b_s", [8, 128])
    c_w_out = din("c_w_out", [D, D])
    cst_d = din("cst", [128, NCST * 128])

    y_o = dout("y", [NTOK, D])
    k_o = dout("k_new", [NTOK, 512])
    v_o = dout("v_new", [NTOK, 512])
    lf_o = dout("lf_new", [NTOK, 8])
    hgp_o = dout("hg_p", [4 * 128, 128])
    hgs_o = dout("hg_s", [NS * 16 * 4 * 128, 128])
    cv_o = dout("cv_s", [NS * 128, D])

    wg_scr = dscr("wg_scr", [4 * NJ * 128, 1024], BF16)
    wu_scr = dscr("wu_scr", [4 * NJ * 128, 1024], BF16)
    wd_scr = dscr("wd_scr", [4 * 4 * 128, NJ * 256], BF16)
    win_scr = dscr("win_scr", [128, 8 * 3592], BF16)
    wout_scr = dscr("wout_scr", [128, 8 * D], BF16)
    cwin_scr = dscr("cwin_scr", [128, 8 * 2048], BF16)
    cwout_scr = dscr("cwout_scr", [128, 8 * D], BF16)
    x1_scr = dscr("x1_scr", [NTOK, D], F32)
    oh_scr = dscr("oh_scr", [NTOK, 512], BF16)
    of_scr = dscr("of_scr", [NTOK, 512], BF16)
    qT_scr = dscr("qT_scr", [NT * 128, 512], BF16)
    kT_scr = dscr("kT_scr", [NT * 128, 512], BF16)
    vp_scr = dscr("vp_scr", [NT * 128, 8 * 65], BF16)

    cst = sb("cstt", [128, NCST, 128])
    cstb = sb("cstb", [128, NCST, 128], BF16)
    identb = cstb[:, C_ID, :]
    ps = [nc.alloc_psum_tensor(f"ps{i}", [128, 512], F32) for i in range(8)]

    def psb(i):
        return ps[i][:].bitcast(BF16)

    xg = sb("xg", [128, 4, D])
    xT = sb("xT", [128, 8, 512], BF16)
    hT = sb("hT", [128, NJ, 512], BF16)
    wgs = [sb(f"wgs{i}", [128, 8, 128], BF16) for i in range(2)]
    wus = [sb(f"wus{i}", [128, 8, 128], BF16) for i in range(2)]
    wds = sb("wds", [128, NJ, 256], BF16)
    wps = [sb(f"wps{i}", [128, 8, 512], BF16) for i in range(2)]
    gb = [sb(f"gb{i}", [128, D]) for i in range(2)]
    zt = sb("zt", [128, D])
    tf = [sb(f"tf{i}", [128, D]) for i in range(4)]
    tb = [sb(f"tb{i}", [128, D], BF16) for i in range(4)]
    S.alias.update({"qh": "tf0", "kk": "tf0", "gl": "tf1", "gs": "tf1", "bcs": "tf2", "ebl": "tf2", "fk_f": "tf3", "fv_f": "tf3",
                    "vh": "tb0", "qtl": "tb0", "ktl": "tb1", "khat": "tb1", "ohb": "tb2", "fq_b": "tb2", "fk_b": "tb3",
                    "uu": "tf0", "vv": "tf1", "vvb": "tb0", "umx": "tb1", "ocat": "tb2", "tmpA": "tf2"})
    xb16 = sb("xb16", [128, D], BF16)
    sg_t = sb("sg_t", [128, 512], BF16)
    stats = sb("stats", [128, 2, 6])
    mv = sb("mv", [128, 2])
    rstd = sb("rstd", [128, 1])

    S.dma("sync", lambda e: e.dma_start(out=cst[:].rearrange("p a b -> p (a b)"), in_=cst_d[:, :]), writes=["cst"])
    S.op("vector", lambda e: e.tensor_copy(out=cstb[:], in_=cst[:]), reads=["cst"], writes=["cstb"])

    def V(fn, reads=(), writes=()):
        return S.op("vector", fn, reads, writes)

    def A(fn, reads=(), writes=()):
        return S.op("scalar", fn, reads, writes)

    def PE(fn, reads=(), writes=()):
        return S.op("tensor", fn, reads, writes)

    def bcast_load(dst, src_row, res):
        S.dma("sync", lambda e: e.dma_start(out=dst, in_=src_row.partition_broadcast(128)), writes=[res])

    def transpose_to(dstT, dst_res, src_bf, src_res, nk, col0, psi=7):
        pb = psb(psi)
        for k in range(nk):
            PE(lambda e, k=k: e.transpose(pb[:, k * 128:(k + 1) * 128], src_bf[:, k * 128:(k + 1) * 128], identb),
               reads=[src_res, "cstb"], writes=[f"ps{psi}"])
        V(lambda e: e.tensor_copy(out=dstT[:, 0:nk, col0:col0 + 128],
                                  in_=pb[:, 0:nk * 128].rearrange("p (k t) -> p k t", k=nk)),
          reads=[f"ps{psi}"], writes=[dst_res])

    def layer_norm(src, src_res, dst, dst_res, gi, g_ap=None, b_ap=None):
        ga = ln_g[gi:gi + 1, :] if g_ap is None else g_ap
        ba = ln_b[gi:gi + 1, :] if b_ap is None else b_ap
        bcast_load(gb[0][:], ga, "gb0")
        bcast_load(gb[1][:], ba, "gb1")
        for h in range(2):
            V(lambda e, h=h: e.bn_stats(out=stats[:, h, :], in_=src[:, h * 512:(h + 1) * 512]),
              reads=[src_res], writes=["stats"] if h == 0 else ["stats"])
        V(lambda e: e.bn_aggr(out=mv[:], in_=stats[:].rearrange("p a b -> p (a b)")), reads=["stats"], writes=["mv"])
        V(lambda e: e.tensor_scalar(out=rstd[:], in0=mv[:, 1:2], scalar1=EPS, scalar2=None, op0=ALU.add), reads=["mv"], writes=["rstd"])
        A(lambda e: e.activation(out=rstd[:], in_=rstd[:], func=AF.Sqrt), reads=["rstd"], writes=["rstd"])
        V(lambda e: e.reciprocal(out=rstd[:], in_=rstd[:]), reads=["rstd"], writes=["rstd"])
        V(lambda e: e.tensor_scalar(out=dst, in0=src, scalar1=mv[:, 0:1], scalar2=rstd[:, 0:1], op0=ALU.subtract, op1=ALU.mult),
          reads=[src_res, "mv", "rstd"], writes=[dst_res])
        V(lambda e: e.tensor_tensor(out=dst, in0=dst, in1=gb[0][:], op=ALU.mult), reads=[dst_res, "gb0"], writes=[dst_res])
        V(lambda e: e.tensor_tensor(out=dst, in0=dst, in1=gb[1][:], op=ALU.add), reads=[dst_res, "gb1"], writes=[dst_res])

    def build_xT(nt):
        for ti in range(nt):
            A(lambda e, ti=ti: e.activation(out=xb16[:], in_=xg[:, ti, :], func=AF.Copy), reads=[f"xg{ti}"], writes=["xb16"])
            transpose_to(xT, "xT", xb16, "xb16", 8, ti * 128)

    def ffn(fi, nt):
        ntk = nt * 128
        build_xT(nt)
        for j in range(NJ):
            b = j % 2
            rj = (fi * NJ + j) * 128
            S.dma("sync", lambda e, rj=rj, b=b: e.dma_start(out=wgs[b][:].rearrange("p k m -> p (k m)"), in_=wg_scr[rj:rj + 128, :]),
                  reads=[f"S_wg{fi}"], writes=[f"wgs{b}"])
            S.dma("sync", lambda e, rj=rj, b=b: e.dma_start(out=wus[b][:].rearrange("p k m -> p (k m)"), in_=wu_scr[rj:rj + 128, :]),
                  reads=[f"S_wu{fi}"], writes=[f"wus{b}"])
            pg, pu = ps[2 * b], ps[2 * b + 1]
            for k in range(8):
                PE(lambda e, k=k, b=b, pg=pg: e.matmul(pg[:, 0:ntk], lhsT=wgs[b][:, k, :], rhs=xT[:, k, 0:ntk], start=(k == 0), stop=(k == 7)),
                   reads=[f"wgs{b}", "xT"], writes=[f"ps{2 * b}"])
            for k in range(8):
                PE(lambda e, k=k, b=b, pu=pu: e.matmul(pu[:, 0:ntk], lhsT=wus[b][:, k, :], rhs=xT[:, k, 0:ntk], start=(k == 0), stop=(k == 7)),
                   reads=[f"wus{b}", "xT"], writes=[f"ps{2 * b + 1}"])
            A(lambda e, pg=pg: e.activation(out=sg_t[:, 0:ntk], in_=pg[:, 0:ntk], func=AF.Silu), reads=[f"ps{2 * b}"], writes=["sg_t"])
            V(lambda e, j=j, pu=pu: e.tensor_tensor(out=hT[:, j, 0:ntk], in0=sg_t[:, 0:ntk], in1=pu[:, 0:ntk], op=ALU.mult),
              reads=["sg_t", f"ps{2 * b + 1}"], writes=["hT"])
        li = (fi // 2) * 3 + (0 if fi % 2 == 0 else 2)
        for c in range(4):
            cs = slice(c * 256, (c + 1) * 256)
            rq = (fi * 4 + c) * 128
            S.dma("sync", lambda e, rq=rq: e.dma_start(out=wds[:].rearrange("p j n -> p (j n)"), in_=wd_scr[rq:rq + 128, :]), reads=[f"S_wd{fi}"], writes=["wds"])
            for ti in range(nt):
                pd = ps[4 + (ti % 2)]
                for j in range(NJ):
                    PE(lambda e, j=j, ti=ti, pd=pd: e.matmul(pd[:, 0:256], lhsT=hT[:, j, ti * 128:(ti + 1) * 128], rhs=wds[:, j, :], start=(j == 0), stop=(j == NJ - 1)),
                       reads=["hT", "wds"], writes=[f"ps{4 + ti % 2}"])
                A(lambda e, pd=pd, cs=cs: e.activation(out=zt[:, cs], in_=pd[:, 0:256], func=AF.Copy, scale=0.5),
                  reads=[f"ps{4 + ti % 2}"], writes=["zt"])
                V(lambda e, ti=ti, cs=cs: e.scalar_tensor_tensor(out=xg[:, ti, cs], in0=xg[:, ti, cs], scalar=ALPHA,
                                                                in1=zt[:, cs], op0=ALU.mult, op1=ALU.add),
                  reads=["zt", f"xg{ti}"], writes=[f"xg{ti}"])
        for ti in range(nt):
            layer_norm(xg[:, ti, :], f"xg{ti}", xg[:, ti, :], f"xg{ti}", li)

    def proj_tokmajor(w_ap, c0, ncols, ti, psi, wb):
        w3, wres = WSCR[w_ap]
        S.dma("sync", lambda e: e.dma_start(out=wps[wb][:, :, 0:ncols], in_=w3[:, :, c0:c0 + ncols]), reads=[wres], writes=[f"wps{wb}"])
        for k in range(8):
            PE(lambda e, k=k: e.matmul(ps[psi][:, 0:ncols], lhsT=xT[:, k, ti * 128:(ti + 1) * 128], rhs=wps[wb][:, k, 0:ncols], start=(k == 0), stop=(k == 7)),
               reads=["xT", f"wps{wb}"], writes=[f"ps{psi}"])

    stg = [sb(f"stg{i}", [128, 8, 256], BF16) for i in range(2)]
    stg_i = [0]

    def conv_cols(src, ncols, dst3, res):
        for c0 in range(0, ncols, 256):
            w = min(256, ncols - c0)
            b = stg_i[0] % 2; stg_i[0] += 1
            S.dma("gpsimd", lambda e, b=b, c0=c0, w=w: e.dma_start(out=stg[b][:, :, 0:w], in_=src[:, c0:c0 + w].rearrange("(k p) n -> p k n", p=128)), writes=[f"stg{b}"])
            S.dma("gpsimd", lambda e, b=b, c0=c0, w=w: e.dma_start(out=dst3[:, :, c0:c0 + w], in_=stg[b][:, :, 0:w]), reads=[f"stg{b}"], writes=[res])

    def conv_gu(src, scr, fi, res):
        for jj in range(NJ // 2):
            b = stg_i[0] % 2; stg_i[0] += 1
            S.dma("gpsimd", lambda e, b=b, jj=jj: e.dma_start(out=stg[b][:], in_=src[fi, :, jj * 256:(jj + 1) * 256].rearrange("(k p) n -> p k n", p=128)), writes=[f"stg{b}"])
            for c in range(2):
                r0 = (fi * NJ + jj * 2 + c) * 128
                S.dma("gpsimd", lambda e, b=b, c=c, r0=r0: e.dma_start(out=scr[r0:r0 + 128, :].rearrange("p (k m) -> p k m", k=8), in_=stg[b][:, :, c * 128:(c + 1) * 128]),
                      reads=[f"stg{b}"], writes=[res])

    def conv_down(fi, res):
        for q in range(4):
            r0 = (fi * 4 + q) * 128
            for (j0, nj) in ((0, 8), (8, 8), (16, 6)):
                b = stg_i[0] % 2; stg_i[0] += 1
                S.dma("gpsimd", lambda e, b=b, q=q, j0=j0, nj=nj: e.dma_start(
                    out=stg[b][:, 0:nj, :], in_=wdown[fi, j0 * 128:(j0 + nj) * 128, q * 256:(q + 1) * 256].rearrange("(j p) n -> p j n", p=128)), writes=[f"stg{b}"])
                S.dma("gpsimd", lambda e, b=b, r0=r0, j0=j0, nj=nj: e.dma_start(
                    out=wd_scr[r0:r0 + 128, :].rearrange("p (j n) -> p j n", j=NJ)[:, j0:j0 + nj, :], in_=stg[b][:, 0:nj, :]), reads=[f"stg{b}"], writes=[res])

    def conv_ffn(fi):
        conv_gu(wgate, wg_scr, fi, f"S_wg{fi}")
        conv_gu(wup, wu_scr, fi, f"S_wu{fi}")
        conv_down(fi, f"S_wd{fi}")

    win3 = win_scr.rearrange("p (k n) -> p k n", k=8)
    wout3 = wout_scr.rearrange("p (k n) -> p k n", k=8)
    cwin3 = cwin_scr.rearrange("p (k n) -> p k n", k=8)
    cwout3 = cwout_scr.rearrange("p (k n) -> p k n", k=8)
    conv_ffn(0)
    conv_cols(w_in, 3592, win3, "S_win")
    conv_cols(w_out, D, wout3, "S_wout")
    conv_ffn(1)
    conv_ffn(2)
    conv_cols(c_w_in, 2048, cwin3, "S_cwin")
    conv_cols(c_w_out, D, cwout3, "S_cwout")
    conv_ffn(3)
    WSCR = {"w_in": (win3, "S_win"), "w_out": (wout3, "S_wout"), "c_w_in": (cwin3, "S_cwin"), "c_w_out": (cwout3, "S_cwout")}

    lbt = [tf[0][:, 0:512], tf[0][:, 512:1024], tf[1][:, 0:512]]
    LR = ["tf0", "tf1"]
    lb = sb("lb", [128, 512])
    oml = sb("oml", [128, 512])
    tmpA = tf[2][:, 0:512]
    ng_bc = sb("ng_bc", [128, 128])
    fb_bc = sb("fb_bc", [128, 8])
    for a in range(3):
        S.dma("sync", lambda e, a=a: e.dma_start(out=lbt[a], in_=lb_log[0:1, a * 512:(a + 1) * 512].partition_broadcast(128)), writes=LR)
    bcast_load(ng_bc[:], hg_ng[0:1, :], "ng_bc")
    bcast_load(fb_bc[:], fbias[0:1, :], "fb_bc")
    V(lambda e: e.tensor_tensor(out=tmpA, in0=lbt[0], in1=lbt[1], op=ALU.max), reads=LR, writes=["tmpA"])
    V(lambda e: e.tensor_tensor(out=tmpA, in0=tmpA, in1=lbt[2], op=ALU.max), reads=LR + ["tmpA"], writes=["tmpA"])
    for a in range(3):
        V(lambda e, a=a: e.tensor_tensor(out=lbt[a], in0=lbt[a], in1=tmpA, op=ALU.subtract), reads=LR + ["tmpA"], writes=LR)
        A(lambda e, a=a: e.activation(out=lbt[a], in_=lbt[a], func=AF.Exp), reads=LR, writes=LR)
    V(lambda e: e.tensor_tensor(out=tmpA, in0=lbt[0], in1=lbt[1], op=ALU.add), reads=LR, writes=["tmpA"])
    V(lambda e: e.tensor_tensor(out=tmpA, in0=tmpA, in1=lbt[2], op=ALU.add), reads=LR + ["tmpA"], writes=["tmpA"])
    V(lambda e: e.reciprocal(out=tmpA, in_=tmpA), reads=["tmpA"], writes=["tmpA"])
    V(lambda e: e.tensor_tensor(out=lb[:], in0=lbt[0], in1=tmpA, op=ALU.mult), reads=LR + ["tmpA"], writes=["lb"])
    V(lambda e: e.tensor_scalar(out=oml[:], in0=lb[:], scalar1=-1.0, scalar2=1.0, op0=ALU.mult, op1=ALU.add), reads=["lb"], writes=["oml"])

    Fk = sb("Fk", [128, NT, 8])
    Fend = sb("Fend", [128, NT, 8])
    Sst = [sb(f"Sst{h}", [128, 128]) for h in range(4)]
    Sb = [sb(f"Sb{h}", [128, 128], BF16) for h in range(4)]
    qh = tf[0][:, 0:512]
    kk = tf[0][:, 512:1024]
    gl = tf[1][:, 0:512]
    vh = tb[0][:, 0:512]
    gs = tf[1][:, 512:1024]
    bcs = tf[2][:, 0:512]
    ebl = tf[2][:, 512:1024]
    qtl = tb[0][:, 512:1024]
    ktl = tb[1][:, 0:512]
    khat = tb[1][:, 512:1024]
    hbuf = [[sb(f"{nm}_{p}", [128, 128], BF16) for nm in ("qtT", "ktT", "attT", "qtT0", "qtT1")] for p in range(2)]
    dcol = sb("dcol", [128, 16])
    ohb = tb[2][:, 0:512]
    ssq = sb("ssq", [128, 4])
    junk = sb("junk", [128, 128])
    fq_b = tb[2][:, 512:1024]
    fk_f = tf[3][:, 0:512]
    fk_b = tb[3][:, 0:512]
    fv_f = tf[3][:, 512:1024]
    vp = sb("vp", [128, 8, 65], BF16)
    lft = sb("lft", [128, 8])
    tq = sb("tq", [128, 4, 128], BF16)
    S0f = sb("S0f", [128, 16, 128])
    S0b = sb("S0b", [128, 16, 128], BF16)
    qz = sb("qz", [128, 16, 128], BF16)
    vbd = sb("vbd", [128, 16, 128], BF16)

    for p in range(2):
        V(lambda e, p=p: e.memset(hbuf[p][3][:], 0.0), writes=[f"qtT0{p}"])
        V(lambda e, p=p: e.memset(hbuf[p][4][:], 0.0), writes=[f"qtT1{p}"])
    V(lambda e: e.memset(qz[:], 0.0), writes=["qz"])
    V(lambda e: e.memset(vp[:], 1.0), writes=["vp"])
    for h in range(4):
        V(lambda e, h=h: e.memset(Sst[h][:], 0.0), writes=[f"Sst{h}"])
        V(lambda e, h=h: e.memset(Sb[h][:], 0.0), writes=[f"Sb{h}"])

    KSTAGE = int(os.environ.get('KSTAGE', '9'))
    dcs = sb("dcs", [128, 64])

    def phaseA_tile(T, ti, sample):
        r0 = T * 128
        tri = C_TRI16 if sample else C_TRI2
        blk = C_BLK16 if sample else C_BLK2
        proj_tokmajor("w_in", 0, 512, ti, 0, 0)
        A(lambda e: e.activation(out=qh[:], in_=ps[0][:], func=AF.Silu), reads=["ps0"], writes=["qh"])
        proj_tokmajor("w_in", 512, 512, ti, 1, 1)
        A(lambda e: e.activation(out=kk[:], in_=ps[1][:], func=AF.Sigmoid), reads=["ps1"], writes=["kk"])
        V(lambda e: e.tensor_tensor(out=gl[:], in0=kk[:], in1=oml[:], op=ALU.mult), reads=["kk", "oml"], writes=["gl"])
        V(lambda e: e.tensor_tensor(out=gl[:], in0=gl[:], in1=lb[:], op=ALU.add), reads=["gl", "lb"], writes=["gl"])
        V(lambda e: e.tensor_scalar(out=kk[:], in0=gl[:], scalar1=-1.0, scalar2=1.0, op0=ALU.mult, op1=ALU.add), reads=["gl"], writes=["kk"])
        A(lambda e: e.activation(out=gl[:], in_=gl[:], func=AF.Ln), reads=["gl"], writes=["gl"])
        proj_tokmajor("w_in", 1024, 512, ti, 0, 0)
        A(lambda e: e.activation(out=vh[:], in_=ps[0][:], func=AF.Copy), reads=["ps0"], writes=["vh"])
        proj_tokmajor("w_in", 1536, 512, ti, 1, 1)
        A(lambda e: e.activation(out=gs[:], in_=ps[1][:], func=AF.Silu), reads=["ps1"], writes=["gs"])
        gs3 = gs.rearrange("p (h v) -> p h v", h=4)
        V(lambda e: e.tensor_tensor(out=gs3, in0=gs3, in1=ng_bc[:].unsqueeze(1).broadcast_to([128, 4, 128]), op=ALU.mult), reads=["gs", "ng_bc"], writes=["gs"])
        if KSTAGE < 2:
            return
        PE(lambda e: e.matmul(ps[2][:], lhsT=cst[:, tri, :], rhs=gl[:], start=True, stop=True), reads=["cst", "gl"], writes=["ps2"])
        PE(lambda e: e.matmul(ps[3][:], lhsT=cst[:, blk, :], rhs=gl[:], start=True, stop=True), reads=["cst", "gl"], writes=["ps3"])
        A(lambda e: e.activation(out=bcs[:], in_=ps[2][:], func=AF.Exp), reads=["ps2"], writes=["bcs"])
        V(lambda e: e.tensor_tensor(out=qtl[:], in0=qh[:], in1=bcs[:], op=ALU.mult), reads=["qh", "bcs"], writes=["qtl"])
        A(lambda e: e.activation(out=bcs[:], in_=ps[2][:], func=AF.Exp, scale=-1.0), reads=["ps2"], writes=["bcs"])
        V(lambda e: e.tensor_tensor(out=ktl[:], in0=kk[:], in1=bcs[:], op=ALU.mult), reads=["kk", "bcs"], writes=["ktl"])
        A(lambda e: e.activation(out=ebl[:], in_=ps[3][:], func=AF.Exp), reads=["ps3"], writes=["ebl"])
        V(lambda e: e.tensor_tensor(out=bcs[:], in0=bcs[:], in1=ebl[:], op=ALU.mult), reads=["bcs", "ebl"], writes=["bcs"])
        V(lambda e: e.tensor_tensor(out=khat[:], in0=kk[:], in1=bcs[:], op=ALU.mult), reads=["kk", "bcs"], writes=["khat"])
        nb = 16 if sample else 2
        selc = C_SELEND16 if sample else C_SELEND2
        for h in range(4):
            PE(lambda e, h=h: e.matmul(ps[3][:, 256 + h * 16:256 + h * 16 + nb], lhsT=ebl[:, h * 128:(h + 1) * 128], rhs=cst[:, selc, 0:nb], start=True, stop=True),
               reads=["ebl", "cst"], writes=["ps3"])
        if sample:
            sq0 = (T - SMP) * 16
            V(lambda e: e.tensor_copy(out=dcs[:], in_=ps[3][:, 256:320]), reads=["ps3"], writes=["dcs"])
        else:
            V(lambda e: e.tensor_copy(out=dcol[:].rearrange("p (h c) -> p h c", h=4)[:, :, 0:2],
                                      in_=ps[3][:, 256:320].rearrange("p (h c) -> p h c", h=4)[:, :, 0:2]), reads=["ps3"], writes=["dcol"])
        if KSTAGE < 3:
            return
        def do_head(h):
            hs = slice(h * 128, (h + 1) * 128)
            par = h % 2
            bo, bs_, bt = (5, 6, 7) if par == 0 else (1, 2, 0)
            pso, pss = ps[bo], ps[bs_]
            qtT, ktT, attT, qtT0, qtT1 = hbuf[par]
            RqtT, RktT, RattT, RqtT0, RqtT1 = [f"{nm}{par}" for nm in ("qtT", "ktT", "attT", "qtT0", "qtT1")]
            Rpo, Rps, Rpt = f"ps{bo}", f"ps{bs_}", f"ps{bt}"
            pb = psb(bt)
            PE(lambda e, hs=hs: e.transpose(pb[:, 0:128], qtl[:, hs], identb), reads=["qtl", "cstb"], writes=[Rpt])
            PE(lambda e, hs=hs: e.transpose(pb[:, 128:256], ktl[:, hs], identb), reads=["ktl", "cstb"], writes=[Rpt])
            V(lambda e: e.tensor_copy(out=qtT[:], in_=pb[:, 0:128]), reads=[Rpt], writes=[RqtT])
            A(lambda e: e.activation(out=ktT[:], in_=pb[:, 128:256], func=AF.Copy), reads=[Rpt], writes=[RktT])
            yield
            PE(lambda e: e.matmul(pss[:, 0:128], lhsT=ktT[:], rhs=qtT[:], start=True, stop=True), reads=[RktT, RqtT], writes=[Rps])
            V(lambda e: e.tensor_tensor(out=attT[:], in0=pss[:, 0:128], in1=cst[:, tri, :], op=ALU.mult), reads=[Rps, "cst"], writes=[RattT])
            if not sample:
                V(lambda e: e.tensor_copy(out=qtT0[:, 0:64], in_=qtT[:, 0:64]), reads=[RqtT], writes=[RqtT0])
                V(lambda e: e.tensor_copy(out=qtT1[:, 64:128], in_=qtT[:, 64:128]), reads=[RqtT], writes=[RqtT1])
                yield
                PE(lambda e, hs=hs: e.matmul(pss[:, 128:256], lhsT=khat[0:64, hs], rhs=vh[0:64, hs], start=True, stop=True),
                   reads=["khat", "vh"], writes=[Rps])
                PE(lambda e, hs=hs: e.matmul(pso[:, 0:128], lhsT=attT[:], rhs=vh[:, hs], start=True, stop=False), reads=[RattT, "vh"], writes=[Rpo])
                PE(lambda e, h=h: e.matmul(pso[:, 0:128], lhsT=qtT0[:], rhs=Sb[h][:], start=False, stop=False), reads=[RqtT0, f"Sb{h}"], writes=[Rpo])
                V(lambda e, h=h: e.scalar_tensor_tensor(out=Sst[h][:], in0=Sst[h][:], scalar=dcol[:, h * 4:h * 4 + 1], in1=pss[:, 128:256], op0=ALU.mult, op1=ALU.add),
                  reads=[f"Sst{h}", "dcol", Rps], writes=[f"Sst{h}"])
                A(lambda e, h=h: e.activation(out=Sb[h][:], in_=Sst[h][:], func=AF.Copy), reads=[f"Sst{h}"], writes=[f"Sb{h}"])
                yield
                PE(lambda e, h=h: e.matmul(pso[:, 0:128], lhsT=qtT1[:], rhs=Sb[h][:], start=False, stop=True), reads=[RqtT1, f"Sb{h}"], writes=[Rpo])
                PE(lambda e, hs=hs: e.matmul(pss[:, 256:384], lhsT=khat[64:128, hs], rhs=vh[64:128, hs], start=True, stop=True),
                   reads=["khat", "vh"], writes=[Rps])
                V(lambda e, h=h: e.scalar_tensor_tensor(out=Sst[h][:], in0=Sst[h][:], scalar=dcol[:, h * 4 + 1:h * 4 + 2], in1=pss[:, 256:384], op0=ALU.mult, op1=ALU.add),
                  reads=[f"Sst{h}", "dcol", Rps], writes=[f"Sst{h}"])
                A(lambda e, h=h: e.activation(out=Sb[h][:], in_=Sst[h][:], func=AF.Copy), reads=[f"Sst{h}"], writes=[f"Sb{h}"])
            else:
                S.dma("sync", lambda e, h=h: e.dma_start(out=S0f[:], in_=st_hg.rearrange("(q h p) v -> h p q v", h=4, p=128)[h][:, sq0:sq0 + 16, :]), writes=["S0f"])
                A(lambda e: e.activation(out=S0b[:], in_=S0f[:], func=AF.Copy), reads=["S0f"], writes=["S0b"])
                for q in range(16):
                    V(lambda e, q=q: e.tensor_copy(out=qz[:, q, q * 8:(q + 1) * 8], in_=qtT[:, q * 8:(q + 1) * 8]), reads=[RqtT], writes=["qz"])
                yield
                PE(lambda e, hs=hs: e.matmul(pso[:, 0:128], lhsT=attT[:], rhs=vh[:, hs], start=True, stop=False), reads=[RattT, "vh"], writes=[Rpo])
                for q in range(16):
                    PE(lambda e, q=q, h=h: e.matmul(pso[:, 0:128], lhsT=qz[:, q, :], rhs=S0b[:, q, :], start=False, stop=(q == 15)),
                       reads=["qz", "S0b"], writes=[Rpo])
                for q in range(16):
                    V(lambda e, q=q, hs=hs: e.tensor_scalar(out=vbd[:, q, :], in0=vh[:, hs], scalar1=cst[:, C_BLK16, q * 8:q * 8 + 1], scalar2=None, op0=ALU.mult),
                      reads=["vh", "cst"], writes=["vbd"])
                dbk = [0, 1, 2, 3] if par == 0 else [3, 4, 5, 6]
                for c4 in range(4):
                    PE(lambda e, c4=c4, hs=hs: e.matmul(ps[dbk[c4]][:, :], lhsT=khat[:, hs], rhs=vbd[:, c4 * 4:(c4 + 1) * 4, :].rearrange("p a b -> p (a b)"), start=True, stop=True),
                       reads=["khat", "vbd"], writes=[f"ps{dbk[c4]}"])
                for q in range(16):
                    V(lambda e, q=q, h=h: e.scalar_tensor_tensor(out=S0f[:, q, :], in0=S0f[:, q, :], scalar=dcs[:, h * 16 + q:h * 16 + q + 1],
                                                                 in1=ps[dbk[q // 4]][:, (q % 4) * 128:(q % 4 + 1) * 128], op0=ALU.mult, op1=ALU.add),
                      reads=["S0f", "dcs", f"ps{dbk[q // 4]}"], writes=["S0f"])
                S.dma("sync", lambda e, h=h: e.dma_start(out=hgs_o.rearrange("(q h p) v -> h p q v", h=4, p=128)[h][:, sq0:sq0 + 16, :], in_=S0f[:]), reads=["S0f"], is_output=True)
            yield
            A(lambda e, h=h: e.activation(out=junk[:], in_=pso[:, 0:128], func=AF.Square, accum_out=ssq[:, h:h + 1]), reads=[Rpo], writes=["junk", f"ssq{h}"])
            V(lambda e, h=h: e.tensor_scalar(out=ssq[:, h:h + 1], in0=ssq[:, h:h + 1], scalar1=1.0 / 128, scalar2=EPS, op0=ALU.mult, op1=ALU.add), reads=[f"ssq{h}"], writes=[f"ssq{h}"])
            A(lambda e, h=h: e.activation(out=ssq[:, h:h + 1], in_=ssq[:, h:h + 1], func=AF.Sqrt), reads=[f"ssq{h}"], writes=[f"ssq{h}"])
            V(lambda e, h=h: e.reciprocal(out=ssq[:, h:h + 1], in_=ssq[:, h:h + 1]), reads=[f"ssq{h}"], writes=[f"ssq{h}"])
            V(lambda e, h=h, hs=hs: e.scalar_tensor_tensor(out=ohb[:, hs], in0=pso[:, 0:128], scalar=ssq[:, h:h + 1], in1=gs[:, hs], op0=ALU.mult, op1=ALU.mult),
              reads=[Rpo, f"ssq{h}", "gs"], writes=["ohb"])
        for h0 in (0, 2):
            if sample:
                for h in (h0, h0 + 1):
                    for _ in do_head(h):
                        pass
            else:
                gens = [do_head(h0), do_head(h0 + 1)]
                while gens:
                    for g in list(gens):
                        try:
                            next(g)
                        except StopIteration:
                            gens.remove(g)
        if KSTAGE < 4:
            return
        S.dma("sync", lambda e: e.dma_start(out=oh_scr[r0:r0 + 128, :], in_=ohb[:]), reads=["ohb"], writes=["oh_scr"])
        proj_tokmajor("w_in", 2048, 512, ti, 0, 0)
        A(lambda e: e.activation(out=fq_b[:], in_=ps[0][:], func=AF.Copy, scale=0.125), reads=["ps0"], writes=["fq_b"])
        proj_tokmajor("w_in", 2560, 512, ti, 1, 1)
        A(lambda e: e.activation(out=fk_f[:], in_=ps[1][:], func=AF.Copy), reads=["ps1"], writes=["fk_f"])
        V(lambda e: e.tensor_copy(out=fk_b[:], in_=ps[1][:]), reads=["ps1"], writes=["fk_b"])
        S.dma("sync", lambda e: e.dma_start(out=k_o[r0:r0 + 128, :], in_=fk_f[:]), reads=["fk_f"], is_output=True)
        proj_tokmajor("w_in", 3072, 512, ti, 0, 0)
        A(lambda e: e.activation(out=fv_f[:], in_=ps[0][:], func=AF.Copy), reads=["ps0"], writes=["fv_f"])
        V(lambda e: e.tensor_copy(out=vp[:, :, 0:64], in_=ps[0][:].rearrange("p (h d) -> p h d", h=8)), reads=["ps0"], writes=["vp"])
        S.dma("sync", lambda e: e.dma_start(out=v_o[r0:r0 + 128, :], in_=fv_f[:]), reads=["fv_f"], is_output=True)
        S.dma("sync", lambda e: e.dma_start(out=vp_scr[r0:r0 + 128, :], in_=vp[:].rearrange("p h d -> p (h d)")), reads=["vp"], writes=["vp_scr"])
        if KSTAGE < 5:
            return
        proj_tokmajor("w_in", 3584, 8, ti, 1, 1)
        V(lambda e: e.tensor_tensor(out=lft[:], in0=ps[1][:, 0:8], in1=fb_bc[:], op=ALU.add), reads=["ps1", "fb_bc"], writes=["lft"])
        A(lambda e: e.activation(out=lft[:], in_=lft[:], func=AF.Exp, scale=-1.0), reads=["lft"], writes=["lft"])
        A(lambda e: e.activation(out=lft[:], in_=lft[:], func=AF.Ln, bias=1.0), reads=["lft"], writes=["lft"])
        V(lambda e: e.tensor_scalar(out=lft[:], in0=lft[:], scalar1=-1.0, scalar2=None, op0=ALU.mult), reads=["lft"], writes=["lft"])
        S.dma("sync", lambda e: e.dma_start(out=lf_o[r0:r0 + 128, :], in_=lft[:]), reads=["lft"], is_output=True)
        if not sample:
            PE(lambda e: e.matmul(ps[2][:, 0:8], lhsT=cst[:, C_TRI, :], rhs=lft[:], start=True, stop=(T == 0)), reads=["cst", "lft"], writes=["ps2"])
            if T > 0:
                PE(lambda e: e.matmul(ps[2][:, 0:8], lhsT=cst[:, C_SEL127, :], rhs=Fk[:, T - 1, :], start=False, stop=True), reads=["cst", "Fk"], writes=["ps2"])
            V(lambda e: e.tensor_copy(out=Fk[:, T, :], in_=ps[2][:, 0:8]), reads=["ps2"], writes=["Fk"])
            PE(lambda e: e.matmul(ps[2][:, 8:16], lhsT=cst[:, C_SEL127, :], rhs=Fk[:, T, :], start=True, stop=True), reads=["cst", "Fk"], writes=["ps2"])
            V(lambda e: e.tensor_copy(out=Fend[:, T, :], in_=ps[2][:, 8:16]), reads=["ps2"], writes=["Fend"])
        else:
            PE(lambda e: e.matmul(ps[2][:, 0:8], lhsT=cst[:, C_TRI16, :], rhs=lft[:], start=True, stop=True), reads=["cst", "lft"], writes=["ps2"])
            V(lambda e: e.tensor_copy(out=Fk[:, T, :], in_=ps[2][:, 0:8]), reads=["ps2"], writes=["Fk"])
        for (src, res, scr) in ((fq_b, "fq_b", qT_scr), (fk_b, "fk_b", kT_scr)):
            transpose_to(tq, "tq", src, res, 4, 0, psi=7)
            S.dma("sync", lambda e, scr=scr: e.dma_start(out=scr[r0:r0 + 128, :], in_=tq[:].rearrange("p a b -> p (a b)")), reads=["tq"], writes=[scr.tensor.name])

    PH0 = os.environ.get('KPH', 'ABSC')
    for G in range(NPT // 4):
        for ti in range(4):
            T = G * 4 + ti
            S.dma("sync", lambda e, T=T, ti=ti: e.dma_start(out=xg[:, ti, :], in_=xin[T * 128:(T + 1) * 128, :]), writes=[f"xg{ti}"])
        if 'f' not in PH0:
            ffn(0, 4)
        build_xT(4)
        for ti in range(4):
            T = G * 4 + ti
            S.dma("sync", lambda e, T=T, ti=ti: e.dma_start(out=x1_scr[T * 128:(T + 1) * 128, :], in_=xg[:, ti, :]), reads=[f"xg{ti}"], writes=["x1_scr"])
            if 'F' in PH0:
                S.dma("sync", lambda e, T=T, ti=ti: e.dma_start(out=y_o[T * 128:(T + 1) * 128, :], in_=xg[:, ti, :]), reads=[f"xg{ti}"], is_output=True)
            else:
                phaseA_tile(T, ti, False)
    for h in range(4):
        S.dma("sync", lambda e, h=h: e.dma_start(out=hgp_o[h * 128:(h + 1) * 128, :], in_=Sst[h][:]), reads=[f"Sst{h}"], is_output=True)
    for st in range(NS):
        S.dma("sync", lambda e, st=st: e.dma_start(out=xg[:, st, :], in_=xin[(SMP + st) * 128:(SMP + st + 1) * 128, :]), writes=[f"xg{st}"])
    if 'F' in PH0:
        S.emit()
        return nc
    ffn(0, NS)
    build_xT(NS)
    for st in range(NS):
        S.dma("sync", lambda e, st=st: e.dma_start(out=x1_scr[(SMP + st) * 128:(SMP + st + 1) * 128, :], in_=xg[:, st, :]), reads=[f"xg{st}"], writes=["x1_scr"])
        phaseA_tile(SMP + st, st, True)

    PH = os.environ.get('KPH', 'ABSC')
    NKB = 2
    kTt = [sb(f"kTt{i}", [128, 4, 128], BF16) for i in range(NKB)]
    vpt = [sb(f"vpt{i}", [128, 8, 65], BF16) for i in range(NKB)]
    qTt = sb("qTt", [128, 4, 128], BF16)
    wq = [sb(f"wq{i}", [128, 8]) for i in range(2)]
    Vs = [sb(f"Vs{i}", [128, 8, 65], BF16) for i in range(2)]
    pT = [sb(f"pT{i}", [128, 2, 4, 128], BF16) for i in range(2)]
    ofb = sb("ofb", [128, 512], BF16)
    rs = sb("rs", [128, 8])
    G_ = lambda fn, reads=(), writes=(): S.op(os.environ.get("KGENG", "vector"), fn, reads, writes)

    qTt2 = [qTt, sb("qTt1", [128, 4, 128], BF16)]

    def attn_qk(b, kq, k_tile, k_res, qt_tile, qt_res, q_cols, v_src, v_res, w_ap, w_res, mask_c, pt_out):
        V(lambda e: e.tensor_tensor(out=Vs[b][:], in0=v_src[:], in1=w_ap[:, :].unsqueeze(2).broadcast_to([128, 8, 65]), op=ALU.mult),
          reads=[v_res, w_res], writes=[f"Vs{b}"])
        nq = q_cols.stop - q_cols.start
        for h in range(8):
            pr, po = h // 2, (h % 2) * 64
            bank = 2 * kq + h % 2
            PE(lambda e, h=h, pr=pr, po=po, bank=bank: e.matmul(ps[bank][:, (h // 2) * nq:(h // 2 + 1) * nq], lhsT=k_tile[po:po + 64, pr, :], rhs=qt_tile[po:po + 64, pr, q_cols],
                                                              start=True, stop=True, skip_group_check=True),
               reads=[k_res, qt_res], writes=[f"ps{bank}"])
        for half in range(2):
            bank = 2 * kq + half
            A(lambda e, half=half, bank=bank: e.activation(out=pt_out(half), in_=ps[bank][:, 0:4 * nq].rearrange("p (h q) -> p h q", h=4), func=AF.Exp),
              reads=[f"ps{bank}"], writes=[f"pT{b}"])
        if mask_c is not None:
            pv = pT[b][:].rearrange("p a b q -> p (a b) q")
            V(lambda e: e.tensor_tensor(out=pv, in0=pv, in1=cstb[:, mask_c, :].unsqueeze(1).broadcast_to([128, 8, 128]), op=ALU.mult), reads=[f"pT{b}", "cstb"], writes=[f"pT{b}"])

    def attn_pv(b, acc0, first, last):
        for h in range(8):
            bank, c0 = acc0 + h // 4, (h % 4) * 65
            PE(lambda e, h=h, bank=bank, c0=c0: e.matmul(ps[bank][:, c0:c0 + 65], lhsT=pT[b][:, h % 2, h // 2, :], rhs=Vs[b][:, h, :],
                                                       start=(first and h % 4 == 0), stop=last, skip_group_check=True),
               reads=[f"pT{b}", f"Vs{b}"], writes=[f"ps{bank}"])

    def attn_finish(row0, acc0):
        for half in range(2):
            bank = acc0 + half
            pv3 = ps[bank][:, 0:260].rearrange("p (h c) -> p h c", c=65)
            V(lambda e, half=half, pv3=pv3: e.reciprocal(out=rs[:, half * 4:(half + 1) * 4].unsqueeze(2), in_=pv3[:, :, 64:65]), reads=[f"ps{bank}"], writes=["rs"])
            V(lambda e, half=half, pv3=pv3: e.tensor_tensor(out=ofb[:, half * 256:(half + 1) * 256].rearrange("p (h d) -> p h d", d=64), in0=pv3[:, :, 0:64],
                                                          in1=rs[:, half * 4:(half + 1) * 4].unsqueeze(2).broadcast_to([128, 4, 64]), op=ALU.mult),
              reads=[f"ps{bank}", "rs"], writes=["ofb"])
        S.dma("sync", lambda e: e.dma_start(out=of_scr[row0:row0 + 128, :], in_=ofb[:]), reads=["ofb"], writes=["of_scr"])

    def run_pipelined(items):
        if not items:
            return
        items[0][0]()
        for i in range(len(items)):
            if i + 1 < len(items):
                items[i + 1][0]()
            items[i][1]()

    items = []
    pairs = [(qt, kt) for qt in range(NPT if 'B' in PH else 0) for kt in range(qt + 1)]
    for i, (qt, kt) in enumerate(pairs):
        b = i % 2; g = i % NKB; qb = qt % 2; acc0 = 4 if qt % 2 == 0 else 6

        def qk(qt=qt, kt=kt, b=b, g=g, qb=qb):
            if kt == 0:
                S.dma("sync", lambda e: e.dma_start(out=qTt2[qb][:].rearrange("p a b -> p (a b)"), in_=qT_scr[qt * 128:(qt + 1) * 128, :]), reads=["qT_scr"], writes=[f"qTt{qb}"])
            S.dma("sync", lambda e: e.dma_start(out=kTt[g][:].rearrange("p a b -> p (a b)"), in_=kT_scr[kt * 128:(kt + 1) * 128, :]), reads=["kT_scr"], writes=[f"kTt{g}"])
            S.dma("sync", lambda e: e.dma_start(out=vpt[g][:].rearrange("p a b -> p (a b)"), in_=vp_scr[kt * 128:(kt + 1) * 128, :]), reads=["vp_scr"], writes=[f"vpt{g}"])
            V(lambda e: e.tensor_tensor(out=wq[b][:], in0=Fend[:, qt, :], in1=Fk[:, kt, :], op=ALU.subtract), reads=["Fend", "Fk"], writes=[f"wq{b}"])
            V(lambda e: e.tensor_scalar(out=wq[b][:], in0=wq[b][:], scalar1=0.0, scalar2=None, op0=ALU.min), reads=[f"wq{b}"], writes=[f"wq{b}"])
            A(lambda e: e.activation(out=wq[b][:], in_=wq[b][:], func=AF.Exp), reads=[f"wq{b}"], writes=[f"wq{b}"])
            attn_qk(b, b, kTt[g], f"kTt{g}", qTt2[qb], f"qTt{qb}", slice(0, 128), vpt[g], f"vpt{g}", wq[b], f"wq{b}",
                    C_TRI if kt == qt else None, lambda half: pT[b][:, half, :, :])

        def pv(qt=qt, kt=kt, b=b, acc0=acc0):
            attn_pv(b, acc0, kt == 0, kt == qt)
            if kt == qt:
                attn_finish(qt * 128, acc0)
        items.append((qk, pv))
    run_pipelined(items)

    def sample_attn():
        pts = sb("pts", [128, NS * 256], I32)
        pidx = pts
        iot = sb("iot", [128, 1], I32)
        NKS = 2
        kpg = [sb(f"kpg{i}", [128, 512], BF16) for i in range(NKS)]
        vpg = [sb(f"vpg{i}", [128, 8, 65], BF16) for i in range(2)]
        vgt = [sb(f"vgt{i}", [128, 512], BF16) for i in range(NKS)]
        lfp = sb("lfp", [128, 16, 8])
        lat = sb("lat", [128, 16, 8])
        Rb = sb("Rb", [128, 16, 8])
        kTp = [sb(f"kTp{i}", [128, 4, 128], BF16) for i in range(2)]
        S.dma("sync", lambda e: e.dma_start(out=pts[:], in_=ptab[0:1, :].partition_broadcast(128)), writes=["pts"])
        S.op("gpsimd", lambda e: e.iota(iot[:], pattern=[[0, 1]], base=0, channel_multiplier=1), writes=["iot"])
        S.op("gpsimd", lambda e: e.tensor_scalar(out=pidx[:], in0=pts[:], scalar1=128, scalar2=iot[:, 0:1], op0=ALU.mult, op1=ALU.add), reads=["pts", "iot"], writes=["pts"])
        for i in range(2):
            V(lambda e, i=i: e.memset(vpg[i][:], 1.0), writes=[f"vpg{i}"])

        def seq_pre(st, q):
            for pg in range(16):
                S.dma("gpsimd", lambda e, pg=pg: e.indirect_dma_start(out=lfp[:, pg, :], out_offset=None, in_=clf,
                                                                     in_offset=bass.IndirectOffsetOnAxis(ap=pidx[:, st * 256 + q * 16 + pg:st * 256 + q * 16 + pg + 1], axis=0)),
                      reads=["pts"], writes=["lfp"])
            V(lambda e: e.memset(lat[:, 15, :], 0.0), writes=["lat"])
            for pg in range(14, -1, -1):
                V(lambda e, pg=pg: e.tensor_tensor(out=lat[:, pg, :], in0=lat[:, pg + 1, :], in1=lfp[:, pg + 1, :], op=ALU.add), reads=["lat", "lfp"], writes=["lat"])
            PE(lambda e: e.matmul(ps[6][:, 0:128], lhsT=cst[:, C_SUP, :], rhs=lfp[:].rearrange("p a b -> p (a b)"), start=True, stop=False), reads=["cst", "lfp"], writes=["ps6"])
            PE(lambda e: e.matmul(ps[6][:, 0:128], lhsT=cst[:, C_ONES, :], rhs=lat[:].rearrange("p a b -> p (a b)"), start=False, stop=True), reads=["cst", "lat"], writes=["ps6"])
            A(lambda e: e.activation(out=Rb[:].rearrange("p a b -> p (a b)"), in_=ps[6][:, 0:128], func=AF.Exp), reads=["ps6"], writes=["Rb"])

        for st in range(NS):
            TS = SMP + st
            items = []

            def qk0(TS=TS):
                S.dma("sync", lambda e: e.dma_start(out=qTt[:].rearrange("p a b -> p (a b)"), in_=qT_scr[TS * 128:(TS + 1) * 128, :]), reads=["qT_scr"], writes=["qTt0"])
                S.dma("sync", lambda e: e.dma_start(out=kTt[0][:].rearrange("p a b -> p (a b)"), in_=kT_scr[TS * 128:(TS + 1) * 128, :]), reads=["kT_scr"], writes=["kTt0"])
                S.dma("sync", lambda e: e.dma_start(out=vpt[0][:].rearrange("p a b -> p (a b)"), in_=vp_scr[TS * 128:(TS + 1) * 128, :]), reads=["vp_scr"], writes=["vpt0"])
                A(lambda e: e.activation(out=wq[0][:], in_=Fk[:, TS, :], func=AF.Exp, scale=-1.0), reads=["Fk"], writes=["wq0"])
                attn_qk(0, 0, kTt[0], "kTt0", qTt, "qTt0", slice(0, 128), vpt[0], "vpt0", wq[0], "wq0", C_TRI16, lambda half: pT[0][:, half, :, :])

            def pv0():
                attn_pv(0, 4, True, False)
                V(lambda e: e.memset(pT[0][:], 0.0), writes=["pT0"])
            items.append((qk0, pv0))
            n = 0
            for q in range(16):
                for pg in range(16):
                    n += 1
                    b = n % 2; g = n % NKS
                    col = st * 256 + q * 16 + pg

                    def qk(st=st, q=q, pg=pg, b=b, g=g, col=col, n=n):
                        if n == 1:
                            V(lambda e: e.memset(pT[1][:], 0.0), writes=["pT1"])
                        if pg == 0:
                            seq_pre(st, q)
                        if q > 0 and pg < 2:
                            V(lambda e: e.memset(pT[b][:].rearrange("p a b q -> p (a b) q")[:, :, (q - 1) * 8:q * 8], 0.0), writes=[f"pT{b}"])
                        S.dma("gpsimd", lambda e: e.indirect_dma_start(out=kpg[g][:], out_offset=None, in_=ck,
                                                                     in_offset=bass.IndirectOffsetOnAxis(ap=pidx[:, col:col + 1], axis=0)),
                              reads=["pts"], writes=[f"kpg{g}"])
                        S.dma("gpsimd", lambda e: e.indirect_dma_start(out=vgt[g][:], out_offset=None, in_=cv_,
                                                                     in_offset=bass.IndirectOffsetOnAxis(ap=pidx[:, col:col + 1], axis=0)),
                              reads=["pts"], writes=[f"vgt{g}"])
                        V(lambda e: e.tensor_copy(out=vpg[b][:, :, 0:64], in_=vgt[g][:].rearrange("p (h d) -> p h d", h=8)), reads=[f"vgt{g}"], writes=[f"vpg{b}"])
                        transpose_to(kTp[b], f"kTp{b}", kpg[g], f"kpg{g}", 4, 0, psi=7)
                        attn_qk(b, b, kTp[b], f"kTp{b}", qTt, "qTt0", slice(q * 8, (q + 1) * 8), vpg[b], f"vpg{b}", Rb[:, pg, :], "Rb", None,
                                lambda half: pT[b][:, half, :, q * 8:(q + 1) * 8])

                    def pv(b=b, q=q, pg=pg):
                        attn_pv(b, 4, False, q == 15 and pg == 15)
                    items.append((qk, pv))
            run_pipelined(items)
            attn_finish(TS * 128, 4)

    if 'S' in PH:
        sample_attn()

    ocat = tb[2][:, :]
    uu = tf[0][:, :]
    vv = tf[1][:, :]
    vvb = tb[0][:, :]
    wsT = sb("wsT", [128, 8, 128], BF16)
    wsS = sb("wsS", [128, 8, 128], BF16)
    bsT = sb("bsT", [128, 8])
    bsS = sb("bsS", [128, 8])
    wsl = sb("wsl", [128, 128])
    bsl = sb("bsl", [8, 128])
    w8 = sb("w8", [8, 8])
    b8 = sb("b8", [8, 128])
    umx = tb[1][:, :]
    gq = tf[2][:, 0:512]
    S.alias["gq"] = "tf2"

    S.dma("sync", lambda e: e.dma_start(out=bsl[:], in_=c_b_s[:, :]), writes=["bsl"])
    PE(lambda e: e.transpose(ps[0][:, 0:8], bsl[:], cst[0:8, C_ID, 0:8]), reads=["bsl", "cst"], writes=["ps0"])
    V(lambda e: e.tensor_copy(out=bsT[:], in_=ps[0][:, 0:8]), reads=["ps0"], writes=["bsT"])
    PE(lambda e: e.matmul(ps[0][:, 8:16], lhsT=cst[0:8, C_R, :], rhs=bsT[0:8, :], start=True, stop=True), reads=["cst", "bsT"], writes=["ps0"])
    V(lambda e: e.tensor_copy(out=bsS[:], in_=ps[0][:, 8:16]), reads=["ps0"], writes=["bsS"])
    for g in range(8):
        S.dma("sync", lambda e, g=g: e.dma_start(out=wsl[:], in_=c_w_s[g * 128:(g + 1) * 128, :]), writes=["wsl"])
        PE(lambda e: e.transpose(ps[1][:, 0:128], wsl[:], cst[:, C_ID, :]), reads=["wsl", "cst"], writes=["ps1"])
        V(lambda e, g=g: e.tensor_tensor(out=wsT[:, g, :], in0=ps[1][:, 0:128], in1=cst[:, C_TRI, :], op=ALU.mult), reads=["ps1", "cst"], writes=["wsT"])
        PE(lambda e: e.matmul(ps[1][0:8, 128:256], lhsT=wsl[0:8, 0:8], rhs=cst[0:8, C_R, :], start=True, stop=True), reads=["wsl", "cst"], writes=["ps1"])
        V(lambda e: e.tensor_copy(out=b8[:], in_=ps[1][0:8, 128:256]), reads=["ps1"], writes=["b8"])
        PE(lambda e: e.matmul(ps[1][:, 256:384], lhsT=cst[0:8, C_R, :], rhs=b8[:], start=True, stop=True), reads=["cst", "b8"], writes=["ps1"])
        V(lambda e, g=g: e.tensor_tensor(out=wsS[:, g, :], in0=ps[1][:, 256:384], in1=cst[:, C_TRI16, :], op=ALU.mult), reads=["ps1", "cst"], writes=["wsS"])

    def phaseC_group(T0, nt, sample):
        for ti in range(nt):
            r0 = (T0 + ti) * 128
            S.dma("sync", lambda e, r0=r0, ti=ti: e.dma_start(out=xg[:, ti, :], in_=x1_scr[r0:r0 + 128, :]), reads=["x1_scr"], writes=[f"xg{ti}"])
            S.dma("sync", lambda e, r0=r0: e.dma_start(out=ocat[:, 0:512], in_=oh_scr[r0:r0 + 128, :]), reads=["oh_scr"], writes=["ocat"])
            S.dma("sync", lambda e, r0=r0: e.dma_start(out=ocat[:, 512:1024], in_=of_scr[r0:r0 + 128, :]), reads=["of_scr"], writes=["ocat"])
            transpose_to(xT, "xT", ocat, "ocat", 8, ti * 128)
        for ti in range(nt):
            for c in range(2):
                proj_tokmajor("w_out", c * 512, 512, ti, 4 + c, c)
                V(lambda e, ti=ti, c=c: e.scalar_tensor_tensor(out=xg[:, ti, c * 512:(c + 1) * 512], in0=xg[:, ti, c * 512:(c + 1) * 512], scalar=ALPHA,
                                                               in1=ps[4 + c][:, :], op0=ALU.mult, op1=ALU.add), reads=[f"ps{4 + c}", f"xg{ti}"], writes=[f"xg{ti}"])
            layer_norm(xg[:, ti, :], f"xg{ti}", xg[:, ti, :], f"xg{ti}", 1)
        ffn(1, nt)
        ffn(2, nt)
        build_xT(nt)
        for ti in range(nt):
            for c in range(4):
                proj_tokmajor("c_w_in", c * 512, 512, ti, c % 2, c % 2)
                dstt = uu if c < 2 else vv
                dres = "uu" if c < 2 else "vv"
                dsl = dstt[:, (c % 2) * 512:(c % 2 + 1) * 512]
                A(lambda e, c=c: e.activation(out=gq[:], in_=ps[c % 2][:, :], func=AF.Square), reads=[f"ps{c % 2}"], writes=["gq"])
                V(lambda e: e.tensor_scalar(out=gq[:], in0=gq[:], scalar1=0.044715, scalar2=1.0, op0=ALU.mult, op1=ALU.add), reads=["gq"], writes=["gq"])
                V(lambda e, c=c: e.tensor_tensor(out=gq[:], in0=gq[:], in1=ps[c % 2][:, :], op=ALU.mult), reads=["gq", f"ps{c % 2}"], writes=["gq"])
                A(lambda e: e.activation(out=gq[:], in_=gq[:], func=AF.Sigmoid, scale=1.5957691216057308), reads=["gq"], writes=["gq"])
                V(lambda e, c=c, dsl=dsl: e.tensor_tensor(out=dsl, in0=gq[:], in1=ps[c % 2][:, :], op=ALU.mult), reads=["gq", f"ps{c % 2}"], writes=[dres])
            layer_norm(vv[:], "vv", vv[:], "vv", 0, g_ap=c_ln_g[0:1, :], b_ap=c_ln_b[0:1, :])
            if sample:
                S.dma("sync", lambda e, ti=ti: e.dma_start(out=cv_o[ti * 128:(ti + 1) * 128, :], in_=vv[:]), reads=["vv"], is_output=True)
            A(lambda e: e.activation(out=vvb[:], in_=vv[:], func=AF.Copy), reads=["vv"], writes=["vvb"])
            wsx = wsS if sample else wsT
            bsx = bsS if sample else bsT
            for g in range(8):
                bank = 4 + g // 4
                cs = slice((g % 4) * 128, (g % 4 + 1) * 128)
                PE(lambda e, g=g, bank=bank, cs=cs, wsx=wsx: e.matmul(ps[bank][:, cs], lhsT=wsx[:, g, :], rhs=vvb[:, g * 128:(g + 1) * 128], start=True, stop=True),
                   reads=["wsT", "wsS", "vvb"], writes=[f"ps{bank}"])
                V(lambda e, g=g, bank=bank, cs=cs, bsx=bsx: e.scalar_tensor_tensor(out=umx[:, g * 128:(g + 1) * 128], in0=ps[bank][:, cs], scalar=bsx[:, g:g + 1],
                                                                                   in1=uu[:, g * 128:(g + 1) * 128], op0=ALU.add, op1=ALU.mult),
                  reads=[f"ps{bank}", "bsT", "bsS", "uu"], writes=["umx"])
            transpose_to(xT, "xT", umx, "umx", 8, ti * 128)
        for ti in range(nt):
            for c in range(2):
                proj_tokmajor("c_w_out", c * 512, 512, ti, 4 + c, c)
                V(lambda e, ti=ti, c=c: e.scalar_tensor_tensor(out=xg[:, ti, c * 512:(c + 1) * 512], in0=xg[:, ti, c * 512:(c + 1) * 512], scalar=ALPHA,
                                                               in1=ps[4 + c][:, :], op0=ALU.mult, op1=ALU.add), reads=[f"ps{4 + c}", f"xg{ti}"], writes=[f"xg{ti}"])
            layer_norm(xg[:, ti, :], f"xg{ti}", xg[:, ti, :], f"xg{ti}", 4)
        ffn(3, nt)
        for ti in range(nt):
            r0 = (T0 + ti) * 128
            S.dma("sync", lambda e, r0=r0, ti=ti: e.dma_start(out=y_o[r0:r0 + 128, :], in_=xg[:, ti, :]), reads=[f"xg{ti}"], is_output=True)

    if 'C' in PH:
        for G in range(NPT // 4):
            phaseC_group(G * 4, 4, False)
        phaseC_group(SMP, NS, True)

    print('sbuf bytes remaining', nc.sbuf_bytes_remaining)
    S.emit()
    print('instr counts', {e: len(S.q[e]) for e in ENGS})
    return nc


_NC_CACHE = {}
NCORES = 2
NS_ = 4


def kernel(x_prompt, x_sample, cache_fox_k, cache_fox_v, cache_fox_logf, state_hg, page_table,
           ln_g, ln_b, ffn_w_gate, ffn_w_up, ffn_w_down, ab_w_in, hg_lb_logits, hg_norm_g, fox_f_bias,
           ab_w_out, c_w_in, c_ln_g, c_ln_b, c_w_s, c_b_s, c_w_out):
    f = lambda a: np.ascontiguousarray(np.asarray(a, dtype=np.float32))
    x_prompt = f(x_prompt); x_sample = f(x_sample)
    P = x_prompt.shape[1]
    nphys = int(os.environ.get('KNPHYS', str(N_PHYS)))
    if nphys != N_PHYS:
        cache_fox_k = np.asarray(cache_fox_k)[:, :nphys]; cache_fox_v = np.asarray(cache_fox_v)[:, :nphys]
        cache_fox_logf = np.asarray(cache_fox_logf)[:, :nphys]; page_table = np.asarray(page_table) % nphys
    if P not in _NC_CACHE:
        _NC_CACHE[P] = build_program(P // 128, nphys, NS_)
    nc = _NC_CACHE[P]
    shared = {
        "cache_k": f(cache_fox_k).reshape(nphys * 128, 512),
        "cache_v": f(cache_fox_v).reshape(nphys * 128, 512),
        "cache_lf": f(cache_fox_logf).reshape(nphys * 128, 8),
        "ln_g": f(ln_g).reshape(6, D), "ln_b": f(ln_b).reshape(6, D),
        "w_gate": f(ffn_w_gate).reshape(4, D, FF), "w_up": f(ffn_w_up).reshape(4, D, FF),
        "w_down": f(ffn_w_down).reshape(4, FF, D),
        "ab_w_in": f(ab_w_in).reshape(D, 3592), "lb_logits": f(hg_lb_logits).reshape(1, 1536),
        "hg_norm_g": f(hg_norm_g).reshape(1, 128), "fox_f_bias": f(fox_f_bias).reshape(1, 8),
        "ab_w_out": f(ab_w_out).reshape(D, D), "c_w_in": f(c_w_in).reshape(D, 2048),
        "c_ln_g": f(c_ln_g).reshape(1, D), "c_ln_b": f(c_ln_b).reshape(1, D),
        "c_w_s": f(c_w_s).reshape(1024, 128), "c_b_s": f(c_b_s).reshape(8, 128),
        "c_w_out": f(c_w_out).reshape(D, D), "cst": make_consts().reshape(128, NCST * 128),
    }
    pt = np.asarray(page_table, dtype=np.int32)
    SQ = 16 * NS_
    in_maps = []
    for c in range(NCORES):
        m = dict(shared)
        m["xin"] = np.concatenate([x_prompt[c], x_sample[SQ * c:SQ * (c + 1)].reshape(SQ * 8, D)], axis=0)
        m["state_hg"] = f(state_hg)[0, SQ * c:SQ * (c + 1)].reshape(SQ * 4 * 128, 128)
        m["ptab"] = np.ascontiguousarray(pt[SQ * c:SQ * (c + 1)].reshape(1, SQ * 16))
        in_maps.append(m)
    ncores = int(os.environ.get('KCORES', str(NCORES)))
    res = run_bass_kernel_spmd(nc, in_maps[:ncores], core_ids=list(range(ncores))).results
    res = list(res) + [res[0]] * (NCORES - ncores)
    R = range(NCORES)
    y_p = np.stack([res[b]["y"][:P] for b in R])
    y_s = np.concatenate([res[c]["y"][P:].reshape(SQ, 8, D) for c in R])
    k_p = np.stack([res[b]["k_new"][:P].reshape(P, 8, 64) for b in R])[None]
    v_p = np.stack([res[b]["v_new"][:P].reshape(P, 8, 64) for b in R])[None]
    lf_p = np.stack([res[b]["lf_new"][:P] for b in R])[None]
    hg_p = np.stack([res[b]["hg_p"].reshape(4, 128, 128) for b in R])[None]
    k_s = np.concatenate([res[c]["k_new"][P:].reshape(SQ, 8, 8, 64) for c in R])[None]
    v_s = np.concatenate([res[c]["v_new"][P:].reshape(SQ, 8, 8, 64) for c in R])[None]
    lf_s = np.concatenate([res[c]["lf_new"][P:].reshape(SQ, 8, 8) for c in R])[None]
    hg_s = np.concatenate([res[c]["hg_s"].reshape(SQ, 4, 128, 128) for c in R])[None]
    cv_s = np.concatenate([res[c]["cv_s"].reshape(SQ, 8, D) for c in R])[None]
    return (y_p, y_s, k_p, v_p, lf_p, hg_p, k_s, v_s, lf_s, hg_s, cv_s)
```

```python
import os
import numpy as np
import concourse.bass as bass
import concourse.mybir as mybir
from concourse.bass_utils import run_bass_kernel_spmd

F32 = mybir.dt.float32
BF16 = mybir.dt.bfloat16
I32 = mybir.dt.int32
AF = mybir.ActivationFunctionType
ALU = mybir.AluOpType

D = 1024
FF = 2816
NJ = 22
NPT = 64
NT = 65
NTOK = NT * 128
ALPHA = 4.0 ** 0.25
EPS = 1e-5
N_PHYS = 2560
ENGS = ("tensor", "vector", "scalar", "gpsimd", "sync")
N_DMA_SEMS = 16
SEM_EPOCH = 16000

C_ID, C_TRI, C_SEL127, C_TRI2, C_BLK2, C_TRI16, C_BLK16, C_SELEND2, C_SELEND16, C_R, C_SUP, C_ONES, C_NEG = range(13)
NCST = 13


def make_consts():
    c = np.zeros((128, NCST, 128), np.float32)
    s = np.arange(128)[:, None]
    t = np.arange(128)[None, :]
    c[:, C_ID] = (s == t)
    c[:, C_TRI] = (s <= t)
    c[:, C_SEL127] = (s == 127) * np.ones((1, 128))
    c[:, C_TRI2] = (s // 64 == t // 64) & (s <= t)
    c[:, C_BLK2] = (s // 64 == t // 64)
    c[:, C_TRI16] = (s // 8 == t // 8) & (s <= t)
    c[:, C_BLK16] = (s // 8 == t // 8)
    c[:, C_SELEND2][:, 0] = (np.arange(128) == 63)
    c[:, C_SELEND2][:, 1] = (np.arange(128) == 127)
    for q in range(16):
        c[q * 8 + 7, C_SELEND16, q] = 1.0
    for i in range(8):
        c[i, C_R, i::8] = 1.0
    c[:, C_SUP] = (s > t)
    c[:, C_ONES] = 1.0
    c[:, C_NEG] = -1.0
    return c


class Sched:
    def __init__(self, nc):
        self.nc = nc
        self.q = {e: [] for e in ENGS}
        self.cnt = {e: 0 for e in ENGS}
        self.epoch = {e: 0 for e in ENGS}
        self.sem = {e: nc.alloc_semaphore(f"s_{e}_0") for e in ENGS}
        self.dsem = {e: [nc.alloc_semaphore(f"d_{e}_{i}") for i in range(N_DMA_SEMS)]
                     for e in ("sync", "scalar", "gpsimd")}
        self.dcnt = {e: [0] * N_DMA_SEMS for e in ("sync", "scalar", "gpsimd")}
        self.dnext = {e: 0 for e in ("sync", "scalar", "gpsimd")}
        self.known = {e: {} for e in ENGS}
        self.lastw = {}
        self.reads = {}
        self.semobj = {}
        for e in ENGS:
            self.semobj[("e", e, 0)] = self.sem[e]
        for e in self.dsem:
            for i, s in enumerate(self.dsem[e]):
                self.semobj[("d", e, i)] = s
        self.out_deps = []
        self.alias = {}

    def _need(self, eng, deps):
        need = {}
        for (k, v) in deps:
            if self.known[eng].get(k, 0) >= v:
                continue
            if need.get(k, 0) < v:
                need[k] = v
        for k, v in need.items():
            self.known[eng][k] = v
        return list(need.items())

    def _deps(self, reads, writes):
        reads = [self.alias.get(r, r) for r in reads]
        writes = [self.alias.get(w, w) for w in writes]
        deps = []
        for r in reads:
            if r in self.lastw:
                deps.append(self.lastw[r])
        for w in writes:
            if w in self.lastw:
                deps.append(self.lastw[w])
            deps.extend(self.reads.get(w, ()))
        return deps

    def _commit(self, tok, reads, writes):
        reads = [self.alias.get(r, r) for r in reads]
        writes = [self.alias.get(w, w) for w in writes]
        for r in reads:
            self.reads.setdefault(r, []).append(tok)
        for w in writes:
            self.lastw[w] = tok
            self.reads[w] = []

    def op(self, eng, fn, reads=(), writes=()):
        pr = [r for r in reads if r.startswith("ps")]
        if pr:
            reads = [r for r in reads if not r.startswith("ps")]
            writes = list(writes) + pr
        deps = self._deps(reads, writes)
        waits = self._need(eng, deps)
        if self.cnt[eng] >= SEM_EPOCH:
            self.epoch[eng] += 1
            self.cnt[eng] = 0
            self.sem[eng] = self.nc.alloc_semaphore(f"s_{eng}_{self.epoch[eng]}")
            self.semobj[("e", eng, self.epoch[eng])] = self.sem[eng]
        self.cnt[eng] += 1
        key = ("e", eng, self.epoch[eng])
        tok = (key, self.cnt[eng])
        if eng == "tensor":
            self.known[eng][key] = self.cnt[eng]
        self.q[eng].append((waits, fn, (self.sem[eng], 1)))
        self._commit(tok, reads, writes)
        return tok

    def dma(self, eng, fn, reads=(), writes=(), is_output=False):
        deps = self._deps(reads, writes)
        i = self.dnext[eng]
        self.dnext[eng] = (i + 1) % N_DMA_SEMS
        key = ("d", eng, i)
        if self.dcnt[eng][i] > 0:
            deps.append((key, self.dcnt[eng][i]))
        waits = self._need(eng, deps)
        self.dcnt[eng][i] += 16
        tok = (key, self.dcnt[eng][i])
        self.q[eng].append((waits, fn, (self.dsem[eng][i], 16)))
        self._commit(tok, reads, writes)
        if is_output:
            self.out_deps.append(tok)
        return tok

    def emit(self):
        nc = self.nc
        fin = list(self.out_deps)
        for r, t in self.lastw.items():
            fin.append(t)
        waits = self._need("sync", fin)
        self.q["sync"].append((waits, None, None))
        with nc.Block() as block:
            def run(engname):
                def body(eng):
                    for waits, fn, inc in self.q[engname]:
                        for k, v in waits:
                            eng.wait_ge(self.semobj[k], v)
                        if fn is not None:
                            ins = fn(eng)
                            ins.then_inc(inc[0], inc[1])
                return body
            block.tensor(run("tensor"))
            block.vector(run("vector"))
            block.scalar(run("scalar"))
            block.gpsimd(run("gpsimd"))
            block.sync(run("sync"))


def build_program(NPT=NPT, N_PHYS=N_PHYS, NS=4):
    NT = NPT + NS
    NTOK = NT * 128
    SMP = NPT
    nc = bass.Bass("TRN2", target_bir_lowering=False)
    S = Sched(nc)

    def din(name, shape, dt=F32):
        return nc.dram_tensor(name, list(shape), dt, kind="ExternalInput").ap()

    def dout(name, shape, dt=F32):
        return nc.dram_tensor(name, list(shape), dt, kind="ExternalOutput").ap()

    def dscr(name, shape, dt):
        return nc.dram_tensor(name, list(shape), dt).ap()

    def sb(name, shape, dt=F32):
        return nc.alloc_sbuf_tensor(name, list(shape), dt)

    xin = din("xin", [NTOK, D])
    ck = din("cache_k", [N_PHYS * 128, 512])
    cv_ = din("cache_v", [N_PHYS * 128, 512])
    clf = din("cache_lf", [N_PHYS * 128, 8])
    st_hg = din("state_hg", [NS * 16 * 4 * 128, 128])
    ptab = din("ptab", [1, NS * 256], I32)
    ln_g = din("ln_g", [6, D])
    ln_b = din("ln_b", [6, D])
    wgate = din("w_gate", [4, D, FF])
    wup = din("w_up", [4, D, FF])
    wdown = din("w_down", [4, FF, D])
    w_in = din("ab_w_in", [D, 3592])
    lb_log = din("lb_logits", [1, 3 * 512])
    hg_ng = din("hg_norm_g", [1, 128])
    fbias = din("fox_f_bias", [1, 8])
    w_out = din("ab_w_out", [D, D])
    c_w_in = din("c_w_in", [D, 2048])
    c_ln_g = din("c_ln_g", [1, D])
    c_ln_b = din("c_ln_b", [1, D])
    c_w_s = din("c_w_s", [8 * 128, 128])
    c_b_s = din("c_b_s", [8, 128])
    c_w_out = din("c_w_out", [D, D])
    cst_d = din("cst", [128, NCST * 128])

    y_o = dout("y", [NTOK, D])
    k_o = dout("k_new", [NTOK, 512])
    v_o = dout("v_new", [NTOK, 512])
    lf_o = dout("lf_new", [NTOK, 8])
    hgp_o = dout("hg_p", [4 * 128, 128])
    hgs_o = dout("hg_s", [NS * 16 * 4 * 128, 128])
    cv_o = dout("cv_s", [NS * 128, D])

    wg_scr = dscr("wg_scr", [4 * NJ * 128, 1024], BF16)
    wu_scr = dscr("wu_scr", [4 * NJ * 128, 1024], BF16)
    wd_scr = dscr("wd_scr", [4 * 4 * 128, NJ * 256], BF16)
    win_scr = dscr("win_scr", [128, 8 * 3592], BF16)
    wout_scr = dscr("wout_scr", [128, 8 * D], BF16)
    cwin_scr = dscr("cwin_scr", [128, 8 * 2048], BF16)
    cwout_scr = dscr("cwout_scr", [128, 8 * D], BF16)
    x1_scr = dscr("x1_scr", [NTOK, D], F32)
    oh_scr = dscr("oh_scr", [NTOK, 512], BF16)
    of_scr = dscr("of_scr", [NTOK, 512], BF16)
    qT_scr = dscr("qT_scr", [NT * 128, 512], BF16)
    kT_scr = dscr("kT_scr", [NT * 128, 512], BF16)
    vp_scr = dscr("vp_scr", [NT * 128, 8 * 65], BF16)

    cst = sb("cstt", [128, NCST, 128])
    cstb = sb("cstb", [128, NCST, 128], BF16)
    identb = cstb[:, C_ID, :]
    ps = [nc.alloc_psum_tensor(f"ps{i}", [128, 512], F32) for i in range(8)]

    def psb(i):
        return ps[i][:].bitcast(BF16)

    xg = sb("xg", [128, 4, D])
    xT = sb("xT", [128, 8, 512], BF16)
    hT = sb("hT", [128, NJ, 512], BF16)
    wgs = [sb(f"wgs{i}", [128, 8, 128], BF16) for i in range(2)]
    wus = [sb(f"wus{i}", [128, 8, 128], BF16) for i in range(2)]
    wds = sb("wds", [128, NJ, 256], BF16)
    wps = [sb(f"wps{i}", [128, 8, 512], BF16) for i in range(2)]
    gb = [sb(f"gb{i}", [128, D]) for i in range(2)]
    zt = sb("zt", [128, D])
    tf = [sb(f"tf{i}", [128, D]) for i in range(4)]
    tb = [sb(f"tb{i}", [128, D], BF16) for i in range(4)]
    S.alias.update({"qh": "tf0", "kk": "tf0", "gl": "tf1", "gs": "tf1", "bcs": "tf2", "ebl": "tf2", "fk_f": "tf3", "fv_f": "tf3",
                    "vh": "tb0", "qtl": "tb0", "ktl": "tb1", "khat": "tb1", "ohb": "tb2", "fq_b": "tb2", "fk_b": "tb3",
                    "uu": "tf0", "vv": "tf1", "vvb": "tb0", "umx": "tb1", "ocat": "tb2", "tmpA": "tf2"})
    xb16 = sb("xb16", [128, D], BF16)
    sg_t = sb("sg_t", [128, 512], BF16)
    stats = sb("stats", [128, 2, 6])
    mv = sb("mv", [128, 2])
    rstd = sb("rstd", [128, 1])

    S.dma("sync", lambda e: e.dma_start(out=cst[:].rearrange("p a b -> p (a b)"), in_=cst_d[:, :]), writes=["cst"])
    S.op("vector", lambda e: e.tensor_copy(out=cstb[:], in_=cst[:]), reads=["cst"], writes=["cstb"])

    def V(fn, reads=(), writes=()):
        return S.op("vector", fn, reads, writes)

    def A(fn, reads=(), writes=()):
        return S.op("scalar", fn, reads, writes)

    def PE(fn, reads=(), writes=()):
        return S.op("tensor", fn, reads, writes)

    def bcast_load(dst, src_row, res):
        S.dma("sync", lambda e: e.dma_start(out=dst, in_=src_row.partition_broadcast(128)), writes=[res])

    def transpose_to(dstT, dst_res, src_bf, src_res, nk, col0, psi=7):
        pb = psb(psi)
        for k in range(nk):
            PE(lambda e, k=k: e.transpose(pb[:, k * 128:(k + 1) * 128], src_bf[:, k * 128:(k + 1) * 128], identb),
               reads=[src_res, "cstb"], writes=[f"ps{psi}"])
        V(lambda e: e.tensor_copy(out=dstT[:, 0:nk, col0:col0 + 128],
                                  in_=pb[:, 0:nk * 128].rearrange("p (k t) -> p k t", k=nk)),
          reads=[f"ps{psi}"], writes=[dst_res])

    def layer_norm(src, src_res, dst, dst_res, gi, g_ap=None, b_ap=None):
        ga = ln_g[gi:gi + 1, :] if g_ap is None else g_ap
        ba = ln_b[gi:gi + 1, :] if b_ap is None else b_ap
        bcast_load(gb[0][:], ga, "gb0")
        bcast_load(gb[1][:], ba, "gb1")
        for h in range(2):
            V(lambda e, h=h: e.bn_stats(out=stats[:, h, :], in_=src[:, h * 512:(h + 1) * 512]),
              reads=[src_res], writes=["stats"] if h == 0 else ["stats"])
        V(lambda e: e.bn_aggr(out=mv[:], in_=stats[:].rearrange("p a b -> p (a b)")), reads=["stats"], writes=["mv"])
        V(lambda e: e.tensor_scalar(out=rstd[:], in0=mv[:, 1:2], scalar1=EPS, scalar2=None, op0=ALU.add), reads=["mv"], writes=["rstd"])
        A(lambda e: e.activation(out=rstd[:], in_=rstd[:], func=AF.Sqrt), reads=["rstd"], writes=["rstd"])
        V(lambda e: e.reciprocal(out=rstd[:], in_=rstd[:]), reads=["rstd"], writes=["rstd"])
        V(lambda e: e.tensor_scalar(out=dst, in0=src, scalar1=mv[:, 0:1], scalar2=rstd[:, 0:1], op0=ALU.subtract, op1=ALU.mult),
          reads=[src_res, "mv", "rstd"], writes=[dst_res])
        V(lambda e: e.tensor_tensor(out=dst, in0=dst, in1=gb[0][:], op=ALU.mult), reads=[dst_res, "gb0"], writes=[dst_res])
        V(lambda e: e.tensor_tensor(out=dst, in0=dst, in1=gb[1][:], op=ALU.add), reads=[dst_res, "gb1"], writes=[dst_res])

    def build_xT(nt):
        for ti in range(nt):
            A(lambda e, ti=ti: e.activation(out=xb16[:], in_=xg[:, ti, :], func=AF.Copy), reads=[f"xg{ti}"], writes=["xb16"])
            transpose_to(xT, "xT", xb16, "xb16", 8, ti * 128)

    def ffn(fi, nt):
        ntk = nt * 128
        build_xT(nt)
        for j in range(NJ):
            b = j % 2
            rj = (fi * NJ + j) * 128
            S.dma("sync", lambda e, rj=rj, b=b: e.dma_start(out=wgs[b][:].rearrange("p k m -> p (k m)"), in_=wg_scr[rj:rj + 128, :]),
                  reads=[f"S_wg{fi}"], writes=[f"wgs{b}"])
            S.dma("sync", lambda e, rj=rj, b=b: e.dma_start(out=wus[b][:].rearrange("p k m -> p (k m)"), in_=wu_scr[rj:rj + 128, :]),
                  reads=[f"S_wu{fi}"], writes=[f"wus{b}"])
            pg, pu = ps[2 * b], ps[2 * b + 1]
            for k in range(8):
                PE(lambda e, k=k, b=b, pg=pg: e.matmul(pg[:, 0:ntk], lhsT=wgs[b][:, k, :], rhs=xT[:, k, 0:ntk], start=(k == 0), stop=(k == 7)),
                   reads=[f"wgs{b}", "xT"], writes=[f"ps{2 * b}"])
            for k in range(8):
                PE(lambda e, k=k, b=b, pu=pu: e.matmul(pu[:, 0:ntk], lhsT=wus[b][:, k, :], rhs=xT[:, k, 0:ntk], start=(k == 0), stop=(k == 7)),
                   reads=[f"wus{b}", "xT"], writes=[f"ps{2 * b + 1}"])
            A(lambda e, pg=pg: e.activation(out=sg_t[:, 0:ntk], in_=pg[:, 0:ntk], func=AF.Silu), reads=[f"ps{2 * b}"], writes=["sg_t"])
            V(lambda e, j=j, pu=pu: e.tensor_tensor(out=hT[:, j, 0:ntk], in0=sg_t[:, 0:ntk], in1=pu[:, 0:ntk], op=ALU.mult),
              reads=["sg_t", f"ps{2 * b + 1}"], writes=["hT"])
        li = (fi // 2) * 3 + (0 if fi % 2 == 0 else 2)
        for c in range(4):
            cs = slice(c * 256, (c + 1) * 256)
            rq = (fi * 4 + c) * 128
            S.dma("sync", lambda e, rq=rq: e.dma_start(out=wds[:].rearrange("p j n -> p (j n)"), in_=wd_scr[rq:rq + 128, :]), reads=[f"S_wd{fi}"], writes=["wds"])
            for ti in range(nt):
                pd = ps[4 + (ti % 2)]
                for j in range(NJ):
                    PE(lambda e, j=j, ti=ti, pd=pd: e.matmul(pd[:, 0:256], lhsT=hT[:, j, ti * 128:(ti + 1) * 128], rhs=wds[:, j, :], start=(j == 0), stop=(j == NJ - 1)),
                       reads=["hT", "wds"], writes=[f"ps{4 + ti % 2}"])
                A(lambda e, pd=pd, cs=cs: e.activation(out=zt[:, cs], in_=pd[:, 0:256], func=AF.Copy, scale=0.5),
                  reads=[f"ps{4 + ti % 2}"], writes=["zt"])
                V(lambda e, ti=ti, cs=cs: e.scalar_tensor_tensor(out=xg[:, ti, cs], in0=xg[:, ti, cs], scalar=ALPHA,
                                                                in1=zt[:, cs], op0=ALU.mult, op1=ALU.add),
                  reads=["zt", f"xg{ti}"], writes=[f"xg{ti}"])
        for ti in range(nt):
            layer_norm(xg[:, ti, :], f"xg{ti}", xg[:, ti, :], f"xg{ti}", li)

    def proj_tokmajor(w_ap, c0, ncols, ti, psi, wb):
        w3, wres = WSCR[w_ap]
        S.dma("sync", lambda e: e.dma_start(out=wps[wb][:, :, 0:ncols], in_=w3[:, :, c0:c0 + ncols]), reads=[wres], writes=[f"wps{wb}"])
        for k in range(8):
            PE(lambda e, k=k: e.matmul(ps[psi][:, 0:ncols], lhsT=xT[:, k, ti * 128:(ti + 1) * 128], rhs=wps[wb][:, k, 0:ncols], start=(k == 0), stop=(k == 7)),
               reads=["xT", f"wps{wb}"], writes=[f"ps{psi}"])

    stg = [sb(f"stg{i}", [128, 8, 256], BF16) for i in range(2)]
    stg_i = [0]

    def conv_cols(src, ncols, dst3, res):
        for c0 in range(0, ncols, 256):
            w = min(256, ncols - c0)
            b = stg_i[0] % 2; stg_i[0] += 1
            S.dma("gpsimd", lambda e, b=b, c0=c0, w=w: e.dma_start(out=stg[b][:, :, 0:w], in_=src[:, c0:c0 + w].rearrange("(k p) n -> p k n", p=128)), writes=[f"stg{b}"])
            S.dma("gpsimd", lambda e, b=b, c0=c0, w=w: e.dma_start(out=dst3[:, :, c0:c0 + w], in_=stg[b][:, :, 0:w]), reads=[f"stg{b}"], writes=[res])

    def conv_gu(src, scr, fi, res):
        for jj in range(NJ // 2):
            b = stg_i[0] % 2; stg_i[0] += 1
            S.dma("gpsimd", lambda e, b=b, jj=jj: e.dma_start(out=stg[b][:], in_=src[fi, :, jj * 256:(jj + 1) * 256].rearrange("(k p) n -> p k n", p=128)), writes=[f"stg{b}"])
            for c in range(2):
                r0 = (fi * NJ + jj * 2 + c) * 128
                S.dma("gpsimd", lambda e, b=b, c=c, r0=r0: e.dma_start(out=scr[r0:r0 + 128, :].rearrange("p (k m) -> p k m", k=8), in_=stg[b][:, :, c * 128:(c + 1) * 128]),
                      reads=[f"stg{b}"], writes=[res])

    def conv_down(fi, res):
        for q in range(4):
            r0 = (fi * 4 + q) * 128
            for (j0, nj) in ((0, 8), (8, 8), (16, 6)):
                b = stg_i[0] % 2; stg_i[0] += 1
                S.dma("gpsimd", lambda e, b=b, q=q, j0=j0, nj=nj: e.dma_start(
                    out=stg[b][:, 0:nj, :], in_=wdown[fi, j0 * 128:(j0 + nj) * 128, q * 256:(q + 1) * 256].rearrange("(j p) n -> p j n", p=128)), writes=[f"stg{b}"])
                S.dma("gpsimd", lambda e, b=b, r0=r0, j0=j0, nj=nj: e.dma_start(
                    out=wd_scr[r0:r0 + 128, :].rearrange("p (j n) -> p j n", j=NJ)[:, j0:j0 + nj, :], in_=stg[b][:, 0:nj, :]), reads=[f"stg{b}"], writes=[res])

    def conv_ffn(fi):
        conv_gu(wgate, wg_scr, fi, f"S_wg{fi}")
        conv_gu(wup, wu_scr, fi, f"S_wu{fi}")
        conv_down(fi, f"S_wd{fi}")

    win3 = win_scr.rearrange("p (k n) -> p k n", k=8)
    wout3 = wout_scr.rearrange("p (k n) -> p k n", k=8)
    cwin3 = cwin_scr.rearrange("p (k n) -> p k n", k=8)
    cwout3 = cwout_scr.rearrange("p (k n) -> p k n", k=8)
    conv_ffn(0)
    conv_cols(w_in, 3592, win3, "S_win")
    conv_cols(w_out, D, wout3, "S_wout")
    conv_ffn(1)
    conv_ffn(2)
    conv_cols(c_w_in, 2048, cwin3, "S_cwin")
    conv_cols(c_w_out, D, cwout3, "S_cwout")
    conv_ffn(3)
    WSCR = {"w_in": (win3, "S_win"), "w_out": (wout3, "S_wout"), "c_w_in": (cwin3, "S_cwin"), "c_w_out": (cwout3, "S_cwout")}

    lbt = [tf[0][:, 0:512], tf[0][:, 512:1024], tf[1][:, 0:512]]
    LR = ["tf0", "tf1"]
    lb = sb("lb", [128, 512])
    oml = sb("oml", [128, 512])
    tmpA = tf[2][:, 0:512]
    ng_bc = sb("ng_bc", [128, 128])
    fb_bc = sb("fb_bc", [128, 8])
    for a in range(3):
        S.dma("sync", lambda e, a=a: e.dma_start(out=lbt[a], in_=lb_log[0:1, a * 512:(a + 1) * 512].partition_broadcast(128)), writes=LR)
    bcast_load(ng_bc[:], hg_ng[0:1, :], "ng_bc")
    bcast_load(fb_bc[:], fbias[0:1, :], "fb_bc")
    V(lambda e: e.tensor_tensor(out=tmpA, in0=lbt[0], in1=lbt[1], op=ALU.max), reads=LR, writes=["tmpA"])
    V(lambda e: e.tensor_tensor(out=tmpA, in0=tmpA, in1=lbt[2], op=ALU.max), reads=LR + ["tmpA"], writes=["tmpA"])
    for a in range(3):
        V(lambda e, a=a: e.tensor_tensor(out=lbt[a], in0=lbt[a], in1=tmpA, op=ALU.subtract), reads=LR + ["tmpA"], writes=LR)
        A(lambda e, a=a: e.activation(out=lbt[a], in_=lbt[a], func=AF.Exp), reads=LR, writes=LR)
    V(lambda e: e.tensor_tensor(out=tmpA, in0=lbt[0], in1=lbt[1], op=ALU.add), reads=LR, writes=["tmpA"])
    V(lambda e: e.tensor_tensor(out=tmpA, in0=tmpA, in1=lbt[2], op=ALU.add), reads=LR + ["tmpA"], writes=["tmpA"])
    V(lambda e: e.reciprocal(out=tmpA, in_=tmpA), reads=["tmpA"], writes=["tmpA"])
    V(lambda e: e.tensor_tensor(out=lb[:], in0=lbt[0], in1=tmpA, op=ALU.mult), reads=LR + ["tmpA"], writes=["lb"])
    V(lambda e: e.tensor_scalar(out=oml[:], in0=lb[:], scalar1=-1.0, scalar2=1.0, op0=ALU.mult, op1=ALU.add), reads=["lb"], writes=["oml"])

    Fk = sb("Fk", [128, NT, 8])
    Fend = sb("Fend", [128, NT, 8])
    Sst = [sb(f"Sst{h}", [128, 128]) for h in range(4)]
    Sb = [sb(f"Sb{h}", [128, 128], BF16) for h in range(4)]
    qh = tf[0][:, 0:512]
    kk = tf[0][:, 512:1024]
    gl = tf[1][:, 0:512]
    vh = tb[0][:, 0:512]
    gs = tf[1][:, 512:1024]
    bcs = tf[2][:, 0:512]
    ebl = tf[2][:, 512:1024]
    qtl = tb[0][:, 512:1024]
    ktl = tb[1][:, 0:512]
    khat = tb[1][:, 512:1024]
    hbuf = [[sb(f"{nm}_{p}", [128, 128], BF16) for nm in ("qtT", "ktT", "attT", "qtT0", "qtT1")] for p in range(2)]
    dcol = sb("dcol", [128, 16])
    ohb = tb[2][:, 0:512]
    ssq = sb("ssq", [128, 4])
    junk = sb("junk", [128, 128])
    fq_b = tb[2][:, 512:1024]
    fk_f = tf[3][:, 0:512]
    fk_b = tb[3][:, 0:512]
    fv_f = tf[3][:, 512:1024]
    vp = sb("vp", [128, 8, 65], BF16)
    lft = sb("lft", [128, 8])
    tq = sb("tq", [128, 4, 128], BF16)
    S0f = sb("S0f", [128, 16, 128])
    S0b = sb("S0b", [128, 16, 128], BF16)
    qz = sb("qz", [128, 16, 128], BF16)
    vbd = sb("vbd", [128, 16, 128], BF16)

    for p in range(2):
        V(lambda e, p=p: e.memset(hbuf[p][3][:], 0.0), writes=[f"qtT0{p}"])
        V(lambda e, p=p: e.memset(hbuf[p][4][:], 0.0), writes=[f"qtT1{p}"])
    V(lambda e: e.memset(qz[:], 0.0), writes=["qz"])
    V(lambda e: e.memset(vp[:], 1.0), writes=["vp"])
    for h in range(4):
        V(lambda e, h=h: e.memset(Sst[h][:], 0.0), writes=[f"Sst{h}"])
        V(lambda e, h=h: e.memset(Sb[h][:], 0.0), writes=[f"Sb{h}"])

    KSTAGE = int(os.environ.get('KSTAGE', '9'))
    dcs = sb("dcs", [128, 64])

    def phaseA_tile(T, ti, sample):
        r0 = T * 128
        tri = C_TRI16 if sample else C_TRI2
        blk = C_BLK16 if sample else C_BLK2
        proj_tokmajor("w_in", 0, 512, ti, 0, 0)
        A(lambda e: e.activation(out=qh[:], in_=ps[0][:], func=AF.Silu), reads=["ps0"], writes=["qh"])
        proj_tokmajor("w_in", 512, 512, ti, 1, 1)
        A(lambda e: e.activation(out=kk[:], in_=ps[1][:], func=AF.Sigmoid), reads=["ps1"], writes=["kk"])
        V(lambda e: e.tensor_tensor(out=gl[:], in0=kk[:], in1=oml[:], op=ALU.mult), reads=["kk", "oml"], writes=["gl"])
        V(lambda e: e.tensor_tensor(out=gl[:], in0=gl[:], in1=lb[:], op=ALU.add), reads=["gl", "lb"], writes=["gl"])
        V(lambda e: e.tensor_scalar(out=kk[:], in0=gl[:], scalar1=-1.0, scalar2=1.0, op0=ALU.mult, op1=ALU.add), reads=["gl"], writes=["kk"])
        A(lambda e: e.activation(out=gl[:], in_=gl[:], func=AF.Ln), reads=["gl"], writes=["gl"])
        proj_tokmajor("w_in", 1024, 512, ti, 0, 0)
        A(lambda e: e.activation(out=vh[:], in_=ps[0][:], func=AF.Copy), reads=["ps0"], writes=["vh"])
        proj_tokmajor("w_in", 1536, 512, ti, 1, 1)
        A(lambda e: e.activation(out=gs[:], in_=ps[1][:], func=AF.Silu), reads=["ps1"], writes=["gs"])
        gs3 = gs.rearrange("p (h v) -> p h v", h=4)
        V(lambda e: e.tensor_tensor(out=gs3, in0=gs3, in1=ng_bc[:].unsqueeze(1).broadcast_to([128, 4, 128]), op=ALU.mult), reads=["gs", "ng_bc"], writes=["gs"])
        if KSTAGE < 2:
            return
        PE(lambda e: e.matmul(ps[2][:], lhsT=cst[:, tri, :], rhs=gl[:], start=True, stop=True), reads=["cst", "gl"], writes=["ps2"])
        PE(lambda e: e.matmul(ps[3][:], lhsT=cst[:, blk, :], rhs=gl[:], start=True, stop=True), reads=["cst", "gl"], writes=["ps3"])
        A(lambda e: e.activation(out=bcs[:], in_=ps[2][:], func=AF.Exp), reads=["ps2"], writes=["bcs"])
        V(lambda e: e.tensor_tensor(out=qtl[:], in0=qh[:], in1=bcs[:], op=ALU.mult), reads=["qh", "bcs"], writes=["qtl"])
        A(lambda e: e.activation(out=bcs[:], in_=ps[2][:], func=AF.Exp, scale=-1.0), reads=["ps2"], writes=["bcs"])
        V(lambda e: e.tensor_tensor(out=ktl[:], in0=kk[:], in1=bcs[:], op=ALU.mult), reads=["kk", "bcs"], writes=["ktl"])
        A(lambda e: e.activation(out=ebl[:], in_=ps[3][:], func=AF.Exp), reads=["ps3"], writes=["ebl"])
        V(lambda e: e.tensor_tensor(out=bcs[:], in0=bcs[:], in1=ebl[:], op=ALU.mult), reads=["bcs", "ebl"], writes=["bcs"])
        V(lambda e: e.tensor_tensor(out=khat[:], in0=kk[:], in1=bcs[:], op=ALU.mult), reads=["kk", "bcs"], writes=["khat"])
        nb = 16 if sample else 2
        selc = C_SELEND16 if sample else C_SELEND2
        for h in range(4):
            PE(lambda e, h=h: e.matmul(ps[3][:, 256 + h * 16:256 + h * 16 + nb], lhsT=ebl[:, h * 128:(h + 1) * 128], rhs=cst[:, selc, 0:nb], start=True, stop=True),
               reads=["ebl", "cst"], writes=["ps3"])
        if sample:
            sq0 = (T - SMP) * 16
            V(lambda e: e.tensor_copy(out=dcs[:], in_=ps[3][:, 256:320]), reads=["ps3"], writes=["dcs"])
        else:
            V(lambda e: e.tensor_copy(out=dcol[:].rearrange("p (h c) -> p h c", h=4)[:, :, 0:2],
                                      in_=ps[3][:, 256:320].rearrange("p (h c) -> p h c", h=4)[:, :, 0:2]), reads=["ps3"], writes=["dcol"])
        if KSTAGE < 3:
            return
        def do_head(h):
            hs = slice(h * 128, (h + 1) * 128)
            par = h % 2
            bo, bs_, bt = (5, 6, 7) if par == 0 else (1, 2, 0)
            pso, pss = ps[bo], ps[bs_]
            qtT, ktT, attT, qtT0, qtT1 = hbuf[par]
            RqtT, RktT, RattT, RqtT0, RqtT1 = [f"{nm}{par}" for nm in ("qtT", "ktT", "attT", "qtT0", "qtT1")]
            Rpo, Rps, Rpt = f"ps{bo}", f"ps{bs_}", f"ps{bt}"
            pb = psb(bt)
            PE(lambda e, hs=hs: e.transpose(pb[:, 0:128], qtl[:, hs], identb), reads=["qtl", "cstb"], writes=[Rpt])
            PE(lambda e, hs=hs: e.transpose(pb[:, 128:256], ktl[:, hs], identb), reads=["ktl", "cstb"], writes=[Rpt])
            V(lambda e: e.tensor_copy(out=qtT[:], in_=pb[:, 0:128]), reads=[Rpt], writes=[RqtT])
            A(lambda e: e.activation(out=ktT[:], in_=pb[:, 128:256], func=AF.Copy), reads=[Rpt], writes=[RktT])
            yield
            PE(lambda e: e.matmul(pss[:, 0:128], lhsT=ktT[:], rhs=qtT[:], start=True, stop=True), reads=[RktT, RqtT], writes=[Rps])
            V(lambda e: e.tensor_tensor(out=attT[:], in0=pss[:, 0:128], in1=cst[:, tri, :], op=ALU.mult), reads=[Rps, "cst"], writes=[RattT])
            if not sample:
                V(lambda e: e.tensor_copy(out=qtT0[:, 0:64], in_=qtT[:, 0:64]), reads=[RqtT], writes=[RqtT0])
                V(lambda e: e.tensor_copy(out=qtT1[:, 64:128], in_=qtT[:, 64:128]), reads=[RqtT], writes=[RqtT1])
                yield
                PE(lambda e, hs=hs: e.matmul(pss[:, 128:256], lhsT=khat[0:64, hs], rhs=vh[0:64, hs], start=True, stop=True),
                   reads=["khat", "vh"], writes=[Rps])
                PE(lambda e, hs=hs: e.matmul(pso[:, 0:128], lhsT=attT[:], rhs=vh[:, hs], start=True, stop=False), reads=[RattT, "vh"], writes=[Rpo])
                PE(lambda e, h=h: e.matmul(pso[:, 0:128], lhsT=qtT0[:], rhs=Sb[h][:], start=False, stop=False), reads=[RqtT0, f"Sb{h}"], writes=[Rpo])
                V(lambda e, h=h: e.scalar_tensor_tensor(out=Sst[h][:], in0=Sst[h][:], scalar=dcol[:, h * 4:h * 4 + 1], in1=pss[:, 128:256], op0=ALU.mult, op1=ALU.add),
                  reads=[f"Sst{h}", "dcol", Rps], writes=[f"Sst{h}"])
                A(lambda e, h=h: e.activation(out=Sb[h][:], in_=Sst[h][:], func=AF.Copy), reads=[f"Sst{h}"], writes=[f"Sb{h}"])
                yield
                PE(lambda e, h=h: e.matmul(pso[:, 0:128], lhsT=qtT1[:], rhs=Sb[h][:], start=False, stop=True), reads=[RqtT1, f"Sb{h}"], writes=[Rpo])
                PE(lambda e, hs=hs: e.matmul(pss[:, 256:384], lhsT=khat[64:128, hs], rhs=vh[64:128, hs], start=True, stop=True),
                   reads=["khat", "vh"], writes=[Rps])
                V(lambda e, h=h: e.scalar_tensor_tensor(out=Sst[h][:], in0=Sst[h][:], scalar=dcol[:, h * 4 + 1:h * 4 + 2], in1=pss[:, 256:384], op0=ALU.mult, op1=ALU.add),
                  reads=[f"Sst{h}", "dcol", Rps], writes=[f"Sst{h}"])
                A(lambda e, h=h: e.activation(out=Sb[h][:], in_=Sst[h][:], func=AF.Copy), reads=[f"Sst{h}"], writes=[f"Sb{h}"])
            else:
                S.dma("sync", lambda e, h=h: e.dma_start(out=S0f[:], in_=st_hg.rearrange("(q h p) v -> h p q v", h=4, p=128)[h][:, sq0:sq0 + 16, :]), writes=["S0f"])
                A(lambda e: e.activation(out=S0b[:], in_=S0f[:], func=AF.Copy), reads=["S0f"], writes=["S0b"])
                for q in range(16):
                    V(lambda e, q=q: e.tensor_copy(out=qz[:, q, q * 8:(q + 1) * 8], in_=qtT[:, q * 8:(q + 1) * 8]), reads=[RqtT], writes=["qz"])
                yield
                PE(lambda e, hs=hs: e.matmul(pso[:, 0:128], lhsT=attT[:], rhs=vh[:, hs], start=True, stop=False), reads=[RattT, "vh"], writes=[Rpo])
                for q in range(16):
                    PE(lambda e, q=q, h=h: e.matmul(pso[:, 0:128], lhsT=qz[:, q, :], rhs=S0b[:, q, :], start=False, stop=(q == 15)),
                       reads=["qz", "S0b"], writes=[Rpo])
                for q in range(16):
                    V(lambda e, q=q, hs=hs: e.tensor_scalar(out=vbd[:, q, :], in0=vh[:, hs], scalar1=cst[:, C_BLK16, q * 8:q * 8 + 1], scalar2=None, op0=ALU.mult),
                      reads=["vh", "cst"], writes=["vbd"])
                dbk = [0, 1, 2, 3] if par == 0 else [3, 4, 5, 6]
                for c4 in range(4):
                    PE(lambda e, c4=c4, hs=hs: e.matmul(ps[dbk[c4]][:, :], lhsT=khat[:, hs], rhs=vbd[:, c4 * 4:(c4 + 1) * 4, :].rearrange("p a b -> p (a b)"), start=True, stop=True),
                       reads=["khat", "vbd"], writes=[f"ps{dbk[c4]}"])
                for q in range(16):
                    V(lambda e, q=q, h=h: e.scalar_tensor_tensor(out=S0f[:, q, :], in0=S0f[:, q, :], scalar=dcs[:, h * 16 + q:h * 16 + q + 1],
                                                                 in1=ps[dbk[q // 4]][:, (q % 4) * 128:(q % 4 + 1) * 128], op0=ALU.mult, op1=ALU.add),
                      reads=["S0f", "dcs", f"ps{dbk[q // 4]}"], writes=["S0f"])
                S.dma("sync", lambda e, h=h: e.dma_start(out=hgs_o.rearrange("(q h p) v -> h p q v", h=4, p=128)[h][:, sq0:sq0 + 16, :], in_=S0f[:]), reads=["S0f"], is_output=True)
            yield
            A(lambda e, h=h: e.activation(out=junk[:], in_=pso[:, 0:128], func=AF.Square, accum_out=ssq[:, h:h + 1]), reads=[Rpo], writes=["junk", f"ssq{h}"])
            V(lambda e, h=h: e.tensor_scalar(out=ssq[:, h:h + 1], in0=ssq[:, h:h + 1], scalar1=1.0 / 128, scalar2=EPS, op0=ALU.mult, op1=ALU.add), reads=[f"ssq{h}"], writes=[f"ssq{h}"])
            A(lambda e, h=h: e.activation(out=ssq[:, h:h + 1], in_=ssq[:, h:h + 1], func=AF.Sqrt), reads=[f"ssq{h}"], writes=[f"ssq{h}"])
            V(lambda e, h=h: e.reciprocal(out=ssq[:, h:h + 1], in_=ssq[:, h:h + 1]), reads=[f"ssq{h}"], writes=[f"ssq{h}"])
            V(lambda e, h=h, hs=hs: e.scalar_tensor_tensor(out=ohb[:, hs], in0=pso[:, 0:128], scalar=ssq[:, h:h + 1], in1=gs[:, hs], op0=ALU.mult, op1=ALU.mult),
              reads=[Rpo, f"ssq{h}", "gs"], writes=["ohb"])
        for h0 in (0, 2):
            if sample:
                for h in (h0, h0 + 1):
                    for _ in do_head(h):
                        pass
            else:
                gens = [do_head(h0), do_head(h0 + 1)]
                while gens:
                    for g in list(gens):
                        try:
                            next(g)
                        except StopIteration:
                            gens.remove(g)
        if KSTAGE < 4:
            return
        S.dma("sync", lambda e: e.dma_start(out=oh_scr[r0:r0 + 128, :], in_=ohb[:]), reads=["ohb"], writes=["oh_scr"])
        proj_tokmajor("w_in", 2048, 512, ti, 0, 0)
        A(lambda e: e.activation(out=fq_b[:], in_=ps[0][:], func=AF.Copy, scale=0.125), reads=["ps0"], writes=["fq_b"])
        proj_tokmajor("w_in", 2560, 512, ti, 1, 1)
        A(lambda e: e.activation(out=fk_f[:], in_=ps[1][:], func=AF.Copy), reads=["ps1"], writes=["fk_f"])
        V(lambda e: e.tensor_copy(out=fk_b[:], in_=ps[1][:]), reads=["ps1"], writes=["fk_b"])
        S.dma("sync", lambda e: e.dma_start(out=k_o[r0:r0 + 128, :], in_=fk_f[:]), reads=["fk_f"], is_output=True)
        proj_tokmajor("w_in", 3072, 512, ti, 0, 0)
        A(lambda e: e.activation(out=fv_f[:], in_=ps[0][:], func=AF.Copy), reads=["ps0"], writes=["fv_f"])
        V(lambda e: e.tensor_copy(out=vp[:, :, 0:64], in_=ps[0][:].rearrange("p (h d) -> p h d", h=8)), reads=["ps0"], writes=["vp"])
        S.dma("sync", lambda e: e.dma_start(out=v_o[r0:r0 + 128, :], in_=fv_f[:]), reads=["fv_f"], is_output=True)
        S.dma("sync", lambda e: e.dma_start(out=vp_scr[r0:r0 + 128, :], in_=vp[:].rearrange("p h d -> p (h d)")), reads=["vp"], writes=["vp_scr"])
        if KSTAGE < 5:
            return
        proj_tokmajor("w_in", 3584, 8, ti, 1, 1)
        V(lambda e: e.tensor_tensor(out=lft[:], in0=ps[1][:, 0:8], in1=fb_bc[:], op=ALU.add), reads=["ps1", "fb_bc"], writes=["lft"])
        A(lambda e: e.activation(out=lft[:], in_=lft[:], func=AF.Exp, scale=-1.0), reads=["lft"], writes=["lft"])
        A(lambda e: e.activation(out=lft[:], in_=lft[:], func=AF.Ln, bias=1.0), reads=["lft"], writes=["lft"])
        V(lambda e: e.tensor_scalar(out=lft[:], in0=lft[:], scalar1=-1.0, scalar2=None, op0=ALU.mult), reads=["lft"], writes=["lft"])
        S.dma("sync", lambda e: e.dma_start(out=lf_o[r0:r0 + 128, :], in_=lft[:]), reads=["lft"], is_output=True)
        if not sample:
            PE(lambda e: e.matmul(ps[2][:, 0:8], lhsT=cst[:, C_TRI, :], rhs=lft[:], start=True, stop=(T == 0)), reads=["cst", "lft"], writes=["ps2"])
            if T > 0:
                PE(lambda e: e.matmul(ps[2][:, 0:8], lhsT=cst[:, C_SEL127, :], rhs=Fk[:, T - 1, :], start=False, stop=True), reads=["cst", "Fk"], writes=["ps2"])
            V(lambda e: e.tensor_copy(out=Fk[:, T, :], in_=ps[2][:, 0:8]), reads=["ps2"], writes=["Fk"])
            PE(lambda e: e.matmul(ps[2][:, 8:16], lhsT=cst[:, C_SEL127, :], rhs=Fk[:, T, :], start=True, stop=True), reads=["cst", "Fk"], writes=["ps2"])
            V(lambda e: e.tensor_copy(out=Fend[:, T, :], in_=ps[2][:, 8:16]), reads=["ps2"], writes=["Fend"])
        else:
            PE(lambda e: e.matmul(ps[2][:, 0:8], lhsT=cst[:, C_TRI16, :], rhs=lft[:], start=True, stop=True), reads=["cst", "lft"], writes=["ps2"])
            V(lambda e: e.tensor_copy(out=Fk[:, T, :], in_=ps[2][:, 0:8]), reads=["ps2"], writes=["Fk"])
        for (src, res, scr) in ((fq_b, "fq_b", qT_scr), (fk_b, "fk_b", kT_scr)):
            transpose_to(tq, "tq", src, res, 4, 0, psi=7)
            S.dma("sync", lambda e, scr=scr: e.dma_start(out=scr[r0:r0 + 128, :], in_=tq[:].rearrange("p a b -> p (a b)")), reads=["tq"], writes=[scr.tensor.name])

    PH0 = os.environ.get('KPH', 'ABSC')
    for G in range(NPT // 4):
        for ti in range(4):
            T = G * 4 + ti
            S.dma("sync", lambda e, T=T, ti=ti: e.dma_start(out=xg[:, ti, :], in_=xin[T * 128:(T + 1) * 128, :]), writes=[f"xg{ti}"])
        if 'f' not in PH0:
            ffn(0, 4)
        build_xT(4)
        for ti in range(4):
            T = G * 4 + ti
            S.dma("sync", lambda e, T=T, ti=ti: e.dma_start(out=x1_scr[T * 128:(T + 1) * 128, :], in_=xg[:, ti, :]), reads=[f"xg{ti}"], writes=["x1_scr"])
            if 'F' in PH0:
                S.dma("sync", lambda e, T=T, ti=ti: e.dma_start(out=y_o[T * 128:(T + 1) * 128, :], in_=xg[:, ti, :]), reads=[f"xg{ti}"], is_output=True)
            else:
                phaseA_tile(T, ti, False)
    for h in range(4):
        S.dma("sync", lambda e, h=h: e.dma_start(out=hgp_o[h * 128:(h + 1) * 128, :], in_=Sst[h][:]), reads=[f"Sst{h}"], is_output=True)
    for st in range(NS):
        S.dma("sync", lambda e, st=st: e.dma_start(out=xg[:, st, :], in_=xin[(SMP + st) * 128:(SMP + st + 1) * 128, :]), writes=[f"xg{st}"])
    if 'F' in PH0:
        S.emit()
        return nc
    ffn(0, NS)
    build_xT(NS)
    for st in range(NS):
        S.dma("sync", lambda e, st=st: e.dma_start(out=x1_scr[(SMP + st) * 128:(SMP + st + 1) * 128, :], in_=xg[:, st, :]), reads=[f"xg{st}"], writes=["x1_scr"])
        phaseA_tile(SMP + st, st, True)

    PH = os.environ.get('KPH', 'ABSC')
    NKB = 2
    kTt = [sb(f"kTt{i}", [128, 4, 128], BF16) for i in range(NKB)]
    vpt = [sb(f"vpt{i}", [128, 8, 65], BF16) for i in range(NKB)]
    qTt = sb("qTt", [128, 4, 128], BF16)
    wq = [sb(f"wq{i}", [128, 8]) for i in range(2)]
    Vs = [sb(f"Vs{i}", [128, 8, 65], BF16) for i in range(2)]
    pT = [sb(f"pT{i}", [128, 2, 4, 128], BF16) for i in range(2)]
    ofb = sb("ofb", [128, 512], BF16)
    rs = sb("rs", [128, 8])
    G_ = lambda fn, reads=(), writes=(): S.op(os.environ.get("KGENG", "vector"), fn, reads, writes)

    qTt2 = [qTt, sb("qTt1", [128, 4, 128], BF16)]

    def attn_qk(b, kq, k_tile, k_res, qt_tile, qt_res, q_cols, v_src, v_res, w_ap, w_res, mask_c, pt_out):
        V(lambda e: e.tensor_tensor(out=Vs[b][:], in0=v_src[:], in1=w_ap[:, :].unsqueeze(2).broadcast_to([128, 8, 65]), op=ALU.mult),
          reads=[v_res, w_res], writes=[f"Vs{b}"])
        nq = q_cols.stop - q_cols.start
        for h in range(8):
            pr, po = h // 2, (h % 2) * 64
            bank = 2 * kq + h % 2
            PE(lambda e, h=h, pr=pr, po=po, bank=bank: e.matmul(ps[bank][:, (h // 2) * nq:(h // 2 + 1) * nq], lhsT=k_tile[po:po + 64, pr, :], rhs=qt_tile[po:po + 64, pr, q_cols],
                                                              start=True, stop=True, skip_group_check=True),
               reads=[k_res, qt_res], writes=[f"ps{bank}"])
        for half in range(2):
            bank = 2 * kq + half
            A(lambda e, half=half, bank=bank: e.activation(out=pt_out(half), in_=ps[bank][:, 0:4 * nq].rearrange("p (h q) -> p h q", h=4), func=AF.Exp),
              reads=[f"ps{bank}"], writes=[f"pT{b}"])
        if mask_c is not None:
            pv = pT[b][:].rearrange("p a b q -> p (a b) q")
            V(lambda e: e.tensor_tensor(out=pv, in0=pv, in1=cstb[:, mask_c, :].unsqueeze(1).broadcast_to([128, 8, 128]), op=ALU.mult), reads=[f"pT{b}", "cstb"], writes=[f"pT{b}"])

    def attn_pv(b, acc0, first, last):
        for h in range(8):
            bank, c0 = acc0 + h // 4, (h % 4) * 65
            PE(lambda e, h=h, bank=bank, c0=c0: e.matmul(ps[bank][:, c0:c0 + 65], lhsT=pT[b][:, h % 2, h // 2, :], rhs=Vs[b][:, h, :],
                                                       start=(first and h % 4 == 0), stop=last, skip_group_check=True),
               reads=[f"pT{b}", f"Vs{b}"], writes=[f"ps{bank}"])

    def attn_finish(row0, acc0):
        for half in range(2):
            bank = acc0 + half
            pv3 = ps[bank][:, 0:260].rearrange("p (h c) -> p h c", c=65)
            V(lambda e, half=half, pv3=pv3: e.reciprocal(out=rs[:, half * 4:(half + 1) * 4].unsqueeze(2), in_=pv3[:, :, 64:65]), reads=[f"ps{bank}"], writes=["rs"])
            V(lambda e, half=half, pv3=pv3: e.tensor_tensor(out=ofb[:, half * 256:(half + 1) * 256].rearrange("p (h d) -> p h d", d=64), in0=pv3[:, :, 0:64],
                                                          in1=rs[:, half * 4:(half + 1) * 4].unsqueeze(2).broadcast_to([128, 4, 64]), op=ALU.mult),
              reads=[f"ps{bank}", "rs"], writes=["ofb"])
        S.dma("sync", lambda e: e.dma_start(out=of_scr[row0:row0 + 128, :], in_=ofb[:]), reads=["ofb"], writes=["of_scr"])

    def run_pipelined(items):
        if not items:
            return
        items[0][0]()
        for i in range(len(items)):
            if i + 1 < len(items):
                items[i + 1][0]()
            items[i][1]()

    items = []
    pairs = [(qt, kt) for qt in range(NPT if 'B' in PH else 0) for kt in range(qt + 1)]
    for i, (qt, kt) in enumerate(pairs):
        b = i % 2; g = i % NKB; qb = qt % 2; acc0 = 4 if qt % 2 == 0 else 6

        def qk(qt=qt, kt=kt, b=b, g=g, qb=qb):
            if kt == 0:
                S.dma("sync", lambda e: e.dma_start(out=qTt2[qb][:].rearrange("p a b -> p (a b)"), in_=qT_scr[qt * 128:(qt + 1) * 128, :]), reads=["qT_scr"], writes=[f"qTt{qb}"])
            S.dma("sync", lambda e: e.dma_start(out=kTt[g][:].rearrange("p a b -> p (a b)"), in_=kT_scr[kt * 128:(kt + 1) * 128, :]), reads=["kT_scr"], writes=[f"kTt{g}"])
            S.dma("sync", lambda e: e.dma_start(out=vpt[g][:].rearrange("p a b -> p (a b)"), in_=vp_scr[kt * 128:(kt + 1) * 128, :]), reads=["vp_scr"], writes=[f"vpt{g}"])
            V(lambda e: e.tensor_tensor(out=wq[b][:], in0=Fend[:, qt, :], in1=Fk[:, kt, :], op=ALU.subtract), reads=["Fend", "Fk"], writes=[f"wq{b}"])
            V(lambda e: e.tensor_scalar(out=wq[b][:], in0=wq[b][:], scalar1=0.0, scalar2=None, op0=ALU.min), reads=[f"wq{b}"], writes=[f"wq{b}"])
            A(lambda e: e.activation(out=wq[b][:], in_=wq[b][:], func=AF.Exp), reads=[f"wq{b}"], writes=[f"wq{b}"])
            attn_qk(b, b, kTt[g], f"kTt{g}", qTt2[qb], f"qTt{qb}", slice(0, 128), vpt[g], f"vpt{g}", wq[b], f"wq{b}",
                    C_TRI if kt == qt else None, lambda half: pT[b][:, half, :, :])

        def pv(qt=qt, kt=kt, b=b, acc0=acc0):
            attn_pv(b, acc0, kt == 0, kt == qt)
            if kt == qt:
                attn_finish(qt * 128, acc0)
        items.append((qk, pv))
    run_pipelined(items)

    def sample_attn():
        pts = sb("pts", [128, NS * 256], I32)
        pidx = pts
        iot = sb("iot", [128, 1], I32)
        NKS = 2
        kpg = [sb(f"kpg{i}", [128, 512], BF16) for i in range(NKS)]
        vpg = [sb(f"vpg{i}", [128, 8, 65], BF16) for i in range(2)]
        vgt = [sb(f"vgt{i}", [128, 512], BF16) for i in range(NKS)]
        lfp = sb("lfp", [128, 16, 8])
        lat = sb("lat", [128, 16, 8])
        Rb = sb("Rb", [128, 16, 8])
        kTp = [sb(f"kTp{i}", [128, 4, 128], BF16) for i in range(2)]
        S.dma("sync", lambda e: e.dma_start(out=pts[:], in_=ptab[0:1, :].partition_broadcast(128)), writes=["pts"])
        S.op("gpsimd", lambda e: e.iota(iot[:], pattern=[[0, 1]], base=0, channel_multiplier=1), writes=["iot"])
        S.op("gpsimd", lambda e: e.tensor_scalar(out=pidx[:], in0=pts[:], scalar1=128, scalar2=iot[:, 0:1], op0=ALU.mult, op1=ALU.add), reads=["pts", "iot"], writes=["pts"])
        for i in range(2):
            V(lambda e, i=i: e.memset(vpg[i][:], 1.0), writes=[f"vpg{i}"])

        def seq_pre(st, q):
            for pg in range(16):
                S.dma("gpsimd", lambda e, pg=pg: e.indirect_dma_start(out=lfp[:, pg, :], out_offset=None, in_=clf,
                                                                     in_offset=bass.IndirectOffsetOnAxis(ap=pidx[:, st * 256 + q * 16 + pg:st * 256 + q * 16 + pg + 1], axis=0)),
                      reads=["pts"], writes=["lfp"])
            V(lambda e: e.memset(lat[:, 15, :], 0.0), writes=["lat"])
            for pg in range(14, -1, -1):
                V(lambda e, pg=pg: e.tensor_tensor(out=lat[:, pg, :], in0=lat[:, pg + 1, :], in1=lfp[:, pg + 1, :], op=ALU.add), reads=["lat", "lfp"], writes=["lat"])
            PE(lambda e: e.matmul(ps[6][:, 0:128], lhsT=cst[:, C_SUP, :], rhs=lfp[:].rearrange("p a b -> p (a b)"), start=True, stop=False), reads=["cst", "lfp"], writes=["ps6"])
            PE(lambda e: e.matmul(ps[6][:, 0:128], lhsT=cst[:, C_ONES, :], rhs=lat[:].rearrange("p a b -> p (a b)"), start=False, stop=True), reads=["cst", "lat"], writes=["ps6"])
            A(lambda e: e.activation(out=Rb[:].rearrange("p a b -> p (a b)"), in_=ps[6][:, 0:128], func=AF.Exp), reads=["ps6"], writes=["Rb"])

        for st in range(NS):
            TS = SMP + st
            items = []

            def qk0(TS=TS):
                S.dma("sync", lambda e: e.dma_start(out=qTt[:].rearrange("p a b -> p (a b)"), in_=qT_scr[TS * 128:(TS + 1) * 128, :]), reads=["qT_scr"], writes=["qTt0"])
                S.dma("sync", lambda e: e.dma_start(out=kTt[0][:].rearrange("p a b -> p (a b)"), in_=kT_scr[TS * 128:(TS + 1) * 128, :]), reads=["kT_scr"], writes=["kTt0"])
                S.dma("sync", lambda e: e.dma_start(out=vpt[0][:].rearrange("p a b -> p (a b)"), in_=vp_scr[TS * 128:(TS + 1) * 128, :]), reads=["vp_scr"], writes=["vpt0"])
                A(lambda e: e.activation(out=wq[0][:], in_=Fk[:, TS, :], func=AF.Exp, scale=-1.0), reads=["Fk"], writes=["wq0"])
                attn_qk(0, 0, kTt[0], "kTt0", qTt, "qTt0", slice(0, 128), vpt[0], "vpt0", wq[0], "wq0", C_TRI16, lambda half: pT[0][:, half, :, :])

            def pv0():
                attn_pv(0, 4, True, False)
                V(lambda e: e.memset(pT[0][:], 0.0), writes=["pT0"])
            items.append((qk0, pv0))
            n = 0
            for q in range(16):
                for pg in range(16):
                    n += 1
                    b = n % 2; g = n % NKS
                    col = st * 256 + q * 16 + pg

                    def qk(st=st, q=q, pg=pg, b=b, g=g, col=col, n=n):
                        if n == 1:
                            V(lambda e: e.memset(pT[1][:], 0.0), writes=["pT1"])
                        if pg == 0:
                            seq_pre(st, q)
                        if q > 0 and pg < 2:
                            V(lambda e: e.memset(pT[b][:].rearrange("p a b q -> p (a b) q")[:, :, (q - 1) * 8:q * 8], 0.0), writes=[f"pT{b}"])
                        S.dma("gpsimd", lambda e: e.indirect_dma_start(out=kpg[g][:], out_offset=None, in_=ck,
                                                                     in_offset=bass.IndirectOffsetOnAxis(ap=pidx[:, col:col + 1], axis=0)),
                              reads=["pts"], writes=[f"kpg{g}"])
                        S.dma("gpsimd", lambda e: e.indirect_dma_start(out=vgt[g][:], out_offset=None, in_=cv_,
                                                                     in_offset=bass.IndirectOffsetOnAxis(ap=pidx[:, col:col + 1], axis=0)),
                              reads=["pts"], writes=[f"vgt{g}"])
                        V(lambda e: e.tensor_copy(out=vpg[b][:, :, 0:64], in_=vgt[g][:].rearrange("p (h d) -> p h d", h=8)), reads=[f"vgt{g}"], writes=[f"vpg{b}"])
                        transpose_to(kTp[b], f"kTp{b}", kpg[g], f"kpg{g}", 4, 0, psi=7)
                        attn_qk(b, b, kTp[b], f"kTp{b}", qTt, "qTt0", slice(q * 8, (q + 1) * 8), vpg[b], f"vpg{b}", Rb[:, pg, :], "Rb", None,
                                lambda half: pT[b][:, half, :, q * 8:(q + 1) * 8])

                    def pv(b=b, q=q, pg=pg):
                        attn_pv(b, 4, False, q == 15 and pg == 15)
                    items.append((qk, pv))
            run_pipelined(items)
            attn_finish(TS * 128, 4)

    if 'S' in PH:
        sample_attn()

    ocat = tb[2][:, :]
    uu = tf[0][:, :]
    vv = tf[1][:, :]
    vvb = tb[0][:, :]
    wsT = sb("wsT", [128, 8, 128], BF16)
    wsS = sb("wsS", [128, 8, 128], BF16)
    bsT = sb("bsT", [128, 8])
    bsS = sb("bsS", [128, 8])
    wsl = sb("wsl", [128, 128])
    bsl = sb("bsl", [8, 128])
    w8 = sb("w8", [8, 8])
    b8 = sb("b8", [8, 128])
    umx = tb[1][:, :]
    gq = tf[2][:, 0:512]
    S.alias["gq"] = "tf2"

    S.dma("sync", lambda e: e.dma_start(out=bsl[:], in_=c_b_s[:, :]), writes=["bsl"])
    PE(lambda e: e.transpose(ps[0][:, 0:8], bsl[:], cst[0:8, C_ID, 0:8]), reads=["bsl", "cst"], writes=["ps0"])
    V(lambda e: e.tensor_copy(out=bsT[:], in_=ps[0][:, 0:8]), reads=["ps0"], writes=["bsT"])
    PE(lambda e: e.matmul(ps[0][:, 8:16], lhsT=cst[0:8, C_R, :], rhs=bsT[0:8, :], start=True, stop=True), reads=["cst", "bsT"], writes=["ps0"])
    V(lambda e: e.tensor_copy(out=bsS[:], in_=ps[0][:, 8:16]), reads=["ps0"], writes=["bsS"])
    for g in range(8):
        S.dma("sync", lambda e, g=g: e.dma_start(out=wsl[:], in_=c_w_s[g * 128:(g + 1) * 128, :]), writes=["wsl"])
        PE(lambda e: e.transpose(ps[1][:, 0:128], wsl[:], cst[:, C_ID, :]), reads=["wsl", "cst"], writes=["ps1"])
        V(lambda e, g=g: e.tensor_tensor(out=wsT[:, g, :], in0=ps[1][:, 0:128], in1=cst[:, C_TRI, :], op=ALU.mult), reads=["ps1", "cst"], writes=["wsT"])
        PE(lambda e: e.matmul(ps[1][0:8, 128:256], lhsT=wsl[0:8, 0:8], rhs=cst[0:8, C_R, :], start=True, stop=True), reads=["wsl", "cst"], writes=["ps1"])
        V(lambda e: e.tensor_copy(out=b8[:], in_=ps[1][0:8, 128:256]), reads=["ps1"], writes=["b8"])
        PE(lambda e: e.matmul(ps[1][:, 256:384], lhsT=cst[0:8, C_R, :], rhs=b8[:], start=True, stop=True), reads=["cst", "b8"], writes=["ps1"])
        V(lambda e, g=g: e.tensor_tensor(out=wsS[:, g, :], in0=ps[1][:, 256:384], in1=cst[:, C_TRI16, :], op=ALU.mult), reads=["ps1", "cst"], writes=["wsS"])

    def proj_residual(wname, nt):
        w3, wres = WSCR[wname]
        for c in range(2):
            S.dma("sync", lambda e, c=c: e.dma_start(out=wps[c][:, :, :], in_=w3[:, :, c * 512:(c + 1) * 512]), reads=[wres], writes=[f"wps{c}"])
            for ti in range(nt):
                pb_ = 4 + (ti % 2)
                for k in range(8):
                    PE(lambda e, k=k, ti=ti, c=c, pb_=pb_: e.matmul(ps[pb_][:, :], lhsT=xT[:, k, ti * 128:(ti + 1) * 128], rhs=wps[c][:, k, :], start=(k == 0), stop=(k == 7)),
                       reads=["xT", f"wps{c}"], writes=[f"ps{pb_}"])
                V(lambda e, ti=ti, c=c, pb_=pb_: e.scalar_tensor_tensor(out=xg[:, ti, c * 512:(c + 1) * 512], in0=xg[:, ti, c * 512:(c + 1) * 512], scalar=ALPHA,
                                                                      in1=ps[pb_][:, :], op0=ALU.mult, op1=ALU.add), reads=[f"ps{pb_}", f"xg{ti}"], writes=[f"xg{ti}"])

    def phaseC_group(T0, nt, sample):
        for ti in range(nt):
            r0 = (T0 + ti) * 128
            S.dma("sync", lambda e, r0=r0, ti=ti: e.dma_start(out=xg[:, ti, :], in_=x1_scr[r0:r0 + 128, :]), reads=["x1_scr"], writes=[f"xg{ti}"])
            S.dma("sync", lambda e, r0=r0: e.dma_start(out=ocat[:, 0:512], in_=oh_scr[r0:r0 + 128, :]), reads=["oh_scr"], writes=["ocat"])
            S.dma("sync", lambda e, r0=r0: e.dma_start(out=ocat[:, 512:1024], in_=of_scr[r0:r0 + 128, :]), reads=["of_scr"], writes=["ocat"])
            transpose_to(xT, "xT", ocat, "ocat", 8, ti * 128)
        proj_residual("w_out", nt)
        for ti in range(nt):
            layer_norm(xg[:, ti, :], f"xg{ti}", xg[:, ti, :], f"xg{ti}", 1)
        ffn(1, nt)
        ffn(2, nt)
        build_xT(nt)
        for ti in range(nt):
            for c in range(4):
                proj_tokmajor("c_w_in", c * 512, 512, ti, c % 2, c % 2)
                dstt = uu if c < 2 else vv
                dres = "uu" if c < 2 else "vv"
                dsl = dstt[:, (c % 2) * 512:(c % 2 + 1) * 512]
                A(lambda e, c=c: e.activation(out=gq[:], in_=ps[c % 2][:, :], func=AF.Square), reads=[f"ps{c % 2}"], writes=["gq"])
                V(lambda e: e.tensor_scalar(out=gq[:], in0=gq[:], scalar1=0.044715, scalar2=1.0, op0=ALU.mult, op1=ALU.add), reads=["gq"], writes=["gq"])
                V(lambda e, c=c: e.tensor_tensor(out=gq[:], in0=gq[:], in1=ps[c % 2][:, :], op=ALU.mult), reads=["gq", f"ps{c % 2}"], writes=["gq"])
                A(lambda e: e.activation(out=gq[:], in_=gq[:], func=AF.Sigmoid, scale=1.5957691216057308), reads=["gq"], writes=["gq"])
                V(lambda e, c=c, dsl=dsl: e.tensor_tensor(out=dsl, in0=gq[:], in1=ps[c % 2][:, :], op=ALU.mult), reads=["gq", f"ps{c % 2}"], writes=[dres])
            layer_norm(vv[:], "vv", vv[:], "vv", 0, g_ap=c_ln_g[0:1, :], b_ap=c_ln_b[0:1, :])
            if sample:
                S.dma("sync", lambda e, ti=ti: e.dma_start(out=cv_o[ti * 128:(ti + 1) * 128, :], in_=vv[:]), reads=["vv"], is_output=True)
            A(lambda e: e.activation(out=vvb[:], in_=vv[:], func=AF.Copy), reads=["vv"], writes=["vvb"])
            wsx = wsS if sample else wsT
            bsx = bsS if sample else bsT
            for g in range(8):
                bank = 4 + g // 4
                cs = slice((g % 4) * 128, (g % 4 + 1) * 128)
                PE(lambda e, g=g, bank=bank, cs=cs, wsx=wsx: e.matmul(ps[bank][:, cs], lhsT=wsx[:, g, :], rhs=vvb[:, g * 128:(g + 1) * 128], start=True, stop=True),
                   reads=["wsT", "wsS", "vvb"], writes=[f"ps{bank}"])
                V(lambda e, g=g, bank=bank, cs=cs, bsx=bsx: e.scalar_tensor_tensor(out=umx[:, g * 128:(g + 1) * 128], in0=ps[bank][:, cs], scalar=bsx[:, g:g + 1],
                                                                                   in1=uu[:, g * 128:(g + 1) * 128], op0=ALU.add, op1=ALU.mult),
                  reads=[f"ps{bank}", "bsT", "bsS", "uu"], writes=["umx"])
            transpose_to(xT, "xT", umx, "umx", 8, ti * 128)
        proj_residual("c_w_out", nt)
        for ti in range(nt):
            layer_norm(xg[:, ti, :], f"xg{ti}", xg[:, ti, :], f"xg{ti}", 4)
        ffn(3, nt)
        for ti in range(nt):
            r0 = (T0 + ti) * 128
            S.dma("sync", lambda e, r0=r0, ti=ti: e.dma_start(out=y_o[r0:r0 + 128, :], in_=xg[:, ti, :]), reads=[f"xg{ti}"], is_output=True)

    if 'C' in PH:
        for G in range(NPT // 4):
            phaseC_group(G * 4, 4, False)
        phaseC_group(SMP, NS, True)

    print('sbuf bytes remaining', nc.sbuf_bytes_remaining)
    S.emit()
    print('instr counts', {e: len(S.q[e]) for e in ENGS})
    return nc


_NC_CACHE = {}
NCORES = 2
NS_ = 4


def kernel(x_prompt, x_sample, cache_fox_k, cache_fox_v, cache_fox_logf, state_hg, page_table,
           ln_g, ln_b, ffn_w_gate, ffn_w_up, ffn_w_down, ab_w_in, hg_lb_logits, hg_norm_g, fox_f_bias,
           ab_w_out, c_w_in, c_ln_g, c_ln_b, c_w_s, c_b_s, c_w_out):
    f = lambda a: np.ascontiguousarray(np.asarray(a, dtype=np.float32))
    x_prompt = f(x_prompt); x_sample = f(x_sample)
    P = x_prompt.shape[1]
    nphys = int(os.environ.get('KNPHYS', str(N_PHYS)))
    if nphys != N_PHYS:
        cache_fox_k = np.asarray(cache_fox_k)[:, :nphys]; cache_fox_v = np.asarray(cache_fox_v)[:, :nphys]
        cache_fox_logf = np.asarray(cache_fox_logf)[:, :nphys]; page_table = np.asarray(page_table) % nphys
    if P not in _NC_CACHE:
        _NC_CACHE[P] = build_program(P // 128, nphys, NS_)
    nc = _NC_CACHE[P]
    shared = {
        "cache_k": f(cache_fox_k).reshape(nphys * 128, 512),
        "cache_v": f(cache_fox_v).reshape(nphys * 128, 512),
        "cache_lf": f(cache_fox_logf).reshape(nphys * 128, 8),
        "ln_g": f(ln_g).reshape(6, D), "ln_b": f(ln_b).reshape(6, D),
        "w_gate": f(ffn_w_gate).reshape(4, D, FF), "w_up": f(ffn_w_up).reshape(4, D, FF),
        "w_down": f(ffn_w_down).reshape(4, FF, D),
        "ab_w_in": f(ab_w_in).reshape(D, 3592), "lb_logits": f(hg_lb_logits).reshape(1, 1536),
        "hg_norm_g": f(hg_norm_g).reshape(1, 128), "fox_f_bias": f(fox_f_bias).reshape(1, 8),
        "ab_w_out": f(ab_w_out).reshape(D, D), "c_w_in": f(c_w_in).reshape(D, 2048),
        "c_ln_g": f(c_ln_g).reshape(1, D), "c_ln_b": f(c_ln_b).reshape(1, D),
        "c_w_s": f(c_w_s).reshape(1024, 128), "c_b_s": f(c_b_s).reshape(8, 128),
        "c_w_out": f(c_w_out).reshape(D, D), "cst": make_consts().reshape(128, NCST * 128),
    }
    pt = np.asarray(page_table, dtype=np.int32)
    SQ = 16 * NS_
    in_maps = []
    for c in range(NCORES):
        m = dict(shared)
        m["xin"] = np.concatenate([x_prompt[c], x_sample[SQ * c:SQ * (c + 1)].reshape(SQ * 8, D)], axis=0)
        m["state_hg"] = f(state_hg)[0, SQ * c:SQ * (c + 1)].reshape(SQ * 4 * 128, 128)
        m["ptab"] = np.ascontiguousarray(pt[SQ * c:SQ * (c + 1)].reshape(1, SQ * 16))
        in_maps.append(m)
    ncores = int(os.environ.get('KCORES', str(NCORES)))
    res = run_bass_kernel_spmd(nc, in_maps[:ncores], core_ids=list(range(ncores))).results
    res = list(res) + [res[0]] * (NCORES - ncores)
    R = range(NCORES)
    y_p = np.stack([res[b]["y"][:P] for b in R])
    y_s = np.concatenate([res[c]["y"][P:].reshape(SQ, 8, D) for c in R])
    k_p = np.stack([res[b]["k_new"][:P].reshape(P, 8, 64) for b in R])[None]
    v_p = np.stack([res[b]["v_new"][:P].reshape(P, 8, 64) for b in R])[None]
    lf_p = np.stack([res[b]["lf_new"][:P] for b in R])[None]
    hg_p = np.stack([res[b]["hg_p"].reshape(4, 128, 128) for b in R])[None]
    k_s = np.concatenate([res[c]["k_new"][P:].reshape(SQ, 8, 8, 64) for c in R])[None]
    v_s = np.concatenate([res[c]["v_new"][P:].reshape(SQ, 8, 8, 64) for c in R])[None]
    lf_s = np.concatenate([res[c]["lf_new"][P:].reshape(SQ, 8, 8) for c in R])[None]
    hg_s = np.concatenate([res[c]["hg_s"].reshape(SQ, 4, 128, 128) for c in R])[None]
    cv_s = np.concatenate([res[c]["cv_s"].reshape(SQ, 8, D) for c in R])[None]
    return (y_p, y_s, k_p, v_p, lf_p, hg_p, k_s, v_s, lf_s, hg_s, cv_s)
```

```python
import os
import numpy as np
import concourse.bass as bass
import concourse.mybir as mybir
from concourse.bass_utils import run_bass_kernel_spmd

F32 = mybir.dt.float32
BF16 = mybir.dt.bfloat16
I32 = mybir.dt.int32
AF = mybir.ActivationFunctionType
ALU = mybir.AluOpType

D = 1024
FF = 2816
NJ = 22
NPT = 64
NT = 65
NTOK = NT * 128
ALPHA = 4.0 ** 0.25
EPS = 1e-5
N_PHYS = 2560
ENGS = ("tensor", "vector", "scalar", "gpsimd", "sync")
N_DMA_SEMS = 16
SEM_EPOCH = 16000

C_ID, C_TRI, C_SEL127, C_TRI2, C_BLK2, C_TRI16, C_BLK16, C_SELEND2, C_SELEND16, C_R, C_SUP, C_ONES, C_NEG = range(13)
NCST = 13


def make_consts():
    c = np.zeros((128, NCST, 128), np.float32)
    s = np.arange(128)[:, None]
    t = np.arange(128)[None, :]
    c[:, C_ID] = (s == t)
    c[:, C_TRI] = (s <= t)
    c[:, C_SEL127] = (s == 127) * np.ones((1, 128))
    c[:, C_TRI2] = (s // 64 == t // 64) & (s <= t)
    c[:, C_BLK2] = (s // 64 == t // 64)
    c[:, C_TRI16] = (s // 8 == t // 8) & (s <= t)
    c[:, C_BLK16] = (s // 8 == t // 8)
    c[:, C_SELEND2][:, 0] = (np.arange(128) == 63)
    c[:, C_SELEND2][:, 1] = (np.arange(128) == 127)
    for q in range(16):
        c[q * 8 + 7, C_SELEND16, q] = 1.0
    for i in range(8):
        c[i, C_R, i::8] = 1.0
    c[:, C_SUP] = (s > t)
    c[:, C_ONES] = 1.0
    c[:, C_NEG] = -1.0
    return c


class Sched:
    def __init__(self, nc):
        self.nc = nc
        self.q = {e: [] for e in ENGS}
        self.cnt = {e: 0 for e in ENGS}
        self.epoch = {e: 0 for e in ENGS}
        self.sem = {e: nc.alloc_semaphore(f"s_{e}_0") for e in ENGS}
        self.dsem = {e: [nc.alloc_semaphore(f"d_{e}_{i}") for i in range(N_DMA_SEMS)]
                     for e in ("sync", "scalar", "gpsimd")}
        self.dcnt = {e: [0] * N_DMA_SEMS for e in ("sync", "scalar", "gpsimd")}
        self.dnext = {e: 0 for e in ("sync", "scalar", "gpsimd")}
        self.known = {e: {} for e in ENGS}
        self.lastw = {}
        self.reads = {}
        self.semobj = {}
        for e in ENGS:
            self.semobj[("e", e, 0)] = self.sem[e]
        for e in self.dsem:
            for i, s in enumerate(self.dsem[e]):
                self.semobj[("d", e, i)] = s
        self.out_deps = []
        self.alias = {}

    def _need(self, eng, deps):
        need = {}
        for (k, v) in deps:
            if self.known[eng].get(k, 0) >= v:
                continue
            if need.get(k, 0) < v:
                need[k] = v
        for k, v in need.items():
            self.known[eng][k] = v
        return list(need.items())

    def _deps(self, reads, writes):
        reads = [self.alias.get(r, r) for r in reads]
        writes = [self.alias.get(w, w) for w in writes]
        deps = []
        for r in reads:
            if r in self.lastw:
                deps.append(self.lastw[r])
        for w in writes:
            if w in self.lastw:
                deps.append(self.lastw[w])
            deps.extend(self.reads.get(w, ()))
        return deps

    def _commit(self, tok, reads, writes):
        reads = [self.alias.get(r, r) for r in reads]
        writes = [self.alias.get(w, w) for w in writes]
        for r in reads:
            self.reads.setdefault(r, []).append(tok)
        for w in writes:
            self.lastw[w] = tok
            self.reads[w] = []

    def op(self, eng, fn, reads=(), writes=()):
        pr = [r for r in reads if r.startswith("ps")]
        if pr:
            reads = [r for r in reads if not r.startswith("ps")]
            writes = list(writes) + pr
        deps = self._deps(reads, writes)
        waits = self._need(eng, deps)
        if self.cnt[eng] >= SEM_EPOCH:
            self.epoch[eng] += 1
            self.cnt[eng] = 0
            self.sem[eng] = self.nc.alloc_semaphore(f"s_{eng}_{self.epoch[eng]}")
            self.semobj[("e", eng, self.epoch[eng])] = self.sem[eng]
        self.cnt[eng] += 1
        key = ("e", eng, self.epoch[eng])
        tok = (key, self.cnt[eng])
        if eng == "tensor":
            self.known[eng][key] = self.cnt[eng]
        self.q[eng].append((waits, fn, (self.sem[eng], 1)))
        self._commit(tok, reads, writes)
        return tok

    def dma(self, eng, fn, reads=(), writes=(), is_output=False):
        deps = self._deps(reads, writes)
        i = self.dnext[eng]
        self.dnext[eng] = (i + 1) % N_DMA_SEMS
        key = ("d", eng, i)
        if self.dcnt[eng][i] > 0:
            deps.append((key, self.dcnt[eng][i]))
        waits = self._need(eng, deps)
        self.dcnt[eng][i] += 16
        tok = (key, self.dcnt[eng][i])
        self.q[eng].append((waits, fn, (self.dsem[eng][i], 16)))
        self._commit(tok, reads, writes)
        if is_output:
            self.out_deps.append(tok)
        return tok

    def emit(self):
        nc = self.nc
        fin = list(self.out_deps)
        for r, t in self.lastw.items():
            fin.append(t)
        waits = self._need("sync", fin)
        self.q["sync"].append((waits, None, None))
        with nc.Block() as block:
            def run(engname):
                def body(eng):
                    for waits, fn, inc in self.q[engname]:
                        for k, v in waits:
                            eng.wait_ge(self.semobj[k], v)
                        if fn is not None:
                            ins = fn(eng)
                            ins.then_inc(inc[0], inc[1])
                return body
            block.tensor(run("tensor"))
            block.vector(run("vector"))
            block.scalar(run("scalar"))
            block.gpsimd(run("gpsimd"))
            block.sync(run("sync"))


def build_program(NPT=NPT, N_PHYS=N_PHYS, NS=4):
    NT = NPT + NS
    NTOK = NT * 128
    SMP = NPT
    nc = bass.Bass("TRN2", target_bir_lowering=False)
    S = Sched(nc)

    def din(name, shape, dt=F32):
        return nc.dram_tensor(name, list(shape), dt, kind="ExternalInput").ap()

    def dout(name, shape, dt=F32):
        return nc.dram_tensor(name, list(shape), dt, kind="ExternalOutput").ap()

    def dscr(name, shape, dt):
        return nc.dram_tensor(name, list(shape), dt).ap()

    def sb(name, shape, dt=F32):
        return nc.alloc_sbuf_tensor(name, list(shape), dt)

    xin = din("xin", [NTOK, D])
    ck = din("cache_k", [N_PHYS * 128, 512])
    cv_ = din("cache_v", [N_PHYS * 128, 512])
    clf = din("cache_lf", [N_PHYS * 128, 8])
    st_hg = din("state_hg", [NS * 16 * 4 * 128, 128])
    ptab = din("ptab", [1, NS * 256], I32)
    ln_g = din("ln_g", [6, D])
    ln_b = din("ln_b", [6, D])
    wgate = din("w_gate", [4, D, FF])
    wup = din("w_up", [4, D, FF])
    wdown = din("w_down", [4, FF, D])
    w_in = din("ab_w_in", [D, 3592])
    lb_log = din("lb_logits", [1, 3 * 512])
    hg_ng = din("hg_norm_g", [1, 128])
    fbias = din("fox_f_bias", [1, 8])
    w_out = din("ab_w_out", [D, D])
    c_w_in = din("c_w_in", [D, 2048])
    c_ln_g = din("c_ln_g", [1, D])
    c_ln_b = din("c_ln_b", [1, D])
    c_w_s = din("c_w_s", [8 * 128, 128])
    c_b_s = din("c_b_s", [8, 128])
    c_w_out = din("c_w_out", [D, D])
    cst_d = din("cst", [128, NCST * 128])

    y_o = dout("y", [NTOK, D])
    k_o = dout("k_new", [NTOK, 512])
    v_o = dout("v_new", [NTOK, 512])
    lf_o = dout("lf_new", [NTOK, 8])
    hgp_o = dout("hg_p", [4 * 128, 128])
    hgs_o = dout("hg_s", [NS * 16 * 4 * 128, 128])
    cv_o = dout("cv_s", [NS * 128, D])

    wg_scr = dscr("wg_scr", [4 * NJ * 128, 1024], BF16)
    wu_scr = dscr("wu_scr", [4 * NJ * 128, 1024], BF16)
    wd_scr = dscr("wd_scr", [4 * 4 * 128, NJ * 256], BF16)
    win_scr = dscr("win_scr", [128, 8 * 3592], BF16)
    wout_scr = dscr("wout_scr", [128, 8 * D], BF16)
    cwin_scr = dscr("cwin_scr", [128, 8 * 2048], BF16)
    cwout_scr = dscr("cwout_scr", [128, 8 * D], BF16)
    x1_scr = dscr("x1_scr", [NTOK, D], F32)
    oh_scr = dscr("oh_scr", [NTOK, 512], BF16)
    of_scr = dscr("of_scr", [NTOK, 512], BF16)
    qT_scr = dscr("qT_scr", [NT * 128, 512], BF16)
    kT_scr = dscr("kT_scr", [NT * 128, 512], BF16)
    vp_scr = dscr("vp_scr", [NT * 128, 8 * 65], BF16)

    cst = sb("cstt", [128, NCST, 128])
    cstb = sb("cstb", [128, NCST, 128], BF16)
    identb = cstb[:, C_ID, :]
    ps = [nc.alloc_psum_tensor(f"ps{i}", [128, 512], F32) for i in range(8)]

    def psb(i):
        return ps[i][:].bitcast(BF16)

    xg = sb("xg", [128, 4, D])
    xT = sb("xT", [128, 8, 512], BF16)
    hT = sb("hT", [128, NJ, 512], BF16)
    wgs = [sb(f"wgs{i}", [128, 8, 128], BF16) for i in range(2)]
    wus = [sb(f"wus{i}", [128, 8, 128], BF16) for i in range(2)]
    wds = sb("wds", [128, NJ, 256], BF16)
    wps = [sb(f"wps{i}", [128, 8, 512], BF16) for i in range(2)]
    gb = [sb(f"gb{i}", [128, D]) for i in range(2)]
    zt = sb("zt", [128, D])
    tf = [sb(f"tf{i}", [128, D]) for i in range(4)]
    tb = [sb(f"tb{i}", [128, D], BF16) for i in range(4)]
    S.alias.update({"qh": "tf0", "kk": "tf0", "gl": "tf1", "gs": "tf1", "bcs": "tf2", "ebl": "tf2", "fk_f": "tf3", "fv_f": "tf3",
                    "vh": "tb0", "qtl": "tb0", "ktl": "tb1", "khat": "tb1", "ohb": "tb2", "fq_b": "tb2", "fk_b": "tb3",
                    "uu": "tf0", "vv": "tf1", "vvb": "tb0", "umx": "tb1", "ocat": "tb2", "tmpA": "tf2"})
    xb16 = sb("xb16", [128, D], BF16)
    sg_t = sb("sg_t", [128, 512], BF16)
    stats = sb("stats", [128, 2, 6])
    mv = sb("mv", [128, 2])
    rstd = sb("rstd", [128, 1])

    S.dma("sync", lambda e: e.dma_start(out=cst[:].rearrange("p a b -> p (a b)"), in_=cst_d[:, :]), writes=["cst"])
    S.op("vector", lambda e: e.tensor_copy(out=cstb[:], in_=cst[:]), reads=["cst"], writes=["cstb"])

    def V(fn, reads=(), writes=()):
        return S.op("vector", fn, reads, writes)

    def A(fn, reads=(), writes=()):
        return S.op("scalar", fn, reads, writes)

    def PE(fn, reads=(), writes=()):
        return S.op("tensor", fn, reads, writes)

    def bcast_load(dst, src_row, res):
        S.dma("sync", lambda e: e.dma_start(out=dst, in_=src_row.partition_broadcast(128)), writes=[res])

    def transpose_to(dstT, dst_res, src_bf, src_res, nk, col0, psi=7):
        pb = psb(psi)
        for k in range(nk):
            PE(lambda e, k=k: e.transpose(pb[:, k * 128:(k + 1) * 128], src_bf[:, k * 128:(k + 1) * 128], identb),
               reads=[src_res, "cstb"], writes=[f"ps{psi}"])
        V(lambda e: e.tensor_copy(out=dstT[:, 0:nk, col0:col0 + 128],
                                  in_=pb[:, 0:nk * 128].rearrange("p (k t) -> p k t", k=nk)),
          reads=[f"ps{psi}"], writes=[dst_res])

    gb_cur = [None]

    def layer_norm(src, src_res, dst, dst_res, gi, g_ap=None, b_ap=None):
        ga = ln_g[gi:gi + 1, :] if g_ap is None else g_ap
        ba = ln_b[gi:gi + 1, :] if b_ap is None else b_ap
        key = ("ln", gi) if g_ap is None else ("custom",)
        if gb_cur[0] != key:
            gb_cur[0] = key
            bcast_load(gb[0][:], ga, "gb0")
            bcast_load(gb[1][:], ba, "gb1")
        for h in range(2):
            V(lambda e, h=h: e.bn_stats(out=stats[:, h, :], in_=src[:, h * 512:(h + 1) * 512]),
              reads=[src_res], writes=["stats"] if h == 0 else ["stats"])
        V(lambda e: e.bn_aggr(out=mv[:], in_=stats[:].rearrange("p a b -> p (a b)")), reads=["stats"], writes=["mv"])
        V(lambda e: e.tensor_scalar(out=rstd[:], in0=mv[:, 1:2], scalar1=EPS, scalar2=None, op0=ALU.add), reads=["mv"], writes=["rstd"])
        A(lambda e: e.activation(out=rstd[:], in_=rstd[:], func=AF.Sqrt), reads=["rstd"], writes=["rstd"])
        V(lambda e: e.reciprocal(out=rstd[:], in_=rstd[:]), reads=["rstd"], writes=["rstd"])
        V(lambda e: e.tensor_scalar(out=dst, in0=src, scalar1=mv[:, 0:1], scalar2=rstd[:, 0:1], op0=ALU.subtract, op1=ALU.mult),
          reads=[src_res, "mv", "rstd"], writes=[dst_res])
        V(lambda e: e.tensor_tensor(out=dst, in0=dst, in1=gb[0][:], op=ALU.mult), reads=[dst_res, "gb0"], writes=[dst_res])
        V(lambda e: e.tensor_tensor(out=dst, in0=dst, in1=gb[1][:], op=ALU.add), reads=[dst_res, "gb1"], writes=[dst_res])

    def build_xT(nt):
        for ti in range(nt):
            A(lambda e, ti=ti: e.activation(out=xb16[:], in_=xg[:, ti, :], func=AF.Copy), reads=[f"xg{ti}"], writes=["xb16"])
            transpose_to(xT, "xT", xb16, "xb16", 8, ti * 128)

    def ffn(fi, nt):
        ntk = nt * 128
        build_xT(nt)
        for j in range(NJ):
            b = j % 2
            rj = (fi * NJ + j) * 128
            S.dma("sync", lambda e, rj=rj, b=b: e.dma_start(out=wgs[b][:].rearrange("p k m -> p (k m)"), in_=wg_scr[rj:rj + 128, :]),
                  reads=[f"S_wg{fi}"], writes=[f"wgs{b}"])
            S.dma("sync", lambda e, rj=rj, b=b: e.dma_start(out=wus[b][:].rearrange("p k m -> p (k m)"), in_=wu_scr[rj:rj + 128, :]),
                  reads=[f"S_wu{fi}"], writes=[f"wus{b}"])
            pg, pu = ps[2 * b], ps[2 * b + 1]
            for k in range(8):
                PE(lambda e, k=k, b=b, pg=pg: e.matmul(pg[:, 0:ntk], lhsT=wgs[b][:, k, :], rhs=xT[:, k, 0:ntk], start=(k == 0), stop=(k == 7)),
                   reads=[f"wgs{b}", "xT"], writes=[f"ps{2 * b}"])
            for k in range(8):
                PE(lambda e, k=k, b=b, pu=pu: e.matmul(pu[:, 0:ntk], lhsT=wus[b][:, k, :], rhs=xT[:, k, 0:ntk], start=(k == 0), stop=(k == 7)),
                   reads=[f"wus{b}", "xT"], writes=[f"ps{2 * b + 1}"])
            A(lambda e, pg=pg: e.activation(out=sg_t[:, 0:ntk], in_=pg[:, 0:ntk], func=AF.Silu), reads=[f"ps{2 * b}"], writes=["sg_t"])
            V(lambda e, j=j, pu=pu: e.tensor_tensor(out=hT[:, j, 0:ntk], in0=sg_t[:, 0:ntk], in1=pu[:, 0:ntk], op=ALU.mult),
              reads=["sg_t", f"ps{2 * b + 1}"], writes=["hT"])
        li = (fi // 2) * 3 + (0 if fi % 2 == 0 else 2)
        for c in range(4):
            cs = slice(c * 256, (c + 1) * 256)
            rq = (fi * 4 + c) * 128
            S.dma("sync", lambda e, rq=rq: e.dma_start(out=wds[:].rearrange("p j n -> p (j n)"), in_=wd_scr[rq:rq + 128, :]), reads=[f"S_wd{fi}"], writes=["wds"])
            for ti in range(nt):
                pd = ps[4 + (ti % 2)]
                for j in range(NJ):
                    PE(lambda e, j=j, ti=ti, pd=pd: e.matmul(pd[:, 0:256], lhsT=hT[:, j, ti * 128:(ti + 1) * 128], rhs=wds[:, j, :], start=(j == 0), stop=(j == NJ - 1)),
                       reads=["hT", "wds"], writes=[f"ps{4 + ti % 2}"])
                A(lambda e, pd=pd, cs=cs: e.activation(out=zt[:, cs], in_=pd[:, 0:256], func=AF.Copy, scale=0.5),
                  reads=[f"ps{4 + ti % 2}"], writes=["zt"])
                V(lambda e, ti=ti, cs=cs: e.scalar_tensor_tensor(out=xg[:, ti, cs], in0=xg[:, ti, cs], scalar=ALPHA,
                                                                in1=zt[:, cs], op0=ALU.mult, op1=ALU.add),
                  reads=["zt", f"xg{ti}"], writes=[f"xg{ti}"])
        for ti in range(nt):
            layer_norm(xg[:, ti, :], f"xg{ti}", xg[:, ti, :], f"xg{ti}", li)

    def proj_tokmajor(w_ap, c0, ncols, ti, psi, wb):
        w3, wres = WSCR[w_ap]
        S.dma("sync", lambda e: e.dma_start(out=wps[wb][:, :, 0:ncols], in_=w3[:, :, c0:c0 + ncols]), reads=[wres], writes=[f"wps{wb}"])
        for k in range(8):
            PE(lambda e, k=k: e.matmul(ps[psi][:, 0:ncols], lhsT=xT[:, k, ti * 128:(ti + 1) * 128], rhs=wps[wb][:, k, 0:ncols], start=(k == 0), stop=(k == 7)),
               reads=["xT", f"wps{wb}"], writes=[f"ps{psi}"])

    stg = [sb(f"stg{i}", [128, 8, 256], BF16) for i in range(2)]
    stg_i = [0]

    def conv_cols(src, ncols, dst3, res):
        for c0 in range(0, ncols, 256):
            w = min(256, ncols - c0)
            b = stg_i[0] % 2; stg_i[0] += 1
            S.dma("gpsimd", lambda e, b=b, c0=c0, w=w: e.dma_start(out=stg[b][:, :, 0:w], in_=src[:, c0:c0 + w].rearrange("(k p) n -> p k n", p=128)), writes=[f"stg{b}"])
            S.dma("gpsimd", lambda e, b=b, c0=c0, w=w: e.dma_start(out=dst3[:, :, c0:c0 + w], in_=stg[b][:, :, 0:w]), reads=[f"stg{b}"], writes=[res])

    def conv_gu(src, scr, fi, res):
        for jj in range(NJ // 2):
            b = stg_i[0] % 2; stg_i[0] += 1
            S.dma("gpsimd", lambda e, b=b, jj=jj: e.dma_start(out=stg[b][:], in_=src[fi, :, jj * 256:(jj + 1) * 256].rearrange("(k p) n -> p k n", p=128)), writes=[f"stg{b}"])
            for c in range(2):
                r0 = (fi * NJ + jj * 2 + c) * 128
                S.dma("gpsimd", lambda e, b=b, c=c, r0=r0: e.dma_start(out=scr[r0:r0 + 128, :].rearrange("p (k m) -> p k m", k=8), in_=stg[b][:, :, c * 128:(c + 1) * 128]),
                      reads=[f"stg{b}"], writes=[res])

    def conv_down(fi, res):
        for q in range(4):
            r0 = (fi * 4 + q) * 128
            for (j0, nj) in ((0, 8), (8, 8), (16, 6)):
                b = stg_i[0] % 2; stg_i[0] += 1
                S.dma("gpsimd", lambda e, b=b, q=q, j0=j0, nj=nj: e.dma_start(
                    out=stg[b][:, 0:nj, :], in_=wdown[fi, j0 * 128:(j0 + nj) * 128, q * 256:(q + 1) * 256].rearrange("(j p) n -> p j n", p=128)), writes=[f"stg{b}"])
                S.dma("gpsimd", lambda e, b=b, r0=r0, j0=j0, nj=nj: e.dma_start(
                    out=wd_scr[r0:r0 + 128, :].rearrange("p (j n) -> p j n", j=NJ)[:, j0:j0 + nj, :], in_=stg[b][:, 0:nj, :]), reads=[f"stg{b}"], writes=[res])

    def conv_ffn(fi):
        conv_gu(wgate, wg_scr, fi, f"S_wg{fi}")
        conv_gu(wup, wu_scr, fi, f"S_wu{fi}")
        conv_down(fi, f"S_wd{fi}")

    win3 = win_scr.rearrange("p (k n) -> p k n", k=8)
    wout3 = wout_scr.rearrange("p (k n) -> p k n", k=8)
    cwin3 = cwin_scr.rearrange("p (k n) -> p k n", k=8)
    cwout3 = cwout_scr.rearrange("p (k n) -> p k n", k=8)
    conv_ffn(0)
    conv_cols(w_in, 3592, win3, "S_win")
    conv_cols(w_out, D, wout3, "S_wout")
    conv_ffn(1)
    conv_ffn(2)
    conv_cols(c_w_in, 2048, cwin3, "S_cwin")
    conv_cols(c_w_out, D, cwout3, "S_cwout")
    conv_ffn(3)
    WSCR = {"w_in": (win3, "S_win"), "w_out": (wout3, "S_wout"), "c_w_in": (cwin3, "S_cwin"), "c_w_out": (cwout3, "S_cwout")}

    lbt = [tf[0][:, 0:512], tf[0][:, 512:1024], tf[1][:, 0:512]]
    LR = ["tf0", "tf1"]
    lb = sb("lb", [128, 512])
    oml = sb("oml", [128, 512])
    tmpA = tf[2][:, 0:512]
    ng_bc = sb("ng_bc", [128, 128])
    fb_bc = sb("fb_bc", [128, 8])
    for a in range(3):
        S.dma("sync", lambda e, a=a: e.dma_start(out=lbt[a], in_=lb_log[0:1, a * 512:(a + 1) * 512].partition_broadcast(128)), writes=LR)
    bcast_load(ng_bc[:], hg_ng[0:1, :], "ng_bc")
    bcast_load(fb_bc[:], fbias[0:1, :], "fb_bc")
    V(lambda e: e.tensor_tensor(out=tmpA, in0=lbt[0], in1=lbt[1], op=ALU.max), reads=LR, writes=["tmpA"])
    V(lambda e: e.tensor_tensor(out=tmpA, in0=tmpA, in1=lbt[2], op=ALU.max), reads=LR + ["tmpA"], writes=["tmpA"])
    for a in range(3):
        V(lambda e, a=a: e.tensor_tensor(out=lbt[a], in0=lbt[a], in1=tmpA, op=ALU.subtract), reads=LR + ["tmpA"], writes=LR)
        A(lambda e, a=a: e.activation(out=lbt[a], in_=lbt[a], func=AF.Exp), reads=LR, writes=LR)
    V(lambda e: e.tensor_tensor(out=tmpA, in0=lbt[0], in1=lbt[1], op=ALU.add), reads=LR, writes=["tmpA"])
    V(lambda e: e.tensor_tensor(out=tmpA, in0=tmpA, in1=lbt[2], op=ALU.add), reads=LR + ["tmpA"], writes=["tmpA"])
    V(lambda e: e.reciprocal(out=tmpA, in_=tmpA), reads=["tmpA"], writes=["tmpA"])
    V(lambda e: e.tensor_tensor(out=lb[:], in0=lbt[0], in1=tmpA, op=ALU.mult), reads=LR + ["tmpA"], writes=["lb"])
    V(lambda e: e.tensor_scalar(out=oml[:], in0=lb[:], scalar1=-1.0, scalar2=1.0, op0=ALU.mult, op1=ALU.add), reads=["lb"], writes=["oml"])

    Fk = sb("Fk", [128, NT, 8])
    Fend = sb("Fend", [128, NT, 8])
    Sst = [sb(f"Sst{h}", [128, 128]) for h in range(4)]
    Sb = [sb(f"Sb{h}", [128, 128], BF16) for h in range(4)]
    qh = tf[0][:, 0:512]
    kk = tf[0][:, 512:1024]
    gl = tf[1][:, 0:512]
    vh = tb[0][:, 0:512]
    gs = tf[1][:, 512:1024]
    bcs = tf[2][:, 0:512]
    ebl = tf[2][:, 512:1024]
    qtl = tb[0][:, 512:1024]
    ktl = tb[1][:, 0:512]
    khat = tb[1][:, 512:1024]
    hbuf = [[sb(f"{nm}_{p}", [128, 128], BF16) for nm in ("qtT", "ktT", "attT", "qtT0", "qtT1")] for p in range(2)]
    dcol = sb("dcol", [128, 16])
    ohb = tb[2][:, 0:512]
    ssq = sb("ssq", [128, 4])
    junk = sb("junk", [128, 128])
    fq_b = tb[2][:, 512:1024]
    fk_f = tf[3][:, 0:512]
    fk_b = tb[3][:, 0:512]
    fv_f = tf[3][:, 512:1024]
    vp = sb("vp", [128, 8, 65], BF16)
    lft = sb("lft", [128, 8])
    tq = sb("tq", [128, 4, 128], BF16)
    S0f = sb("S0f", [128, 16, 128])
    S0b = sb("S0b", [128, 16, 128], BF16)
    qz = sb("qz", [128, 16, 128], BF16)
    vbd = sb("vbd", [128, 16, 128], BF16)

    for p in range(2):
        V(lambda e, p=p: e.memset(hbuf[p][3][:], 0.0), writes=[f"qtT0{p}"])
        V(lambda e, p=p: e.memset(hbuf[p][4][:], 0.0), writes=[f"qtT1{p}"])
    V(lambda e: e.memset(qz[:], 0.0), writes=["qz"])
    V(lambda e: e.memset(vp[:], 1.0), writes=["vp"])
    for h in range(4):
        V(lambda e, h=h: e.memset(Sst[h][:], 0.0), writes=[f"Sst{h}"])
        V(lambda e, h=h: e.memset(Sb[h][:], 0.0), writes=[f"Sb{h}"])

    KSTAGE = int(os.environ.get('KSTAGE', '9'))
    dcs = sb("dcs", [128, 64])

    def phaseA_tile(T, ti, sample):
        r0 = T * 128
        tri = C_TRI16 if sample else C_TRI2
        blk = C_BLK16 if sample else C_BLK2
        proj_tokmajor("w_in", 0, 512, ti, 0, 0)
        A(lambda e: e.activation(out=qh[:], in_=ps[0][:], func=AF.Silu), reads=["ps0"], writes=["qh"])
        proj_tokmajor("w_in", 512, 512, ti, 1, 1)
        A(lambda e: e.activation(out=kk[:], in_=ps[1][:], func=AF.Sigmoid), reads=["ps1"], writes=["kk"])
        V(lambda e: e.tensor_tensor(out=gl[:], in0=kk[:], in1=oml[:], op=ALU.mult), reads=["kk", "oml"], writes=["gl"])
        V(lambda e: e.tensor_tensor(out=gl[:], in0=gl[:], in1=lb[:], op=ALU.add), reads=["gl", "lb"], writes=["gl"])
        V(lambda e: e.tensor_scalar(out=kk[:], in0=gl[:], scalar1=-1.0, scalar2=1.0, op0=ALU.mult, op1=ALU.add), reads=["gl"], writes=["kk"])
        A(lambda e: e.activation(out=gl[:], in_=gl[:], func=AF.Ln), reads=["gl"], writes=["gl"])
        proj_tokmajor("w_in", 1024, 512, ti, 0, 0)
        A(lambda e: e.activation(out=vh[:], in_=ps[0][:], func=AF.Copy), reads=["ps0"], writes=["vh"])
        proj_tokmajor("w_in", 1536, 512, ti, 1, 1)
        A(lambda e: e.activation(out=gs[:], in_=ps[1][:], func=AF.Silu), reads=["ps1"], writes=["gs"])
        gs3 = gs.rearrange("p (h v) -> p h v", h=4)
        V(lambda e: e.tensor_tensor(out=gs3, in0=gs3, in1=ng_bc[:].unsqueeze(1).broadcast_to([128, 4, 128]), op=ALU.mult), reads=["gs", "ng_bc"], writes=["gs"])
        if KSTAGE < 2:
            return
        PE(lambda e: e.matmul(ps[2][:], lhsT=cst[:, tri, :], rhs=gl[:], start=True, stop=True), reads=["cst", "gl"], writes=["ps2"])
        PE(lambda e: e.matmul(ps[3][:], lhsT=cst[:, blk, :], rhs=gl[:], start=True, stop=True), reads=["cst", "gl"], writes=["ps3"])
        A(lambda e: e.activation(out=bcs[:], in_=ps[2][:], func=AF.Exp), reads=["ps2"], writes=["bcs"])
        V(lambda e: e.tensor_tensor(out=qtl[:], in0=qh[:], in1=bcs[:], op=ALU.mult), reads=["qh", "bcs"], writes=["qtl"])
        A(lambda e: e.activation(out=bcs[:], in_=ps[2][:], func=AF.Exp, scale=-1.0), reads=["ps2"], writes=["bcs"])
        V(lambda e: e.tensor_tensor(out=ktl[:], in0=kk[:], in1=bcs[:], op=ALU.mult), reads=["kk", "bcs"], writes=["ktl"])
        A(lambda e: e.activation(out=ebl[:], in_=ps[3][:], func=AF.Exp), reads=["ps3"], writes=["ebl"])
        V(lambda e: e.tensor_tensor(out=bcs[:], in0=bcs[:], in1=ebl[:], op=ALU.mult), reads=["bcs", "ebl"], writes=["bcs"])
        V(lambda e: e.tensor_tensor(out=khat[:], in0=kk[:], in1=bcs[:], op=ALU.mult), reads=["kk", "bcs"], writes=["khat"])
        nb = 16 if sample else 2
        selc = C_SELEND16 if sample else C_SELEND2
        for h in range(4):
            PE(lambda e, h=h: e.matmul(ps[3][:, 256 + h * 16:256 + h * 16 + nb], lhsT=ebl[:, h * 128:(h + 1) * 128], rhs=cst[:, selc, 0:nb], start=True, stop=True),
               reads=["ebl", "cst"], writes=["ps3"])
        if sample:
            sq0 = (T - SMP) * 16
            V(lambda e: e.tensor_copy(out=dcs[:], in_=ps[3][:, 256:320]), reads=["ps3"], writes=["dcs"])
        else:
            V(lambda e: e.tensor_copy(out=dcol[:].rearrange("p (h c) -> p h c", h=4)[:, :, 0:2],
                                      in_=ps[3][:, 256:320].rearrange("p (h c) -> p h c", h=4)[:, :, 0:2]), reads=["ps3"], writes=["dcol"])
        if KSTAGE < 3:
            return
        def do_head(h):
            hs = slice(h * 128, (h + 1) * 128)
            par = h % 2
            bo, bs_, bt = (5, 6, 7) if par == 0 else (1, 2, 0)
            pso, pss = ps[bo], ps[bs_]
            qtT, ktT, attT, qtT0, qtT1 = hbuf[par]
            RqtT, RktT, RattT, RqtT0, RqtT1 = [f"{nm}{par}" for nm in ("qtT", "ktT", "attT", "qtT0", "qtT1")]
            Rpo, Rps, Rpt = f"ps{bo}", f"ps{bs_}", f"ps{bt}"
            pb = psb(bt)
            PE(lambda e, hs=hs: e.transpose(pb[:, 0:128], qtl[:, hs], identb), reads=["qtl", "cstb"], writes=[Rpt])
            PE(lambda e, hs=hs: e.transpose(pb[:, 128:256], ktl[:, hs], identb), reads=["ktl", "cstb"], writes=[Rpt])
            V(lambda e: e.tensor_copy(out=qtT[:], in_=pb[:, 0:128]), reads=[Rpt], writes=[RqtT])
            A(lambda e: e.activation(out=ktT[:], in_=pb[:, 128:256], func=AF.Copy), reads=[Rpt], writes=[RktT])
            yield
            PE(lambda e: e.matmul(pss[:, 0:128], lhsT=ktT[:], rhs=qtT[:], start=True, stop=True), reads=[RktT, RqtT], writes=[Rps])
            V(lambda e: e.tensor_tensor(out=attT[:], in0=pss[:, 0:128], in1=cst[:, tri, :], op=ALU.mult), reads=[Rps, "cst"], writes=[RattT])
            if not sample:
                V(lambda e: e.tensor_copy(out=qtT0[:, 0:64], in_=qtT[:, 0:64]), reads=[RqtT], writes=[RqtT0])
                V(lambda e: e.tensor_copy(out=qtT1[:, 64:128], in_=qtT[:, 64:128]), reads=[RqtT], writes=[RqtT1])
                yield
                PE(lambda e, hs=hs: e.matmul(pss[:, 128:256], lhsT=khat[0:64, hs], rhs=vh[0:64, hs], start=True, stop=True),
                   reads=["khat", "vh"], writes=[Rps])
                PE(lambda e, hs=hs: e.matmul(pso[:, 0:128], lhsT=attT[:], rhs=vh[:, hs], start=True, stop=False), reads=[RattT, "vh"], writes=[Rpo])
                PE(lambda e, h=h: e.matmul(pso[:, 0:128], lhsT=qtT0[:], rhs=Sb[h][:], start=False, stop=False), reads=[RqtT0, f"Sb{h}"], writes=[Rpo])
                V(lambda e, h=h: e.scalar_tensor_tensor(out=Sst[h][:], in0=Sst[h][:], scalar=dcol[:, h * 4:h * 4 + 1], in1=pss[:, 128:256], op0=ALU.mult, op1=ALU.add),
                  reads=[f"Sst{h}", "dcol", Rps], writes=[f"Sst{h}"])
                A(lambda e, h=h: e.activation(out=Sb[h][:], in_=Sst[h][:], func=AF.Copy), reads=[f"Sst{h}"], writes=[f"Sb{h}"])
                yield
                PE(lambda e, h=h: e.matmul(pso[:, 0:128], lhsT=qtT1[:], rhs=Sb[h][:], start=False, stop=True), reads=[RqtT1, f"Sb{h}"], writes=[Rpo])
                PE(lambda e, hs=hs: e.matmul(pss[:, 256:384], lhsT=khat[64:128, hs], rhs=vh[64:128, hs], start=True, stop=True),
                   reads=["khat", "vh"], writes=[Rps])
                V(lambda e, h=h: e.scalar_tensor_tensor(out=Sst[h][:], in0=Sst[h][:], scalar=dcol[:, h * 4 + 1:h * 4 + 2], in1=pss[:, 256:384], op0=ALU.mult, op1=ALU.add),
                  reads=[f"Sst{h}", "dcol", Rps], writes=[f"Sst{h}"])
                A(lambda e, h=h: e.activation(out=Sb[h][:], in_=Sst[h][:], func=AF.Copy), reads=[f"Sst{h}"], writes=[f"Sb{h}"])
            else:
                S.dma("sync", lambda e, h=h: e.dma_start(out=S0f[:], in_=st_hg.rearrange("(q h p) v -> h p q v", h=4, p=128)[h][:, sq0:sq0 + 16, :]), writes=["S0f"])
                A(lambda e: e.activation(out=S0b[:], in_=S0f[:], func=AF.Copy), reads=["S0f"], writes=["S0b"])
                for q in range(16):
                    V(lambda e, q=q: e.tensor_copy(out=qz[:, q, q * 8:(q + 1) * 8], in_=qtT[:, q * 8:(q + 1) * 8]), reads=[RqtT], writes=["qz"])
                yield
                PE(lambda e, hs=hs: e.matmul(pso[:, 0:128], lhsT=attT[:], rhs=vh[:, hs], start=True, stop=False), reads=[RattT, "vh"], writes=[Rpo])
                for q in range(16):
                    PE(lambda e, q=q, h=h: e.matmul(pso[:, 0:128], lhsT=qz[:, q, :], rhs=S0b[:, q, :], start=False, stop=(q == 15)),
                       reads=["qz", "S0b"], writes=[Rpo])
                for q in range(16):
                    V(lambda e, q=q, hs=hs: e.tensor_scalar(out=vbd[:, q, :], in0=vh[:, hs], scalar1=cst[:, C_BLK16, q * 8:q * 8 + 1], scalar2=None, op0=ALU.mult),
                      reads=["vh", "cst"], writes=["vbd"])
                dbk = [0, 1, 2, 3] if par == 0 else [3, 4, 5, 6]
                for c4 in range(4):
                    PE(lambda e, c4=c4, hs=hs: e.matmul(ps[dbk[c4]][:, :], lhsT=khat[:, hs], rhs=vbd[:, c4 * 4:(c4 + 1) * 4, :].rearrange("p a b -> p (a b)"), start=True, stop=True),
                       reads=["khat", "vbd"], writes=[f"ps{dbk[c4]}"])
                for q in range(16):
                    V(lambda e, q=q, h=h: e.scalar_tensor_tensor(out=S0f[:, q, :], in0=S0f[:, q, :], scalar=dcs[:, h * 16 + q:h * 16 + q + 1],
                                                                 in1=ps[dbk[q // 4]][:, (q % 4) * 128:(q % 4 + 1) * 128], op0=ALU.mult, op1=ALU.add),
                      reads=["S0f", "dcs", f"ps{dbk[q // 4]}"], writes=["S0f"])
                S.dma("sync", lambda e, h=h: e.dma_start(out=hgs_o.rearrange("(q h p) v -> h p q v", h=4, p=128)[h][:, sq0:sq0 + 16, :], in_=S0f[:]), reads=["S0f"], is_output=True)
            yield
            A(lambda e, h=h: e.activation(out=junk[:], in_=pso[:, 0:128], func=AF.Square, accum_out=ssq[:, h:h + 1]), reads=[Rpo], writes=["junk", f"ssq{h}"])
            V(lambda e, h=h: e.tensor_scalar(out=ssq[:, h:h + 1], in0=ssq[:, h:h + 1], scalar1=1.0 / 128, scalar2=EPS, op0=ALU.mult, op1=ALU.add), reads=[f"ssq{h}"], writes=[f"ssq{h}"])
            A(lambda e, h=h: e.activation(out=ssq[:, h:h + 1], in_=ssq[:, h:h + 1], func=AF.Sqrt), reads=[f"ssq{h}"], writes=[f"ssq{h}"])
            V(lambda e, h=h: e.reciprocal(out=ssq[:, h:h + 1], in_=ssq[:, h:h + 1]), reads=[f"ssq{h}"], writes=[f"ssq{h}"])
            V(lambda e, h=h, hs=hs: e.scalar_tensor_tensor(out=ohb[:, hs], in0=pso[:, 0:128], scalar=ssq[:, h:h + 1], in1=gs[:, hs], op0=ALU.mult, op1=ALU.mult),
              reads=[Rpo, f"ssq{h}", "gs"], writes=["ohb"])
        for h0 in (0, 2):
            if sample:
                for h in (h0, h0 + 1):
                    for _ in do_head(h):
                        pass
            else:
                gens = [do_head(h0), do_head(h0 + 1)]
                while gens:
                    for g in list(gens):
                        try:
                            next(g)
                        except StopIteration:
                            gens.remove(g)
        if KSTAGE < 4:
            return
        S.dma("sync", lambda e: e.dma_start(out=oh_scr[r0:r0 + 128, :], in_=ohb[:]), reads=["ohb"], writes=["oh_scr"])
        proj_tokmajor("w_in", 2048, 512, ti, 0, 0)
        A(lambda e: e.activation(out=fq_b[:], in_=ps[0][:], func=AF.Copy, scale=0.125), reads=["ps0"], writes=["fq_b"])
        proj_tokmajor("w_in", 2560, 512, ti, 1, 1)
        A(lambda e: e.activation(out=fk_f[:], in_=ps[1][:], func=AF.Copy), reads=["ps1"], writes=["fk_f"])
        V(lambda e: e.tensor_copy(out=fk_b[:], in_=ps[1][:]), reads=["ps1"], writes=["fk_b"])
        S.dma("sync", lambda e: e.dma_start(out=k_o[r0:r0 + 128, :], in_=fk_f[:]), reads=["fk_f"], is_output=True)
        proj_tokmajor("w_in", 3072, 512, ti, 0, 0)
        A(lambda e: e.activation(out=fv_f[:], in_=ps[0][:], func=AF.Copy), reads=["ps0"], writes=["fv_f"])
        V(lambda e: e.tensor_copy(out=vp[:, :, 0:64], in_=ps[0][:].rearrange("p (h d) -> p h d", h=8)), reads=["ps0"], writes=["vp"])
        S.dma("sync", lambda e: e.dma_start(out=v_o[r0:r0 + 128, :], in_=fv_f[:]), reads=["fv_f"], is_output=True)
        S.dma("sync", lambda e: e.dma_start(out=vp_scr[r0:r0 + 128, :], in_=vp[:].rearrange("p h d -> p (h d)")), reads=["vp"], writes=["vp_scr"])
        if KSTAGE < 5:
            return
        proj_tokmajor("w_in", 3584, 8, ti, 1, 1)
        V(lambda e: e.tensor_tensor(out=lft[:], in0=ps[1][:, 0:8], in1=fb_bc[:], op=ALU.add), reads=["ps1", "fb_bc"], writes=["lft"])
        A(lambda e: e.activation(out=lft[:], in_=lft[:], func=AF.Exp, scale=-1.0), reads=["lft"], writes=["lft"])
        A(lambda e: e.activation(out=lft[:], in_=lft[:], func=AF.Ln, bias=1.0), reads=["lft"], writes=["lft"])
        V(lambda e: e.tensor_scalar(out=lft[:], in0=lft[:], scalar1=-1.0, scalar2=None, op0=ALU.mult), reads=["lft"], writes=["lft"])
        S.dma("sync", lambda e: e.dma_start(out=lf_o[r0:r0 + 128, :], in_=lft[:]), reads=["lft"], is_output=True)
        if not sample:
            PE(lambda e: e.matmul(ps[2][:, 0:8], lhsT=cst[:, C_TRI, :], rhs=lft[:], start=True, stop=(T == 0)), reads=["cst", "lft"], writes=["ps2"])
            if T > 0:
                PE(lambda e: e.matmul(ps[2][:, 0:8], lhsT=cst[:, C_SEL127, :], rhs=Fk[:, T - 1, :], start=False, stop=True), reads=["cst", "Fk"], writes=["ps2"])
            V(lambda e: e.tensor_copy(out=Fk[:, T, :], in_=ps[2][:, 0:8]), reads=["ps2"], writes=["Fk"])
            PE(lambda e: e.matmul(ps[2][:, 8:16], lhsT=cst[:, C_SEL127, :], rhs=Fk[:, T, :], start=True, stop=True), reads=["cst", "Fk"], writes=["ps2"])
            V(lambda e: e.tensor_copy(out=Fend[:, T, :], in_=ps[2][:, 8:16]), reads=["ps2"], writes=["Fend"])
        else:
            PE(lambda e: e.matmul(ps[2][:, 0:8], lhsT=cst[:, C_TRI16, :], rhs=lft[:], start=True, stop=True), reads=["cst", "lft"], writes=["ps2"])
            V(lambda e: e.tensor_copy(out=Fk[:, T, :], in_=ps[2][:, 0:8]), reads=["ps2"], writes=["Fk"])
        for (src, res, scr) in ((fq_b, "fq_b", qT_scr), (fk_b, "fk_b", kT_scr)):
            transpose_to(tq, "tq", src, res, 4, 0, psi=7)
            S.dma("sync", lambda e, scr=scr: e.dma_start(out=scr[r0:r0 + 128, :], in_=tq[:].rearrange("p a b -> p (a b)")), reads=["tq"], writes=[scr.tensor.name])

    PH0 = os.environ.get('KPH', 'ABSC')
    for G in range(NPT // 4):
        for ti in range(4):
            T = G * 4 + ti
            S.dma("sync", lambda e, T=T, ti=ti: e.dma_start(out=xg[:, ti, :], in_=xin[T * 128:(T + 1) * 128, :]), writes=[f"xg{ti}"])
        if 'f' not in PH0:
            ffn(0, 4)
        build_xT(4)
        for ti in range(4):
            T = G * 4 + ti
            S.dma("sync", lambda e, T=T, ti=ti: e.dma_start(out=x1_scr[T * 128:(T + 1) * 128, :], in_=xg[:, ti, :]), reads=[f"xg{ti}"], writes=["x1_scr"])
            if 'F' in PH0:
                S.dma("sync", lambda e, T=T, ti=ti: e.dma_start(out=y_o[T * 128:(T + 1) * 128, :], in_=xg[:, ti, :]), reads=[f"xg{ti}"], is_output=True)
            else:
                phaseA_tile(T, ti, False)
    for h in range(4):
        S.dma("sync", lambda e, h=h: e.dma_start(out=hgp_o[h * 128:(h + 1) * 128, :], in_=Sst[h][:]), reads=[f"Sst{h}"], is_output=True)
    for st in range(NS):
        S.dma("sync", lambda e, st=st: e.dma_start(out=xg[:, st, :], in_=xin[(SMP + st) * 128:(SMP + st + 1) * 128, :]), writes=[f"xg{st}"])
    if 'F' in PH0:
        S.emit()
        return nc
    ffn(0, NS)
    build_xT(NS)
    for st in range(NS):
        S.dma("sync", lambda e, st=st: e.dma_start(out=x1_scr[(SMP + st) * 128:(SMP + st + 1) * 128, :], in_=xg[:, st, :]), reads=[f"xg{st}"], writes=["x1_scr"])
        phaseA_tile(SMP + st, st, True)

    PH = os.environ.get('KPH', 'ABSC')
    NKB = 2
    kTt = [sb(f"kTt{i}", [128, 4, 128], BF16) for i in range(NKB)]
    vpt = [sb(f"vpt{i}", [128, 8, 65], BF16) for i in range(NKB)]
    qTt = sb("qTt", [128, 4, 128], BF16)
    wq = [sb(f"wq{i}", [128, 8]) for i in range(2)]
    Vs = [sb(f"Vs{i}", [128, 8, 65], BF16) for i in range(2)]
    pT = [sb(f"pT{i}", [128, 2, 4, 128], BF16) for i in range(2)]
    ofb = sb("ofb", [128, 512], BF16)
    rs = sb("rs", [128, 8])
    G_ = lambda fn, reads=(), writes=(): S.op(os.environ.get("KGENG", "vector"), fn, reads, writes)

    qTt2 = [qTt, sb("qTt1", [128, 4, 128], BF16)]

    def attn_qk(b, kq, k_tile, k_res, qt_tile, qt_res, q_cols, v_src, v_res, w_ap, w_res, mask_c, pt_out):
        V(lambda e: e.tensor_tensor(out=Vs[b][:], in0=v_src[:], in1=w_ap[:, :].unsqueeze(2).broadcast_to([128, 8, 65]), op=ALU.mult),
          reads=[v_res, w_res], writes=[f"Vs{b}"])
        nq = q_cols.stop - q_cols.start
        for h in range(8):
            pr, po = h // 2, (h % 2) * 64
            bank = 2 * kq + h % 2
            PE(lambda e, h=h, pr=pr, po=po, bank=bank: e.matmul(ps[bank][:, (h // 2) * nq:(h // 2 + 1) * nq], lhsT=k_tile[po:po + 64, pr, :], rhs=qt_tile[po:po + 64, pr, q_cols],
                                                              start=True, stop=True, skip_group_check=True),
               reads=[k_res, qt_res], writes=[f"ps{bank}"])
        for half in range(2):
            bank = 2 * kq + half
            A(lambda e, half=half, bank=bank: e.activation(out=pt_out(half), in_=ps[bank][:, 0:4 * nq].rearrange("p (h q) -> p h q", h=4), func=AF.Exp),
              reads=[f"ps{bank}"], writes=[f"pT{b}"])
        if mask_c is not None:
            pv = pT[b][:].rearrange("p a b q -> p (a b) q")
            V(lambda e: e.tensor_tensor(out=pv, in0=pv, in1=cstb[:, mask_c, :].unsqueeze(1).broadcast_to([128, 8, 128]), op=ALU.mult), reads=[f"pT{b}", "cstb"], writes=[f"pT{b}"])

    def attn_pv(b, acc0, first, last):
        for h in range(8):
            bank, c0 = acc0 + h // 4, (h % 4) * 65
            PE(lambda e, h=h, bank=bank, c0=c0: e.matmul(ps[bank][:, c0:c0 + 65], lhsT=pT[b][:, h % 2, h // 2, :], rhs=Vs[b][:, h, :],
                                                       start=(first and h % 4 == 0), stop=last, skip_group_check=True),
               reads=[f"pT{b}", f"Vs{b}"], writes=[f"ps{bank}"])

    def attn_finish(row0, acc0):
        for half in range(2):
            bank = acc0 + half
            pv3 = ps[bank][:, 0:260].rearrange("p (h c) -> p h c", c=65)
            V(lambda e, half=half, pv3=pv3: e.reciprocal(out=rs[:, half * 4:(half + 1) * 4].unsqueeze(2), in_=pv3[:, :, 64:65]), reads=[f"ps{bank}"], writes=["rs"])
            V(lambda e, half=half, pv3=pv3: e.tensor_tensor(out=ofb[:, half * 256:(half + 1) * 256].rearrange("p (h d) -> p h d", d=64), in0=pv3[:, :, 0:64],
                                                          in1=rs[:, half * 4:(half + 1) * 4].unsqueeze(2).broadcast_to([128, 4, 64]), op=ALU.mult),
              reads=[f"ps{bank}", "rs"], writes=["ofb"])
        S.dma("sync", lambda e: e.dma_start(out=of_scr[row0:row0 + 128, :], in_=ofb[:]), reads=["ofb"], writes=["of_scr"])

    def run_pipelined(items):
        if not items:
            return
        items[0][0]()
        for i in range(len(items)):
            if i + 1 < len(items):
                items[i + 1][0]()
            items[i][1]()

    items = []
    pairs = [(qt, kt) for qt in range(NPT if 'B' in PH else 0) for kt in range(qt + 1)]
    for i, (qt, kt) in enumerate(pairs):
        b = i % 2; g = i % NKB; qb = qt % 2; acc0 = 4 if qt % 2 == 0 else 6

        def qk(qt=qt, kt=kt, b=b, g=g, qb=qb):
            if kt == 0:
                S.dma("sync", lambda e: e.dma_start(out=qTt2[qb][:].rearrange("p a b -> p (a b)"), in_=qT_scr[qt * 128:(qt + 1) * 128, :]), reads=["qT_scr"], writes=[f"qTt{qb}"])
            S.dma("sync", lambda e: e.dma_start(out=kTt[g][:].rearrange("p a b -> p (a b)"), in_=kT_scr[kt * 128:(kt + 1) * 128, :]), reads=["kT_scr"], writes=[f"kTt{g}"])
            S.dma("sync", lambda e: e.dma_start(out=vpt[g][:].rearrange("p a b -> p (a b)"), in_=vp_scr[kt * 128:(kt + 1) * 128, :]), reads=["vp_scr"], writes=[f"vpt{g}"])
            V(lambda e: e.tensor_tensor(out=wq[b][:], in0=Fend[:, qt, :], in1=Fk[:, kt, :], op=ALU.subtract), reads=["Fend", "Fk"], writes=[f"wq{b}"])
            V(lambda e: e.tensor_scalar(out=wq[b][:], in0=wq[b][:], scalar1=0.0, scalar2=None, op0=ALU.min), reads=[f"wq{b}"], writes=[f"wq{b}"])
            A(lambda e: e.activation(out=wq[b][:], in_=wq[b][:], func=AF.Exp), reads=[f"wq{b}"], writes=[f"wq{b}"])
            attn_qk(b, b, kTt[g], f"kTt{g}", qTt2[qb], f"qTt{qb}", slice(0, 128), vpt[g], f"vpt{g}", wq[b], f"wq{b}",
                    C_TRI if kt == qt else None, lambda half: pT[b][:, half, :, :])

        def pv(qt=qt, kt=kt, b=b, acc0=acc0):
            attn_pv(b, acc0, kt == 0, kt == qt)
            if kt == qt:
                attn_finish(qt * 128, acc0)
        items.append((qk, pv))
    run_pipelined(items)

    def sample_attn():
        pts = sb("pts", [128, NS * 256], I32)
        pidx = pts
        iot = sb("iot", [128, 1], I32)
        NKS = 2
        kpg = [sb(f"kpg{i}", [128, 512], BF16) for i in range(NKS)]
        vpg = [sb(f"vpg{i}", [128, 8, 65], BF16) for i in range(2)]
        vgt = [sb(f"vgt{i}", [128, 512], BF16) for i in range(NKS)]
        lfp = sb("lfp", [128, 16, 8])
        lat = sb("lat", [128, 16, 8])
        Rb = sb("Rb", [128, 16, 8])
        kTp = [sb(f"kTp{i}", [128, 4, 128], BF16) for i in range(2)]
        S.dma("sync", lambda e: e.dma_start(out=pts[:], in_=ptab[0:1, :].partition_broadcast(128)), writes=["pts"])
        S.op("gpsimd", lambda e: e.iota(iot[:], pattern=[[0, 1]], base=0, channel_multiplier=1), writes=["iot"])
        S.op("gpsimd", lambda e: e.tensor_scalar(out=pidx[:], in0=pts[:], scalar1=128, scalar2=iot[:, 0:1], op0=ALU.mult, op1=ALU.add), reads=["pts", "iot"], writes=["pts"])
        for i in range(2):
            V(lambda e, i=i: e.memset(vpg[i][:], 1.0), writes=[f"vpg{i}"])

        def seq_pre(st, q):
            for pg in range(16):
                S.dma("gpsimd", lambda e, pg=pg: e.indirect_dma_start(out=lfp[:, pg, :], out_offset=None, in_=clf,
                                                                     in_offset=bass.IndirectOffsetOnAxis(ap=pidx[:, st * 256 + q * 16 + pg:st * 256 + q * 16 + pg + 1], axis=0)),
                      reads=["pts"], writes=["lfp"])
            V(lambda e: e.memset(lat[:, 15, :], 0.0), writes=["lat"])
            for pg in range(14, -1, -1):
                V(lambda e, pg=pg: e.tensor_tensor(out=lat[:, pg, :], in0=lat[:, pg + 1, :], in1=lfp[:, pg + 1, :], op=ALU.add), reads=["lat", "lfp"], writes=["lat"])
            PE(lambda e: e.matmul(ps[6][:, 0:128], lhsT=cst[:, C_SUP, :], rhs=lfp[:].rearrange("p a b -> p (a b)"), start=True, stop=False), reads=["cst", "lfp"], writes=["ps6"])
            PE(lambda e: e.matmul(ps[6][:, 0:128], lhsT=cst[:, C_ONES, :], rhs=lat[:].rearrange("p a b -> p (a b)"), start=False, stop=True), reads=["cst", "lat"], writes=["ps6"])
            A(lambda e: e.activation(out=Rb[:].rearrange("p a b -> p (a b)"), in_=ps[6][:, 0:128], func=AF.Exp), reads=["ps6"], writes=["Rb"])

        for st in range(NS):
            TS = SMP + st
            items = []

            def qk0(TS=TS):
                S.dma("sync", lambda e: e.dma_start(out=qTt[:].rearrange("p a b -> p (a b)"), in_=qT_scr[TS * 128:(TS + 1) * 128, :]), reads=["qT_scr"], writes=["qTt0"])
                S.dma("sync", lambda e: e.dma_start(out=kTt[0][:].rearrange("p a b -> p (a b)"), in_=kT_scr[TS * 128:(TS + 1) * 128, :]), reads=["kT_scr"], writes=["kTt0"])
                S.dma("sync", lambda e: e.dma_start(out=vpt[0][:].rearrange("p a b -> p (a b)"), in_=vp_scr[TS * 128:(TS + 1) * 128, :]), reads=["vp_scr"], writes=["vpt0"])
                A(lambda e: e.activation(out=wq[0][:], in_=Fk[:, TS, :], func=AF.Exp, scale=-1.0), reads=["Fk"], writes=["wq0"])
                attn_qk(0, 0, kTt[0], "kTt0", qTt, "qTt0", slice(0, 128), vpt[0], "vpt0", wq[0], "wq0", C_TRI16, lambda half: pT[0][:, half, :, :])

            def pv0():
                attn_pv(0, 4, True, False)
                V(lambda e: e.memset(pT[0][:], 0.0), writes=["pT0"])
            items.append((qk0, pv0))
            n = 0
            for q in range(16):
                for pg in range(16):
                    n += 1
                    b = n % 2; g = n % NKS
                    col = st * 256 + q * 16 + pg

                    def qk(st=st, q=q, pg=pg, b=b, g=g, col=col, n=n):
                        if n == 1:
                            V(lambda e: e.memset(pT[1][:], 0.0), writes=["pT1"])
                        if pg == 0:
                            seq_pre(st, q)
                        if q > 0 and pg < 2:
                            V(lambda e: e.memset(pT[b][:].rearrange("p a b q -> p (a b) q")[:, :, (q - 1) * 8:q * 8], 0.0), writes=[f"pT{b}"])
                        S.dma("gpsimd", lambda e: e.indirect_dma_start(out=kpg[g][:], out_offset=None, in_=ck,
                                                                     in_offset=bass.IndirectOffsetOnAxis(ap=pidx[:, col:col + 1], axis=0)),
                              reads=["pts"], writes=[f"kpg{g}"])
                        S.dma("gpsimd", lambda e: e.indirect_dma_start(out=vgt[g][:], out_offset=None, in_=cv_,
                                                                     in_offset=bass.IndirectOffsetOnAxis(ap=pidx[:, col:col + 1], axis=0)),
                              reads=["pts"], writes=[f"vgt{g}"])
                        V(lambda e: e.tensor_copy(out=vpg[b][:, :, 0:64], in_=vgt[g][:].rearrange("p (h d) -> p h d", h=8)), reads=[f"vgt{g}"], writes=[f"vpg{b}"])
                        transpose_to(kTp[b], f"kTp{b}", kpg[g], f"kpg{g}", 4, 0, psi=7)
                        attn_qk(b, b, kTp[b], f"kTp{b}", qTt, "qTt0", slice(q * 8, (q + 1) * 8), vpg[b], f"vpg{b}", Rb[:, pg, :], "Rb", None,
                                lambda half: pT[b][:, half, :, q * 8:(q + 1) * 8])

                    def pv(b=b, q=q, pg=pg):
                        attn_pv(b, 4, False, q == 15 and pg == 15)
                    items.append((qk, pv))
            run_pipelined(items)
            attn_finish(TS * 128, 4)

    if 'S' in PH:
        sample_attn()

    ocat = tb[2][:, :]
    uu = tf[0][:, :]
    vv = tf[1][:, :]
    vvb = tb[0][:, :]
    wsT = sb("wsT", [128, 8, 128], BF16)
    wsS = sb("wsS", [128, 8, 128], BF16)
    bsT = sb("bsT", [128, 8])
    bsS = sb("bsS", [128, 8])
    wsl = sb("wsl", [128, 128])
    bsl = sb("bsl", [8, 128])
    w8 = sb("w8", [8, 8])
    b8 = sb("b8", [8, 128])
    umx = tb[1][:, :]
    gq = tf[2][:, 0:512]
    S.alias["gq"] = "tf2"

    S.dma("sync", lambda e: e.dma_start(out=bsl[:], in_=c_b_s[:, :]), writes=["bsl"])
    PE(lambda e: e.transpose(ps[0][:, 0:8], bsl[:], cst[0:8, C_ID, 0:8]), reads=["bsl", "cst"], writes=["ps0"])
    V(lambda e: e.tensor_copy(out=bsT[:], in_=ps[0][:, 0:8]), reads=["ps0"], writes=["bsT"])
    PE(lambda e: e.matmul(ps[0][:, 8:16], lhsT=cst[0:8, C_R, :], rhs=bsT[0:8, :], start=True, stop=True), reads=["cst", "bsT"], writes=["ps0"])
    V(lambda e: e.tensor_copy(out=bsS[:], in_=ps[0][:, 8:16]), reads=["ps0"], writes=["bsS"])
    for g in range(8):
        S.dma("sync", lambda e, g=g: e.dma_start(out=wsl[:], in_=c_w_s[g * 128:(g + 1) * 128, :]), writes=["wsl"])
        PE(lambda e: e.transpose(ps[1][:, 0:128], wsl[:], cst[:, C_ID, :]), reads=["wsl", "cst"], writes=["ps1"])
        V(lambda e, g=g: e.tensor_tensor(out=wsT[:, g, :], in0=ps[1][:, 0:128], in1=cst[:, C_TRI, :], op=ALU.mult), reads=["ps1", "cst"], writes=["wsT"])
        PE(lambda e: e.matmul(ps[1][0:8, 128:256], lhsT=wsl[0:8, 0:8], rhs=cst[0:8, C_R, :], start=True, stop=True), reads=["wsl", "cst"], writes=["ps1"])
        V(lambda e: e.tensor_copy(out=b8[:], in_=ps[1][0:8, 128:256]), reads=["ps1"], writes=["b8"])
        PE(lambda e: e.matmul(ps[1][:, 256:384], lhsT=cst[0:8, C_R, :], rhs=b8[:], start=True, stop=True), reads=["cst", "b8"], writes=["ps1"])
        V(lambda e, g=g: e.tensor_tensor(out=wsS[:, g, :], in0=ps[1][:, 256:384], in1=cst[:, C_TRI16, :], op=ALU.mult), reads=["ps1", "cst"], writes=["wsS"])

    def proj_residual(wname, nt):
        w3, wres = WSCR[wname]
        for c in range(2):
            S.dma("sync", lambda e, c=c: e.dma_start(out=wps[c][:, :, :], in_=w3[:, :, c * 512:(c + 1) * 512]), reads=[wres], writes=[f"wps{c}"])
            for ti in range(nt):
                pb_ = 4 + (ti % 2)
                for k in range(8):
                    PE(lambda e, k=k, ti=ti, c=c, pb_=pb_: e.matmul(ps[pb_][:, :], lhsT=xT[:, k, ti * 128:(ti + 1) * 128], rhs=wps[c][:, k, :], start=(k == 0), stop=(k == 7)),
                       reads=["xT", f"wps{c}"], writes=[f"ps{pb_}"])
                V(lambda e, ti=ti, c=c, pb_=pb_: e.scalar_tensor_tensor(out=xg[:, ti, c * 512:(c + 1) * 512], in0=xg[:, ti, c * 512:(c + 1) * 512], scalar=ALPHA,
                                                                      in1=ps[pb_][:, :], op0=ALU.mult, op1=ALU.add), reads=[f"ps{pb_}", f"xg{ti}"], writes=[f"xg{ti}"])

    def phaseC_group(T0, nt, sample):
        for ti in range(nt):
            r0 = (T0 + ti) * 128
            S.dma("sync", lambda e, r0=r0, ti=ti: e.dma_start(out=xg[:, ti, :], in_=x1_scr[r0:r0 + 128, :]), reads=["x1_scr"], writes=[f"xg{ti}"])
            S.dma("sync", lambda e, r0=r0: e.dma_start(out=ocat[:, 0:512], in_=oh_scr[r0:r0 + 128, :]), reads=["oh_scr"], writes=["ocat"])
            S.dma("sync", lambda e, r0=r0: e.dma_start(out=ocat[:, 512:1024], in_=of_scr[r0:r0 + 128, :]), reads=["of_scr"], writes=["ocat"])
            transpose_to(xT, "xT", ocat, "ocat", 8, ti * 128)
        proj_residual("w_out", nt)
        for ti in range(nt):
            layer_norm(xg[:, ti, :], f"xg{ti}", xg[:, ti, :], f"xg{ti}", 1)
        ffn(1, nt)
        ffn(2, nt)
        build_xT(nt)
        for ti in range(nt):
            for c in range(4):
                proj_tokmajor("c_w_in", c * 512, 512, ti, c % 2, c % 2)
                dstt = uu if c < 2 else vv
                dres = "uu" if c < 2 else "vv"
                dsl = dstt[:, (c % 2) * 512:(c % 2 + 1) * 512]
                A(lambda e, c=c: e.activation(out=gq[:], in_=ps[c % 2][:, :], func=AF.Square), reads=[f"ps{c % 2}"], writes=["gq"])
                V(lambda e: e.tensor_scalar(out=gq[:], in0=gq[:], scalar1=0.044715, scalar2=1.0, op0=ALU.mult, op1=ALU.add), reads=["gq"], writes=["gq"])
                V(lambda e, c=c: e.tensor_tensor(out=gq[:], in0=gq[:], in1=ps[c % 2][:, :], op=ALU.mult), reads=["gq", f"ps{c % 2}"], writes=["gq"])
                A(lambda e: e.activation(out=gq[:], in_=gq[:], func=AF.Sigmoid, scale=1.5957691216057308), reads=["gq"], writes=["gq"])
                V(lambda e, c=c, dsl=dsl: e.tensor_tensor(out=dsl, in0=gq[:], in1=ps[c % 2][:, :], op=ALU.mult), reads=["gq", f"ps{c % 2}"], writes=[dres])
            layer_norm(vv[:], "vv", vv[:], "vv", 0, g_ap=c_ln_g[0:1, :], b_ap=c_ln_b[0:1, :])
            if sample:
                S.dma("sync", lambda e, ti=ti: e.dma_start(out=cv_o[ti * 128:(ti + 1) * 128, :], in_=vv[:]), reads=["vv"], is_output=True)
            A(lambda e: e.activation(out=vvb[:], in_=vv[:], func=AF.Copy), reads=["vv"], writes=["vvb"])
            wsx = wsS if sample else wsT
            bsx = bsS if sample else bsT
            for g in range(8):
                bank = 4 + g // 4
                cs = slice((g % 4) * 128, (g % 4 + 1) * 128)
                PE(lambda e, g=g, bank=bank, cs=cs, wsx=wsx: e.matmul(ps[bank][:, cs], lhsT=wsx[:, g, :], rhs=vvb[:, g * 128:(g + 1) * 128], start=True, stop=True),
                   reads=["wsT", "wsS", "vvb"], writes=[f"ps{bank}"])
                V(lambda e, g=g, bank=bank, cs=cs, bsx=bsx: e.scalar_tensor_tensor(out=umx[:, g * 128:(g + 1) * 128], in0=ps[bank][:, cs], scalar=bsx[:, g:g + 1],
                                                                                   in1=uu[:, g * 128:(g + 1) * 128], op0=ALU.add, op1=ALU.mult),
                  reads=[f"ps{bank}", "bsT", "bsS", "uu"], writes=["umx"])
            transpose_to(xT, "xT", umx, "umx", 8, ti * 128)
        proj_residual("c_w_out", nt)
        for ti in range(nt):
            layer_norm(xg[:, ti, :], f"xg{ti}", xg[:, ti, :], f"xg{ti}", 4)
        ffn(3, nt)
        for ti in range(nt):
            r0 = (T0 + ti) * 128
            S.dma("sync", lambda e, r0=r0, ti=ti: e.dma_start(out=y_o[r0:r0 + 128, :], in_=xg[:, ti, :]), reads=[f"xg{ti}"], is_output=True)

    if 'C' in PH:
        for G in range(NPT // 4):
            phaseC_group(G * 4, 4, False)
        phaseC_group(SMP, NS, True)

    print('sbuf bytes remaining', nc.sbuf_bytes_remaining)
    S.emit()
    print('instr counts', {e: len(S.q[e]) for e in ENGS})
    return nc


_NC_CACHE = {}
NCORES = 2
NS_ = 4


def kernel(x_prompt, x_sample, cache_fox_k, cache_fox_v, cache_fox_logf, state_hg, page_table,
           ln_g, ln_b, ffn_w_gate, ffn_w_up, ffn_w_down, ab_w_in, hg_lb_logits, hg_norm_g, fox_f_bias,
           ab_w_out, c_w_in, c_ln_g, c_ln_b, c_w_s, c_b_s, c_w_out):
    f = lambda a: np.ascontiguousarray(np.asarray(a, dtype=np.float32))
    x_prompt = f(x_prompt); x_sample = f(x_sample)
    P = x_prompt.shape[1]
    nphys = int(os.environ.get('KNPHYS', str(N_PHYS)))
    if nphys != N_PHYS:
        cache_fox_k = np.asarray(cache_fox_k)[:, :nphys]; cache_fox_v = np.asarray(cache_fox_v)[:, :nphys]
        cache_fox_logf = np.asarray(cache_fox_logf)[:, :nphys]; page_table = np.asarray(page_table) % nphys
    if P not in _NC_CACHE:
        _NC_CACHE[P] = build_program(P // 128, nphys, NS_)
    nc = _NC_CACHE[P]
    shared = {
        "cache_k": f(cache_fox_k).reshape(nphys * 128, 512),
        "cache_v": f(cache_fox_v).reshape(nphys * 128, 512),
        "cache_lf": f(cache_fox_logf).reshape(nphys * 128, 8),
        "ln_g": f(ln_g).reshape(6, D), "ln_b": f(ln_b).reshape(6, D),
        "w_gate": f(ffn_w_gate).reshape(4, D, FF), "w_up": f(ffn_w_up).reshape(4, D, FF),
        "w_down": f(ffn_w_down).reshape(4, FF, D),
        "ab_w_in": f(ab_w_in).reshape(D, 3592), "lb_logits": f(hg_lb_logits).reshape(1, 1536),
        "hg_norm_g": f(hg_norm_g).reshape(1, 128), "fox_f_bias": f(fox_f_bias).reshape(1, 8),
        "ab_w_out": f(ab_w_out).reshape(D, D), "c_w_in": f(c_w_in).reshape(D, 2048),
        "c_ln_g": f(c_ln_g).reshape(1, D), "c_ln_b": f(c_ln_b).reshape(1, D),
        "c_w_s": f(c_w_s).reshape(1024, 128), "c_b_s": f(c_b_s).reshape(8, 128),
        "c_w_out": f(c_w_out).reshape(D, D), "cst": make_consts().reshape(128, NCST * 128),
    }
    pt = np.asarray(page_table, dtype=np.int32)
    SQ = 16 * NS_
    in_maps = []
    for c in range(NCORES):
        m = dict(shared)
        m["xin"] = np.concatenate([x_prompt[c], x_sample[SQ * c:SQ * (c + 1)].reshape(SQ * 8, D)], axis=0)
        m["state_hg"] = f(state_hg)[0, SQ * c:SQ * (c + 1)].reshape(SQ * 4 * 128, 128)
        m["ptab"] = np.ascontiguousarray(pt[SQ * c:SQ * (c + 1)].reshape(1, SQ * 16))
        in_maps.append(m)
    ncores = int(os.environ.get('KCORES', str(NCORES)))
    res = run_bass_kernel_spmd(nc, in_maps[:ncores], core_ids=list(range(ncores))).results
    res = list(res) + [res[0]] * (NCORES - ncores)
    R = range(NCORES)
    y_p = np.stack([res[b]["y"][:P] for b in R])
    y_s = np.concatenate([res[c]["y"][P:].reshape(SQ, 8, D) for c in R])
    k_p = np.stack([res[b]["k_new"][:P].reshape(P, 8, 64) for b in R])[None]
    v_p = np.stack([res[b]["v_new"][:P].reshape(P, 8, 64) for b in R])[None]
    lf_p = np.stack([res[b]["lf_new"][:P] for b in R])[None]
    hg_p = np.stack([res[b]["hg_p"].reshape(4, 128, 128) for b in R])[None]
    k_s = np.concatenate([res[c]["k_new"][P:].reshape(SQ, 8, 8, 64) for c in R])[None]
    v_s = np.concatenate([res[c]["v_new"][P:].reshape(SQ, 8, 8, 64) for c in R])[None]
    lf_s = np.concatenate([res[c]["lf_new"][P:].reshape(SQ, 8, 8) for c in R])[None]
    hg_s = np.concatenate([res[c]["hg_s"].reshape(SQ, 4, 128, 128) for c in R])[None]
    cv_s = np.concatenate([res[c]["cv_s"].reshape(SQ, 8, D) for c in R])[None]
    return (y_p, y_s, k_p, v_p, lf_p, hg_p, k_s, v_s, lf_s, hg_s, cv_s)
```
